# Optimizing a Trainium2 kernel written in Bass

```python
import math
import jax, jax.numpy as jnp
from jax import lax
import numpy as np

D_MODEL = 1024
BATCH = 4
SEQ = 8192
DEPTH = 2

CHUNK = 64
QBLOCK = 128
PLE_DIM = 256
N_BRANCH = 4
GDN_HEADS = 4
GDN_HEAD_DIM = 128
GDN_WIDTH = GDN_HEADS * GDN_HEAD_DIM
GDN_CONV = 4
CONV_WIDTH = D_MODEL // 2
CONV_K = 31
SB_HEADS = 8
SB_HEAD_DIM = 64
SB_WIDTH = SB_HEADS * SB_HEAD_DIM
FOX_HEADS = 8
FOX_HEAD_DIM = 64
FOX_WIDTH = FOX_HEADS * FOX_HEAD_DIM
BRANCH_WIDTH = 512
ALPHA = (2 * DEPTH) ** 0.25
BETA = (8 * DEPTH) ** -0.25
EPS = 1e-5

SIZES = ([GDN_WIDTH] * 4 + [GDN_HEADS, GDN_HEADS]
         + [2 * CONV_WIDTH, CONV_WIDTH]
         + [SB_WIDTH] * 4
         + [FOX_WIDTH] * 4 + [FOX_HEADS]
         + [N_BRANCH * D_MODEL])
PROJ_WIDTH = sum(SIZES)
SPLIT_POINTS = [sum(SIZES[:i + 1]) for i in range(len(SIZES) - 1)]

kernel_name = "hybrid_gdn_conformer_stickbreak_fox_encoder"


def layer_norm(x, g, b):
    xf = x.astype(jnp.float32)
    mu = jnp.mean(xf, axis=-1, keepdims=True)
    var = jnp.mean(jnp.square(xf - mu), axis=-1, keepdims=True)
    return ((xf - mu) * lax.rsqrt(var + EPS) * g.astype(jnp.float32) + b.astype(jnp.float32)).astype(x.dtype)


def rms_norm(x, g):
    xf = x.astype(jnp.float32)
    return xf * lax.rsqrt(jnp.mean(jnp.square(xf), axis=-1, keepdims=True) + EPS) * g.astype(jnp.float32)


def l2_normalize(x):
    xf = x.astype(jnp.float32)
    return xf * lax.rsqrt(jnp.sum(jnp.square(xf), axis=-1, keepdims=True) + 1e-6)


def causal_dwconv(x, w):
    k_width, c = w.shape
    xp = jnp.pad(x, ((0, 0), (k_width - 1, 0), (0, 0)))
    return lax.conv_general_dilated(xp, w[:, None, :].astype(x.dtype), window_strides=(1,), padding='VALID',
                                    dimension_numbers=('NWC', 'WIO', 'NWC'), feature_group_count=c)


def gated_delta_rule(q, k, v, g, beta):
    B, T, H, dk = q.shape
    dv = v.shape[-1]
    n = T // CHUNK
    f32 = jnp.float32

    def chunks(a):
        a = a.astype(f32).reshape((B, n, CHUNK, H) + a.shape[3:])
        return jnp.moveaxis(a, 3, 1)

    qc = chunks(q) * dk ** -0.5
    kc, vc, bc = chunks(k), chunks(v), chunks(beta)
    gc = jnp.cumsum(chunks(g), axis=-1)
    idx = jnp.arange(CHUNK)
    incl = idx[:, None] >= idx[None, :]
    strict = idx[:, None] > idx[None, :]
    decay = jnp.where(incl, jnp.exp(jnp.where(incl, gc[..., :, None] - gc[..., None, :], 0.0)), 0.0)
    kb = kc * bc[..., None]
    lower = jnp.where(strict, jnp.einsum('bhncd,bhnsd->bhncs', kb, kc) * decay, 0.0)
    eye = jnp.eye(CHUNK, dtype=f32)
    rhs = jnp.concatenate([vc * bc[..., None], kb * jnp.exp(gc)[..., None]], axis=-1)
    sol = lax.linalg.triangular_solve(eye + lower, rhs, left_side=True, lower=True, unit_diagonal=True)
    u, w = sol[..., :dv], sol[..., dv:]
    a_qk = jnp.einsum('bhncd,bhnsd->bhncs', qc, kc) * decay
    q_dec = qc * jnp.exp(gc)[..., None]
    k_dec = kc * jnp.exp(gc[..., -1:] - gc)[..., None]
    g_last = jnp.exp(gc[..., -1])

    def step(S, xs):
        a_i, q_i, u_i, w_i, k_i, gl = xs
        v_new = u_i - jnp.einsum('bhck,bhkv->bhcv', w_i, S)
        o = jnp.einsum('bhck,bhkv->bhcv', q_i, S) + jnp.einsum('bhcs,bhsv->bhcv', a_i, v_new)
        S = S * gl[..., None, None] + jnp.einsum('bhck,bhcv->bhkv', k_i, v_new)
        return S, o

    xs = tuple(jnp.moveaxis(a, 2, 0) for a in (a_qk, q_dec, u, w, k_dec, g_last))
    _, o = lax.scan(step, jnp.zeros((B, H, dk, dv), f32), xs)
    return o.transpose(1, 0, 3, 2, 4).reshape(B, T, H, dv)


def stick_breaking_attention(q, k, v):
    B, T, H, d = q.shape
    nb = T // QBLOCK
    scale = d ** -0.5
    kh = k.transpose(0, 2, 1, 3)
    vh = v.transpose(0, 2, 1, 3)
    qb = q.reshape(B, nb, QBLOCK, H, d).transpose(1, 0, 3, 2, 4)
    kpos = jnp.arange(T)

    def block(args):
        qi, i = args
        qpos = i * QBLOCK + jnp.arange(QBLOCK)
        mask = kpos[None, :] < qpos[:, None]
        z = jnp.einsum('bhqd,bhsd->bhqs', qi, kh).astype(jnp.float32) * scale
        log_beta = jax.nn.log_sigmoid(z)
        log_rest = jnp.where(mask, jax.nn.log_sigmoid(-z), 0.0)
        between = lax.cumsum(log_rest, axis=3, reverse=True) - log_rest
        att = jnp.where(mask, jnp.exp(log_beta + between), 0.0)
        return jnp.einsum('bhqs,bhsd->bhqd', att, vh.astype(jnp.float32)).astype(q.dtype)

    o = lax.map(block, (qb, jnp.arange(nb)))
    return o.transpose(1, 0, 3, 2, 4).reshape(B, T, H, d)


def forgetting_attention(q, k, v, log_f):
    B, T, H, d = q.shape
    nb = T // QBLOCK
    scale = d ** -0.5
    F = jnp.cumsum(log_f.astype(jnp.float32), axis=1).transpose(0, 2, 1)
    kh = k.transpose(0, 2, 1, 3)
    vh = v.transpose(0, 2, 1, 3)
    qb = q.reshape(B, nb, QBLOCK, H, d).transpose(1, 0, 3, 2, 4)
    Fq = F.reshape(B, H, nb, QBLOCK).transpose(2, 0, 1, 3)
    kpos = jnp.arange(T)

    def block(args):
        qi, fi, i = args
        qpos = i * QBLOCK + jnp.arange(QBLOCK)
        z = jnp.einsum('bhqd,bhsd->bhqs', qi, kh).astype(jnp.float32) * scale
        z = z + fi[..., :, None] - F[:, :, None, :]
        z = jnp.where(kpos[None, :] <= qpos[:, None], z, -jnp.inf)
        att = jax.nn.softmax(z, axis=-1)
        return jnp.einsum('bhqs,bhsd->bhqd', att, vh.astype(jnp.float32)).astype(q.dtype)

    o = lax.map(block, (qb, Fq, jnp.arange(nb)))
    return o.transpose(1, 0, 3, 2, 4).reshape(B, T, H, d)


def hybrid_layer(x, p_i, w_in, b_gate, conv_qkv, a_log, dt_bias, gdn_norm, conv_dw, conv_dw_bias,
                 conv_ln_g, conv_ln_b, forget_bias, w_branch, w_out, w_ple, w_ple_gate, b_ple_gate,
                 ln_g, ln_b):
    B, T, _ = x.shape
    dt = x.dtype
    proj = x @ w_in
    (qA, kA, vA, zA, aA, bA, glu, zB, qC, kC, vC, zC,
     qD, kD, vD, zD, fD, gates) = jnp.split(proj, SPLIT_POINTS, axis=-1)

    qkvA = jax.nn.silu(causal_dwconv(jnp.concatenate([qA, kA, vA], axis=-1), conv_qkv))
    qA, kA, vA = jnp.split(qkvA, 3, axis=-1)
    qA = l2_normalize(qA.reshape(B, T, GDN_HEADS, GDN_HEAD_DIM))
    kA = l2_normalize(kA.reshape(B, T, GDN_HEADS, GDN_HEAD_DIM))
    vA = vA.reshape(B, T, GDN_HEADS, GDN_HEAD_DIM)
    gA = -jnp.exp(a_log.astype(jnp.float32)) * jax.nn.softplus(aA.astype(jnp.float32) + dt_bias.astype(jnp.float32))
    betaA = jax.nn.sigmoid(bA.astype(jnp.float32))
    oA = gated_delta_rule(qA, kA, vA, gA, betaA)
    yA = rms_norm(oA, gdn_norm).reshape(B, T, GDN_WIDTH).astype(dt) * jax.nn.silu(zA)

    g_lin, g_gate = jnp.split(glu, 2, axis=-1)
    hB = g_lin * jax.nn.sigmoid(g_gate)
    hB = causal_dwconv(hB, conv_dw) + conv_dw_bias
    hB = jax.nn.silu(layer_norm(hB, conv_ln_g, conv_ln_b))
    yB = hB * jax.nn.silu(zB)

    oC = stick_breaking_attention(qC.reshape(B, T, SB_HEADS, SB_HEAD_DIM),
                                  kC.reshape(B, T, SB_HEADS, SB_HEAD_DIM),
                                  vC.reshape(B, T, SB_HEADS, SB_HEAD_DIM))
    yC = oC.reshape(B, T, SB_WIDTH) * jax.nn.silu(zC)

    log_f = jax.nn.log_sigmoid(fD.astype(jnp.float32) + forget_bias.astype(jnp.float32))
    oD = forgetting_attention(qD.reshape(B, T, FOX_HEADS, FOX_HEAD_DIM),
                              kD.reshape(B, T, FOX_HEADS, FOX_HEAD_DIM),
                              vD.reshape(B, T, FOX_HEADS, FOX_HEAD_DIM), log_f)
    yD = oD.reshape(B, T, FOX_WIDTH) * jax.nn.silu(zD)

    gate = jax.nn.sigmoid(gates + b_gate).reshape(B, T, N_BRANCH, D_MODEL)
    merged = gate[:, :, 0] * (yA @ w_branch[0])
    merged = merged + gate[:, :, 1] * (yB @ w_branch[1])
    merged = merged + gate[:, :, 2] * (yC @ w_branch[2])
    merged = merged + gate[:, :, 3] * (yD @ w_branch[3])
    mix = merged @ w_out

    r = ALPHA * x + mix
    r = r + jax.nn.sigmoid(r @ w_ple_gate + b_ple_gate) * (p_i @ w_ple)
    return layer_norm(r, ln_g, ln_b)


def setup_inputs(seed: int = 0) -> dict:
    key = jax.random.key(seed)
    ks = jax.random.split(key, 24)
    f32 = jnp.float32

    def nrm(k, shape, s):
        return jax.random.normal(k, shape, f32) * s

    dt_init = jnp.exp(jax.random.uniform(ks[6], (DEPTH, GDN_HEADS), f32, math.log(1e-3), math.log(1e-1)))
    return {
        "x": nrm(ks[0], (BATCH, SEQ, D_MODEL), 1.0),
        "p": nrm(ks[1], (DEPTH, BATCH, SEQ, PLE_DIM), 1.0),
        "w_in": nrm(ks[2], (DEPTH, D_MODEL, PROJ_WIDTH), D_MODEL ** -0.5),
        "b_gate": nrm(ks[3], (DEPTH, N_BRANCH * D_MODEL), 0.02),
        "conv_qkv": nrm(ks[4], (DEPTH, GDN_CONV, 3 * GDN_WIDTH), GDN_CONV ** -0.5),
        "a_log": jnp.log(jax.random.uniform(ks[5], (DEPTH, GDN_HEADS), f32, 1.0, 16.0)),
        "dt_bias": dt_init + jnp.log(-jnp.expm1(-dt_init)),
        "gdn_norm": 1.0 + nrm(ks[7], (DEPTH, GDN_HEAD_DIM), 0.02),
        "conv_dw": nrm(ks[8], (DEPTH, CONV_K, CONV_WIDTH), CONV_K ** -0.5),
        "conv_dw_bias": nrm(ks[9], (DEPTH, CONV_WIDTH), 0.02),
        "conv_ln_g": 1.0 + nrm(ks[10], (DEPTH, CONV_WIDTH), 0.02),
        "conv_ln_b": nrm(ks[11], (DEPTH, CONV_WIDTH), 0.02),
        "forget_bias": 2.0 + nrm(ks[12], (DEPTH, FOX_HEADS), 0.5),
        "w_branch": nrm(ks[13], (DEPTH, N_BRANCH, BRANCH_WIDTH, D_MODEL), BETA * BRANCH_WIDTH ** -0.5),
        "w_out": nrm(ks[14], (DEPTH, D_MODEL, D_MODEL), BETA * D_MODEL ** -0.5),
        "w_ple": nrm(ks[15], (DEPTH, PLE_DIM, D_MODEL), PLE_DIM ** -0.5),
        "w_ple_gate": nrm(ks[16], (DEPTH, D_MODEL, D_MODEL), D_MODEL ** -0.5),
        "b_ple_gate": nrm(ks[17], (DEPTH, D_MODEL), 0.02),
        "ln_g": 1.0 + nrm(ks[18], (DEPTH, D_MODEL), 0.02),
        "ln_b": nrm(ks[19], (DEPTH, D_MODEL), 0.02),
    }


def reference(x, p, w_in, b_gate, conv_qkv, a_log, dt_bias, gdn_norm, conv_dw, conv_dw_bias,
              conv_ln_g, conv_ln_b, forget_bias, w_branch, w_out, w_ple, w_ple_gate, b_ple_gate,
              ln_g, ln_b):
    for i in range(DEPTH):
        x = hybrid_layer(x, p[i], w_in[i], b_gate[i], conv_qkv[i], a_log[i], dt_bias[i], gdn_norm[i],
                         conv_dw[i], conv_dw_bias[i], conv_ln_g[i], conv_ln_b[i], forget_bias[i],
                         w_branch[i], w_out[i], w_ple[i], w_ple_gate[i], b_ple_gate[i], ln_g[i], ln_b[i])
    return x
```

```python
import numpy as np
from contextlib import ExitStack
import concourse.bass as bass
import concourse.mybir as mybir
from concourse.bass_utils import run_bass_kernel_spmd

F32 = mybir.dt.float32
BF16 = mybir.dt.bfloat16
AF = mybir.ActivationFunctionType
ALU = mybir.AluOpType

D = 1024
PW = 11792
ALPHA = 4 ** 0.25
EPS = 1e-5
SEM_EPOCH = 20000
DBG_STOP = 0

O_QA, O_KA, O_VA, O_ZA, O_AA, O_BA = 0, 512, 1024, 1536, 2048, 2052
O_GL, O_GG, O_ZB = 2056, 2568, 3080
O_QC, O_KC, O_VC, O_ZC = 3592, 4104, 4616, 5128
O_QD, O_KD, O_VD, O_ZD, O_FD = 5640, 6152, 6664, 7176, 7688
O_G = 7696


class Ctx:
    def __init__(self, nc, es):
        self.nc, self.es = nc, es
        self.eng = {"pe": nc.tensor, "act": nc.scalar, "dve": nc.vector, "pool": nc.gpsimd, "sp": nc.sync}
        self.sem = {}
        self.cnt = {}
        self.nsem = 0
        for e in self.eng:
            self._new_sem(e)
        self.waited = {e: {} for e in self.eng}
        self.lastw = {}
        self.readers = {}
        self.dsem = [es.enter_context(nc.semaphore("dq%d" % i)) for i in range(24)]
        self.dcnt = [0] * 24
        self.dnext = 0
        self.ninst = 0

    def _new_sem(self, e):
        self.sem[e] = self.es.enter_context(self.nc.semaphore("s_%s_%d" % (e, self.nsem)))
        self.nsem += 1
        self.cnt[e] = 0

    def _wait(self, e, tok):
        sem, val, src = tok
        if src == "pe" and e == "pe":
            return
        w = self.waited[e]
        k = id(sem)
        if w.get(k, 0) >= val:
            return
        w[k] = val
        self.eng[e].wait_ge(sem, val)

    def _deps(self, e, reads, writes):
        for k in reads:
            t = self.lastw.get(k)
            if t is not None:
                self._wait(e, t)
        for k in writes:
            t = self.lastw.get(k)
            if t is not None:
                self._wait(e, t)
            rd = self.readers.get(k)
            if rd:
                for key, t in rd.items():
                    self._wait(e, t)

    def _commit(self, tok, reads, writes):
        for k in writes:
            self.lastw[k] = tok
            self.readers[k] = {}
        for k in reads:
            rd = self.readers.setdefault(k, {})
            if tok[2] == "dma":
                rd[("dma", id(tok[0]))] = tok
            else:
                rd[tok[2]] = tok

    def op(self, e, fn, reads=(), writes=()):
        self._deps(e, reads, writes)
        ins = fn(self.eng[e])
        if self.cnt[e] >= SEM_EPOCH:
            self._new_sem(e)
        self.cnt[e] += 1
        ins.then_inc(self.sem[e], 1)
        tok = (self.sem[e], self.cnt[e], e)
        self._commit(tok, reads, writes)
        self.ninst += 1
        return tok

    def dma(self, out, in_, reads=(), writes=(), q="sp"):
        j = self.dnext
        self.dnext = (self.dnext + 1) % len(self.dsem)
        self._deps(q, reads, writes)
        if self.dcnt[j] > 0:
            self._wait(q, (self.dsem[j], self.dcnt[j], "dma"))
        self.eng[q].dma_start(out=out, in_=in_).then_inc(self.dsem[j], 16)
        self.dcnt[j] += 16
        tok = (self.dsem[j], self.dcnt[j], "dma")
        self._commit(tok, reads, writes)
        self.ninst += 1
        return tok

    def finish(self):
        for j in range(len(self.dsem)):
            if self.dcnt[j] > 0:
                self._wait("sp", (self.dsem[j], self.dcnt[j], "dma"))
        for e in ("pe", "act", "dve", "pool"):
            if self.cnt[e] > 0:
                self._wait("sp", (self.sem[e], self.cnt[e], e))

    def sb(self, name, shape, dt):
        return self.pes.enter_context(self.nc.sbuf_tensor("%s_%d" % (name, self.phase_no), shape, dt))

    def ps(self, name, shape, dt=F32):
        return self.pes.enter_context(self.nc.psum_tensor("%s_%d" % (name, self.phase_no), shape, dt))

    def begin_phase(self):
        self.pes = ExitStack()
        self.phase_no = getattr(self, "phase_no", 0) + 1

    def end_phase(self):
        self.barrier()
        self.pes.close()

    def barrier(self):
        for e in self.eng:
            for j in range(len(self.dsem)):
                if self.dcnt[j] > 0:
                    self._wait(e, (self.dsem[j], self.dcnt[j], "dma"))
            for f in ("pe", "act", "dve", "pool"):
                if f != e and self.cnt[f] > 0:
                    self._wait(e, (self.sem[f], self.cnt[f], f))
        self.lastw.clear()
        self.readers.clear()


def proj_chunks():
    ch = []

    def add(o, n, kind):
        for i in range(n // 128):
            ch.append((o + 128 * i, 128, kind))
    add(O_QA, 1536, 0)
    add(O_GL, 512, 0)
    add(O_KC, 1024, 0)
    add(O_KD, 1024, 0)
    add(O_QC, 512, 3)
    add(O_QD, 512, 3)
    add(O_ZA, 512, 1)
    add(O_ZB, 512, 1)
    add(O_ZC, 512, 1)
    add(O_ZD, 512, 1)
    add(O_GG, 512, 2)
    add(O_G, 4096, 4)
    return ch


def phase_proj(c, T, xT, w_in, b_gate, projT, smallT):
    nc = c.nc
    c.begin_phase()
    ST = min(4096, T)
    nst = T // ST
    nsub = ST // 512
    xs = [c.sb("p1_xs%d" % i, [128, 8, 512], F32) for i in range(2)]
    xb = c.sb("p1_xb", [128, 8, ST], BF16)
    ws = [c.sb("p1_ws%d" % i, [128, 8, 512], F32) for i in range(2)]
    wb = [c.sb("p1_wb%d" % i, [128, 8, 512], BF16) for i in range(2)]
    wss = c.sb("p1_wss", [128, 8, 16], F32)
    wsb = c.sb("p1_wsb", [128, 8, 16], BF16)
    bg = c.sb("p1_bg", [128, 32], F32)
    ob = [c.sb("p1_ob%d" % i, [128, 512], BF16) for i in range(4)]
    osm = c.sb("p1_osm", [16, 512], F32)
    pss = [c.ps("p1_ps%d" % i, [128, 512]) for i in range(4)]
    c.dma(bg[:], b_gate, writes=["p1_bg"])
    c.dma(wss[:, :, 0:8], w_in[:, O_AA:O_AA + 8].rearrange("(k p) n -> p k n", p=128), writes=["p1_wss"])
    c.dma(wss[:, :, 8:16], w_in[:, O_FD:O_FD + 8].rearrange("(k p) n -> p k n", p=128), writes=["p1_wss"])
    c.op("pool", lambda e: e.tensor_copy(out=wsb[:], in_=wss[:]), reads=["p1_wss"], writes=["p1_wsb"])
    chunks = proj_chunks()
    groups = [chunks[i:i + 4] for i in range(0, len(chunks), 4)]
    xTv = xT.rearrange("(k p) t -> p k t", p=128)
    w_v = w_in.rearrange("(k p) n -> p k n", p=128)
    it = 0
    for st in range(nst):
        for sub in range(nsub):
            t0 = st * ST + sub * 512
            s = xs[(st * nsub + sub) % 2]
            key = "p1_xs%d" % ((st * nsub + sub) % 2)
            c.dma(s[:], xTv[:, :, t0:t0 + 512], writes=[key])
            c.op("pool", lambda e: e.tensor_copy(out=xb[:, :, sub * 512:(sub + 1) * 512], in_=s[:]),
                 reads=[key], writes=["p1_xb%d" % sub])
        for sub in range(nsub):
            t0 = st * ST + sub * 512
            p = pss[it % 4]; pk = "p1_ps%d" % (it % 4)
            for k in range(8):
                c.op("pe", lambda e: e.matmul(p[0:16, :], wsb[:, k, :], xb[:, k, sub * 512:(sub + 1) * 512],
                                              start=(k == 0), stop=(k == 7)),
                     reads=["p1_wsb", "p1_xb%d" % sub], writes=[pk])
            c.op("dve", lambda e: e.tensor_copy(out=osm[:], in_=p[0:16, :]), reads=[pk], writes=["p1_osm"])
            c.dma(smallT[:, t0:t0 + 512], osm[:], reads=["p1_osm"])
            it += 1
        for gi, grp_ in enumerate(groups):
            g0 = grp_[0][0]
            assert all(grp_[j][0] == g0 + 128 * j for j in range(len(grp_)))
            gw = 128 * len(grp_)
            wsl = ws[gi % 2]; wbl = wb[gi % 2]
            c.dma(wsl[:, :, 0:gw], w_v[:, :, g0:g0 + gw], writes=["p1_ws%d" % (gi % 2)])
            c.op("pool", lambda e: e.tensor_copy(out=wbl[:, :, 0:gw], in_=wsl[:, :, 0:gw]), reads=["p1_ws%d" % (gi % 2)],
                 writes=["p1_wb%d" % (gi % 2)])
            for cj, (off, wd, kind) in enumerate(grp_):
                for sub in range(nsub):
                    t0 = st * ST + sub * 512
                    p = pss[it % 4]; pk = "p1_ps%d" % (it % 4)
                    o = ob[it % 4]; ok = "p1_ob%d" % (it % 4)
                    for k in range(8):
                        c.op("pe", lambda e: e.matmul(p[:], wbl[:, k, cj * 128:(cj + 1) * 128],
                                                      xb[:, k, sub * 512:(sub + 1) * 512],
                                                      start=(k == 0), stop=(k == 7)),
                             reads=["p1_wb%d" % (gi % 2), "p1_xb%d" % sub], writes=[pk])
                    if kind == 0:
                        c.op("dve", lambda e: e.tensor_copy(out=o[:], in_=p[:]), reads=[pk], writes=[ok])
                    elif kind == 3:
                        c.op("dve", lambda e: e.tensor_scalar(out=o[:], in0=p[:], scalar1=0.125, scalar2=None,
                                                              op0=ALU.mult), reads=[pk], writes=[ok])
                    elif kind == 1:
                        c.op("act", lambda e: e.activation(out=o[:], in_=p[:], func=AF.Silu), reads=[pk], writes=[ok])
                    elif kind == 2:
                        c.op("act", lambda e: e.activation(out=o[:], in_=p[:], func=AF.Sigmoid), reads=[pk], writes=[ok])
                    else:
                        gidx = (off - O_G) // 128
                        c.op("act", lambda e: e.activation(out=o[:], in_=p[:], func=AF.Sigmoid, bias=bg[:, gidx:gidx + 1]),
                             reads=[pk, "p1_bg"], writes=[ok])
                    c.dma(projT[off:off + 128, t0:t0 + 512], o[:], reads=[ok])
                    it += 1
    c.end_phase()


def load_w_bf16(c, dst, dst_key, src_rows, stg, n):
    i = c.stg_i = getattr(c, "stg_i", 0) + 1
    s = stg[i % len(stg)]
    sk = "stg%d" % (i % len(stg))
    c.dma(s[:, 0:n], src_rows, writes=[sk])
    c.op("pool", lambda e: e.tensor_copy(out=dst, in_=s[:, 0:n]), reads=[sk], writes=[dst_key])


def phase_out(c, T, xT, pT, yT, projT, w_branch, w_out, w_ple, w_pg, b_pg, ln_g, ln_b, ones_f, outT):
    nc = c.nc
    c.begin_phase()
    TT = 256
    stg = [c.sb("po_stg%d" % i, [128, 1024], F32) for i in range(2)]
    wbr = c.sb("po_wbr", [128, 16, 1024], BF16)
    wo = c.sb("po_wo", [128, 8, 1024], BF16)
    wpg = c.sb("po_wpg", [128, 8, 1024], BF16)
    wpl = c.sb("po_wpl", [128, 2, 1024], BF16)
    vb = c.sb("po_vec", [128, 3, 8], F32)
    ones = c.sb("po_ones", [128, 128], F32)
    c.dma(vb[:, 0, :], b_pg, writes=["po_vec"])
    c.dma(vb[:, 1, :], ln_g, writes=["po_vec"])
    c.dma(vb[:, 2, :], ln_b, writes=["po_vec"])
    c.dma(ones[:], ones_f, writes=["po_ones"])
    wbv = w_branch.rearrange("b (k p) n -> p (b k) n", p=128)
    for i in range(16):
        load_w_bf16(c, wbr[:, i, :], "po_wbr", wbv[:, i, :], stg, 1024)
    for nm, dst, src, nk in (("po_wo", wo, w_out, 8), ("po_wpg", wpg, w_pg, 8), ("po_wpl", wpl, w_ple, 2)):
        sv = src.rearrange("(k p) n -> p k n", p=128)
        for i in range(nk):
            load_w_bf16(c, dst[:, i, :], nm, sv[:, i, :], stg, 1024)
    y = c.sb("po_y", [128, 16, TT], BF16)
    gt = [c.sb("po_gt%d" % i, [128, 8, TT], BF16) for i in range(2)]
    x = c.sb("po_x", [128, 8, TT], F32)
    pf = c.sb("po_pf", [128, 2, TT], F32)
    pb = c.sb("po_pb", [128, 2, TT], BF16)
    m = c.sb("po_m", [128, 8, TT], F32)
    mb = c.sb("po_mb", [128, 8, TT], BF16)
    r = c.sb("po_r", [128, 8, TT], F32)
    rb = c.sb("po_rb", [128, 8, TT], BF16)
    tmp = [c.sb("po_tmp%d" % i, [128, TT], F32) for i in range(2)]
    gp = [c.sb("po_gp%d" % i, [128, TT], F32) for i in range(2)]
    sq = [c.sb("po_sq%d" % i, [128, TT], F32) for i in range(2)]
    mean = c.sb("po_mean", [128, TT], F32)
    msq = c.sb("po_msq", [128, TT], F32)
    rstd = c.sb("po_rstd", [128, TT], F32)
    ot = [c.sb("po_ot%d" % i, [128, TT], F32) for i in range(2)]
    ps = [c.ps("po_ps%d" % i, [128, 512]) for i in range(4)]
    pstat = [c.ps("po_pst%d" % i, [128, 512]) for i in range(2)]
    yv = yT.rearrange("(k p) t -> p k t", p=128)
    xv = xT.rearrange("(k p) t -> p k t", p=128)
    pv = pT.rearrange("(k p) t -> p k t", p=128)
    ov = outT.rearrange("(k p) t -> p k t", p=128)
    it = 0
    for tt in range(T // TT):
        t0 = tt * TT
        c.dma(y[:], yv[:, :, t0:t0 + TT], writes=["po_y"])
        c.dma(x[:], xv[:, :, t0:t0 + TT], writes=["po_x"])
        c.dma(pf[:], pv[:, :, t0:t0 + TT], writes=["po_pf"])
        c.op("pool", lambda e: e.tensor_copy(out=pb[:], in_=pf[:]), reads=["po_pf"], writes=["po_pb"])
        for br in range(4):
            g = gt[br % 2]; gk = "po_gt%d" % (br % 2)
            c.dma(g[:], projT[O_G + br * 1024:O_G + (br + 1) * 1024, t0:t0 + TT].rearrange("(k p) t -> p k t", p=128),
                  writes=[gk])
            for fo in range(8):
                p = ps[it % 4]; pk = "po_ps%d" % (it % 4); it += 1
                for kc in range(4):
                    c.op("pe", lambda e: e.matmul(p[:, 0:TT], wbr[:, br * 4 + kc, fo * 128:(fo + 1) * 128],
                                                  y[:, br * 4 + kc, :], start=(kc == 0), stop=(kc == 3)),
                         reads=["po_wbr", "po_y"], writes=[pk])
                mk = "po_m%d" % fo
                if br == 0:
                    c.op("dve", lambda e: e.tensor_tensor(out=m[:, fo, :], in0=p[:, 0:TT], in1=g[:, fo, :], op=ALU.mult),
                         reads=[pk, gk], writes=[mk])
                else:
                    t_ = tmp[it % 2]; tk = "po_tmp%d" % (it % 2)
                    c.op("dve", lambda e: e.tensor_tensor(out=t_[:], in0=p[:, 0:TT], in1=g[:, fo, :], op=ALU.mult),
                         reads=[pk, gk], writes=[tk])
                    if br < 3:
                        c.op("pool", lambda e: e.tensor_tensor(out=m[:, fo, :], in0=m[:, fo, :], in1=t_[:], op=ALU.add),
                             reads=[mk, tk], writes=[mk])
                    else:
                        c.op("pool", lambda e: e.tensor_tensor(out=mb[:, fo, :], in0=m[:, fo, :], in1=t_[:], op=ALU.add),
                             reads=[mk, tk], writes=["po_mb%d" % fo])
        for fo in range(8):
            p = ps[it % 4]; pk = "po_ps%d" % (it % 4); it += 1
            for k in range(8):
                c.op("pe", lambda e: e.matmul(p[:, 0:TT], wo[:, k, fo * 128:(fo + 1) * 128], mb[:, k, :],
                                              start=(k == 0), stop=(k == 7)),
                     reads=["po_wo"] + ["po_mb%d" % k], writes=[pk])
            c.op("dve", lambda e: e.scalar_tensor_tensor(out=r[:, fo, :], in0=x[:, fo, :], scalar=ALPHA, in1=p[:, 0:TT],
                                                         op0=ALU.mult, op1=ALU.add),
                 reads=[pk, "po_x"], writes=["po_r%d" % fo])
            c.op("act", lambda e: e.activation(out=rb[:, fo, :], in_=r[:, fo, :], func=AF.Identity),
                 reads=["po_r%d" % fo], writes=["po_rb%d" % fo])
        pst_s = pstat[0]; pst_q = pstat[1]
        for fo in range(8):
            p = ps[it % 4]; pk = "po_ps%d" % (it % 4); it += 1
            for k in range(8):
                c.op("pe", lambda e: e.matmul(p[:, 0:TT], wpg[:, k, fo * 128:(fo + 1) * 128], rb[:, k, :],
                                              start=(k == 0), stop=(k == 7)),
                     reads=["po_wpg", "po_rb%d" % k], writes=[pk])
            g_ = gp[fo % 2]; gk = "po_gp%d" % (fo % 2)
            c.op("act", lambda e: e.activation(out=g_[:], in_=p[:, 0:TT], func=AF.Sigmoid, bias=vb[:, 0, fo:fo + 1]),
                 reads=[pk, "po_vec"], writes=[gk])
            p2 = ps[it % 4]; pk2 = "po_ps%d" % (it % 4); it += 1
            for k in range(2):
                c.op("pe", lambda e: e.matmul(p2[:, 0:TT], wpl[:, k, fo * 128:(fo + 1) * 128], pb[:, k, :],
                                              start=(k == 0), stop=(k == 1)),
                     reads=["po_wpl", "po_pb"], writes=[pk2])
            t_ = tmp[fo % 2]; tk = "po_tmp%d" % (fo % 2)
            c.op("dve", lambda e: e.tensor_tensor(out=t_[:], in0=p2[:, 0:TT], in1=g_[:], op=ALU.mult),
                 reads=[pk2, gk], writes=[tk])
            c.op("pool", lambda e: e.tensor_tensor(out=r[:, fo, :], in0=r[:, fo, :], in1=t_[:], op=ALU.add),
                 reads=["po_r%d" % fo, tk], writes=["po_r%d" % fo])
            s_ = sq[fo % 2]; sk = "po_sq%d" % (fo % 2)
            c.op("act", lambda e: e.activation(out=s_[:], in_=r[:, fo, :], func=AF.Square),
                 reads=["po_r%d" % fo], writes=[sk])
            c.op("pe", lambda e: e.matmul(pst_s[:, 0:TT], ones[:], r[:, fo, :], start=(fo == 0), stop=(fo == 7)),
                 reads=["po_ones", "po_r%d" % fo], writes=["po_pst0"])
            c.op("pe", lambda e: e.matmul(pst_q[:, 0:TT], ones[:], s_[:], start=(fo == 0), stop=(fo == 7)),
                 reads=["po_ones", sk], writes=["po_pst1"])
        c.op("dve", lambda e: e.tensor_scalar(out=mean[:], in0=pst_s[:, 0:TT], scalar1=1.0 / D, scalar2=None, op0=ALU.mult),
             reads=["po_pst0"], writes=["po_mean"])
        c.op("dve", lambda e: e.tensor_tensor(out=msq[:], in0=mean[:], in1=mean[:], op=ALU.mult),
             reads=["po_mean"], writes=["po_msq"])
        c.op("dve", lambda e: e.scalar_tensor_tensor(out=msq[:], in0=pst_q[:, 0:TT], scalar=1.0 / D, in1=msq[:],
                                                     op0=ALU.mult, op1=ALU.subtract),
             reads=["po_pst1", "po_msq"], writes=["po_msq"])
        c.op("dve", lambda e: e.tensor_scalar(out=msq[:], in0=msq[:], scalar1=EPS, scalar2=None, op0=ALU.add),
             reads=["po_msq"], writes=["po_msq"])
        c.op("act", lambda e: e.activation(out=rstd[:], in_=msq[:], func=AF.Sqrt),
             reads=["po_msq"], writes=["po_rstd"])
        c.op("dve", lambda e: e.reciprocal(out=rstd[:], in_=rstd[:]), reads=["po_rstd"], writes=["po_rstd"])
        for fo in range(8):
            t_ = tmp[fo % 2]; tk = "po_tmp%d" % (fo % 2)
            o_ = ot[fo % 2]; ok = "po_ot%d" % (fo % 2)
            c.op("dve", lambda e: e.tensor_tensor(out=t_[:], in0=r[:, fo, :], in1=mean[:], op=ALU.subtract),
                 reads=["po_r%d" % fo, "po_mean"], writes=[tk])
            c.op("pool", lambda e: e.tensor_tensor(out=t_[:], in0=t_[:], in1=rstd[:], op=ALU.mult),
                 reads=[tk, "po_rstd"], writes=[tk])
            c.op("act", lambda e: e.activation(out=o_[:], in_=t_[:], func=AF.Identity, scale=vb[:, 1, fo:fo + 1],
                                               bias=vb[:, 2, fo:fo + 1]),
                 reads=[tk, "po_vec"], writes=[ok])
            c.dma(ov[:, fo, t0:t0 + TT], o_[:], reads=[ok])
    c.end_phase()


def phase_conv(c, T, projT, cw_d, cvec_d, ident_d, ones_f, yT):
    nc = c.nc
    c.begin_phase()
    TT = 512 if T >= 512 else T
    cw = c.sb("pc_cw", [128, 4, 31], F32)
    cvec = c.sb("pc_cvec", [128, 3, 4], F32)
    ident = c.sb("pc_ident", [128, 128], F32)
    ones = c.sb("pc_ones", [128, 128], F32)
    dg = c.sb("pc_dg", [128, 124, 128], BF16)
    hb = c.sb("pc_hb", [128, 4, 32 + T], BF16)
    c.dma(cw[:], cw_d, writes=["pc_cw"])
    c.dma(cvec[:], cvec_d, writes=["pc_cvec"])
    c.dma(ident[:], ident_d, writes=["pc_ident"])
    c.dma(ones[:], ones_f, writes=["pc_ones"])
    for ch in range(4):
        for k in range(31):
            c.op("dve", lambda e: e.tensor_scalar(out=dg[:, ch * 31 + k, :], in0=ident[:], scalar1=cw[:, ch, k:k + 1],
                                                  scalar2=None, op0=ALU.mult),
                 reads=["pc_cw", "pc_ident"], writes=["pc_dg"])
    ld = [c.sb("pc_ld%d" % i, [128, 2048], BF16) for i in range(4)]
    PT = min(2048, T)
    i = 0
    for ch in range(4):
        c.op("pool", lambda e: e.memset(hb[:, ch, 0:32], 0.0), writes=["pc_hb%d" % ch])
        for t0 in range(0, T, PT):
            a = ld[i % 4]; ak = "pc_ld%d" % (i % 4); i += 1
            b = ld[i % 4]; bk = "pc_ld%d" % (i % 4); i += 1
            c.dma(a[:, 0:PT], projT[O_GL + ch * 128:O_GL + (ch + 1) * 128, t0:t0 + PT], writes=[ak])
            c.dma(b[:, 0:PT], projT[O_GG + ch * 128:O_GG + (ch + 1) * 128, t0:t0 + PT], writes=[bk])
            c.op("pool", lambda e: e.tensor_tensor(out=hb[:, ch, 32 + t0:32 + t0 + PT], in0=a[:, 0:PT], in1=b[:, 0:PT],
                                                   op=ALU.mult),
                 reads=[ak, bk], writes=["pc_hb%d" % ch])
    cv = c.sb("pc_cv", [128, 4, TT], F32)
    sq = [c.sb("pc_sq%d" % i, [128, TT], F32) for i in range(2)]
    zb = [c.sb("pc_zb%d" % i, [128, TT], BF16) for i in range(2)]
    mean = c.sb("pc_mean", [128, TT], F32)
    msq = c.sb("pc_msq", [128, TT], F32)
    rstd = c.sb("pc_rstd", [128, TT], F32)
    tmp = [c.sb("pc_tmp%d" % i, [128, TT], F32) for i in range(2)]
    yo = [c.sb("pc_yo%d" % i, [128, TT], BF16) for i in range(2)]
    ps = [c.ps("pc_ps%d" % i, [128, 512]) for i in range(3)]
    pst = [c.ps("pc_pst%d" % i, [128, 512]) for i in range(2)]
    it = 0
    for tt in range(T // TT):
        t0 = tt * TT
        for ch in range(4):
            p = ps[it % 3]; pk = "pc_ps%d" % (it % 3); it += 1
            for k in range(31):
                c.op("pe", lambda e: e.matmul(p[:, 0:TT], dg[:, ch * 31 + k, :], hb[:, ch, 2 + t0 + k:2 + t0 + k + TT],
                                              start=(k == 0), stop=(k == 30)),
                     reads=["pc_dg", "pc_hb%d" % ch], writes=[pk])
            c.op("act", lambda e: e.activation(out=cv[:, ch, :], in_=p[:, 0:TT], func=AF.Identity, bias=cvec[:, 0, ch:ch + 1]),
                 reads=[pk, "pc_cvec"], writes=["pc_cv%d" % ch])
            s_ = sq[ch % 2]; sk = "pc_sq%d" % (ch % 2)
            c.op("act", lambda e: e.activation(out=s_[:], in_=cv[:, ch, :], func=AF.Square),
                 reads=["pc_cv%d" % ch], writes=[sk])
            c.op("pe", lambda e: e.matmul(pst[0][:, 0:TT], ones[:], cv[:, ch, :], start=(ch == 0), stop=(ch == 3)),
                 reads=["pc_ones", "pc_cv%d" % ch], writes=["pc_pst0"])
            c.op("pe", lambda e: e.matmul(pst[1][:, 0:TT], ones[:], s_[:], start=(ch == 0), stop=(ch == 3)),
                 reads=["pc_ones", sk], writes=["pc_pst1"])
        c.op("dve", lambda e: e.tensor_scalar(out=mean[:], in0=pst[0][:, 0:TT], scalar1=1.0 / 512, scalar2=None, op0=ALU.mult),
             reads=["pc_pst0"], writes=["pc_mean"])
        c.op("dve", lambda e: e.tensor_tensor(out=msq[:], in0=mean[:], in1=mean[:], op=ALU.mult),
             reads=["pc_mean"], writes=["pc_msq"])
        c.op("dve", lambda e: e.scalar_tensor_tensor(out=msq[:], in0=pst[1][:, 0:TT], scalar=1.0 / 512, in1=msq[:],
                                                     op0=ALU.mult, op1=ALU.subtract),
             reads=["pc_pst1", "pc_msq"], writes=["pc_msq"])
        c.op("dve", lambda e: e.tensor_scalar(out=msq[:], in0=msq[:], scalar1=EPS, scalar2=None, op0=ALU.add),
             reads=["pc_msq"], writes=["pc_msq"])
        c.op("act", lambda e: e.activation(out=rstd[:], in_=msq[:], func=AF.Sqrt), reads=["pc_msq"], writes=["pc_rstd"])
        c.op("dve", lambda e: e.reciprocal(out=rstd[:], in_=rstd[:]), reads=["pc_rstd"], writes=["pc_rstd"])
        for ch in range(4):
            z = zb[ch % 2]; zk = "pc_zb%d" % (ch % 2)
            c.dma(z[:], projT[O_ZB + ch * 128:O_ZB + (ch + 1) * 128, t0:t0 + TT], writes=[zk])
            t_ = tmp[ch % 2]; tk = "pc_tmp%d" % (ch % 2)
            c.op("dve", lambda e: e.tensor_tensor(out=t_[:], in0=cv[:, ch, :], in1=mean[:], op=ALU.subtract),
                 reads=["pc_cv%d" % ch, "pc_mean"], writes=[tk])
            c.op("pool", lambda e: e.tensor_tensor(out=t_[:], in0=t_[:], in1=rstd[:], op=ALU.mult),
                 reads=[tk, "pc_rstd"], writes=[tk])
            c.op("act", lambda e: e.activation(out=t_[:], in_=t_[:], func=AF.Silu, scale=cvec[:, 1, ch:ch + 1],
                                               bias=cvec[:, 2, ch:ch + 1]),
                 reads=[tk, "pc_cvec"], writes=[tk])
            o_ = yo[ch % 2]; ok = "pc_yo%d" % (ch % 2)
            c.op("dve", lambda e: e.tensor_tensor(out=o_[:], in0=t_[:], in1=z[:], op=ALU.mult),
                 reads=[tk, zk], writes=[ok])
            c.dma(yT[512 + ch * 128:512 + (ch + 1) * 128, t0:t0 + TT], o_[:], reads=[ok])
    c.end_phase()


def phase_fox(c, T, projT, smallT, fb_d, negmask_d, identb_d, fxq, fxk, yT):
    nc = c.nc
    c.begin_phase()
    QB = min(512, T)
    NT = T // 128
    fb = c.sb("pf_fb", [8, 1], F32)
    nfb = c.sb("pf_nfb", [8, 1], F32)
    c.dma(fb[:], fb_d, writes=["pf_fb"])
    c.op("dve", lambda e: e.tensor_scalar(out=nfb[:], in0=fb[:], scalar1=-1.0, scalar2=None, op0=ALU.mult),
         reads=["pf_fb"], writes=["pf_nfb"])
    PT = min(2048, T)
    onesr = c.sb("pf_onesr", [8, PT], F32)
    c.op("pool", lambda e: e.memset(onesr[:], 1.0), writes=["pf_onesr"])
    fin = c.sb("pf_fin", [8, PT], F32)
    l_ = c.sb("pf_l", [8, PT], F32)
    fn = [c.sb("pf_fn%d" % i, [8, PT], F32) for i in range(2)]
    hi = c.sb("pf_hi", [8, PT], BF16)
    r1 = c.sb("pf_r1", [8, PT], F32)
    mid = c.sb("pf_mid", [8, PT], BF16)
    lo = c.sb("pf_lo", [8, PT], BF16)
    nhi = c.sb("pf_nhi", [8, PT], BF16)
    for pi, t0 in enumerate(range(0, T, PT)):
        f_ = fn[pi % 2]; fk = "pf_fn%d" % (pi % 2)
        c.dma(fin[:], smallT[8:16, t0:t0 + PT], writes=["pf_fin"])
        c.op("act", lambda e: e.activation(out=l_[:], in_=fin[:], func=AF.Exp, scale=-1.0, bias=nfb[:]),
             reads=["pf_fin", "pf_nfb"], writes=["pf_l"])
        c.op("act", lambda e: e.activation(out=l_[:], in_=l_[:], func=AF.Ln, bias=1.0), reads=["pf_l"], writes=["pf_l"])
        if pi == 0:
            c.op("dve", lambda e: e.tensor_tensor_scan(out=f_[:], data0=onesr[:], data1=l_[:], initial=0.0,
                                                       op0=ALU.mult, op1=ALU.add),
                 reads=["pf_onesr", "pf_l"], writes=[fk])
        else:
            pf_ = fn[(pi - 1) % 2]
            c.op("dve", lambda e: e.tensor_tensor_scan(out=f_[:], data0=onesr[:], data1=l_[:], initial=pf_[:, PT - 1:PT],
                                                       op0=ALU.mult, op1=ALU.add),
                 reads=["pf_onesr", "pf_l", "pf_fn%d" % ((pi - 1) % 2)], writes=[fk])
        c.op("dve", lambda e: e.tensor_copy(out=hi[:], in_=f_[:]), reads=[fk], writes=["pf_hi"])
        c.op("dve", lambda e: e.tensor_tensor(out=r1[:], in0=f_[:], in1=hi[:], op=ALU.subtract),
             reads=[fk, "pf_hi"], writes=["pf_r1"])
        c.op("dve", lambda e: e.tensor_copy(out=mid[:], in_=r1[:]), reads=["pf_r1"], writes=["pf_mid"])
        c.op("dve", lambda e: e.tensor_tensor(out=r1[:], in0=r1[:], in1=mid[:], op=ALU.subtract),
             reads=["pf_r1", "pf_mid"], writes=["pf_r1"])
        c.op("dve", lambda e: e.tensor_copy(out=lo[:], in_=r1[:]), reads=["pf_r1"], writes=["pf_lo"])
        c.op("dve", lambda e: e.tensor_scalar(out=nhi[:], in0=hi[:], scalar1=-1.0, scalar2=None, op0=ALU.mult),
             reads=["pf_hi"], writes=["pf_nhi"])
        c.dma(fxq[:, t0:t0 + PT], nhi[:], reads=["pf_nhi"], writes=["fxq"])
        c.dma(fxk[:, 0, t0:t0 + PT], hi[:], reads=["pf_hi"], writes=["fxk"])
        c.dma(fxk[:, 1, t0:t0 + PT], mid[:], reads=["pf_mid"], writes=["fxk"])
        c.dma(fxk[:, 2, t0:t0 + PT], lo[:], reads=["pf_lo"], writes=["fxk"])
    negm = c.sb("pf_negm", [128, 4, 512], BF16)
    identb = c.sb("pf_identb", [128, 128], BF16)
    ones64 = c.sb("pf_ones64", [128, 64], BF16)
    c.dma(negm[:], negmask_d, writes=["pf_negm"])
    c.dma(identb[:], identb_d, writes=["pf_identb"])
    c.op("pool", lambda e: e.memset(ones64[:], 1.0), writes=["pf_ones64"])
    qp = [c.sb("pf_qp%d" % i, [68, T], BF16) for i in range(2)]
    kp = [c.sb("pf_kp%d" % i, [68, T], BF16) for i in range(2)]
    vT = [c.sb("pf_vT%d" % i, [64, T], BF16) for i in range(2)]
    vt = [c.sb("pf_vt%d" % i, [128, NT, 64], BF16) for i in range(2)]
    att = [c.sb("pf_att%d" % i, [128, 512], BF16) for i in range(3)]
    rec = c.sb("pf_rec", [64, 512], F32)
    o_ = c.sb("pf_o", [64, 512], F32)
    zt = [c.sb("pf_zt%d" % i, [64, 512], BF16) for i in range(2)]
    yo = [c.sb("pf_yo%d" % i, [64, 512], BF16) for i in range(2)]
    psz = [c.ps("pf_psz%d" % i, [128, 512]) for i in range(3)]
    psn = [c.ps("pf_psn%d" % i, [64, 512]) for i in range(2)]
    psd = [c.ps("pf_psd%d" % i, [64, 512]) for i in range(2)]
    pst = c.ps("pf_pst", [128, 8, 64], BF16)

    def setup(h):
        b = h % 2
        q_, k_, vT_, vt_ = qp[b], kp[b], vT[b], vt[b]
        qk, kk, vTk, vtk = "pf_qp%d" % b, "pf_kp%d" % b, "pf_vT%d" % b, "pf_vt%d" % b
        c.op("pool", lambda e: e.memset(q_[64:68, :], 1.0), writes=[qk])
        c.op("pool", lambda e: e.memset(k_[64:68, :], 1.0), writes=[kk])
        c.dma(q_[0:64, :], projT[O_QD + h * 64:O_QD + (h + 1) * 64, :], writes=[qk])
        c.dma(k_[0:64, :], projT[O_KD + h * 64:O_KD + (h + 1) * 64, :], writes=[kk])
        c.dma(q_[64:65, :], fxq[h:h + 1, :], reads=["fxq"], writes=[qk])
        c.dma(k_[65:68, :], fxk[h, :, :], reads=["fxk"], writes=[kk])
        c.dma(vT_[:], projT[O_VD + h * 64:O_VD + (h + 1) * 64, :], writes=[vTk])
        for g in range(0, NT, 8):
            n = min(8, NT - g)
            for j in range(n):
                c.op("pe", lambda e: e.transpose(pst[:, j, :], vT_[:, (g + j) * 128:(g + j + 1) * 128], identb[0:64, 0:64]),
                     reads=[vTk, "pf_identb"], writes=["pf_pst"])
            c.op("dve", lambda e: e.tensor_copy(out=vt_[:, g:g + n, :], in_=pst[:, 0:n, :]), reads=["pf_pst"], writes=[vtk])

    def stage1(d):
        h, qb, kt, nk, i = d
        b = h % 2
        t0 = qb * QB; s0 = kt * 128
        p = psz[i % 3]; pk = "pf_psz%d" % (i % 3)
        a = att[i % 3]; ak = "pf_att%d" % (i % 3)
        diag = s0 >= t0
        c.op("pe", lambda e: e.matmul(p[:, 0:QB], kp[b][:, s0:s0 + 128], qp[b][:, t0:t0 + QB], start=True, stop=not diag),
             reads=["pf_qp%d" % b, "pf_kp%d" % b], writes=[pk])
        if diag:
            j = (s0 - t0) // 128
            c.op("pe", lambda e: e.matmul(p[:, 0:QB], identb[:], negm[:, j, 0:QB], start=False, stop=True),
                 reads=["pf_identb", "pf_negm"], writes=[pk])
        c.op("act", lambda e: e.activation(out=a[:, 0:QB], in_=p[:, 0:QB], func=AF.Exp), reads=[pk], writes=[ak])

    def stage2(d):
        h, qb, kt, nk, i = d
        b = h % 2
        t0 = qb * QB
        a = att[i % 3]; ak = "pf_att%d" % (i % 3)
        pn, pnk = psn[qb % 2], "pf_psn%d" % (qb % 2)
        pd, pdk = psd[qb % 2], "pf_psd%d" % (qb % 2)
        c.op("pe", lambda e: e.matmul(pn[:, 0:QB], vt[b][:, kt, :], a[:, 0:QB], start=(kt == 0), stop=(kt == nk - 1)),
             reads=["pf_vt%d" % b, ak], writes=[pnk])
        c.op("pe", lambda e: e.matmul(pd[:, 0:QB], ones64[:], a[:, 0:QB], start=(kt == 0), stop=(kt == nk - 1)),
             reads=["pf_ones64", ak], writes=[pdk])
        if kt == nk - 1:
            z_ = zt[qb % 2]; zk = "pf_zt%d" % (qb % 2)
            y_ = yo[qb % 2]; yk = "pf_yo%d" % (qb % 2)
            c.dma(z_[:, 0:QB], projT[O_ZD + h * 64:O_ZD + (h + 1) * 64, t0:t0 + QB], writes=[zk])
            c.op("dve", lambda e: e.reciprocal(out=rec[:, 0:QB], in_=pd[:, 0:QB]), reads=[pdk], writes=["pf_rec"])
            c.op("dve", lambda e: e.tensor_tensor(out=o_[:, 0:QB], in0=pn[:, 0:QB], in1=rec[:, 0:QB], op=ALU.mult),
                 reads=[pnk, "pf_rec"], writes=["pf_o"])
            c.op("pool", lambda e: e.tensor_tensor(out=y_[:, 0:QB], in0=o_[:, 0:QB], in1=z_[:, 0:QB], op=ALU.mult),
                 reads=["pf_o", zk], writes=[yk])
            c.dma(yT[1536 + h * 64:1536 + (h + 1) * 64, t0:t0 + QB], y_[:, 0:QB], reads=[yk])

    blocks = []
    i = 0
    for h in range(8):
        for qb in range(T // QB):
            nk = (qb * QB + QB) // 128
            for kt in range(nk):
                blocks.append((h, qb, kt, nk, i)); i += 1
    setup(0)
    n = len(blocks)
    for s_ in range(n + 1):
        if s_ < n:
            stage1(blocks[s_])
        if s_ >= 1:
            d = blocks[s_ - 1]
            stage2(d)
            if d[1] == 0 and d[2] == 0 and d[0] + 1 < 8:
                setup(d[0] + 1)
    c.end_phase()


def phase_sb(c, T, projT, negmask_d, mask01_d, identb_d, negu_d, yT):
    nc = c.nc
    c.begin_phase()
    QB = min(512, T)
    NT = T // 128
    negm = c.sb("sb_negm", [128, 4, 512], BF16)
    m01 = c.sb("sb_m01", [128, 4, 512], BF16)
    identb = c.sb("sb_identb", [128, 128], BF16)
    negu = c.sb("sb_negu", [128, 128], BF16)
    negones = c.sb("sb_negones", [128, 128], BF16)
    c.dma(negm[:], negmask_d, writes=["sb_negm"])
    c.dma(m01[:], mask01_d, writes=["sb_m01"])
    c.dma(identb[:], identb_d, writes=["sb_identb"])
    c.dma(negu[:], negu_d, writes=["sb_negu"])
    c.op("pool", lambda e: e.memset(negones[:], -1.0), writes=["sb_negones"])
    qp = [c.sb("sb_qp%d" % i, [64, T], BF16) for i in range(2)]
    kp = [c.sb("sb_kp%d" % i, [64, T], BF16) for i in range(2)]
    vT = [c.sb("sb_vT%d" % i, [64, T], BF16) for i in range(2)]
    vt = [c.sb("sb_vt%d" % i, [128, NT, 64], BF16) for i in range(2)]
    ee = [c.sb("sb_e%d" % i, [128, 512], F32) for i in range(2)]
    sp = [c.sb("sb_sp%d" % i, [128, 512], BF16) for i in range(3)]
    att = [c.sb("sb_att%d" % i, [128, 512], BF16) for i in range(3)]
    sl = [c.sb("sb_sl%d" % i, [128, 512], BF16) for i in range(2)]
    zt = [c.sb("sb_zt%d" % i, [64, 512], BF16) for i in range(2)]
    yo = [c.sb("sb_yo%d" % i, [64, 512], BF16) for i in range(2)]
    psa = [c.ps("sb_psa%d" % i, [128, 512]) for i in range(2)]
    psb = [c.ps("sb_psb%d" % i, [128, 512]) for i in range(2)]
    pso = [c.ps("sb_pso%d" % i, [64, 512]) for i in range(2)]
    pst = c.ps("sb_pst", [128, 8, 64], BF16)

    def setup(h):
        b = h % 2
        q_, k_, vT_, vt_ = qp[b], kp[b], vT[b], vt[b]
        qk, kk, vTk, vtk = "sb_qp%d" % b, "sb_kp%d" % b, "sb_vT%d" % b, "sb_vt%d" % b
        c.dma(q_[:], projT[O_QC + h * 64:O_QC + (h + 1) * 64, :], writes=[qk])
        c.dma(k_[:], projT[O_KC + h * 64:O_KC + (h + 1) * 64, :], writes=[kk])
        c.dma(vT_[:], projT[O_VC + h * 64:O_VC + (h + 1) * 64, :], writes=[vTk])
        for g in range(0, NT, 8):
            n = min(8, NT - g)
            for j in range(n):
                c.op("pe", lambda e: e.transpose(pst[:, j, :], vT_[:, (g + j) * 128:(g + j + 1) * 128], identb[0:64, 0:64]),
                     reads=[vTk, "sb_identb"], writes=["sb_pst"])
            c.op("dve", lambda e: e.tensor_copy(out=vt_[:, g:g + n, :], in_=pst[:, 0:n, :]), reads=["sb_pst"], writes=[vtk])

    def stA(d):
        h, qb, kt, nk, i, si = d
        b = h % 2
        t0 = qb * QB; s0 = kt * 128
        pa = psa[i % 2]; pak = "sb_psa%d" % (i % 2)
        e_ = ee[i % 2]; ek = "sb_e%d" % (i % 2)
        s_ = sp[i % 3]; sk = "sb_sp%d" % (i % 3)
        c.op("pe", lambda e: e.matmul(pa[:, 0:QB], kp[b][:, s0:s0 + 128], qp[b][:, t0:t0 + QB], start=True, stop=True),
             reads=["sb_qp%d" % b, "sb_kp%d" % b], writes=[pak])
        c.op("act", lambda e: e.activation(out=e_[:, 0:QB], in_=pa[:, 0:QB], func=AF.Exp), reads=[pak], writes=[ek])
        c.op("act", lambda e: e.activation(out=s_[:, 0:QB], in_=e_[:, 0:QB], func=AF.Ln, bias=1.0), reads=[ek], writes=[sk])
        if s0 >= t0:
            j = (s0 - t0) // 128
            c.op("pool", lambda e: e.tensor_tensor(out=s_[:, 0:QB], in0=s_[:, 0:QB], in1=m01[:, j, 0:QB], op=ALU.mult),
                 reads=[sk, "sb_m01"], writes=[sk])

    def stB(d):
        h, qb, kt, nk, i, si = d
        b = h % 2
        t0 = qb * QB; s0 = kt * 128
        first = kt == nk - 1
        diag = s0 >= t0
        pb = psb[i % 2]; pbk = "sb_psb%d" % (i % 2)
        s_ = sp[i % 3]; sk = "sb_sp%d" % (i % 3)
        a = att[i % 3]; ak = "sb_att%d" % (i % 3)
        c.op("pe", lambda e: e.matmul(pb[:, 0:QB], kp[b][:, s0:s0 + 128], qp[b][:, t0:t0 + QB], start=True, stop=False),
             reads=["sb_qp%d" % b, "sb_kp%d" % b], writes=[pbk])
        if diag:
            j = (s0 - t0) // 128
            c.op("pe", lambda e: e.matmul(pb[:, 0:QB], identb[:], negm[:, j, 0:QB], start=False, stop=False),
                 reads=["sb_identb", "sb_negm"], writes=[pbk])
        sl_ = sl[si % 2]; slk = "sb_sl%d" % (si % 2)
        if not first:
            c.op("pe", lambda e: e.matmul(pb[:, 0:QB], negones[:], sl_[:, 0:QB], start=False, stop=False),
                 reads=["sb_negones", slk], writes=[pbk])
        c.op("pe", lambda e: e.matmul(pb[:, 0:QB], negu[:], s_[:, 0:QB], start=False, stop=True),
             reads=["sb_negu", sk], writes=[pbk])
        c.op("act", lambda e: e.activation(out=a[:, 0:QB], in_=pb[:, 0:QB], func=AF.Exp), reads=[pbk], writes=[ak])
        if kt > 0:
            nsl = sl[(si + 1) % 2]; nslk = "sb_sl%d" % ((si + 1) % 2)
            if first:
                c.op("dve", lambda e: e.tensor_copy(out=nsl[:, 0:QB], in_=s_[:, 0:QB]), reads=[sk], writes=[nslk])
            else:
                c.op("dve", lambda e: e.tensor_tensor(out=nsl[:, 0:QB], in0=sl_[:, 0:QB], in1=s_[:, 0:QB], op=ALU.add),
                     reads=[slk, sk], writes=[nslk])

    def stC(d):
        h, qb, kt, nk, i, si = d
        b = h % 2
        t0 = qb * QB
        first = kt == nk - 1
        a = att[i % 3]; ak = "sb_att%d" % (i % 3)
        po, pok = pso[qb % 2], "sb_pso%d" % (qb % 2)
        c.op("pe", lambda e: e.matmul(po[:, 0:QB], vt[b][:, kt, :], a[:, 0:QB], start=first, stop=(kt == 0)),
             reads=["sb_vt%d" % b, ak], writes=[pok])
        if kt == 0:
            z_ = zt[qb % 2]; zk = "sb_zt%d" % (qb % 2)
            y_ = yo[qb % 2]; yk = "sb_yo%d" % (qb % 2)
            c.dma(z_[:, 0:QB], projT[O_ZC + h * 64:O_ZC + (h + 1) * 64, t0:t0 + QB], writes=[zk])
            c.op("dve", lambda e: e.tensor_tensor(out=y_[:, 0:QB], in0=po[:, 0:QB], in1=z_[:, 0:QB], op=ALU.mult),
                 reads=[pok, zk], writes=[yk])
            c.dma(yT[1024 + h * 64:1024 + (h + 1) * 64, t0:t0 + QB], y_[:, 0:QB], reads=[yk])

    blocks = []
    i = 0
    si = 0
    for h in range(8):
        for qb in range(T // QB):
            nk = (qb * QB + QB) // 128
            for kt in range(nk - 1, -1, -1):
                blocks.append((h, qb, kt, nk, i, si)); i += 1
                if kt > 0:
                    si += 1
    setup(0)
    n = len(blocks)
    for s_i in range(n + 2):
        if s_i < n:
            stA(blocks[s_i])
        if 1 <= s_i <= n:
            stB(blocks[s_i - 1])
        if s_i >= 2:
            d = blocks[s_i - 2]
            stC(d)
            if d[1] == 0 and d[2] == d[3] - 1 and d[0] + 1 < 8:
                setup(d[0] + 1)
    c.end_phase()


def phase_gdn_pre(c, T, projT, cw4_d, ident_d, ones_f, gqkv):
    nc = c.nc
    c.begin_phase()
    TT = min(512, T)
    cw4 = c.sb("g0_cw4", [128, 12, 4], F32)
    ident = c.sb("g0_ident", [128, 128], F32)
    ones = c.sb("g0_ones", [128, 128], F32)
    dg = c.sb("g0_dg", [128, 48, 128], BF16)
    c.dma(cw4[:], cw4_d, writes=["g0_cw4"])
    c.dma(ident[:], ident_d, writes=["g0_ident"])
    c.dma(ones[:], ones_f, writes=["g0_ones"])
    for ch in range(12):
        for k in range(4):
            c.op("dve", lambda e: e.tensor_scalar(out=dg[:, ch * 4 + k, :], in0=ident[:], scalar1=cw4[:, ch, k:k + 1],
                                                  scalar2=None, op0=ALU.mult),
                 reads=["g0_cw4", "g0_ident"], writes=["g0_dg"])
    hq = [c.sb("g0_hq%d" % i, [128, 4 + T], BF16) for i in range(2)]
    cs = [c.sb("g0_c%d" % i, [128, TT], F32) for i in range(2)]
    sq = [c.sb("g0_sq%d" % i, [128, TT], F32) for i in range(2)]
    rt = [c.sb("g0_rt%d" % i, [128, TT], F32) for i in range(2)]
    oo = [c.sb("g0_o%d" % i, [128, TT], F32) for i in range(2)]
    ps = [c.ps("g0_ps%d" % i, [128, 512]) for i in range(2)]
    pq = [c.ps("g0_pq%d" % i, [128, 512]) for i in range(2)]
    it = 0
    for ch in range(12):
        h_ = hq[ch % 2]; hk = "g0_hq%d" % (ch % 2)
        c.op("pool", lambda e: e.memset(h_[:, 0:4], 0.0), writes=[hk])
        c.dma(h_[:, 4:4 + T], projT[O_QA + ch * 128:O_QA + (ch + 1) * 128, :], writes=[hk])
        for tt in range(T // TT):
            t0 = tt * TT
            i2 = it % 2; it += 1
            p = ps[i2]; pk = "g0_ps%d" % i2
            for k in range(4):
                c.op("pe", lambda e: e.matmul(p[:, 0:TT], dg[:, ch * 4 + k, :], h_[:, 1 + t0 + k:1 + t0 + k + TT],
                                              start=(k == 0), stop=(k == 3)),
                     reads=["g0_dg", hk], writes=[pk])
            c_ = cs[i2]; ck = "g0_c%d" % i2
            c.op("act", lambda e: e.activation(out=c_[:], in_=p[:, 0:TT], func=AF.Silu), reads=[pk], writes=[ck])
            if ch < 8:
                s_ = sq[i2]; sk = "g0_sq%d" % i2
                c.op("dve", lambda e: e.tensor_tensor(out=s_[:], in0=c_[:], in1=c_[:], op=ALU.mult), reads=[ck], writes=[sk])
                q = pq[i2]; qk = "g0_pq%d" % i2
                c.op("pe", lambda e: e.matmul(q[:, 0:TT], ones[:], s_[:], start=True, stop=True),
                     reads=["g0_ones", sk], writes=[qk])
                r_ = rt[i2]; rk = "g0_rt%d" % i2
                c.op("dve", lambda e: e.tensor_scalar(out=r_[:], in0=q[:, 0:TT], scalar1=1e-6, scalar2=None, op0=ALU.add),
                     reads=[qk], writes=[rk])
                c.op("act", lambda e: e.activation(out=r_[:], in_=r_[:], func=AF.Sqrt), reads=[rk], writes=[rk])
                c.op("dve", lambda e: e.reciprocal(out=r_[:], in_=r_[:]), reads=[rk], writes=[rk])
                o_ = oo[i2]; ok = "g0_o%d" % i2
                sc = 128 ** -0.5 if ch < 4 else 1.0
                c.op("dve", lambda e: e.scalar_tensor_tensor(out=o_[:], in0=c_[:], scalar=sc, in1=r_[:], op0=ALU.mult,
                                                             op1=ALU.mult), reads=[ck, rk], writes=[ok])
                c.dma(gqkv[ch, :, t0:t0 + TT], o_[:], reads=[ok], writes=["gqkv"])
            else:
                c.dma(gqkv[ch, :, t0:t0 + TT], c_[:], reads=[ck], writes=["gqkv"])
    c.end_phase()


def phase_gdn(c, T, projT, smallT, gqkv, alog_d, dtb_d, gn_d, ident_d, ones_f, triu_d, sl_d, mks_d, mki_d, yT):
    nc = c.nc
    c.begin_phase()
    NC = T // 128
    GS = 4 if NC >= 4 else NC
    W = GS * 128
    ident = c.sb("g_ident", [128, 128], F32)
    ones = c.sb("g_ones", [128, 128], F32)
    triu = c.sb("g_triu", [128, 128], F32)
    slm = c.sb("g_sl", [128, 128], F32)
    mks = c.sb("g_mks", [128, GS, 128], F32)
    mki = c.sb("g_mki", [128, GS, 128], F32)
    identg = c.sb("g_identg", [128, GS, 128], F32)
    gnb = c.sb("g_gnb", [128, 128], F32)
    alog = c.sb("g_alog", [128, 4], F32)
    dtb = c.sb("g_dtb", [128, 4], F32)
    nea = c.sb("g_nea", [128, 4], F32)
    c.dma(ident[:], ident_d, writes=["g_ident"])
    c.dma(ones[:], ones_f, writes=["g_ones"])
    c.dma(triu[:], triu_d, writes=["g_triu"])
    c.dma(slm[:], sl_d, writes=["g_sl"])
    for g in range(GS):
        c.dma(mks[:, g, :], mks_d, writes=["g_mks"])
        c.dma(mki[:, g, :], mki_d, writes=["g_mki"])
        c.dma(identg[:, g, :], ident_d, writes=["g_identg"])
    c.dma(gnb[:], gn_d, writes=["g_gnb"])
    c.dma(alog[:], alog_d, writes=["g_alog"])
    c.dma(dtb[:], dtb_d, writes=["g_dtb"])
    c.op("act", lambda e: e.activation(out=nea[:], in_=alog[:], func=AF.Exp), reads=["g_alog"], writes=["g_nea"])
    c.op("dve", lambda e: e.tensor_scalar(out=nea[:], in0=nea[:], scalar1=-1.0, scalar2=None, op0=ALU.mult),
         reads=["g_nea"], writes=["g_nea"])
    banks = [c.ps("g_pb%d" % i, [128, GS, 128]) for i in range(4)]
    pscan_b = [c.ps("g_pscan%d" % i, [128, 512]) for i in range(4)]

    class _PS:
        def __getitem__(self, idx):
            return pscan_b[idx[1]][:, 0:128]
    pscan = _PS()
    bi = [0]

    def bank():
        i = bi[0] % 4
        bi[0] += 1
        return banks[i], "g_pb%d" % i

    sm = c.sb("g_sm", [8, T], F32)
    c.dma(sm[:], smallT[0:8, :], writes=["g_sm"])
    abt = c.sb("g_abt", [128, NC, 8], F32)
    for n0 in range(0, NC, 64):
        nn = min(64, NC - n0)
        pbs, pbsk = bank()
        psmall = pbs[:].rearrange("p g c -> p (g c)")
        for n in range(nn):
            c.op("pe", lambda e: e.transpose(psmall[:, n * 8:(n + 1) * 8], sm[:, (n0 + n) * 128:(n0 + n + 1) * 128],
                                             ident[0:8, 0:8]), reads=["g_sm", "g_ident"], writes=[pbsk])
        c.op("dve", lambda e: e.tensor_copy(out=abt[:, n0:n0 + nn, :],
                                            in_=psmall[:, 0:nn * 8].rearrange("p (n k) -> p n k", k=8)),
             reads=[pbsk], writes=["g_abt"])

    if DBG_STOP == 1:
        c.end_phase(); return

    def t2(name):
        return c.sb(name, [128, NC], F32)
    gg, beta, gc, gl, egc, egl, kdf, bgc, tmpn = [t2("g_" + n) for n in
                                                  ("gg", "beta", "gc", "gl", "egc", "egl", "kdf", "bgc", "tmpn")]

    def grp(name, n=2):
        return [c.sb("%s%d" % (name, i), [128, GS, 128], F32) for i in range(n)]
    kT, qT, vT = grp("g_kT"), grp("g_qT"), grp("g_vT")
    ktok, vtok = grp("g_ktok", 1)[0], grp("g_vtok", 1)[0]
    trig, E, decs, deci, L, Aq, AqT = [grp("g_" + n, 1)[0] for n in ("trig", "E", "decs", "deci", "L", "Aq", "AqT")]
    X, Y = grp("g_X"), grp("g_Y")
    R = grp("g_R", 1)[0]
    vb, kbg, kdec, u_, wT = [grp("g_" + n, 1)[0] for n in ("vb", "kbg", "kdec", "u", "wT")]
    o_, osq, on = [grp("g_" + n, 1)[0] for n in ("o", "osq", "on")]
    vnew = [c.sb("g_vnew%d" % i, [128, 128], F32) for i in range(2)]
    tq = [c.sb("g_tq%d" % i, [128, 128], F32) for i in range(2)]
    S = [c.sb("g_S%d" % i, [128, 128], F32) for i in range(2)]
    rs = c.sb("g_rs", [128, GS], F32)
    zt = [c.sb("g_zt%d" % i, [128, W], BF16) for i in range(2)]
    yo = [c.sb("g_yo%d" % i, [128, W], BF16) for i in range(2)]

    for h in range(4):
        c.op("act", lambda e: e.activation(out=tmpn[:], in_=abt[:, :, h], func=AF.Exp, bias=dtb[:, h:h + 1]),
             reads=["g_abt", "g_dtb"], writes=["g_tmpn"])
        c.op("act", lambda e: e.activation(out=tmpn[:], in_=tmpn[:], func=AF.Ln, bias=1.0), reads=["g_tmpn"], writes=["g_tmpn"])
        c.op("dve", lambda e: e.tensor_scalar(out=gg[:], in0=tmpn[:], scalar1=nea[:, h:h + 1], scalar2=None, op0=ALU.mult),
             reads=["g_tmpn", "g_nea"], writes=["g_gg"])
        c.op("act", lambda e: e.activation(out=beta[:], in_=abt[:, :, 4 + h], func=AF.Sigmoid), reads=["g_abt"], writes=["g_beta"])
        pbs, pbsk = bank()
        psmall = pbs[:].rearrange("p g c -> p (g c)")
        c.op("pe", lambda e: e.matmul(psmall[:, 0:NC], triu[:], gg[:], start=True, stop=True),
             reads=["g_triu", "g_gg"], writes=[pbsk])
        c.op("dve", lambda e: e.tensor_copy(out=gc[:], in_=psmall[:, 0:NC]), reads=[pbsk], writes=["g_gc"])
        pbs, pbsk = bank()
        psmall = pbs[:].rearrange("p g c -> p (g c)")
        c.op("pe", lambda e: e.matmul(psmall[:, 0:NC], ones[:], gg[:], start=True, stop=True),
             reads=["g_ones", "g_gg"], writes=[pbsk])
        c.op("dve", lambda e: e.tensor_copy(out=gl[:], in_=psmall[:, 0:NC]), reads=[pbsk], writes=["g_gl"])
        c.op("act", lambda e: e.activation(out=egc[:], in_=gc[:], func=AF.Exp), reads=["g_gc"], writes=["g_egc"])
        c.op("act", lambda e: e.activation(out=egl[:], in_=gl[:], func=AF.Exp), reads=["g_gl"], writes=["g_egl"])
        c.op("dve", lambda e: e.tensor_tensor(out=kdf[:], in0=gl[:], in1=gc[:], op=ALU.subtract),
             reads=["g_gl", "g_gc"], writes=["g_kdf"])
        c.op("act", lambda e: e.activation(out=kdf[:], in_=kdf[:], func=AF.Exp), reads=["g_kdf"], writes=["g_kdf"])
        c.op("dve", lambda e: e.tensor_tensor(out=bgc[:], in0=beta[:], in1=egc[:], op=ALU.mult),
             reads=["g_beta", "g_egc"], writes=["g_bgc"])
        c.op("pool", lambda e: e.memset(S[0][:], 0.0), writes=["g_S0"])
        sidx = 0
        if DBG_STOP == 2:
            c.end_phase(); return
        for gi in range(NC // GS):
            t0 = gi * W
            b2 = gi % 2
            kT_, qT_, vT_ = kT[b2], qT[b2], vT[b2]
            kTk, qTk, vTk = "g_kT%d" % b2, "g_qT%d" % b2, "g_vT%d" % b2
            c.dma(qT_[:], gqkv[h, :, t0:t0 + W].rearrange("p (g c) -> p g c", c=128), reads=["gqkv"], writes=[qTk])
            c.dma(kT_[:], gqkv[4 + h, :, t0:t0 + W].rearrange("p (g c) -> p g c", c=128), reads=["gqkv"], writes=[kTk])
            c.dma(vT_[:], gqkv[8 + h, :, t0:t0 + W].rearrange("p (g c) -> p g c", c=128), reads=["gqkv"], writes=[vTk])
            z_ = zt[b2]; zk = "g_zt%d" % b2
            c.dma(z_[:], projT[O_ZA + h * 128:O_ZA + (h + 1) * 128, t0:t0 + W], writes=[zk])
            pb, pbk = bank()
            for g in range(GS):
                c.op("pe", lambda e: e.transpose(pb[:, g, :], kT_[:, g, :], ident[:]), reads=[kTk, "g_ident"], writes=[pbk])
            c.op("act", lambda e: e.activation(out=ktok[:], in_=pb[:], func=AF.Identity), reads=[pbk], writes=["g_ktok"])
            pb, pbk = bank()
            for g in range(GS):
                c.op("pe", lambda e: e.transpose(pb[:, g, :], vT_[:, g, :], ident[:]), reads=[vTk, "g_ident"], writes=[pbk])
            c.op("dve", lambda e: e.tensor_copy(out=vtok[:], in_=pb[:]), reads=[pbk], writes=["g_vtok"])
            for g in range(GS):
                n = gi * GS + g
                c.op("dve", lambda e: e.tensor_scalar(out=trig[:, g, :], in0=triu[:], scalar1=gg[:, n:n + 1], scalar2=None,
                                                      op0=ALU.mult), reads=["g_triu", "g_gg"], writes=["g_trig"])
            pb, pbk = bank()
            for g in range(GS):
                c.op("pe", lambda e: e.matmul(pb[:, g, :], trig[:, g, :], slm[:], start=True, stop=True),
                     reads=["g_trig", "g_sl"], writes=[pbk])
            c.op("act", lambda e: e.activation(out=E[:], in_=pb[:], func=AF.Exp), reads=[pbk], writes=["g_E"])
            c.op("pool", lambda e: e.tensor_tensor(out=decs[:], in0=E[:], in1=mks[:], op=ALU.mult),
                 reads=["g_E", "g_mks"], writes=["g_decs"])
            c.op("pool", lambda e: e.tensor_tensor(out=deci[:], in0=E[:], in1=mki[:], op=ALU.mult),
                 reads=["g_E", "g_mki"], writes=["g_deci"])
            pb, pbk = bank()
            for g in range(GS):
                c.op("pe", lambda e: e.matmul(pb[:, g, :], kT_[:, g, :], kT_[:, g, :], start=True, stop=True),
                     reads=[kTk], writes=[pbk])
            for g in range(GS):
                n = gi * GS + g
                c.op("dve", lambda e: e.scalar_tensor_tensor(out=L[:, g, :], in0=pb[:, g, :], scalar=beta[:, n:n + 1],
                                                             in1=decs[:, g, :], op0=ALU.mult, op1=ALU.mult),
                     reads=[pbk, "g_beta", "g_decs"], writes=["g_L"])
            pb, pbk = bank()
            for g in range(GS):
                c.op("pe", lambda e: e.matmul(pb[:, g, :], qT_[:, g, :], kT_[:, g, :], start=True, stop=True),
                     reads=[qTk, kTk], writes=[pbk])
            c.op("dve", lambda e: e.tensor_tensor(out=Aq[:], in0=pb[:], in1=deci[:], op=ALU.mult),
                 reads=[pbk, "g_deci"], writes=["g_Aq"])
            if DBG_STOP == 3:
                c.end_phase(); return
            pb, pbk = bank()
            for g in range(GS):
                c.op("pe", lambda e: e.transpose(pb[:, g, :], Aq[:, g, :], ident[:]), reads=["g_Aq", "g_ident"], writes=[pbk])
            if DBG_STOP == 29:
                c.end_phase(); return
            c.op("act", lambda e: e.activation(out=AqT[:], in_=pb[:], func=AF.Identity), reads=[pbk], writes=["g_AqT"])
            if DBG_STOP == 30:
                c.end_phase(); return
            pb, pbk = bank()
            for g in range(GS):
                c.op("pe", lambda e: e.transpose(pb[:, g, :], L[:, g, :], ident[:]), reads=["g_L", "g_ident"], writes=[pbk])
            if DBG_STOP == 305:
                c.end_phase(); return
            c.op("act", lambda e: e.activation(out=X[0][:], in_=pb[:], func=AF.Identity), reads=[pbk], writes=["g_X0"])
            if DBG_STOP == 306:
                c.end_phase(); return
            c.op("dve", lambda e: e.tensor_tensor(out=R[:], in0=identg[:], in1=X[0][:], op=ALU.subtract),
                 reads=["g_X0", "g_identg"], writes=["g_R"])
            Yc, Yk = L, "g_L"
            Xc, Xk = X[0], "g_X0"
            if DBG_STOP == 31:
                c.end_phase(); return
            for lvl in range(6):
                if DBG_STOP == 32 + lvl and lvl > 0:
                    c.end_phase(); return
                last = lvl == 5
                nX, nXk = X[(lvl + 1) % 2], "g_X%d" % ((lvl + 1) % 2)
                nY, nYk = Y[lvl % 2], "g_Y%d" % (lvl % 2)
                if not last:
                    pbx, pbxk = bank()
                    for g in range(GS):
                        c.op("pe", lambda e: e.matmul(pbx[:, g, :], Yc[:, g, :], Xc[:, g, :], start=True, stop=True),
                             reads=[Yk, Xk], writes=[pbxk])
                pby, pbyk = bank()
                for g in range(GS):
                    c.op("pe", lambda e: e.matmul(pby[:, g, :], Xc[:, g, :], Yc[:, g, :], start=True, stop=True),
                         reads=[Yk, Xk], writes=[pbyk])
                c.op("dve", lambda e: e.tensor_copy(out=nY[:], in_=pby[:]), reads=[pbyk], writes=[nYk])
                if not last:
                    c.op("act", lambda e: e.activation(out=nX[:], in_=pbx[:], func=AF.Identity), reads=[pbxk], writes=[nXk])
                pbr, pbrk = bank()
                for g in range(GS):
                    c.op("pe", lambda e: e.matmul(pbr[:, g, :], nY[:, g, :], R[:, g, :], start=True, stop=True),
                         reads=[nYk, "g_R"], writes=[pbrk])
                c.op("dve", lambda e: e.tensor_tensor(out=R[:], in0=R[:], in1=pbr[:], op=ALU.add),
                     reads=["g_R", pbrk], writes=["g_R"])
                Yc, Yk = nY, nYk
                Xc, Xk = nX, nXk
            if DBG_STOP == 4:
                c.end_phase(); return
            for g in range(GS):
                n = gi * GS + g
                c.op("pool", lambda e: e.tensor_scalar(out=vb[:, g, :], in0=vtok[:, g, :], scalar1=beta[:, n:n + 1],
                                                       scalar2=None, op0=ALU.mult), reads=["g_vtok", "g_beta"], writes=["g_vb"])
                c.op("pool", lambda e: e.tensor_scalar(out=kbg[:, g, :], in0=ktok[:, g, :], scalar1=bgc[:, n:n + 1],
                                                       scalar2=None, op0=ALU.mult), reads=["g_ktok", "g_bgc"], writes=["g_kbg"])
                c.op("pool", lambda e: e.tensor_scalar(out=kdec[:, g, :], in0=ktok[:, g, :], scalar1=kdf[:, n:n + 1],
                                                       scalar2=None, op0=ALU.mult), reads=["g_ktok", "g_kdf"], writes=["g_kdec"])
            pb, pbk = bank()
            for g in range(GS):
                c.op("pe", lambda e: e.matmul(pb[:, g, :], R[:, g, :], vb[:, g, :], start=True, stop=True),
                     reads=["g_R", "g_vb"], writes=[pbk])
            c.op("act", lambda e: e.activation(out=u_[:], in_=pb[:], func=AF.Identity), reads=[pbk], writes=["g_u"])
            pb, pbk = bank()
            for g in range(GS):
                c.op("pe", lambda e: e.matmul(pb[:, g, :], kbg[:, g, :], R[:, g, :], start=True, stop=True),
                     reads=["g_R", "g_kbg"], writes=[pbk])
            c.op("dve", lambda e: e.tensor_copy(out=wT[:], in_=pb[:]), reads=[pbk], writes=["g_wT"])
            for g in range(GS):
                n = gi * GS + g
                Sc, Sk = S[sidx % 2], "g_S%d" % (sidx % 2)
                Sn, Snk = S[(sidx + 1) % 2], "g_S%d" % ((sidx + 1) % 2)
                sidx += 1
                vn, vnk = vnew[n % 2], "g_vnew%d" % (n % 2)
                tq_, tqk = tq[n % 2], "g_tq%d" % (n % 2)
                c.op("pe", lambda e: e.matmul(pscan[:, 0, :], wT[:, g, :], Sc[:], start=True, stop=True),
                     reads=["g_wT", Sk], writes=["g_ps0"])
                c.op("dve", lambda e: e.tensor_tensor(out=vn[:], in0=u_[:, g, :], in1=pscan[:, 0, :], op=ALU.subtract),
                     reads=["g_u", "g_ps0"], writes=[vnk])
                c.op("pe", lambda e: e.matmul(pscan[:, 1, :], qT_[:, g, :], Sc[:], start=True, stop=True),
                     reads=[qTk, Sk], writes=["g_ps1"])
                c.op("pe", lambda e: e.matmul(pscan[:, 2, :], AqT[:, g, :], vn[:], start=True, stop=True),
                     reads=["g_AqT", vnk], writes=["g_ps2"])
                c.op("pe", lambda e: e.matmul(pscan[:, 3, :], kdec[:, g, :], vn[:], start=True, stop=True),
                     reads=["g_kdec", vnk], writes=["g_ps3"])
                c.op("act", lambda e: e.activation(out=tq_[:], in_=pscan[:, 1, :], func=AF.Identity, scale=egc[:, n:n + 1]),
                     reads=["g_ps1", "g_egc"], writes=[tqk])
                c.op("dve", lambda e: e.tensor_tensor(out=o_[:, g, :], in0=tq_[:], in1=pscan[:, 2, :], op=ALU.add),
                     reads=[tqk, "g_ps2"], writes=["g_o"])
                c.op("dve", lambda e: e.scalar_tensor_tensor(out=Sn[:], in0=Sc[:], scalar=egl[:, n:n + 1], in1=pscan[:, 3, :],
                                                             op0=ALU.mult, op1=ALU.add),
                     reads=[Sk, "g_egl", "g_ps3"], writes=[Snk])
            if DBG_STOP == 5:
                c.end_phase(); return
            c.op("pool", lambda e: e.tensor_tensor(out=osq[:], in0=o_[:], in1=o_[:], op=ALU.mult), reads=["g_o"], writes=["g_osq"])
            c.op("dve", lambda e: e.tensor_reduce(out=rs[:], in_=osq[:], axis=mybir.AxisListType.X, op=ALU.add),
                 reads=["g_osq"], writes=["g_rs"])
            c.op("dve", lambda e: e.tensor_scalar(out=rs[:], in0=rs[:], scalar1=1.0 / 128, scalar2=EPS, op0=ALU.mult,
                                                  op1=ALU.add), reads=["g_rs"], writes=["g_rs"])
            c.op("act", lambda e: e.activation(out=rs[:], in_=rs[:], func=AF.Sqrt), reads=["g_rs"], writes=["g_rs"])
            c.op("dve", lambda e: e.reciprocal(out=rs[:], in_=rs[:]), reads=["g_rs"], writes=["g_rs"])
            for g in range(GS):
                c.op("dve", lambda e: e.scalar_tensor_tensor(out=on[:, g, :], in0=o_[:, g, :], scalar=rs[:, g:g + 1],
                                                             in1=gnb[:], op0=ALU.mult, op1=ALU.mult),
                     reads=["g_o", "g_rs", "g_gnb"], writes=["g_on"])
            pb, pbk = bank()
            for g in range(GS):
                c.op("pe", lambda e: e.transpose(pb[:, g, :], on[:, g, :], ident[:]), reads=["g_on", "g_ident"], writes=[pbk])
            y_ = yo[b2]; yk = "g_yo%d" % b2
            c.op("dve", lambda e: e.tensor_tensor(out=y_[:], in0=pb[:].rearrange("p g c -> p (g c)"), in1=z_[:], op=ALU.mult),
                 reads=[pbk, zk], writes=[yk])
            c.dma(yT[h * 128:(h + 1) * 128, t0:t0 + W], y_[:], reads=[yk])
    c.end_phase()


def build(T, nlayers=2, only=None):
    nc = bass.Bass("TRN2", target_bir_lowering=False)

    def di(n, shape, dt=F32):
        return nc.dram_tensor(n, shape, dt, kind="ExternalInput").ap()

    def ds(n, shape, dt=F32):
        return nc.dram_tensor(n, shape, dt, kind="Internal").ap()
    L = nlayers
    xT = di("xT", [D, T])
    pT = di("pT", [L, 256, T])
    w_in = di("w_in", [L, D, PW])
    w_branch = di("w_branch", [L, 4, 512, 1024])
    w_out = di("w_out", [L, 1024, 1024])
    w_ple = di("w_ple", [L, 256, 1024])
    w_pg = di("w_pg", [L, 1024, 1024])
    b_gate = di("b_gate", [L, 128, 32])
    b_pg = di("b_pg", [L, 128, 8])
    ln_g = di("ln_g", [L, 128, 8])
    ln_b = di("ln_b", [L, 128, 8])
    cw = di("cw", [L, 128, 4, 31])
    cvec = di("cvec", [L, 128, 3, 4])
    cw4 = di("cw4", [L, 128, 12, 4])
    alog = di("alog", [L, 128, 4])
    dtb = di("dtb", [L, 128, 4])
    gn = di("gn", [L, 128, 128])
    fb = di("fb", [L, 8, 1])
    ident = di("ident", [128, 128])
    ones_f = di("ones_f", [128, 128])
    identb = di("identb", [128, 128], BF16)
    negm_i = di("negm_i", [128, 4, 512], BF16)
    negm_s = di("negm_s", [128, 4, 512], BF16)
    m01_s = di("m01_s", [128, 4, 512], BF16)
    negu = di("negu", [128, 128], BF16)
    triu = di("triu", [128, 128])
    slm = di("slm", [128, 128])
    mki = di("mki", [128, 128])
    outT = nc.dram_tensor("outT", [D, T], F32, kind="ExternalOutput").ap()
    projT = ds("projT", [PW, T], BF16)
    smallT = ds("smallT", [16, T])
    yT = ds("yT", [2048, T], BF16)
    gqkv = ds("gqkv", [12, 128, T])
    fxq = ds("fxq", [8, T], BF16)
    fxk = ds("fxk", [8, 3, T], BF16)
    xmid = [ds("xmid%d" % i, [D, T]) for i in range(max(L - 1, 1))]
    with ExitStack() as es:
        c = Ctx(nc, es)
        xin = xT
        for l in range(L):
            xo = outT if l == L - 1 else xmid[l]
            on = lambda n: only is None or n in only
            if on("proj"):
                phase_proj(c, T, xin, w_in[l], b_gate[l], projT, smallT)
            if on("conv"):
                phase_conv(c, T, projT, cw[l], cvec[l], ident, ones_f, yT)
            if on("fox"):
                phase_fox(c, T, projT, smallT, fb[l], negm_i, identb, fxq, fxk, yT)
            if on("sb"):
                phase_sb(c, T, projT, negm_s, m01_s, identb, negu, yT)
            if on("gdn_pre"):
                phase_gdn_pre(c, T, projT, cw4[l], ident, ones_f, gqkv)
            if on("gdn"):
                phase_gdn(c, T, projT, smallT, gqkv, alog[l], dtb[l], gn[l], ident, ones_f, triu, slm, slm, mki, yT)
            if on("out"):
                phase_out(c, T, xin, pT[l], yT, projT, w_branch[l], w_out[l], w_ple[l], w_pg[l], b_pg[l], ln_g[l],
                          ln_b[l], ones_f, xo)
            xin = xo
        c.finish()
        ninst = c.ninst
    return nc, ninst


def host_inputs(x_b, p_b, w, L=2):
    import ml_dtypes
    bf = lambda a: np.ascontiguousarray(a).astype(ml_dtypes.bfloat16)
    f32 = lambda a: np.ascontiguousarray(a, dtype=np.float32)
    v8 = lambda v: f32(np.stack([v[l].reshape(8, 128).T for l in range(L)]))
    rep = lambda v: f32(np.stack([np.broadcast_to(v[l][None, :], (128, v[l].shape[0])) for l in range(L)]))
    v4 = lambda v: v.reshape(4, 128).T
    s_ = np.arange(128)[:, None, None]; j_ = np.arange(4)[None, :, None]; q_ = np.arange(512)[None, None, :]
    incl = (s_ + 128 * j_ <= q_); strict = (s_ + 128 * j_ < q_)
    jj = np.arange(128)[:, None]; ss = np.arange(128)[None, :]
    d = {
        "xT": f32(x_b.T), "pT": f32(np.stack([p_b[l].T for l in range(L)])),
        "w_in": f32(w["w_in"][:L]), "w_branch": f32(w["w_branch"][:L]), "w_out": f32(w["w_out"][:L]),
        "w_ple": f32(w["w_ple"][:L]), "w_pg": f32(w["w_ple_gate"][:L]),
        "b_gate": f32(np.stack([w["b_gate"][l].reshape(32, 128).T for l in range(L)])),
        "b_pg": v8(w["b_ple_gate"]), "ln_g": v8(w["ln_g"]), "ln_b": v8(w["ln_b"]),
        "cw": f32(np.stack([w["conv_dw"][l].T.reshape(4, 128, 31).transpose(1, 0, 2) for l in range(L)])),
        "cvec": f32(np.stack([np.stack([v4(w["conv_dw_bias"][l]), v4(w["conv_ln_g"][l]), v4(w["conv_ln_b"][l])], axis=1)
                              for l in range(L)])),
        "cw4": f32(np.stack([w["conv_qkv"][l].T.reshape(12, 128, 4).transpose(1, 0, 2) for l in range(L)])),
        "alog": rep(w["a_log"]), "dtb": rep(w["dt_bias"]), "gn": rep(w["gdn_norm"]),
        "fb": f32(np.stack([w["forget_bias"][l].reshape(8, 1) for l in range(L)])),
        "ident": np.eye(128, dtype=np.float32), "ones_f": np.ones((128, 128), np.float32),
        "identb": bf(np.eye(128, dtype=np.float32)),
        "negm_i": bf(np.where(incl, 0.0, -30000.0).astype(np.float32)),
        "negm_s": bf(np.where(strict, 0.0, -30000.0).astype(np.float32)),
        "m01_s": bf(strict.astype(np.float32)),
        "negu": bf(-(jj >= ss).astype(np.float32)),
        "triu": (jj <= ss).astype(np.float32), "slm": (jj > ss).astype(np.float32), "mki": (jj >= ss).astype(np.float32),
    }
    return d


_CACHE = {}


def kernel(**inputs):
    x = np.asarray(inputs["x"])
    p = np.asarray(inputs["p"])
    B, T, _ = x.shape
    w = {k: np.asarray(v) for k, v in inputs.items() if k not in ("x", "p")}
    L = w["w_in"].shape[0]
    if (T, L) not in _CACHE:
        _CACHE[(T, L)] = build(T, L)[0]
    nc = _CACHE[(T, L)]
    in_maps = [host_inputs(x[b], p[:, b], w, L) for b in range(B)]
    res = run_bass_kernel_spmd(nc, in_maps, core_ids=list(range(B)))
    out = np.stack([np.asarray(r["outT"]).T for r in res.results])
    return np.ascontiguousarray(out, dtype=np.float32)
```

```python
import numpy as np
from contextlib import ExitStack
import concourse.bass as bass
import concourse.mybir as mybir
from concourse.bass_utils import run_bass_kernel_spmd

F32 = mybir.dt.float32
BF16 = mybir.dt.bfloat16
AF = mybir.ActivationFunctionType
ALU = mybir.AluOpType

D = 1024
PW = 11792
PWL = 8704
NHA = 4
NHG = 2
YL = 1280
Y_A, Y_B, Y_C, Y_D = 0, 256, 768, 1024
ALPHA = 4 ** 0.25
EPS = 1e-5
SEM_EPOCH = 20000
DBG_STOP = 0

R_QA, R_KA, R_VA, R_ZA, R_AA, R_BA = 0, 512, 1024, 1536, 2048, 2052
R_GL, R_GG, R_ZB = 2056, 2568, 3080
R_QC, R_KC, R_VC, R_ZC = 3592, 4104, 4616, 5128
R_QD, R_KD, R_VD, R_ZD, R_FD = 5640, 6152, 6664, 7176, 7688
R_G = 7696
O_QA, O_KA, O_VA, O_ZA = 0, 256, 512, 768
O_GL, O_GG, O_ZB = 1024, 1536, 2048
O_QC, O_KC, O_VC, O_ZC = 2560, 2816, 3072, 3328
O_QD, O_KD, O_VD, O_ZD = 3584, 3840, 4096, 4352
O_G = 4608


class Ctx:
    def __init__(self, nc, es):
        self.nc, self.es = nc, es
        self.eng = {"pe": nc.tensor, "act": nc.scalar, "dve": nc.vector, "pool": nc.gpsimd, "sp": nc.sync}
        self.sem = {}
        self.cnt = {}
        self.nsem = 0
        for e in self.eng:
            self._new_sem(e)
        self.waited = {e: {} for e in self.eng}
        self.lastw = {}
        self.readers = {}
        self.dsem = [es.enter_context(nc.semaphore("dq%d" % i)) for i in range(24)]
        self.dcnt = [0] * 24
        self.dnext = 0
        self.ninst = 0

    def _new_sem(self, e):
        self.sem[e] = self.es.enter_context(self.nc.semaphore("s_%s_%d" % (e, self.nsem)))
        self.nsem += 1
        self.cnt[e] = 0

    def _wait(self, e, tok):
        sem, val, src = tok
        if src == "pe" and e == "pe":
            return
        w = self.waited[e]
        k = id(sem)
        if w.get(k, 0) >= val:
            return
        w[k] = val
        self.eng[e].wait_ge(sem, val)

    def _deps(self, e, reads, writes):
        for k in reads:
            t = self.lastw.get(k)
            if t is not None:
                self._wait(e, t)
        for k in writes:
            t = self.lastw.get(k)
            if t is not None:
                self._wait(e, t)
            rd = self.readers.get(k)
            if rd:
                for key, t in rd.items():
                    self._wait(e, t)

    def _commit(self, tok, reads, writes):
        for k in writes:
            self.lastw[k] = tok
            self.readers[k] = {}
        for k in reads:
            rd = self.readers.setdefault(k, {})
            if tok[2] == "dma":
                rd[("dma", id(tok[0]))] = tok
            else:
                rd[tok[2]] = tok

    def op(self, e, fn, reads=(), writes=()):
        self._deps(e, reads, writes)
        ins = fn(self.eng[e])
        if self.cnt[e] >= SEM_EPOCH:
            self._new_sem(e)
        self.cnt[e] += 1
        ins.then_inc(self.sem[e], 1)
        tok = (self.sem[e], self.cnt[e], e)
        self._commit(tok, reads, writes)
        self.ninst += 1
        return tok

    def dma(self, out, in_, reads=(), writes=(), q="sp"):
        j = self.dnext
        self.dnext = (self.dnext + 1) % len(self.dsem)
        self._deps(q, reads, writes)
        if self.dcnt[j] > 0:
            self._wait(q, (self.dsem[j], self.dcnt[j], "dma"))
        self.eng[q].dma_start(out=out, in_=in_).then_inc(self.dsem[j], 16)
        self.dcnt[j] += 16
        tok = (self.dsem[j], self.dcnt[j], "dma")
        self._commit(tok, reads, writes)
        self.ninst += 1
        return tok

    def collective(self, kind, ins, outs, groups, reads=(), writes=()):
        if not hasattr(self, "ccsem"):
            self.ccsem = self.es.enter_context(self.nc.semaphore("ccsem"))
            self.cccnt = 0
        self._deps("pool", reads, writes)
        self.nc.gpsimd.collective_compute(kind, ALU.bypass, replica_groups=groups, ins=ins, outs=outs).then_inc(self.ccsem)
        self.cccnt += 1
        tok = (self.ccsem, self.cccnt, "dma")
        self._commit(tok, reads, writes)
        for e in self.eng:
            self._wait(e, tok)
        return tok

    def finish(self):
        for j in range(len(self.dsem)):
            if self.dcnt[j] > 0:
                self._wait("sp", (self.dsem[j], self.dcnt[j], "dma"))
        for e in ("pe", "act", "dve", "pool"):
            if self.cnt[e] > 0:
                self._wait("sp", (self.sem[e], self.cnt[e], e))

    def sb(self, name, shape, dt):
        return self.pes.enter_context(self.nc.sbuf_tensor("%s_%d" % (name, self.phase_no), shape, dt))

    def ps(self, name, shape, dt=F32):
        return self.pes.enter_context(self.nc.psum_tensor("%s_%d" % (name, self.phase_no), shape, dt))

    def begin_phase(self):
        self.pes = ExitStack()
        self.phase_no = getattr(self, "phase_no", 0) + 1

    def end_phase(self):
        self.barrier()
        self.pes.close()

    def barrier(self):
        for e in self.eng:
            for j in range(len(self.dsem)):
                if self.dcnt[j] > 0:
                    self._wait(e, (self.dsem[j], self.dcnt[j], "dma"))
            for f in ("pe", "act", "dve", "pool"):
                if f != e and self.cnt[f] > 0:
                    self._wait(e, (self.sem[f], self.cnt[f], f))
        self.lastw.clear()
        self.readers.clear()


def proj_chunks():
    ch = []

    def add(o, n, kind):
        for i in range(n // 128):
            ch.append((o + 128 * i, 128, kind))
    add(O_QA, 768, 0)
    add(O_ZA, 256, 1)
    add(O_GL, 512, 0)
    add(O_GG, 512, 2)
    add(O_ZB, 512, 1)
    add(O_QC, 256, 3)
    add(O_KC, 512, 0)
    add(O_ZC, 256, 1)
    add(O_QD, 256, 3)
    add(O_KD, 512, 0)
    add(O_ZD, 256, 1)
    add(O_G, 4096, 4)
    return ch


def phase_proj(c, T, xT, w_in, w_small, b_gate, projT, smallT):
    nc = c.nc
    c.begin_phase()
    ST = min(4096, T)
    nst = T // ST
    nsub = ST // 512
    xs = [c.sb("p1_xs%d" % i, [128, 8, 512], F32) for i in range(2)]
    xb = c.sb("p1_xb", [128, 8, ST], BF16)
    ws = [c.sb("p1_ws%d" % i, [128, 8, 512], F32) for i in range(2)]
    wb = [c.sb("p1_wb%d" % i, [128, 8, 512], BF16) for i in range(2)]
    wss = c.sb("p1_wss", [128, 8, 8], F32)
    wsb = c.sb("p1_wsb", [128, 8, 8], BF16)
    bg = c.sb("p1_bg", [128, 32], F32)
    ob = [c.sb("p1_ob%d" % i, [128, 512], BF16) for i in range(4)]
    osm = c.sb("p1_osm", [8, 512], F32)
    pss = [c.ps("p1_ps%d" % i, [128, 512]) for i in range(4)]
    c.dma(bg[:], b_gate, writes=["p1_bg"])
    c.dma(wss[:], w_small.rearrange("(k p) n -> p k n", p=128), writes=["p1_wss"])
    c.op("pool", lambda e: e.tensor_copy(out=wsb[:], in_=wss[:]), reads=["p1_wss"], writes=["p1_wsb"])
    chunks = proj_chunks()
    groups = [chunks[i:i + 4] for i in range(0, len(chunks), 4)]
    xTv = xT.rearrange("(k p) t -> p k t", p=128)
    w_v = w_in.rearrange("(k p) n -> p k n", p=128)
    it = 0
    for st in range(nst):
        for sub in range(nsub):
            t0 = st * ST + sub * 512
            s = xs[(st * nsub + sub) % 2]
            key = "p1_xs%d" % ((st * nsub + sub) % 2)
            c.dma(s[:], xTv[:, :, t0:t0 + 512], writes=[key])
            c.op("pool", lambda e: e.tensor_copy(out=xb[:, :, sub * 512:(sub + 1) * 512], in_=s[:]),
                 reads=[key], writes=["p1_xb%d" % sub])
        for sub in range(nsub):
            t0 = st * ST + sub * 512
            p = pss[it % 4]; pk = "p1_ps%d" % (it % 4)
            for k in range(8):
                c.op("pe", lambda e: e.matmul(p[0:8, :], wsb[:, k, :], xb[:, k, sub * 512:(sub + 1) * 512],
                                              start=(k == 0), stop=(k == 7)),
                     reads=["p1_wsb", "p1_xb%d" % sub], writes=[pk])
            c.op("dve", lambda e: e.tensor_copy(out=osm[:], in_=p[0:8, :]), reads=[pk], writes=["p1_osm"])
            c.dma(smallT[:, t0:t0 + 512], osm[:], reads=["p1_osm"])
            it += 1
        for gi, grp_ in enumerate(groups):
            g0 = grp_[0][0]
            assert all(grp_[j][0] == g0 + 128 * j for j in range(len(grp_)))
            gw = 128 * len(grp_)
            wsl = ws[gi % 2]; wbl = wb[gi % 2]
            c.dma(wsl[:, :, 0:gw], w_v[:, :, g0:g0 + gw], writes=["p1_ws%d" % (gi % 2)])
            c.op("pool", lambda e: e.tensor_copy(out=wbl[:, :, 0:gw], in_=wsl[:, :, 0:gw]), reads=["p1_ws%d" % (gi % 2)],
                 writes=["p1_wb%d" % (gi % 2)])
            for cj, (off, wd, kind) in enumerate(grp_):
                for sub in range(nsub):
                    t0 = st * ST + sub * 512
                    p = pss[it % 4]; pk = "p1_ps%d" % (it % 4)
                    o = ob[it % 4]; ok = "p1_ob%d" % (it % 4)
                    for k in range(8):
                        c.op("pe", lambda e: e.matmul(p[:], wbl[:, k, cj * 128:(cj + 1) * 128],
                                                      xb[:, k, sub * 512:(sub + 1) * 512],
                                                      start=(k == 0), stop=(k == 7)),
                             reads=["p1_wb%d" % (gi % 2), "p1_xb%d" % sub], writes=[pk])
                    if kind == 0:
                        c.op("dve", lambda e: e.tensor_copy(out=o[:], in_=p[:]), reads=[pk], writes=[ok])
                    elif kind == 3:
                        c.op("dve", lambda e: e.tensor_scalar(out=o[:], in0=p[:], scalar1=0.125, scalar2=None,
                                                              op0=ALU.mult), reads=[pk], writes=[ok])
                    elif kind == 1:
                        c.op("act", lambda e: e.activation(out=o[:], in_=p[:], func=AF.Silu), reads=[pk], writes=[ok])
                    elif kind == 2:
                        c.op("act", lambda e: e.activation(out=o[:], in_=p[:], func=AF.Sigmoid), reads=[pk], writes=[ok])
                    else:
                        gidx = (off - O_G) // 128
                        c.op("act", lambda e: e.activation(out=o[:], in_=p[:], func=AF.Sigmoid, bias=bg[:, gidx:gidx + 1]),
                             reads=[pk, "p1_bg"], writes=[ok])
                    c.dma(projT[off:off + 128, t0:t0 + 512], o[:], reads=[ok])
                    it += 1
    c.end_phase()


def load_w_bf16(c, dst, dst_key, src_rows, stg, n):
    i = c.stg_i = getattr(c, "stg_i", 0) + 1
    s = stg[i % len(stg)]
    sk = "stg%d" % (i % len(stg))
    c.dma(s[:, 0:n], src_rows, writes=[sk])
    c.op("pool", lambda e: e.tensor_copy(out=dst, in_=s[:, 0:n]), reads=[sk], writes=[dst_key])


def phase_out(c, T, xT, pT, yT, yg, projT, w_branch, w_out, w_ple, w_pg, b_pg, ln_g, ln_b, ones_f, outT):
    nc = c.nc
    c.begin_phase()
    TT = 256
    stg = [c.sb("po_stg%d" % i, [128, 1024], F32) for i in range(2)]
    wbr = c.sb("po_wbr", [128, 16, 1024], BF16)
    wo = c.sb("po_wo", [128, 8, 1024], BF16)
    wpg = c.sb("po_wpg", [128, 8, 1024], BF16)
    wpl = c.sb("po_wpl", [128, 2, 1024], BF16)
    vb = c.sb("po_vec", [128, 3, 8], F32)
    ones = c.sb("po_ones", [128, 128], F32)
    c.dma(vb[:, 0, :], b_pg, writes=["po_vec"])
    c.dma(vb[:, 1, :], ln_g, writes=["po_vec"])
    c.dma(vb[:, 2, :], ln_b, writes=["po_vec"])
    c.dma(ones[:], ones_f, writes=["po_ones"])
    wbv = w_branch.rearrange("b (k p) n -> p (b k) n", p=128)
    for i in range(16):
        load_w_bf16(c, wbr[:, i, :], "po_wbr", wbv[:, i, :], stg, 1024)
    for nm, dst, src, nk in (("po_wo", wo, w_out, 8), ("po_wpg", wpg, w_pg, 8), ("po_wpl", wpl, w_ple, 2)):
        sv = src.rearrange("(k p) n -> p k n", p=128)
        for i in range(nk):
            load_w_bf16(c, dst[:, i, :], nm, sv[:, i, :], stg, 1024)
    y = c.sb("po_y", [128, 16, TT], BF16)
    gt = [c.sb("po_gt%d" % i, [128, 8, TT], BF16) for i in range(2)]
    x = c.sb("po_x", [128, 8, TT], F32)
    pf = c.sb("po_pf", [128, 2, TT], F32)
    pb = c.sb("po_pb", [128, 2, TT], BF16)
    m = c.sb("po_m", [128, 8, TT], F32)
    mb = c.sb("po_mb", [128, 8, TT], BF16)
    r = c.sb("po_r", [128, 8, TT], F32)
    rb = c.sb("po_rb", [128, 8, TT], BF16)
    tmp = [c.sb("po_tmp%d" % i, [128, TT], F32) for i in range(2)]
    gp = [c.sb("po_gp%d" % i, [128, TT], F32) for i in range(2)]
    sq = [c.sb("po_sq%d" % i, [128, TT], F32) for i in range(2)]
    mean = c.sb("po_mean", [128, TT], F32)
    msq = c.sb("po_msq", [128, TT], F32)
    rstd = c.sb("po_rstd", [128, TT], F32)
    ot = [c.sb("po_ot%d" % i, [128, TT], F32) for i in range(2)]
    ps = [c.ps("po_ps%d" % i, [128, 512]) for i in range(4)]
    pstat = [c.ps("po_pst%d" % i, [128, 512]) for i in range(2)]
    def ysrc(i):
        br, kc = i // 4, i % 4
        if br == 1:
            return yT[Y_B + kc * 128:Y_B + (kc + 1) * 128, :]
        j = (0, None, 1, 2)[br] * 2 + (kc % 2)
        r_ = kc // 2
        return yg[j][r_ * 128:(r_ + 1) * 128, :]
    xv = xT.rearrange("(k p) t -> p k t", p=128)
    pv = pT.rearrange("(k p) t -> p k t", p=128)
    ov = outT.rearrange("(k p) t -> p k t", p=128)
    it = 0
    for tt in range(T // TT):
        t0 = tt * TT
        for i in range(16):
            c.dma(y[:, i, :], ysrc(i)[:, t0:t0 + TT], writes=["po_y"])
        c.dma(x[:], xv[:, :, t0:t0 + TT], writes=["po_x"])
        c.dma(pf[:], pv[:, :, t0:t0 + TT], writes=["po_pf"])
        c.op("pool", lambda e: e.tensor_copy(out=pb[:], in_=pf[:]), reads=["po_pf"], writes=["po_pb"])
        for br in range(4):
            g = gt[br % 2]; gk = "po_gt%d" % (br % 2)
            c.dma(g[:], projT[O_G + br * 1024:O_G + (br + 1) * 1024, t0:t0 + TT].rearrange("(k p) t -> p k t", p=128),
                  writes=[gk])
            for fo in range(8):
                p = ps[it % 4]; pk = "po_ps%d" % (it % 4); it += 1
                for kc in range(4):
                    c.op("pe", lambda e: e.matmul(p[:, 0:TT], wbr[:, br * 4 + kc, fo * 128:(fo + 1) * 128],
                                                  y[:, br * 4 + kc, :], start=(kc == 0), stop=(kc == 3)),
                         reads=["po_wbr", "po_y"], writes=[pk])
                mk = "po_m%d" % fo
                if br == 0:
                    c.op("dve", lambda e: e.tensor_tensor(out=m[:, fo, :], in0=p[:, 0:TT], in1=g[:, fo, :], op=ALU.mult),
                         reads=[pk, gk], writes=[mk])
                else:
                    t_ = tmp[it % 2]; tk = "po_tmp%d" % (it % 2)
                    c.op("dve", lambda e: e.tensor_tensor(out=t_[:], in0=p[:, 0:TT], in1=g[:, fo, :], op=ALU.mult),
                         reads=[pk, gk], writes=[tk])
                    if br < 3:
                        c.op("pool", lambda e: e.tensor_tensor(out=m[:, fo, :], in0=m[:, fo, :], in1=t_[:], op=ALU.add),
                             reads=[mk, tk], writes=[mk])
                    else:
                        c.op("pool", lambda e: e.tensor_tensor(out=mb[:, fo, :], in0=m[:, fo, :], in1=t_[:], op=ALU.add),
                             reads=[mk, tk], writes=["po_mb%d" % fo])
        for fo in range(8):
            p = ps[it % 4]; pk = "po_ps%d" % (it % 4); it += 1
            for k in range(8):
                c.op("pe", lambda e: e.matmul(p[:, 0:TT], wo[:, k, fo * 128:(fo + 1) * 128], mb[:, k, :],
                                              start=(k == 0), stop=(k == 7)),
                     reads=["po_wo"] + ["po_mb%d" % k], writes=[pk])
            c.op("dve", lambda e: e.scalar_tensor_tensor(out=r[:, fo, :], in0=x[:, fo, :], scalar=ALPHA, in1=p[:, 0:TT],
                                                         op0=ALU.mult, op1=ALU.add),
                 reads=[pk, "po_x"], writes=["po_r%d" % fo])
            c.op("act", lambda e: e.activation(out=rb[:, fo, :], in_=r[:, fo, :], func=AF.Identity),
                 reads=["po_r%d" % fo], writes=["po_rb%d" % fo])
        pst_s = pstat[0]; pst_q = pstat[1]
        for fo in range(8):
            p = ps[it % 4]; pk = "po_ps%d" % (it % 4); it += 1
            for k in range(8):
                c.op("pe", lambda e: e.matmul(p[:, 0:TT], wpg[:, k, fo * 128:(fo + 1) * 128], rb[:, k, :],
                                              start=(k == 0), stop=(k == 7)),
                     reads=["po_wpg", "po_rb%d" % k], writes=[pk])
            g_ = gp[fo % 2]; gk = "po_gp%d" % (fo % 2)
            c.op("act", lambda e: e.activation(out=g_[:], in_=p[:, 0:TT], func=AF.Sigmoid, bias=vb[:, 0, fo:fo + 1]),
                 reads=[pk, "po_vec"], writes=[gk])
            p2 = ps[it % 4]; pk2 = "po_ps%d" % (it % 4); it += 1
            for k in range(2):
                c.op("pe", lambda e: e.matmul(p2[:, 0:TT], wpl[:, k, fo * 128:(fo + 1) * 128], pb[:, k, :],
                                              start=(k == 0), stop=(k == 1)),
                     reads=["po_wpl", "po_pb"], writes=[pk2])
            t_ = tmp[fo % 2]; tk = "po_tmp%d" % (fo % 2)
            c.op("dve", lambda e: e.tensor_tensor(out=t_[:], in0=p2[:, 0:TT], in1=g_[:], op=ALU.mult),
                 reads=[pk2, gk], writes=[tk])
            c.op("pool", lambda e: e.tensor_tensor(out=r[:, fo, :], in0=r[:, fo, :], in1=t_[:], op=ALU.add),
                 reads=["po_r%d" % fo, tk], writes=["po_r%d" % fo])
            s_ = sq[fo % 2]; sk = "po_sq%d" % (fo % 2)
            c.op("act", lambda e: e.activation(out=s_[:], in_=r[:, fo, :], func=AF.Square),
                 reads=["po_r%d" % fo], writes=[sk])
            c.op("pe", lambda e: e.matmul(pst_s[:, 0:TT], ones[:], r[:, fo, :], start=(fo == 0), stop=(fo == 7)),
                 reads=["po_ones", "po_r%d" % fo], writes=["po_pst0"])
            c.op("pe", lambda e: e.matmul(pst_q[:, 0:TT], ones[:], s_[:], start=(fo == 0), stop=(fo == 7)),
                 reads=["po_ones", sk], writes=["po_pst1"])
        c.op("dve", lambda e: e.tensor_scalar(out=mean[:], in0=pst_s[:, 0:TT], scalar1=1.0 / D, scalar2=None, op0=ALU.mult),
             reads=["po_pst0"], writes=["po_mean"])
        c.op("dve", lambda e: e.tensor_tensor(out=msq[:], in0=mean[:], in1=mean[:], op=ALU.mult),
             reads=["po_mean"], writes=["po_msq"])
        c.op("dve", lambda e: e.scalar_tensor_tensor(out=msq[:], in0=pst_q[:, 0:TT], scalar=1.0 / D, in1=msq[:],
                                                     op0=ALU.mult, op1=ALU.subtract),
             reads=["po_pst1", "po_msq"], writes=["po_msq"])
        c.op("dve", lambda e: e.tensor_scalar(out=msq[:], in0=msq[:], scalar1=EPS, scalar2=None, op0=ALU.add),
             reads=["po_msq"], writes=["po_msq"])
        c.op("act", lambda e: e.activation(out=rstd[:], in_=msq[:], func=AF.Sqrt),
             reads=["po_msq"], writes=["po_rstd"])
        c.op("dve", lambda e: e.reciprocal(out=rstd[:], in_=rstd[:]), reads=["po_rstd"], writes=["po_rstd"])
        for fo in range(8):
            t_ = tmp[fo % 2]; tk = "po_tmp%d" % (fo % 2)
            o_ = ot[fo % 2]; ok = "po_ot%d" % (fo % 2)
            c.op("dve", lambda e: e.tensor_tensor(out=t_[:], in0=r[:, fo, :], in1=mean[:], op=ALU.subtract),
                 reads=["po_r%d" % fo, "po_mean"], writes=[tk])
            c.op("pool", lambda e: e.tensor_tensor(out=t_[:], in0=t_[:], in1=rstd[:], op=ALU.mult),
                 reads=[tk, "po_rstd"], writes=[tk])
            c.op("act", lambda e: e.activation(out=o_[:], in_=t_[:], func=AF.Identity, scale=vb[:, 1, fo:fo + 1],
                                               bias=vb[:, 2, fo:fo + 1]),
                 reads=[tk, "po_vec"], writes=[ok])
            c.dma(ov[:, fo, t0:t0 + TT], o_[:], reads=[ok])
    c.end_phase()


def phase_conv(c, T, projT, cw_d, cvec_d, ident_d, ones_f, yT):
    nc = c.nc
    c.begin_phase()
    TT = 512 if T >= 512 else T
    cw = c.sb("pc_cw", [128, 4, 31], F32)
    cvec = c.sb("pc_cvec", [128, 3, 4], F32)
    ident = c.sb("pc_ident", [128, 128], F32)
    ones = c.sb("pc_ones", [128, 128], F32)
    dg = c.sb("pc_dg", [128, 124, 128], BF16)
    hb = c.sb("pc_hb", [128, 4, 32 + T], BF16)
    c.dma(cw[:], cw_d, writes=["pc_cw"])
    c.dma(cvec[:], cvec_d, writes=["pc_cvec"])
    c.dma(ident[:], ident_d, writes=["pc_ident"])
    c.dma(ones[:], ones_f, writes=["pc_ones"])
    for ch in range(4):
        for k in range(31):
            c.op("dve", lambda e: e.tensor_scalar(out=dg[:, ch * 31 + k, :], in0=ident[:], scalar1=cw[:, ch, k:k + 1],
                                                  scalar2=None, op0=ALU.mult),
                 reads=["pc_cw", "pc_ident"], writes=["pc_dg"])
    ld = [c.sb("pc_ld%d" % i, [128, 2048], BF16) for i in range(4)]
    PT = min(2048, T)
    i = 0
    for ch in range(4):
        c.op("pool", lambda e: e.memset(hb[:, ch, 0:32], 0.0), writes=["pc_hb%d" % ch])
        for t0 in range(0, T, PT):
            a = ld[i % 4]; ak = "pc_ld%d" % (i % 4); i += 1
            b = ld[i % 4]; bk = "pc_ld%d" % (i % 4); i += 1
            c.dma(a[:, 0:PT], projT[O_GL + ch * 128:O_GL + (ch + 1) * 128, t0:t0 + PT], writes=[ak])
            c.dma(b[:, 0:PT], projT[O_GG + ch * 128:O_GG + (ch + 1) * 128, t0:t0 + PT], writes=[bk])
            c.op("pool", lambda e: e.tensor_tensor(out=hb[:, ch, 32 + t0:32 + t0 + PT], in0=a[:, 0:PT], in1=b[:, 0:PT],
                                                   op=ALU.mult),
                 reads=[ak, bk], writes=["pc_hb%d" % ch])
    cv = c.sb("pc_cv", [128, 4, TT], F32)
    sq = [c.sb("pc_sq%d" % i, [128, TT], F32) for i in range(2)]
    zb = [c.sb("pc_zb%d" % i, [128, TT], BF16) for i in range(2)]
    mean = c.sb("pc_mean", [128, TT], F32)
    msq = c.sb("pc_msq", [128, TT], F32)
    rstd = c.sb("pc_rstd", [128, TT], F32)
    tmp = [c.sb("pc_tmp%d" % i, [128, TT], F32) for i in range(2)]
    yo = [c.sb("pc_yo%d" % i, [128, TT], BF16) for i in range(2)]
    ps = [c.ps("pc_ps%d" % i, [128, 512]) for i in range(3)]
    pst = [c.ps("pc_pst%d" % i, [128, 512]) for i in range(2)]
    it = 0
    for tt in range(T // TT):
        t0 = tt * TT
        for ch in range(4):
            p = ps[it % 3]; pk = "pc_ps%d" % (it % 3); it += 1
            for k in range(31):
                c.op("pe", lambda e: e.matmul(p[:, 0:TT], dg[:, ch * 31 + k, :], hb[:, ch, 2 + t0 + k:2 + t0 + k + TT],
                                              start=(k == 0), stop=(k == 30)),
                     reads=["pc_dg", "pc_hb%d" % ch], writes=[pk])
            c.op("act", lambda e: e.activation(out=cv[:, ch, :], in_=p[:, 0:TT], func=AF.Identity, bias=cvec[:, 0, ch:ch + 1]),
                 reads=[pk, "pc_cvec"], writes=["pc_cv%d" % ch])
            s_ = sq[ch % 2]; sk = "pc_sq%d" % (ch % 2)
            c.op("act", lambda e: e.activation(out=s_[:], in_=cv[:, ch, :], func=AF.Square),
                 reads=["pc_cv%d" % ch], writes=[sk])
            c.op("pe", lambda e: e.matmul(pst[0][:, 0:TT], ones[:], cv[:, ch, :], start=(ch == 0), stop=(ch == 3)),
                 reads=["pc_ones", "pc_cv%d" % ch], writes=["pc_pst0"])
            c.op("pe", lambda e: e.matmul(pst[1][:, 0:TT], ones[:], s_[:], start=(ch == 0), stop=(ch == 3)),
                 reads=["pc_ones", sk], writes=["pc_pst1"])
        c.op("dve", lambda e: e.tensor_scalar(out=mean[:], in0=pst[0][:, 0:TT], scalar1=1.0 / 512, scalar2=None, op0=ALU.mult),
             reads=["pc_pst0"], writes=["pc_mean"])
        c.op("dve", lambda e: e.tensor_tensor(out=msq[:], in0=mean[:], in1=mean[:], op=ALU.mult),
             reads=["pc_mean"], writes=["pc_msq"])
        c.op("dve", lambda e: e.scalar_tensor_tensor(out=msq[:], in0=pst[1][:, 0:TT], scalar=1.0 / 512, in1=msq[:],
                                                     op0=ALU.mult, op1=ALU.subtract),
             reads=["pc_pst1", "pc_msq"], writes=["pc_msq"])
        c.op("dve", lambda e: e.tensor_scalar(out=msq[:], in0=msq[:], scalar1=EPS, scalar2=None, op0=ALU.add),
             reads=["pc_msq"], writes=["pc_msq"])
        c.op("act", lambda e: e.activation(out=rstd[:], in_=msq[:], func=AF.Sqrt), reads=["pc_msq"], writes=["pc_rstd"])
        c.op("dve", lambda e: e.reciprocal(out=rstd[:], in_=rstd[:]), reads=["pc_rstd"], writes=["pc_rstd"])
        for ch in range(4):
            z = zb[ch % 2]; zk = "pc_zb%d" % (ch % 2)
            c.dma(z[:], projT[O_ZB + ch * 128:O_ZB + (ch + 1) * 128, t0:t0 + TT], writes=[zk])
            t_ = tmp[ch % 2]; tk = "pc_tmp%d" % (ch % 2)
            c.op("dve", lambda e: e.tensor_tensor(out=t_[:], in0=cv[:, ch, :], in1=mean[:], op=ALU.subtract),
                 reads=["pc_cv%d" % ch, "pc_mean"], writes=[tk])
            c.op("pool", lambda e: e.tensor_tensor(out=t_[:], in0=t_[:], in1=rstd[:], op=ALU.mult),
                 reads=[tk, "pc_rstd"], writes=[tk])
            c.op("act", lambda e: e.activation(out=t_[:], in_=t_[:], func=AF.Silu, scale=cvec[:, 1, ch:ch + 1],
                                               bias=cvec[:, 2, ch:ch + 1]),
                 reads=[tk, "pc_cvec"], writes=[tk])
            o_ = yo[ch % 2]; ok = "pc_yo%d" % (ch % 2)
            c.op("dve", lambda e: e.tensor_tensor(out=o_[:], in0=t_[:], in1=z[:], op=ALU.mult),
                 reads=[tk, zk], writes=[ok])
            c.dma(yT[Y_B + ch * 128:Y_B + (ch + 1) * 128, t0:t0 + TT], o_[:], reads=[ok])
    c.end_phase()


def phase_fox(c, T, projT, smallT, fb_d, negmask_d, identb_d, fxq, fxk, yT):
    nc = c.nc
    c.begin_phase()
    QB = min(512, T)
    NT = T // 128
    fb = c.sb("pf_fb", [NHA, 1], F32)
    nfb = c.sb("pf_nfb", [NHA, 1], F32)
    c.dma(fb[:], fb_d, writes=["pf_fb"])
    c.op("dve", lambda e: e.tensor_scalar(out=nfb[:], in0=fb[:], scalar1=-1.0, scalar2=None, op0=ALU.mult),
         reads=["pf_fb"], writes=["pf_nfb"])
    PT = min(2048, T)
    onesr = c.sb("pf_onesr", [NHA, PT], F32)
    c.op("pool", lambda e: e.memset(onesr[:], 1.0), writes=["pf_onesr"])
    fin = c.sb("pf_fin", [NHA, PT], F32)
    l_ = c.sb("pf_l", [NHA, PT], F32)
    fn = [c.sb("pf_fn%d" % i, [NHA, PT], F32) for i in range(2)]
    hi = c.sb("pf_hi", [NHA, PT], BF16)
    r1 = c.sb("pf_r1", [NHA, PT], F32)
    mid = c.sb("pf_mid", [NHA, PT], BF16)
    lo = c.sb("pf_lo", [NHA, PT], BF16)
    nhi = c.sb("pf_nhi", [NHA, PT], BF16)
    for pi, t0 in enumerate(range(0, T, PT)):
        f_ = fn[pi % 2]; fk = "pf_fn%d" % (pi % 2)
        c.dma(fin[:], smallT[4:8, t0:t0 + PT], writes=["pf_fin"])
        c.op("act", lambda e: e.activation(out=l_[:], in_=fin[:], func=AF.Exp, scale=-1.0, bias=nfb[:]),
             reads=["pf_fin", "pf_nfb"], writes=["pf_l"])
        c.op("act", lambda e: e.activation(out=l_[:], in_=l_[:], func=AF.Ln, bias=1.0), reads=["pf_l"], writes=["pf_l"])
        if pi == 0:
            c.op("dve", lambda e: e.tensor_tensor_scan(out=f_[:], data0=onesr[:], data1=l_[:], initial=0.0,
                                                       op0=ALU.mult, op1=ALU.add),
                 reads=["pf_onesr", "pf_l"], writes=[fk])
        else:
            pf_ = fn[(pi - 1) % 2]
            c.op("dve", lambda e: e.tensor_tensor_scan(out=f_[:], data0=onesr[:], data1=l_[:], initial=pf_[:, PT - 1:PT],
                                                       op0=ALU.mult, op1=ALU.add),
                 reads=["pf_onesr", "pf_l", "pf_fn%d" % ((pi - 1) % 2)], writes=[fk])
        c.op("dve", lambda e: e.tensor_copy(out=hi[:], in_=f_[:]), reads=[fk], writes=["pf_hi"])
        c.op("dve", lambda e: e.tensor_tensor(out=r1[:], in0=f_[:], in1=hi[:], op=ALU.subtract),
             reads=[fk, "pf_hi"], writes=["pf_r1"])
        c.op("dve", lambda e: e.tensor_copy(out=mid[:], in_=r1[:]), reads=["pf_r1"], writes=["pf_mid"])
        c.op("dve", lambda e: e.tensor_tensor(out=r1[:], in0=r1[:], in1=mid[:], op=ALU.subtract),
             reads=["pf_r1", "pf_mid"], writes=["pf_r1"])
        c.op("dve", lambda e: e.tensor_copy(out=lo[:], in_=r1[:]), reads=["pf_r1"], writes=["pf_lo"])
        c.op("dve", lambda e: e.tensor_scalar(out=nhi[:], in0=hi[:], scalar1=-1.0, scalar2=None, op0=ALU.mult),
             reads=["pf_hi"], writes=["pf_nhi"])
        c.dma(fxq[:, t0:t0 + PT], nhi[:], reads=["pf_nhi"], writes=["fxq"])
        c.dma(fxk[:, 0, t0:t0 + PT], hi[:], reads=["pf_hi"], writes=["fxk"])
        c.dma(fxk[:, 1, t0:t0 + PT], mid[:], reads=["pf_mid"], writes=["fxk"])
        c.dma(fxk[:, 2, t0:t0 + PT], lo[:], reads=["pf_lo"], writes=["fxk"])
    negm = c.sb("pf_negm", [128, 4, 512], BF16)
    identb = c.sb("pf_identb", [128, 128], BF16)
    ones64 = c.sb("pf_ones64", [128, 64], BF16)
    c.dma(negm[:], negmask_d, writes=["pf_negm"])
    c.dma(identb[:], identb_d, writes=["pf_identb"])
    c.op("pool", lambda e: e.memset(ones64[:], 1.0), writes=["pf_ones64"])
    qp = [c.sb("pf_qp%d" % i, [68, T], BF16) for i in range(2)]
    kp = [c.sb("pf_kp%d" % i, [68, T], BF16) for i in range(2)]
    vT = [c.sb("pf_vT%d" % i, [64, T], BF16) for i in range(2)]
    vt = [c.sb("pf_vt%d" % i, [128, NT, 64], BF16) for i in range(2)]
    att = [c.sb("pf_att%d" % i, [128, 512], BF16) for i in range(3)]
    rec = c.sb("pf_rec", [64, 512], F32)
    o_ = c.sb("pf_o", [64, 512], F32)
    zt = [c.sb("pf_zt%d" % i, [64, 512], BF16) for i in range(2)]
    yo = [c.sb("pf_yo%d" % i, [64, 512], BF16) for i in range(2)]
    psz = [c.ps("pf_psz%d" % i, [128, 512]) for i in range(3)]
    psn = [c.ps("pf_psn%d" % i, [64, 512]) for i in range(2)]
    psd = [c.ps("pf_psd%d" % i, [64, 512]) for i in range(2)]
    pst = c.ps("pf_pst", [128, 8, 64], BF16)

    def setup(h):
        b = h % 2
        q_, k_, vT_, vt_ = qp[b], kp[b], vT[b], vt[b]
        qk, kk, vTk, vtk = "pf_qp%d" % b, "pf_kp%d" % b, "pf_vT%d" % b, "pf_vt%d" % b
        c.op("pool", lambda e: e.memset(q_[64:68, :], 1.0), writes=[qk])
        c.op("pool", lambda e: e.memset(k_[64:68, :], 1.0), writes=[kk])
        c.dma(q_[0:64, :], projT[O_QD + h * 64:O_QD + (h + 1) * 64, :], writes=[qk])
        c.dma(k_[0:64, :], projT[O_KD + h * 64:O_KD + (h + 1) * 64, :], writes=[kk])
        c.dma(q_[64:65, :], fxq[h:h + 1, :], reads=["fxq"], writes=[qk])
        c.dma(k_[65:68, :], fxk[h, :, :], reads=["fxk"], writes=[kk])
        c.dma(vT_[:], projT[O_VD + h * 64:O_VD + (h + 1) * 64, :], writes=[vTk])
        for g in range(0, NT, 8):
            n = min(8, NT - g)
            for j in range(n):
                c.op("pe", lambda e: e.transpose(pst[:, j, :], vT_[:, (g + j) * 128:(g + j + 1) * 128], identb[0:64, 0:64]),
                     reads=[vTk, "pf_identb"], writes=["pf_pst"])
            c.op("dve", lambda e: e.tensor_copy(out=vt_[:, g:g + n, :], in_=pst[:, 0:n, :]), reads=["pf_pst"], writes=[vtk])

    def stage1(d):
        h, qb, kt, nk, i = d
        b = h % 2
        t0 = qb * QB; s0 = kt * 128
        p = psz[i % 3]; pk = "pf_psz%d" % (i % 3)
        a = att[i % 3]; ak = "pf_att%d" % (i % 3)
        diag = s0 >= t0
        c.op("pe", lambda e: e.matmul(p[:, 0:QB], kp[b][:, s0:s0 + 128], qp[b][:, t0:t0 + QB], start=True, stop=not diag),
             reads=["pf_qp%d" % b, "pf_kp%d" % b], writes=[pk])
        if diag:
            j = (s0 - t0) // 128
            c.op("pe", lambda e: e.matmul(p[:, 0:QB], identb[:], negm[:, j, 0:QB], start=False, stop=True),
                 reads=["pf_identb", "pf_negm"], writes=[pk])
        c.op("act", lambda e: e.activation(out=a[:, 0:QB], in_=p[:, 0:QB], func=AF.Exp), reads=[pk], writes=[ak])

    def stage2(d):
        h, qb, kt, nk, i = d
        b = h % 2
        t0 = qb * QB
        a = att[i % 3]; ak = "pf_att%d" % (i % 3)
        pn, pnk = psn[qb % 2], "pf_psn%d" % (qb % 2)
        pd, pdk = psd[qb % 2], "pf_psd%d" % (qb % 2)
        c.op("pe", lambda e: e.matmul(pn[:, 0:QB], vt[b][:, kt, :], a[:, 0:QB], start=(kt == 0), stop=(kt == nk - 1)),
             reads=["pf_vt%d" % b, ak], writes=[pnk])
        c.op("pe", lambda e: e.matmul(pd[:, 0:QB], ones64[:], a[:, 0:QB], start=(kt == 0), stop=(kt == nk - 1)),
             reads=["pf_ones64", ak], writes=[pdk])
        if kt == nk - 1:
            z_ = zt[qb % 2]; zk = "pf_zt%d" % (qb % 2)
            y_ = yo[qb % 2]; yk = "pf_yo%d" % (qb % 2)
            c.dma(z_[:, 0:QB], projT[O_ZD + h * 64:O_ZD + (h + 1) * 64, t0:t0 + QB], writes=[zk])
            c.op("dve", lambda e: e.reciprocal(out=rec[:, 0:QB], in_=pd[:, 0:QB]), reads=[pdk], writes=["pf_rec"])
            c.op("dve", lambda e: e.tensor_tensor(out=o_[:, 0:QB], in0=pn[:, 0:QB], in1=rec[:, 0:QB], op=ALU.mult),
                 reads=[pnk, "pf_rec"], writes=["pf_o"])
            c.op("pool", lambda e: e.tensor_tensor(out=y_[:, 0:QB], in0=o_[:, 0:QB], in1=z_[:, 0:QB], op=ALU.mult),
                 reads=["pf_o", zk], writes=[yk])
            c.dma(yT[Y_D + h * 64:Y_D + (h + 1) * 64, t0:t0 + QB], y_[:, 0:QB], reads=[yk])

    blocks = []
    i = 0
    for h in range(NHA):
        for qb in range(T // QB):
            nk = (qb * QB + QB) // 128
            for kt in range(nk):
                blocks.append((h, qb, kt, nk, i)); i += 1
    setup(0)
    n = len(blocks)
    for s_ in range(n + 1):
        if s_ < n:
            stage1(blocks[s_])
        if s_ >= 1:
            d = blocks[s_ - 1]
            stage2(d)
            if d[1] == 0 and d[2] == 0 and d[0] + 1 < NHA:
                setup(d[0] + 1)
    c.end_phase()


def phase_sb(c, T, projT, negmask_d, mask01_d, identb_d, negu_d, yT):
    nc = c.nc
    c.begin_phase()
    QB = min(512, T)
    NT = T // 128
    negm = c.sb("sb_negm", [128, 4, 512], BF16)
    m01 = c.sb("sb_m01", [128, 4, 512], BF16)
    identb = c.sb("sb_identb", [128, 128], BF16)
    negu = c.sb("sb_negu", [128, 128], BF16)
    negones = c.sb("sb_negones", [128, 128], BF16)
    c.dma(negm[:], negmask_d, writes=["sb_negm"])
    c.dma(m01[:], mask01_d, writes=["sb_m01"])
    c.dma(identb[:], identb_d, writes=["sb_identb"])
    c.dma(negu[:], negu_d, writes=["sb_negu"])
    c.op("pool", lambda e: e.memset(negones[:], -1.0), writes=["sb_negones"])
    qp = [c.sb("sb_qp%d" % i, [64, T], BF16) for i in range(2)]
    kp = [c.sb("sb_kp%d" % i, [64, T], BF16) for i in range(2)]
    vT = [c.sb("sb_vT%d" % i, [64, T], BF16) for i in range(2)]
    vt = [c.sb("sb_vt%d" % i, [128, NT, 64], BF16) for i in range(2)]
    ee = [c.sb("sb_e%d" % i, [128, 512], F32) for i in range(2)]
    sp = [c.sb("sb_sp%d" % i, [128, 512], BF16) for i in range(3)]
    att = [c.sb("sb_att%d" % i, [128, 512], BF16) for i in range(3)]
    sl = [c.sb("sb_sl%d" % i, [128, 512], BF16) for i in range(2)]
    zt = [c.sb("sb_zt%d" % i, [64, 512], BF16) for i in range(2)]
    yo = [c.sb("sb_yo%d" % i, [64, 512], BF16) for i in range(2)]
    psa = [c.ps("sb_psa%d" % i, [128, 512]) for i in range(2)]
    psb = [c.ps("sb_psb%d" % i, [128, 512]) for i in range(2)]
    pso = [c.ps("sb_pso%d" % i, [64, 512]) for i in range(2)]
    pst = c.ps("sb_pst", [128, 8, 64], BF16)

    def setup(h):
        b = h % 2
        q_, k_, vT_, vt_ = qp[b], kp[b], vT[b], vt[b]
        qk, kk, vTk, vtk = "sb_qp%d" % b, "sb_kp%d" % b, "sb_vT%d" % b, "sb_vt%d" % b
        c.dma(q_[:], projT[O_QC + h * 64:O_QC + (h + 1) * 64, :], writes=[qk])
        c.dma(k_[:], projT[O_KC + h * 64:O_KC + (h + 1) * 64, :], writes=[kk])
        c.dma(vT_[:], projT[O_VC + h * 64:O_VC + (h + 1) * 64, :], writes=[vTk])
        for g in range(0, NT, 8):
            n = min(8, NT - g)
            for j in range(n):
                c.op("pe", lambda e: e.transpose(pst[:, j, :], vT_[:, (g + j) * 128:(g + j + 1) * 128], identb[0:64, 0:64]),
                     reads=[vTk, "sb_identb"], writes=["sb_pst"])
            c.op("dve", lambda e: e.tensor_copy(out=vt_[:, g:g + n, :], in_=pst[:, 0:n, :]), reads=["sb_pst"], writes=[vtk])

    def stA(d):
        h, qb, kt, nk, i, si = d
        b = h % 2
        t0 = qb * QB; s0 = kt * 128
        pa = psa[i % 2]; pak = "sb_psa%d" % (i % 2)
        e_ = ee[i % 2]; ek = "sb_e%d" % (i % 2)
        s_ = sp[i % 3]; sk = "sb_sp%d" % (i % 3)
        c.op("pe", lambda e: e.matmul(pa[:, 0:QB], kp[b][:, s0:s0 + 128], qp[b][:, t0:t0 + QB], start=True, stop=True),
             reads=["sb_qp%d" % b, "sb_kp%d" % b], writes=[pak])
        c.op("act", lambda e: e.activation(out=e_[:, 0:QB], in_=pa[:, 0:QB], func=AF.Exp), reads=[pak], writes=[ek])
        c.op("act", lambda e: e.activation(out=s_[:, 0:QB], in_=e_[:, 0:QB], func=AF.Ln, bias=1.0), reads=[ek], writes=[sk])
        if s0 >= t0:
            j = (s0 - t0) // 128
            c.op("pool", lambda e: e.tensor_tensor(out=s_[:, 0:QB], in0=s_[:, 0:QB], in1=m01[:, j, 0:QB], op=ALU.mult),
                 reads=[sk, "sb_m01"], writes=[sk])

    def stB(d):
        h, qb, kt, nk, i, si = d
        b = h % 2
        t0 = qb * QB; s0 = kt * 128
        first = kt == nk - 1
        diag = s0 >= t0
        pb = psb[i % 2]; pbk = "sb_psb%d" % (i % 2)
        s_ = sp[i % 3]; sk = "sb_sp%d" % (i % 3)
        a = att[i % 3]; ak = "sb_att%d" % (i % 3)
        c.op("pe", lambda e: e.matmul(pb[:, 0:QB], kp[b][:, s0:s0 + 128], qp[b][:, t0:t0 + QB], start=True, stop=False),
             reads=["sb_qp%d" % b, "sb_kp%d" % b], writes=[pbk])
        if diag:
            j = (s0 - t0) // 128
            c.op("pe", lambda e: e.matmul(pb[:, 0:QB], identb[:], negm[:, j, 0:QB], start=False, stop=False),
                 reads=["sb_identb", "sb_negm"], writes=[pbk])
        sl_ = sl[si % 2]; slk = "sb_sl%d" % (si % 2)
        if not first:
            c.op("pe", lambda e: e.matmul(pb[:, 0:QB], negones[:], sl_[:, 0:QB], start=False, stop=False),
                 reads=["sb_negones", slk], writes=[pbk])
        c.op("pe", lambda e: e.matmul(pb[:, 0:QB], negu[:], s_[:, 0:QB], start=False, stop=True),
             reads=["sb_negu", sk], writes=[pbk])
        c.op("act", lambda e: e.activation(out=a[:, 0:QB], in_=pb[:, 0:QB], func=AF.Exp), reads=[pbk], writes=[ak])
        if kt > 0:
            nsl = sl[(si + 1) % 2]; nslk = "sb_sl%d" % ((si + 1) % 2)
            if first:
                c.op("dve", lambda e: e.tensor_copy(out=nsl[:, 0:QB], in_=s_[:, 0:QB]), reads=[sk], writes=[nslk])
            else:
                c.op("dve", lambda e: e.tensor_tensor(out=nsl[:, 0:QB], in0=sl_[:, 0:QB], in1=s_[:, 0:QB], op=ALU.add),
                     reads=[slk, sk], writes=[nslk])

    def stC(d):
        h, qb, kt, nk, i, si = d
        b = h % 2
        t0 = qb * QB
        first = kt == nk - 1
        a = att[i % 3]; ak = "sb_att%d" % (i % 3)
        po, pok = pso[qb % 2], "sb_pso%d" % (qb % 2)
        c.op("pe", lambda e: e.matmul(po[:, 0:QB], vt[b][:, kt, :], a[:, 0:QB], start=first, stop=(kt == 0)),
             reads=["sb_vt%d" % b, ak], writes=[pok])
        if kt == 0:
            z_ = zt[qb % 2]; zk = "sb_zt%d" % (qb % 2)
            y_ = yo[qb % 2]; yk = "sb_yo%d" % (qb % 2)
            c.dma(z_[:, 0:QB], projT[O_ZC + h * 64:O_ZC + (h + 1) * 64, t0:t0 + QB], writes=[zk])
            c.op("dve", lambda e: e.tensor_tensor(out=y_[:, 0:QB], in0=po[:, 0:QB], in1=z_[:, 0:QB], op=ALU.mult),
                 reads=[pok, zk], writes=[yk])
            c.dma(yT[Y_C + h * 64:Y_C + (h + 1) * 64, t0:t0 + QB], y_[:, 0:QB], reads=[yk])

    blocks = []
    i = 0
    si = 0
    for h in range(NHA):
        for qb in range(T // QB):
            nk = (qb * QB + QB) // 128
            for kt in range(nk - 1, -1, -1):
                blocks.append((h, qb, kt, nk, i, si)); i += 1
                if kt > 0:
                    si += 1
    setup(0)
    n = len(blocks)
    for s_i in range(n + 2):
        if s_i < n:
            stA(blocks[s_i])
        if 1 <= s_i <= n:
            stB(blocks[s_i - 1])
        if s_i >= 2:
            d = blocks[s_i - 2]
            stC(d)
            if d[1] == 0 and d[2] == d[3] - 1 and d[0] + 1 < NHA:
                setup(d[0] + 1)
    c.end_phase()


def phase_gdn_pre(c, T, projT, cw4_d, ident_d, ones_f, gqkv):
    nc = c.nc
    c.begin_phase()
    TT = min(512, T)
    cw4 = c.sb("g0_cw4", [128, 3 * NHG, 4], F32)
    ident = c.sb("g0_ident", [128, 128], F32)
    ones = c.sb("g0_ones", [128, 128], F32)
    dg = c.sb("g0_dg", [128, 12 * NHG, 128], BF16)
    c.dma(cw4[:], cw4_d, writes=["g0_cw4"])
    c.dma(ident[:], ident_d, writes=["g0_ident"])
    c.dma(ones[:], ones_f, writes=["g0_ones"])
    for ch in range(3 * NHG):
        for k in range(4):
            c.op("dve", lambda e: e.tensor_scalar(out=dg[:, ch * 4 + k, :], in0=ident[:], scalar1=cw4[:, ch, k:k + 1],
                                                  scalar2=None, op0=ALU.mult),
                 reads=["g0_cw4", "g0_ident"], writes=["g0_dg"])
    hq = [c.sb("g0_hq%d" % i, [128, 4 + T], BF16) for i in range(2)]
    cs = [c.sb("g0_c%d" % i, [128, TT], F32) for i in range(2)]
    sq = [c.sb("g0_sq%d" % i, [128, TT], F32) for i in range(2)]
    rt = [c.sb("g0_rt%d" % i, [128, TT], F32) for i in range(2)]
    oo = [c.sb("g0_o%d" % i, [128, TT], F32) for i in range(2)]
    ps = [c.ps("g0_ps%d" % i, [128, 512]) for i in range(2)]
    pq = [c.ps("g0_pq%d" % i, [128, 512]) for i in range(2)]
    it = 0
    for ch in range(3 * NHG):
        h_ = hq[ch % 2]; hk = "g0_hq%d" % (ch % 2)
        c.op("pool", lambda e: e.memset(h_[:, 0:4], 0.0), writes=[hk])
        c.dma(h_[:, 4:4 + T], projT[O_QA + ch * 128:O_QA + (ch + 1) * 128, :], writes=[hk])
        for tt in range(T // TT):
            t0 = tt * TT
            i2 = it % 2; it += 1
            p = ps[i2]; pk = "g0_ps%d" % i2
            for k in range(4):
                c.op("pe", lambda e: e.matmul(p[:, 0:TT], dg[:, ch * 4 + k, :], h_[:, 1 + t0 + k:1 + t0 + k + TT],
                                              start=(k == 0), stop=(k == 3)),
                     reads=["g0_dg", hk], writes=[pk])
            c_ = cs[i2]; ck = "g0_c%d" % i2
            c.op("act", lambda e: e.activation(out=c_[:], in_=p[:, 0:TT], func=AF.Silu), reads=[pk], writes=[ck])
            if ch < 2 * NHG:
                s_ = sq[i2]; sk = "g0_sq%d" % i2
                c.op("dve", lambda e: e.tensor_tensor(out=s_[:], in0=c_[:], in1=c_[:], op=ALU.mult), reads=[ck], writes=[sk])
                q = pq[i2]; qk = "g0_pq%d" % i2
                c.op("pe", lambda e: e.matmul(q[:, 0:TT], ones[:], s_[:], start=True, stop=True),
                     reads=["g0_ones", sk], writes=[qk])
                r_ = rt[i2]; rk = "g0_rt%d" % i2
                c.op("dve", lambda e: e.tensor_scalar(out=r_[:], in0=q[:, 0:TT], scalar1=1e-6, scalar2=None, op0=ALU.add),
                     reads=[qk], writes=[rk])
                c.op("act", lambda e: e.activation(out=r_[:], in_=r_[:], func=AF.Sqrt), reads=[rk], writes=[rk])
                c.op("dve", lambda e: e.reciprocal(out=r_[:], in_=r_[:]), reads=[rk], writes=[rk])
                o_ = oo[i2]; ok = "g0_o%d" % i2
                sc = 128 ** -0.5 if ch < NHG else 1.0
                c.op("dve", lambda e: e.scalar_tensor_tensor(out=o_[:], in0=c_[:], scalar=sc, in1=r_[:], op0=ALU.mult,
                                                             op1=ALU.mult), reads=[ck, rk], writes=[ok])
                c.dma(gqkv[ch, :, t0:t0 + TT], o_[:], reads=[ok], writes=["gqkv"])
            else:
                c.dma(gqkv[ch, :, t0:t0 + TT], c_[:], reads=[ck], writes=["gqkv"])
    c.end_phase()


def phase_gdn(c, T, projT, smallT, gqkv, alog_d, dtb_d, gn_d, ident_d, ones_f, triu_d, sl_d, mks_d, mki_d, yT):
    nc = c.nc
    c.begin_phase()
    NC = T // 128
    GS = 4 if NC >= 4 else NC
    W = GS * 128
    ident = c.sb("g_ident", [128, 128], F32)
    ones = c.sb("g_ones", [128, 128], F32)
    triu = c.sb("g_triu", [128, 128], F32)
    slm = c.sb("g_sl", [128, 128], F32)
    mks = c.sb("g_mks", [128, GS, 128], F32)
    mki = c.sb("g_mki", [128, GS, 128], F32)
    identg = c.sb("g_identg", [128, GS, 128], F32)
    gnb = c.sb("g_gnb", [128, 128], F32)
    alog = c.sb("g_alog", [128, NHG], F32)
    dtb = c.sb("g_dtb", [128, NHG], F32)
    nea = c.sb("g_nea", [128, NHG], F32)
    c.dma(ident[:], ident_d, writes=["g_ident"])
    c.dma(ones[:], ones_f, writes=["g_ones"])
    c.dma(triu[:], triu_d, writes=["g_triu"])
    c.dma(slm[:], sl_d, writes=["g_sl"])
    for g in range(GS):
        c.dma(mks[:, g, :], mks_d, writes=["g_mks"])
        c.dma(mki[:, g, :], mki_d, writes=["g_mki"])
        c.dma(identg[:, g, :], ident_d, writes=["g_identg"])
    c.dma(gnb[:], gn_d, writes=["g_gnb"])
    c.dma(alog[:], alog_d, writes=["g_alog"])
    c.dma(dtb[:], dtb_d, writes=["g_dtb"])
    c.op("act", lambda e: e.activation(out=nea[:], in_=alog[:], func=AF.Exp), reads=["g_alog"], writes=["g_nea"])
    c.op("dve", lambda e: e.tensor_scalar(out=nea[:], in0=nea[:], scalar1=-1.0, scalar2=None, op0=ALU.mult),
         reads=["g_nea"], writes=["g_nea"])
    banks = [c.ps("g_pb%d" % i, [128, GS, 128]) for i in range(4)]
    pscan_b = [c.ps("g_pscan%d" % i, [128, 512]) for i in range(4)]

    class _PS:
        def __getitem__(self, idx):
            return pscan_b[idx[1]][:, 0:128]
    pscan = _PS()
    bi = [0]

    def bank():
        i = bi[0] % 4
        bi[0] += 1
        return banks[i], "g_pb%d" % i

    sm = c.sb("g_sm", [2 * NHG, T], F32)
    c.dma(sm[:], smallT[0:2 * NHG, :], writes=["g_sm"])
    abt = c.sb("g_abt", [128, NC, 2 * NHG], F32)
    for n0 in range(0, NC, 64):
        nn = min(64, NC - n0)
        pbs, pbsk = bank()
        psmall = pbs[:].rearrange("p g c -> p (g c)")
        for n in range(nn):
            c.op("pe", lambda e: e.transpose(psmall[:, n * 4:(n + 1) * 4], sm[:, (n0 + n) * 128:(n0 + n + 1) * 128],
                                             ident[0:4, 0:4]), reads=["g_sm", "g_ident"], writes=[pbsk])
        c.op("dve", lambda e: e.tensor_copy(out=abt[:, n0:n0 + nn, :],
                                            in_=psmall[:, 0:nn * 4].rearrange("p (n k) -> p n k", k=4)),
             reads=[pbsk], writes=["g_abt"])

    if DBG_STOP == 1:
        c.end_phase(); return

    def t2(name):
        return c.sb(name, [128, NC], F32)
    gg, beta, gc, gl, egc, egl, kdf, bgc, tmpn = [t2("g_" + n) for n in
                                                  ("gg", "beta", "gc", "gl", "egc", "egl", "kdf", "bgc", "tmpn")]

    def grp(name, n=2):
        return [c.sb("%s%d" % (name, i), [128, GS, 128], F32) for i in range(n)]
    kT, qT, vT = grp("g_kT"), grp("g_qT"), grp("g_vT")
    ktok, vtok = grp("g_ktok", 1)[0], grp("g_vtok", 1)[0]
    trig, E, decs, deci, L, Aq, AqT = [grp("g_" + n, 1)[0] for n in ("trig", "E", "decs", "deci", "L", "Aq", "AqT")]
    X, Y = grp("g_X"), grp("g_Y")
    R = grp("g_R", 1)[0]
    vb, kbg, kdec, u_, wT = [grp("g_" + n, 1)[0] for n in ("vb", "kbg", "kdec", "u", "wT")]
    o_, osq, on = [grp("g_" + n, 1)[0] for n in ("o", "osq", "on")]
    vnew = [c.sb("g_vnew%d" % i, [128, 128], F32) for i in range(2)]
    tq = [c.sb("g_tq%d" % i, [128, 128], F32) for i in range(2)]
    S = [c.sb("g_S%d" % i, [128, 128], F32) for i in range(2)]
    rs = c.sb("g_rs", [128, GS], F32)
    zt = [c.sb("g_zt%d" % i, [128, W], BF16) for i in range(2)]
    yo = [c.sb("g_yo%d" % i, [128, W], BF16) for i in range(2)]

    for h in range(NHG):
        c.op("act", lambda e: e.activation(out=tmpn[:], in_=abt[:, :, h], func=AF.Exp, bias=dtb[:, h:h + 1]),
             reads=["g_abt", "g_dtb"], writes=["g_tmpn"])
        c.op("act", lambda e: e.activation(out=tmpn[:], in_=tmpn[:], func=AF.Ln, bias=1.0), reads=["g_tmpn"], writes=["g_tmpn"])
        c.op("dve", lambda e: e.tensor_scalar(out=gg[:], in0=tmpn[:], scalar1=nea[:, h:h + 1], scalar2=None, op0=ALU.mult),
             reads=["g_tmpn", "g_nea"], writes=["g_gg"])
        c.op("act", lambda e: e.activation(out=beta[:], in_=abt[:, :, NHG + h], func=AF.Sigmoid), reads=["g_abt"], writes=["g_beta"])
        pbs, pbsk = bank()
        psmall = pbs[:].rearrange("p g c -> p (g c)")
        c.op("pe", lambda e: e.matmul(psmall[:, 0:NC], triu[:], gg[:], start=True, stop=True),
             reads=["g_triu", "g_gg"], writes=[pbsk])
        c.op("dve", lambda e: e.tensor_copy(out=gc[:], in_=psmall[:, 0:NC]), reads=[pbsk], writes=["g_gc"])
        pbs, pbsk = bank()
        psmall = pbs[:].rearrange("p g c -> p (g c)")
        c.op("pe", lambda e: e.matmul(psmall[:, 0:NC], ones[:], gg[:], start=True, stop=True),
             reads=["g_ones", "g_gg"], writes=[pbsk])
        c.op("dve", lambda e: e.tensor_copy(out=gl[:], in_=psmall[:, 0:NC]), reads=[pbsk], writes=["g_gl"])
        c.op("act", lambda e: e.activation(out=egc[:], in_=gc[:], func=AF.Exp), reads=["g_gc"], writes=["g_egc"])
        c.op("act", lambda e: e.activation(out=egl[:], in_=gl[:], func=AF.Exp), reads=["g_gl"], writes=["g_egl"])
        c.op("dve", lambda e: e.tensor_tensor(out=kdf[:], in0=gl[:], in1=gc[:], op=ALU.subtract),
             reads=["g_gl", "g_gc"], writes=["g_kdf"])
        c.op("act", lambda e: e.activation(out=kdf[:], in_=kdf[:], func=AF.Exp), reads=["g_kdf"], writes=["g_kdf"])
        c.op("dve", lambda e: e.tensor_tensor(out=bgc[:], in0=beta[:], in1=egc[:], op=ALU.mult),
             reads=["g_beta", "g_egc"], writes=["g_bgc"])
        c.op("pool", lambda e: e.memset(S[0][:], 0.0), writes=["g_S0"])
        sidx = 0
        if DBG_STOP == 2:
            c.end_phase(); return
        for gi in range(NC // GS):
            t0 = gi * W
            b2 = gi % 2
            kT_, qT_, vT_ = kT[b2], qT[b2], vT[b2]
            kTk, qTk, vTk = "g_kT%d" % b2, "g_qT%d" % b2, "g_vT%d" % b2
            c.dma(qT_[:], gqkv[h, :, t0:t0 + W].rearrange("p (g c) -> p g c", c=128), reads=["gqkv"], writes=[qTk])
            c.dma(kT_[:], gqkv[NHG + h, :, t0:t0 + W].rearrange("p (g c) -> p g c", c=128), reads=["gqkv"], writes=[kTk])
            c.dma(vT_[:], gqkv[2 * NHG + h, :, t0:t0 + W].rearrange("p (g c) -> p g c", c=128), reads=["gqkv"], writes=[vTk])
            z_ = zt[b2]; zk = "g_zt%d" % b2
            c.dma(z_[:], projT[O_ZA + h * 128:O_ZA + (h + 1) * 128, t0:t0 + W], writes=[zk])
            pb, pbk = bank()
            for g in range(GS):
                c.op("pe", lambda e: e.transpose(pb[:, g, :], kT_[:, g, :], ident[:]), reads=[kTk, "g_ident"], writes=[pbk])
            c.op("act", lambda e: e.activation(out=ktok[:], in_=pb[:], func=AF.Identity), reads=[pbk], writes=["g_ktok"])
            pb, pbk = bank()
            for g in range(GS):
                c.op("pe", lambda e: e.transpose(pb[:, g, :], vT_[:, g, :], ident[:]), reads=[vTk, "g_ident"], writes=[pbk])
            c.op("dve", lambda e: e.tensor_copy(out=vtok[:], in_=pb[:]), reads=[pbk], writes=["g_vtok"])
            for g in range(GS):
                n = gi * GS + g
                c.op("dve", lambda e: e.tensor_scalar(out=trig[:, g, :], in0=triu[:], scalar1=gg[:, n:n + 1], scalar2=None,
                                                      op0=ALU.mult), reads=["g_triu", "g_gg"], writes=["g_trig"])
            pb, pbk = bank()
            for g in range(GS):
                c.op("pe", lambda e: e.matmul(pb[:, g, :], trig[:, g, :], slm[:], start=True, stop=True),
                     reads=["g_trig", "g_sl"], writes=[pbk])
            c.op("act", lambda e: e.activation(out=E[:], in_=pb[:], func=AF.Exp), reads=[pbk], writes=["g_E"])
            c.op("pool", lambda e: e.tensor_tensor(out=decs[:], in0=E[:], in1=mks[:], op=ALU.mult),
                 reads=["g_E", "g_mks"], writes=["g_decs"])
            c.op("pool", lambda e: e.tensor_tensor(out=deci[:], in0=E[:], in1=mki[:], op=ALU.mult),
                 reads=["g_E", "g_mki"], writes=["g_deci"])
            pb, pbk = bank()
            for g in range(GS):
                c.op("pe", lambda e: e.matmul(pb[:, g, :], kT_[:, g, :], kT_[:, g, :], start=True, stop=True),
                     reads=[kTk], writes=[pbk])
            for g in range(GS):
                n = gi * GS + g
                c.op("dve", lambda e: e.scalar_tensor_tensor(out=L[:, g, :], in0=pb[:, g, :], scalar=beta[:, n:n + 1],
                                                             in1=decs[:, g, :], op0=ALU.mult, op1=ALU.mult),
                     reads=[pbk, "g_beta", "g_decs"], writes=["g_L"])
            pb, pbk = bank()
            for g in range(GS):
                c.op("pe", lambda e: e.matmul(pb[:, g, :], qT_[:, g, :], kT_[:, g, :], start=True, stop=True),
                     reads=[qTk, kTk], writes=[pbk])
            c.op("dve", lambda e: e.tensor_tensor(out=Aq[:], in0=pb[:], in1=deci[:], op=ALU.mult),
                 reads=[pbk, "g_deci"], writes=["g_Aq"])
            if DBG_STOP == 3:
                c.end_phase(); return
            pb, pbk = bank()
            for g in range(GS):
                c.op("pe", lambda e: e.transpose(pb[:, g, :], Aq[:, g, :], ident[:]), reads=["g_Aq", "g_ident"], writes=[pbk])
            if DBG_STOP == 29:
                c.end_phase(); return
            c.op("act", lambda e: e.activation(out=AqT[:], in_=pb[:], func=AF.Identity), reads=[pbk], writes=["g_AqT"])
            if DBG_STOP == 30:
                c.end_phase(); return
            pb, pbk = bank()
            for g in range(GS):
                c.op("pe", lambda e: e.transpose(pb[:, g, :], L[:, g, :], ident[:]), reads=["g_L", "g_ident"], writes=[pbk])
            if DBG_STOP == 305:
                c.end_phase(); return
            c.op("act", lambda e: e.activation(out=X[0][:], in_=pb[:], func=AF.Identity), reads=[pbk], writes=["g_X0"])
            if DBG_STOP == 306:
                c.end_phase(); return
            c.op("dve", lambda e: e.tensor_tensor(out=R[:], in0=identg[:], in1=X[0][:], op=ALU.subtract),
                 reads=["g_X0", "g_identg"], writes=["g_R"])
            Yc, Yk = L, "g_L"
            Xc, Xk = X[0], "g_X0"
            if DBG_STOP == 31:
                c.end_phase(); return
            for lvl in range(6):
                if DBG_STOP == 32 + lvl and lvl > 0:
                    c.end_phase(); return
                last = lvl == 5
                nX, nXk = X[(lvl + 1) % 2], "g_X%d" % ((lvl + 1) % 2)
                nY, nYk = Y[lvl % 2], "g_Y%d" % (lvl % 2)
                if not last:
                    pbx, pbxk = bank()
                    for g in range(GS):
                        c.op("pe", lambda e: e.matmul(pbx[:, g, :], Yc[:, g, :], Xc[:, g, :], start=True, stop=True),
                             reads=[Yk, Xk], writes=[pbxk])
                pby, pbyk = bank()
                for g in range(GS):
                    c.op("pe", lambda e: e.matmul(pby[:, g, :], Xc[:, g, :], Yc[:, g, :], start=True, stop=True),
                         reads=[Yk, Xk], writes=[pbyk])
                c.op("dve", lambda e: e.tensor_copy(out=nY[:], in_=pby[:]), reads=[pbyk], writes=[nYk])
                if not last:
                    c.op("act", lambda e: e.activation(out=nX[:], in_=pbx[:], func=AF.Identity), reads=[pbxk], writes=[nXk])
                pbr, pbrk = bank()
                for g in range(GS):
                    c.op("pe", lambda e: e.matmul(pbr[:, g, :], nY[:, g, :], R[:, g, :], start=True, stop=True),
                         reads=[nYk, "g_R"], writes=[pbrk])
                c.op("dve", lambda e: e.tensor_tensor(out=R[:], in0=R[:], in1=pbr[:], op=ALU.add),
                     reads=["g_R", pbrk], writes=["g_R"])
                Yc, Yk = nY, nYk
                Xc, Xk = nX, nXk
            if DBG_STOP == 4:
                c.end_phase(); return
            for g in range(GS):
                n = gi * GS + g
                c.op("pool", lambda e: e.tensor_scalar(out=vb[:, g, :], in0=vtok[:, g, :], scalar1=beta[:, n:n + 1],
                                                       scalar2=None, op0=ALU.mult), reads=["g_vtok", "g_beta"], writes=["g_vb"])
                c.op("pool", lambda e: e.tensor_scalar(out=kbg[:, g, :], in0=ktok[:, g, :], scalar1=bgc[:, n:n + 1],
                                                       scalar2=None, op0=ALU.mult), reads=["g_ktok", "g_bgc"], writes=["g_kbg"])
                c.op("pool", lambda e: e.tensor_scalar(out=kdec[:, g, :], in0=ktok[:, g, :], scalar1=kdf[:, n:n + 1],
                                                       scalar2=None, op0=ALU.mult), reads=["g_ktok", "g_kdf"], writes=["g_kdec"])
            pb, pbk = bank()
            for g in range(GS):
                c.op("pe", lambda e: e.matmul(pb[:, g, :], R[:, g, :], vb[:, g, :], start=True, stop=True),
                     reads=["g_R", "g_vb"], writes=[pbk])
            c.op("act", lambda e: e.activation(out=u_[:], in_=pb[:], func=AF.Identity), reads=[pbk], writes=["g_u"])
            pb, pbk = bank()
            for g in range(GS):
                c.op("pe", lambda e: e.matmul(pb[:, g, :], kbg[:, g, :], R[:, g, :], start=True, stop=True),
                     reads=["g_R", "g_kbg"], writes=[pbk])
            c.op("dve", lambda e: e.tensor_copy(out=wT[:], in_=pb[:]), reads=[pbk], writes=["g_wT"])
            for g in range(GS):
                n = gi * GS + g
                Sc, Sk = S[sidx % 2], "g_S%d" % (sidx % 2)
                Sn, Snk = S[(sidx + 1) % 2], "g_S%d" % ((sidx + 1) % 2)
                sidx += 1
                vn, vnk = vnew[n % 2], "g_vnew%d" % (n % 2)
                tq_, tqk = tq[n % 2], "g_tq%d" % (n % 2)
                c.op("pe", lambda e: e.matmul(pscan[:, 0, :], wT[:, g, :], Sc[:], start=True, stop=True),
                     reads=["g_wT", Sk], writes=["g_ps0"])
                c.op("dve", lambda e: e.tensor_tensor(out=vn[:], in0=u_[:, g, :], in1=pscan[:, 0, :], op=ALU.subtract),
                     reads=["g_u", "g_ps0"], writes=[vnk])
                c.op("pe", lambda e: e.matmul(pscan[:, 1, :], qT_[:, g, :], Sc[:], start=True, stop=True),
                     reads=[qTk, Sk], writes=["g_ps1"])
                c.op("pe", lambda e: e.matmul(pscan[:, 2, :], AqT[:, g, :], vn[:], start=True, stop=True),
                     reads=["g_AqT", vnk], writes=["g_ps2"])
                c.op("pe", lambda e: e.matmul(pscan[:, 3, :], kdec[:, g, :], vn[:], start=True, stop=True),
                     reads=["g_kdec", vnk], writes=["g_ps3"])
                c.op("act", lambda e: e.activation(out=tq_[:], in_=pscan[:, 1, :], func=AF.Identity, scale=egc[:, n:n + 1]),
                     reads=["g_ps1", "g_egc"], writes=[tqk])
                c.op("dve", lambda e: e.tensor_tensor(out=o_[:, g, :], in0=tq_[:], in1=pscan[:, 2, :], op=ALU.add),
                     reads=[tqk, "g_ps2"], writes=["g_o"])
                c.op("dve", lambda e: e.scalar_tensor_tensor(out=Sn[:], in0=Sc[:], scalar=egl[:, n:n + 1], in1=pscan[:, 3, :],
                                                             op0=ALU.mult, op1=ALU.add),
                     reads=[Sk, "g_egl", "g_ps3"], writes=[Snk])
            if DBG_STOP == 5:
                c.end_phase(); return
            c.op("pool", lambda e: e.tensor_tensor(out=osq[:], in0=o_[:], in1=o_[:], op=ALU.mult), reads=["g_o"], writes=["g_osq"])
            c.op("dve", lambda e: e.tensor_reduce(out=rs[:], in_=osq[:], axis=mybir.AxisListType.X, op=ALU.add),
                 reads=["g_osq"], writes=["g_rs"])
            c.op("dve", lambda e: e.tensor_scalar(out=rs[:], in0=rs[:], scalar1=1.0 / 128, scalar2=EPS, op0=ALU.mult,
                                                  op1=ALU.add), reads=["g_rs"], writes=["g_rs"])
            c.op("act", lambda e: e.activation(out=rs[:], in_=rs[:], func=AF.Sqrt), reads=["g_rs"], writes=["g_rs"])
            c.op("dve", lambda e: e.reciprocal(out=rs[:], in_=rs[:]), reads=["g_rs"], writes=["g_rs"])
            for g in range(GS):
                c.op("dve", lambda e: e.scalar_tensor_tensor(out=on[:, g, :], in0=o_[:, g, :], scalar=rs[:, g:g + 1],
                                                             in1=gnb[:], op0=ALU.mult, op1=ALU.mult),
                     reads=["g_o", "g_rs", "g_gnb"], writes=["g_on"])
            pb, pbk = bank()
            for g in range(GS):
                c.op("pe", lambda e: e.transpose(pb[:, g, :], on[:, g, :], ident[:]), reads=["g_on", "g_ident"], writes=[pbk])
            y_ = yo[b2]; yk = "g_yo%d" % b2
            c.op("dve", lambda e: e.tensor_tensor(out=y_[:], in0=pb[:].rearrange("p g c -> p (g c)"), in1=z_[:], op=ALU.mult),
                 reads=[pbk, zk], writes=[yk])
            c.dma(yT[Y_A + h * 128:Y_A + (h + 1) * 128, t0:t0 + W], y_[:], reads=[yk])
    c.end_phase()


PAIRS = [[0, 1], [2, 3], [4, 5], [6, 7]]


def build(T, nlayers=2, only=None, pairs=PAIRS):
    nc = bass.Bass("TRN2", target_bir_lowering=False)

    def di(n, shape, dt=F32):
        return nc.dram_tensor(n, shape, dt, kind="ExternalInput").ap()

    def ds(n, shape, dt=F32):
        return nc.dram_tensor(n, shape, dt, kind="Internal").ap()
    L = nlayers
    xT = di("xT", [D, T])
    pT = di("pT", [L, 256, T])
    w_in = di("w_in", [L, D, PWL])
    w_small = di("w_small", [L, D, 8])
    w_branch = di("w_branch", [L, 4, 512, 1024])
    w_out = di("w_out", [L, 1024, 1024])
    w_ple = di("w_ple", [L, 256, 1024])
    w_pg = di("w_pg", [L, 1024, 1024])
    b_gate = di("b_gate", [L, 128, 32])
    b_pg = di("b_pg", [L, 128, 8])
    ln_g = di("ln_g", [L, 128, 8])
    ln_b = di("ln_b", [L, 128, 8])
    cw = di("cw", [L, 128, 4, 31])
    cvec = di("cvec", [L, 128, 3, 4])
    cw4 = di("cw4", [L, 128, 3 * NHG, 4])
    alog = di("alog", [L, 128, NHG])
    dtb = di("dtb", [L, 128, NHG])
    gn = di("gn", [L, 128, 128])
    fb = di("fb", [L, NHA, 1])
    ident = di("ident", [128, 128])
    ones_f = di("ones_f", [128, 128])
    identb = di("identb", [128, 128], BF16)
    negm_i = di("negm_i", [128, 4, 512], BF16)
    negm_s = di("negm_s", [128, 4, 512], BF16)
    m01_s = di("m01_s", [128, 4, 512], BF16)
    negu = di("negu", [128, 128], BF16)
    triu = di("triu", [128, 128])
    slm = di("slm", [128, 128])
    mki = di("mki", [128, 128])
    outT = nc.dram_tensor("outT", [D, T], F32, kind="ExternalOutput").ap()
    projT = ds("projT", [PWL, T], BF16)
    smallT = ds("smallT", [8, T])
    yT = ds("yT", [YL, T], BF16)
    yg = [ds("yg%d" % i, [256, T], BF16) for i in range(6)]
    gqkv = ds("gqkv", [3 * NHG, 128, T])
    fxq = ds("fxq", [NHA, T], BF16)
    fxk = ds("fxk", [NHA, 3, T], BF16)
    xmid = [ds("xmid%d" % i, [D, T]) for i in range(max(L - 1, 1))]
    with ExitStack() as es:
        c = Ctx(nc, es)
        xin = xT
        for l in range(L):
            xo = outT if l == L - 1 else xmid[l]
            on = lambda n: only is None or n in only
            if on("proj"):
                phase_proj(c, T, xin, w_in[l], w_small[l], b_gate[l], projT, smallT)
            if on("conv"):
                phase_conv(c, T, projT, cw[l], cvec[l], ident, ones_f, yT)
            if on("fox"):
                phase_fox(c, T, projT, smallT, fb[l], negm_i, identb, fxq, fxk, yT)
            if on("sb"):
                phase_sb(c, T, projT, negm_s, m01_s, identb, negu, yT)
            if on("gdn_pre"):
                phase_gdn_pre(c, T, projT, cw4[l], ident, ones_f, gqkv)
            if on("gdn"):
                phase_gdn(c, T, projT, smallT, gqkv, alog[l], dtb[l], gn[l], ident, ones_f, triu, slm, slm, mki, yT)
            if on("out"):
                c.barrier()
                for j, row in enumerate((Y_A, Y_A + 128, Y_C, Y_C + 128, Y_D, Y_D + 128)):
                    c.collective("AllGather", [yT[row:row + 128, :]], [yg[j]], pairs)
                c.barrier()
                phase_out(c, T, xin, pT[l], yT, yg, projT, w_branch[l], w_out[l], w_ple[l],
                          w_pg[l], b_pg[l], ln_g[l], ln_b[l], ones_f, xo)
            xin = xo
        c.finish()
        ninst = c.ninst
    return nc, ninst


def host_inputs(x_b, p_b, w, L, r):
    import ml_dtypes
    bf = lambda a: np.ascontiguousarray(a).astype(ml_dtypes.bfloat16)
    f32 = lambda a: np.ascontiguousarray(a, dtype=np.float32)
    v8 = lambda v: f32(np.stack([v[l].reshape(8, 128).T for l in range(L)]))
    rep = lambda v: f32(np.stack([np.broadcast_to(v[l][None, :], (128, v[l].shape[0])) for l in range(L)]))
    v4 = lambda v: v.reshape(4, 128).T
    ar = np.arange
    gh = np.concatenate([(NHG * r + h) * 128 + ar(128) for h in range(NHG)])
    ah = np.concatenate([(NHA * r + h) * 64 + ar(64) for h in range(NHA)])
    cols = np.concatenate([R_QA + gh, R_KA + gh, R_VA + gh, R_ZA + gh,
                           R_GL + ar(512), R_GG + ar(512), R_ZB + ar(512),
                           R_QC + ah, R_KC + ah, R_VC + ah, R_ZC + ah,
                           R_QD + ah, R_KD + ah, R_VD + ah, R_ZD + ah,
                           R_G + ar(4096)])
    assert cols.shape[0] == PWL
    scols = np.concatenate([R_AA + NHG * r + ar(NHG), R_BA + NHG * r + ar(NHG), R_FD + NHA * r + ar(NHA)])
    gch = np.concatenate([gh, 512 + gh, 1024 + gh])
    s_ = ar(128)[:, None, None]; j_ = ar(4)[None, :, None]; q_ = ar(512)[None, None, :]
    incl = (s_ + 128 * j_ <= q_); strict = (s_ + 128 * j_ < q_)
    jj = ar(128)[:, None]; ss = ar(128)[None, :]
    d = {
        "xT": f32(x_b.T), "pT": f32(np.stack([p_b[l].T for l in range(L)])),
        "w_in": f32(w["w_in"][:L][:, :, cols]), "w_small": f32(w["w_in"][:L][:, :, scols]),
        "w_branch": f32(w["w_branch"][:L]), "w_out": f32(w["w_out"][:L]),
        "w_ple": f32(w["w_ple"][:L]), "w_pg": f32(w["w_ple_gate"][:L]),
        "b_gate": f32(np.stack([w["b_gate"][l].reshape(32, 128).T for l in range(L)])),
        "b_pg": v8(w["b_ple_gate"]), "ln_g": v8(w["ln_g"]), "ln_b": v8(w["ln_b"]),
        "cw": f32(np.stack([w["conv_dw"][l].T.reshape(4, 128, 31).transpose(1, 0, 2) for l in range(L)])),
        "cvec": f32(np.stack([np.stack([v4(w["conv_dw_bias"][l]), v4(w["conv_ln_g"][l]), v4(w["conv_ln_b"][l])], axis=1)
                              for l in range(L)])),
        "cw4": f32(np.stack([w["conv_qkv"][l][:, gch].T.reshape(3 * NHG, 128, 4).transpose(1, 0, 2) for l in range(L)])),
        "alog": rep(w["a_log"][:, NHG * r:NHG * (r + 1)]), "dtb": rep(w["dt_bias"][:, NHG * r:NHG * (r + 1)]),
        "gn": rep(w["gdn_norm"]),
        "fb": f32(np.stack([w["forget_bias"][l][NHA * r:NHA * (r + 1)].reshape(NHA, 1) for l in range(L)])),
        "ident": np.eye(128, dtype=np.float32), "ones_f": np.ones((128, 128), np.float32),
        "identb": bf(np.eye(128, dtype=np.float32)),
        "negm_i": bf(np.where(incl, 0.0, -30000.0).astype(np.float32)),
        "negm_s": bf(np.where(strict, 0.0, -30000.0).astype(np.float32)),
        "m01_s": bf(strict.astype(np.float32)),
        "negu": bf(-(jj >= ss).astype(np.float32)),
        "triu": (jj <= ss).astype(np.float32), "slm": (jj > ss).astype(np.float32), "mki": (jj >= ss).astype(np.float32),
    }
    return d


_CACHE = {}


def kernel(**inputs):
    x = np.asarray(inputs["x"])
    p = np.asarray(inputs["p"])
    B, T, _ = x.shape
    w = {k: np.asarray(v) for k, v in inputs.items() if k not in ("x", "p")}
    L = w["w_in"].shape[0]
    ncores = 2 * B
    pairs = [[2 * b, 2 * b + 1] for b in range(B)]
    key = (T, L, B)
    if key not in _CACHE:
        _CACHE[key] = build(T, L, pairs=pairs)[0]
    nc = _CACHE[key]
    in_maps = [host_inputs(x[i // 2], p[:, i // 2], w, L, i % 2) for i in range(ncores)]
    res = run_bass_kernel_spmd(nc, in_maps, core_ids=list(range(ncores)))
    out = np.stack([np.asarray(res.results[2 * b]["outT"]).T for b in range(B)])
    return np.ascontiguousarray(out, dtype=np.float32)
```

```python
import numpy as np
from contextlib import ExitStack
import concourse.bass as bass
import concourse.mybir as mybir
from concourse.bass_utils import run_bass_kernel_spmd

F32 = mybir.dt.float32
BF16 = mybir.dt.bfloat16
AF = mybir.ActivationFunctionType
ALU = mybir.AluOpType

D = 1024
PW = 11792
PWL = 8704
NHA = 4
NHG = 2
YL = 1280
Y_A, Y_B, Y_C, Y_D = 0, 256, 768, 1024
ALPHA = 4 ** 0.25
EPS = 1e-5
SEM_EPOCH = 20000
DBG_STOP = 0

R_QA, R_KA, R_VA, R_ZA, R_AA, R_BA = 0, 512, 1024, 1536, 2048, 2052
R_GL, R_GG, R_ZB = 2056, 2568, 3080
R_QC, R_KC, R_VC, R_ZC = 3592, 4104, 4616, 5128
R_QD, R_KD, R_VD, R_ZD, R_FD = 5640, 6152, 6664, 7176, 7688
R_G = 7696
O_QA, O_KA, O_VA, O_ZA = 0, 256, 512, 768
O_GL, O_GG, O_ZB = 1024, 1536, 2048
O_QC, O_KC, O_VC, O_ZC = 2560, 2816, 3072, 3328
O_QD, O_KD, O_VD, O_ZD = 3584, 3840, 4096, 4352
O_G = 4608


class Ctx:
    def __init__(self, nc, es):
        self.nc, self.es = nc, es
        self.eng = {"pe": nc.tensor, "act": nc.scalar, "dve": nc.vector, "pool": nc.gpsimd, "sp": nc.sync}
        self.sem = {}
        self.cnt = {}
        self.nsem = 0
        for e in self.eng:
            self._new_sem(e)
        self.waited = {e: {} for e in self.eng}
        self.lastw = {}
        self.readers = {}
        self.dsem = [es.enter_context(nc.semaphore("dq%d" % i)) for i in range(24)]
        self.dcnt = [0] * 24
        self.dnext = 0
        self.ninst = 0

    def _new_sem(self, e):
        self.sem[e] = self.es.enter_context(self.nc.semaphore("s_%s_%d" % (e, self.nsem)))
        self.nsem += 1
        self.cnt[e] = 0

    def _wait(self, e, tok):
        sem, val, src = tok
        if src == "pe" and e == "pe":
            return
        w = self.waited[e]
        k = id(sem)
        if w.get(k, 0) >= val:
            return
        w[k] = val
        self.eng[e].wait_ge(sem, val)

    def _deps(self, e, reads, writes):
        for k in reads:
            t = self.lastw.get(k)
            if t is not None:
                self._wait(e, t)
        for k in writes:
            t = self.lastw.get(k)
            if t is not None:
                self._wait(e, t)
            rd = self.readers.get(k)
            if rd:
                for key, t in rd.items():
                    self._wait(e, t)

    def _commit(self, tok, reads, writes):
        for k in writes:
            self.lastw[k] = tok
            self.readers[k] = {}
        for k in reads:
            rd = self.readers.setdefault(k, {})
            if tok[2] == "dma":
                rd[("dma", id(tok[0]))] = tok
            else:
                rd[tok[2]] = tok

    def op(self, e, fn, reads=(), writes=()):
        self._deps(e, reads, writes)
        ins = fn(self.eng[e])
        if self.cnt[e] >= SEM_EPOCH:
            self._new_sem(e)
        self.cnt[e] += 1
        ins.then_inc(self.sem[e], 1)
        tok = (self.sem[e], self.cnt[e], e)
        self._commit(tok, reads, writes)
        self.ninst += 1
        return tok

    def dma(self, out, in_, reads=(), writes=(), q="sp"):
        j = self.dnext
        self.dnext = (self.dnext + 1) % len(self.dsem)
        self._deps(q, reads, writes)
        if self.dcnt[j] > 0:
            self._wait(q, (self.dsem[j], self.dcnt[j], "dma"))
        self.eng[q].dma_start(out=out, in_=in_).then_inc(self.dsem[j], 16)
        self.dcnt[j] += 16
        tok = (self.dsem[j], self.dcnt[j], "dma")
        self._commit(tok, reads, writes)
        self.ninst += 1
        return tok

    def collective(self, kind, ins, outs, groups, reads=(), writes=()):
        if not hasattr(self, "ccsem"):
            self.ccsem = self.es.enter_context(self.nc.semaphore("ccsem"))
            self.cccnt = 0
        self._deps("pool", reads, writes)
        self.nc.gpsimd.collective_compute(kind, ALU.bypass, replica_groups=groups, ins=ins, outs=outs).then_inc(self.ccsem)
        self.cccnt += 1
        tok = (self.ccsem, self.cccnt, "dma")
        self._commit(tok, reads, writes)
        for e in self.eng:
            self._wait(e, tok)
        return tok

    def finish(self):
        for j in range(len(self.dsem)):
            if self.dcnt[j] > 0:
                self._wait("sp", (self.dsem[j], self.dcnt[j], "dma"))
        for e in ("pe", "act", "dve", "pool"):
            if self.cnt[e] > 0:
                self._wait("sp", (self.sem[e], self.cnt[e], e))

    def sb(self, name, shape, dt):
        return self.pes.enter_context(self.nc.sbuf_tensor("%s_%d" % (name, self.phase_no), shape, dt))

    def ps(self, name, shape, dt=F32):
        return self.pes.enter_context(self.nc.psum_tensor("%s_%d" % (name, self.phase_no), shape, dt))

    def begin_phase(self):
        self.pes = ExitStack()
        self.phase_no = getattr(self, "phase_no", 0) + 1

    def end_phase(self):
        self.barrier()
        self.pes.close()

    def barrier(self):
        for e in self.eng:
            for j in range(len(self.dsem)):
                if self.dcnt[j] > 0:
                    self._wait(e, (self.dsem[j], self.dcnt[j], "dma"))
            for f in ("pe", "act", "dve", "pool"):
                if f != e and self.cnt[f] > 0:
                    self._wait(e, (self.sem[f], self.cnt[f], f))
        self.lastw.clear()
        self.readers.clear()


def proj_chunks():
    ch = []

    def add(o, n, kind):
        for i in range(n // 128):
            ch.append((o + 128 * i, 128, kind))
    add(O_QA, 768, 0)
    add(O_ZA, 256, 1)
    add(O_GL, 512, 0)
    add(O_GG, 512, 2)
    add(O_ZB, 512, 1)
    add(O_QC, 256, 3)
    add(O_KC, 512, 0)
    add(O_ZC, 256, 1)
    add(O_QD, 256, 3)
    add(O_KD, 512, 0)
    add(O_ZD, 256, 1)
    add(O_G, 4096, 4)
    return ch


def phase_proj(c, T, xT, w_in, w_small, b_gate, projT, smallT):
    nc = c.nc
    c.begin_phase()
    ST = min(4096, T)
    nst = T // ST
    nsub = ST // 512
    xs = [c.sb("p1_xs%d" % i, [128, 8, 512], F32) for i in range(2)]
    xb = c.sb("p1_xb", [128, 8, ST], BF16)
    ws = [c.sb("p1_ws%d" % i, [128, 8, 512], F32) for i in range(2)]
    wb = [c.sb("p1_wb%d" % i, [128, 8, 512], BF16) for i in range(2)]
    wss = c.sb("p1_wss", [128, 8, 8], F32)
    wsb = c.sb("p1_wsb", [128, 8, 8], BF16)
    bg = c.sb("p1_bg", [128, 32], F32)
    ob = [c.sb("p1_ob%d" % i, [128, 4, 512], BF16) for i in range(3)]
    osm = c.sb("p1_osm", [8, 512], F32)
    pss = [c.ps("p1_ps%d" % i, [128, 512]) for i in range(4)]
    c.dma(bg[:], b_gate, writes=["p1_bg"])
    c.dma(wss[:], w_small.rearrange("(k p) n -> p k n", p=128), writes=["p1_wss"])
    c.op("pool", lambda e: e.tensor_copy(out=wsb[:], in_=wss[:]), reads=["p1_wss"], writes=["p1_wsb"])
    chunks = proj_chunks()
    groups = [chunks[i:i + 4] for i in range(0, len(chunks), 4)]
    xTv = xT.rearrange("(k p) t -> p k t", p=128)
    w_v = w_in.rearrange("(k p) n -> p k n", p=128)
    it = 0
    for st in range(nst):
        for sub in range(nsub):
            t0 = st * ST + sub * 512
            s = xs[(st * nsub + sub) % 2]
            key = "p1_xs%d" % ((st * nsub + sub) % 2)
            c.dma(s[:], xTv[:, :, t0:t0 + 512], writes=[key])
            c.op("pool", lambda e: e.tensor_copy(out=xb[:, :, sub * 512:(sub + 1) * 512], in_=s[:]),
                 reads=[key], writes=["p1_xb%d" % sub])
        for sub in range(nsub):
            t0 = st * ST + sub * 512
            p = pss[it % 4]; pk = "p1_ps%d" % (it % 4)
            for k in range(8):
                c.op("pe", lambda e: e.matmul(p[0:8, :], wsb[:, k, :], xb[:, k, sub * 512:(sub + 1) * 512],
                                              start=(k == 0), stop=(k == 7)),
                     reads=["p1_wsb", "p1_xb%d" % sub], writes=[pk])
            c.op("dve", lambda e: e.tensor_copy(out=osm[:], in_=p[0:8, :]), reads=[pk], writes=["p1_osm"])
            c.dma(smallT[:, t0:t0 + 512], osm[:], reads=["p1_osm"])
            it += 1
        def wload(gi):
            g0 = groups[gi][0][0]
            gw = 128 * len(groups[gi])
            wsl = ws[gi % 2]; wbl = wb[gi % 2]
            c.dma(wsl[:, :, 0:gw], w_v[:, :, g0:g0 + gw], writes=["p1_ws%d" % (gi % 2)])
            c.op("pool", lambda e: e.tensor_copy(out=wbl[:, :, 0:gw], in_=wsl[:, :, 0:gw]), reads=["p1_ws%d" % (gi % 2)],
                 writes=["p1_wb%d" % (gi % 2)])
        wload(0)
        for gi, grp_ in enumerate(groups):
            g0 = grp_[0][0]
            assert all(grp_[j][0] == g0 + 128 * j for j in range(len(grp_)))
            wbl = wb[gi % 2]
            if gi + 1 < len(groups):
                wload(gi + 1)
            for sub in range(nsub):
                t0 = st * ST + sub * 512
                obi = (gi * nsub + sub) % 3
                for cj, (off, wd, kind) in enumerate(grp_):
                    p = pss[it % 4]; pk = "p1_ps%d" % (it % 4)
                    o = ob[obi][:, cj, :]; ok = "p1_ob%d" % obi
                    for k in range(8):
                        c.op("pe", lambda e: e.matmul(p[:], wbl[:, k, cj * 128:(cj + 1) * 128],
                                                      xb[:, k, sub * 512:(sub + 1) * 512],
                                                      start=(k == 0), stop=(k == 7)),
                             reads=["p1_wb%d" % (gi % 2), "p1_xb%d" % sub], writes=[pk])
                    if kind == 0:
                        c.op("dve", lambda e: e.tensor_copy(out=o, in_=p[:]), reads=[pk], writes=[ok])
                    elif kind == 3:
                        c.op("dve", lambda e: e.tensor_scalar(out=o, in0=p[:], scalar1=0.125, scalar2=None,
                                                              op0=ALU.mult), reads=[pk], writes=[ok])
                    elif kind == 1:
                        c.op("act", lambda e: e.activation(out=o, in_=p[:], func=AF.Silu), reads=[pk], writes=[ok])
                    elif kind == 2:
                        c.op("act", lambda e: e.activation(out=o, in_=p[:], func=AF.Sigmoid), reads=[pk], writes=[ok])
                    else:
                        gidx = (off - O_G) // 128
                        c.op("act", lambda e: e.activation(out=o, in_=p[:], func=AF.Sigmoid, bias=bg[:, gidx:gidx + 1]),
                             reads=[pk, "p1_bg"], writes=[ok])
                    it += 1
                ng = len(grp_)
                c.dma(projT[g0:g0 + 128 * ng, t0:t0 + 512].rearrange("(c p) t -> p c t", p=128), ob[obi][:, 0:ng, :],
                      reads=["p1_ob%d" % obi])
    c.end_phase()


def load_w_bf16(c, dst, dst_key, src_rows, stg, n):
    i = c.stg_i = getattr(c, "stg_i", 0) + 1
    s = stg[i % len(stg)]
    sk = "stg%d" % (i % len(stg))
    c.dma(s[:, 0:n], src_rows, writes=[sk])
    c.op("pool", lambda e: e.tensor_copy(out=dst, in_=s[:, 0:n]), reads=[sk], writes=[dst_key])


def phase_out(c, T, xT, pT, yT, yg, projT, w_branch, w_out, w_ple, w_pg, b_pg, ln_g, ln_b, ones_f, outT):
    nc = c.nc
    c.begin_phase()
    TT = 256
    stg = [c.sb("po_stg%d" % i, [128, 1024], F32) for i in range(2)]
    wbr = c.sb("po_wbr", [128, 16, 1024], BF16)
    wo = c.sb("po_wo", [128, 8, 1024], BF16)
    wpg = c.sb("po_wpg", [128, 8, 1024], BF16)
    wpl = c.sb("po_wpl", [128, 2, 1024], BF16)
    vb = c.sb("po_vec", [128, 3, 8], F32)
    ones = c.sb("po_ones", [128, 128], F32)
    c.dma(vb[:, 0, :], b_pg, writes=["po_vec"])
    c.dma(vb[:, 1, :], ln_g, writes=["po_vec"])
    c.dma(vb[:, 2, :], ln_b, writes=["po_vec"])
    c.dma(ones[:], ones_f, writes=["po_ones"])
    wbv = w_branch.rearrange("b (k p) n -> p (b k) n", p=128)
    for i in range(16):
        load_w_bf16(c, wbr[:, i, :], "po_wbr", wbv[:, i, :], stg, 1024)
    for nm, dst, src, nk in (("po_wo", wo, w_out, 8), ("po_wpg", wpg, w_pg, 8), ("po_wpl", wpl, w_ple, 2)):
        sv = src.rearrange("(k p) n -> p k n", p=128)
        for i in range(nk):
            load_w_bf16(c, dst[:, i, :], nm, sv[:, i, :], stg, 1024)
    ys = [c.sb("po_y%d" % i, [128, 16, TT], BF16) for i in range(2)]
    gt = [c.sb("po_gt%d" % i, [128, 8, TT], BF16) for i in range(2)]
    xs_ = [c.sb("po_x%d" % i, [128, 8, TT], F32) for i in range(2)]
    pfs = [c.sb("po_pf%d" % i, [128, 2, TT], F32) for i in range(2)]
    pbs = [c.sb("po_pb%d" % i, [128, 2, TT], BF16) for i in range(2)]
    m = c.sb("po_m", [128, 8, TT], F32)
    mb = c.sb("po_mb", [128, 8, TT], BF16)
    rs_ = [c.sb("po_r%d_" % i, [128, 8, TT], F32) for i in range(2)]
    rbs_ = [c.sb("po_rb%d_" % i, [128, 8, TT], BF16) for i in range(2)]
    tmp = [c.sb("po_tmp%d" % i, [128, TT], F32) for i in range(2)]
    gp = [c.sb("po_gp%d" % i, [128, TT], F32) for i in range(4)]
    sq = [c.sb("po_sq%d" % i, [128, TT], F32) for i in range(8)]
    mean = c.sb("po_mean", [128, TT], F32)
    msq = c.sb("po_msq", [128, TT], F32)
    rstd = c.sb("po_rstd", [128, TT], F32)
    ot = [c.sb("po_ot%d" % i, [128, TT], F32) for i in range(4)]
    ps = [c.ps("po_ps%d" % i, [128, 512]) for i in range(4)]
    pstat = [c.ps("po_pst%d" % i, [128, 512]) for i in range(2)]
    xv = xT.rearrange("(k p) t -> p k t", p=128)
    pv = pT.rearrange("(k p) t -> p k t", p=128)
    ov = outT.rearrange("(k p) t -> p k t", p=128)
    itc = [0]

    def stL(tt):
        t0 = tt * TT
        y = ys[tt % 2]; yk_ = "po_y%d" % (tt % 2)
        x = xs_[tt % 2]; xk_ = "po_x%d" % (tt % 2)
        pf = pfs[tt % 2]; pfk = "po_pf%d" % (tt % 2)
        pb = pbs[tt % 2]; pbk_ = "po_pb%d" % (tt % 2)
        c.dma(y[:, 4:8, :], yT[Y_B:Y_B + 512, t0:t0 + TT].rearrange("(k p) t -> p k t", p=128), writes=[yk_])
        for gi_, br_ in enumerate((0, 2, 3)):
            for par in range(2):
                i0_ = br_ * 4 + par
                c.dma(y[:, i0_:i0_ + 3:2, :], yg[gi_ * 2 + par][:, t0:t0 + TT].rearrange("(r p) t -> p r t", p=128),
                      writes=[yk_])
        c.dma(x[:], xv[:, :, t0:t0 + TT], writes=[xk_])
        c.dma(pf[:], pv[:, :, t0:t0 + TT], writes=[pfk])
        c.op("pool", lambda e: e.tensor_copy(out=pb[:], in_=pf[:]), reads=[pfk], writes=[pbk_])

    def stA(tt):
        t0 = tt * TT
        y = ys[tt % 2]; yk_ = "po_y%d" % (tt % 2)
        x = xs_[tt % 2]; xk_ = "po_x%d" % (tt % 2)
        pf = pfs[tt % 2]; pfk = "po_pf%d" % (tt % 2)
        pb = pbs[tt % 2]; pbk_ = "po_pb%d" % (tt % 2)
        pst_s = pstat[0]; pst_q = pstat[1]
        r = rs_[tt % 2]; rb = rbs_[tt % 2]; rp_ = "po_r%d_" % (tt % 2); rbp_ = "po_rb%d_" % (tt % 2)
        for br in range(4):
            g = gt[br % 2]; gk = "po_gt%d" % (br % 2)
            c.dma(g[:], projT[O_G + br * 1024:O_G + (br + 1) * 1024, t0:t0 + TT].rearrange("(k p) t -> p k t", p=128),
                  writes=[gk])
            for fo in range(8):
                p = ps[itc[0] % 4]; pk = "po_ps%d" % (itc[0] % 4); itc[0] += 1
                for kc in range(4):
                    c.op("pe", lambda e: e.matmul(p[:, 0:TT], wbr[:, br * 4 + kc, fo * 128:(fo + 1) * 128],
                                                  y[:, br * 4 + kc, :], start=(kc == 0), stop=(kc == 3)),
                         reads=["po_wbr", yk_], writes=[pk])
                mk = "po_m%d" % fo
                if br == 0:
                    c.op("dve", lambda e: e.tensor_tensor(out=m[:, fo, :], in0=p[:, 0:TT], in1=g[:, fo, :], op=ALU.mult),
                         reads=[pk, gk], writes=[mk])
                else:
                    t_ = tmp[itc[0] % 2]; tk = "po_tmp%d" % (itc[0] % 2)
                    c.op("dve", lambda e: e.tensor_tensor(out=t_[:], in0=p[:, 0:TT], in1=g[:, fo, :], op=ALU.mult),
                         reads=[pk, gk], writes=[tk])
                    if br < 3:
                        c.op("pool", lambda e: e.tensor_tensor(out=m[:, fo, :], in0=m[:, fo, :], in1=t_[:], op=ALU.add),
                             reads=[mk, tk], writes=[mk])
                    else:
                        c.op("pool", lambda e: e.tensor_tensor(out=mb[:, fo, :], in0=m[:, fo, :], in1=t_[:], op=ALU.add),
                             reads=[mk, tk], writes=["po_mb%d" % fo])

    def stB(tt):
        t0 = tt * TT
        y = ys[tt % 2]; yk_ = "po_y%d" % (tt % 2)
        x = xs_[tt % 2]; xk_ = "po_x%d" % (tt % 2)
        pf = pfs[tt % 2]; pfk = "po_pf%d" % (tt % 2)
        pb = pbs[tt % 2]; pbk_ = "po_pb%d" % (tt % 2)
        pst_s = pstat[0]; pst_q = pstat[1]
        r = rs_[tt % 2]; rb = rbs_[tt % 2]; rp_ = "po_r%d_" % (tt % 2); rbp_ = "po_rb%d_" % (tt % 2)
        for fo in range(8):
            p = ps[itc[0] % 4]; pk = "po_ps%d" % (itc[0] % 4); itc[0] += 1
            for k in range(8):
                c.op("pe", lambda e: e.matmul(p[:, 0:TT], wo[:, k, fo * 128:(fo + 1) * 128], mb[:, k, :],
                                              start=(k == 0), stop=(k == 7)),
                     reads=["po_wo"] + ["po_mb%d" % k], writes=[pk])
            c.op("dve", lambda e: e.scalar_tensor_tensor(out=r[:, fo, :], in0=x[:, fo, :], scalar=ALPHA, in1=p[:, 0:TT],
                                                         op0=ALU.mult, op1=ALU.add),
                 reads=[pk, xk_], writes=[rp_ + str(fo)])
            c.op("act", lambda e: e.activation(out=rb[:, fo, :], in_=r[:, fo, :], func=AF.Identity),
                 reads=[rp_ + str(fo)], writes=[rbp_ + str(fo)])

    def stC(tt):
        t0 = tt * TT
        y = ys[tt % 2]; yk_ = "po_y%d" % (tt % 2)
        x = xs_[tt % 2]; xk_ = "po_x%d" % (tt % 2)
        pf = pfs[tt % 2]; pfk = "po_pf%d" % (tt % 2)
        pb = pbs[tt % 2]; pbk_ = "po_pb%d" % (tt % 2)
        pst_s = pstat[0]; pst_q = pstat[1]
        r = rs_[tt % 2]; rb = rbs_[tt % 2]; rp_ = "po_r%d_" % (tt % 2); rbp_ = "po_rb%d_" % (tt % 2)
        for fo in range(8):
            p = ps[itc[0] % 4]; pk = "po_ps%d" % (itc[0] % 4); itc[0] += 1
            for k in range(8):
                c.op("pe", lambda e: e.matmul(p[:, 0:TT], wpg[:, k, fo * 128:(fo + 1) * 128], rb[:, k, :],
                                              start=(k == 0), stop=(k == 7)),
                     reads=["po_wpg", rbp_ + str(k)], writes=[pk])
            g_ = gp[fo % 4]; gk = "po_gp%d" % (fo % 4)
            c.op("act", lambda e: e.activation(out=g_[:], in_=p[:, 0:TT], func=AF.Sigmoid, bias=vb[:, 0, fo:fo + 1]),
                 reads=[pk, "po_vec"], writes=[gk])
            p2 = ps[itc[0] % 4]; pk2 = "po_ps%d" % (itc[0] % 4); itc[0] += 1
            for k in range(2):
                c.op("pe", lambda e: e.matmul(p2[:, 0:TT], wpl[:, k, fo * 128:(fo + 1) * 128], pb[:, k, :],
                                              start=(k == 0), stop=(k == 1)),
                     reads=["po_wpl", pbk_], writes=[pk2])
            t_ = tmp[fo % 2]; tk = "po_tmp%d" % (fo % 2)
            c.op("dve", lambda e: e.tensor_tensor(out=t_[:], in0=p2[:, 0:TT], in1=g_[:], op=ALU.mult),
                 reads=[pk2, gk], writes=[tk])
            c.op("pool", lambda e: e.tensor_tensor(out=r[:, fo, :], in0=r[:, fo, :], in1=t_[:], op=ALU.add),
                 reads=[rp_ + str(fo), tk], writes=[rp_ + str(fo)])
        for fo in range(8):
            s_ = sq[fo]; sk = "po_sq%d" % fo
            c.op("act", lambda e: e.activation(out=s_[:], in_=r[:, fo, :], func=AF.Square),
                 reads=[rp_ + str(fo)], writes=[sk])
        for fo in range(8):
            s_ = sq[fo]; sk = "po_sq%d" % fo
            c.op("pe", lambda e: e.matmul(pst_s[:, 0:TT], ones[:], r[:, fo, :], start=(fo == 0), stop=(fo == 7)),
                 reads=["po_ones", rp_ + str(fo)], writes=["po_pst0"])
            c.op("pe", lambda e: e.matmul(pst_q[:, 0:TT], ones[:], s_[:], start=(fo == 0), stop=(fo == 7)),
                 reads=["po_ones", sk], writes=["po_pst1"])

    def stD(tt):
        t0 = tt * TT
        y = ys[tt % 2]; yk_ = "po_y%d" % (tt % 2)
        x = xs_[tt % 2]; xk_ = "po_x%d" % (tt % 2)
        pf = pfs[tt % 2]; pfk = "po_pf%d" % (tt % 2)
        pb = pbs[tt % 2]; pbk_ = "po_pb%d" % (tt % 2)
        pst_s = pstat[0]; pst_q = pstat[1]
        r = rs_[tt % 2]; rb = rbs_[tt % 2]; rp_ = "po_r%d_" % (tt % 2); rbp_ = "po_rb%d_" % (tt % 2)
        c.op("dve", lambda e: e.tensor_scalar(out=mean[:], in0=pst_s[:, 0:TT], scalar1=1.0 / D, scalar2=None, op0=ALU.mult),
             reads=["po_pst0"], writes=["po_mean"])
        c.op("dve", lambda e: e.tensor_tensor(out=msq[:], in0=mean[:], in1=mean[:], op=ALU.mult),
             reads=["po_mean"], writes=["po_msq"])
        c.op("dve", lambda e: e.scalar_tensor_tensor(out=msq[:], in0=pst_q[:, 0:TT], scalar=1.0 / D, in1=msq[:],
                                                     op0=ALU.mult, op1=ALU.subtract),
             reads=["po_pst1", "po_msq"], writes=["po_msq"])
        c.op("dve", lambda e: e.tensor_scalar(out=msq[:], in0=msq[:], scalar1=EPS, scalar2=None, op0=ALU.add),
             reads=["po_msq"], writes=["po_msq"])
        c.op("act", lambda e: e.activation(out=rstd[:], in_=msq[:], func=AF.Sqrt),
             reads=["po_msq"], writes=["po_rstd"])
        c.op("dve", lambda e: e.reciprocal(out=rstd[:], in_=rstd[:]), reads=["po_rstd"], writes=["po_rstd"])
        for fo in range(8):
            t_ = tmp[fo % 2]; tk = "po_tmp%d" % (fo % 2)
            o_ = ot[fo % 4]; ok = "po_ot%d" % (fo % 4)
            c.op("dve", lambda e: e.tensor_tensor(out=t_[:], in0=r[:, fo, :], in1=mean[:], op=ALU.subtract),
                 reads=[rp_ + str(fo), "po_mean"], writes=[tk])
            c.op("pool", lambda e: e.tensor_tensor(out=t_[:], in0=t_[:], in1=rstd[:], op=ALU.mult),
                 reads=[tk, "po_rstd"], writes=[tk])
            c.op("act", lambda e: e.activation(out=o_[:], in_=t_[:], func=AF.Identity, scale=vb[:, 1, fo:fo + 1],
                                               bias=vb[:, 2, fo:fo + 1]),
                 reads=[tk, "po_vec"], writes=[ok])
            c.dma(ov[:, fo, t0:t0 + TT], o_[:], reads=[ok])

    NTT = T // TT
    stL(0)
    stA(0)
    stB(0)
    if NTT > 1:
        stL(1)
        stA(1)
    for tt in range(NTT):
        stC(tt)
        if tt + 1 < NTT:
            stB(tt + 1)
        if tt + 2 < NTT:
            stL(tt + 2)
        stD(tt)
        if tt + 2 < NTT:
            stA(tt + 2)
    c.end_phase()


def phase_conv(c, T, projT, cw_d, cvec_d, ident_d, ones_f, yT):
    nc = c.nc
    c.begin_phase()
    TT = 512 if T >= 512 else T
    cw = c.sb("pc_cw", [128, 4, 31], F32)
    cvec = c.sb("pc_cvec", [128, 3, 4], F32)
    ident = c.sb("pc_ident", [128, 128], F32)
    ones = c.sb("pc_ones", [128, 128], F32)
    dg = c.sb("pc_dg", [128, 124, 128], BF16)
    hb = c.sb("pc_hb", [128, 4, 32 + T], BF16)
    c.dma(cw[:], cw_d, writes=["pc_cw"])
    c.dma(cvec[:], cvec_d, writes=["pc_cvec"])
    c.dma(ident[:], ident_d, writes=["pc_ident"])
    c.dma(ones[:], ones_f, writes=["pc_ones"])
    for ch in range(4):
        for k in range(31):
            c.op("dve", lambda e: e.tensor_scalar(out=dg[:, ch * 31 + k, :], in0=ident[:], scalar1=cw[:, ch, k:k + 1],
                                                  scalar2=None, op0=ALU.mult),
                 reads=["pc_cw", "pc_ident"], writes=["pc_dg"])
    ld = [c.sb("pc_ld%d" % i, [128, 2048], BF16) for i in range(4)]
    PT = min(2048, T)
    i = 0
    for ch in range(4):
        c.op("pool", lambda e: e.memset(hb[:, ch, 0:32], 0.0), writes=["pc_hb%d" % ch])
        for t0 in range(0, T, PT):
            a = ld[i % 4]; ak = "pc_ld%d" % (i % 4); i += 1
            b = ld[i % 4]; bk = "pc_ld%d" % (i % 4); i += 1
            c.dma(a[:, 0:PT], projT[O_GL + ch * 128:O_GL + (ch + 1) * 128, t0:t0 + PT], writes=[ak])
            c.dma(b[:, 0:PT], projT[O_GG + ch * 128:O_GG + (ch + 1) * 128, t0:t0 + PT], writes=[bk])
            c.op("pool", lambda e: e.tensor_tensor(out=hb[:, ch, 32 + t0:32 + t0 + PT], in0=a[:, 0:PT], in1=b[:, 0:PT],
                                                   op=ALU.mult),
                 reads=[ak, bk], writes=["pc_hb%d" % ch])
    cv = c.sb("pc_cv", [128, 4, TT], F32)
    sq = [c.sb("pc_sq%d" % i, [128, TT], F32) for i in range(2)]
    zb = [c.sb("pc_zb%d" % i, [128, TT], BF16) for i in range(2)]
    mean = c.sb("pc_mean", [128, TT], F32)
    msq = c.sb("pc_msq", [128, TT], F32)
    rstd = c.sb("pc_rstd", [128, TT], F32)
    tmp = [c.sb("pc_tmp%d" % i, [128, TT], F32) for i in range(2)]
    yo = [c.sb("pc_yo%d" % i, [128, TT], BF16) for i in range(2)]
    ps = [c.ps("pc_ps%d" % i, [128, 512]) for i in range(3)]
    pst = [c.ps("pc_pst%d" % i, [128, 512]) for i in range(2)]
    it = 0
    for tt in range(T // TT):
        t0 = tt * TT
        for ch in range(4):
            p = ps[it % 3]; pk = "pc_ps%d" % (it % 3); it += 1
            for k in range(31):
                c.op("pe", lambda e: e.matmul(p[:, 0:TT], dg[:, ch * 31 + k, :], hb[:, ch, 2 + t0 + k:2 + t0 + k + TT],
                                              start=(k == 0), stop=(k == 30)),
                     reads=["pc_dg", "pc_hb%d" % ch], writes=[pk])
            c.op("act", lambda e: e.activation(out=cv[:, ch, :], in_=p[:, 0:TT], func=AF.Identity, bias=cvec[:, 0, ch:ch + 1]),
                 reads=[pk, "pc_cvec"], writes=["pc_cv%d" % ch])
            s_ = sq[ch % 2]; sk = "pc_sq%d" % (ch % 2)
            c.op("act", lambda e: e.activation(out=s_[:], in_=cv[:, ch, :], func=AF.Square),
                 reads=["pc_cv%d" % ch], writes=[sk])
            c.op("pe", lambda e: e.matmul(pst[0][:, 0:TT], ones[:], cv[:, ch, :], start=(ch == 0), stop=(ch == 3)),
                 reads=["pc_ones", "pc_cv%d" % ch], writes=["pc_pst0"])
            c.op("pe", lambda e: e.matmul(pst[1][:, 0:TT], ones[:], s_[:], start=(ch == 0), stop=(ch == 3)),
                 reads=["pc_ones", sk], writes=["pc_pst1"])
        c.op("dve", lambda e: e.tensor_scalar(out=mean[:], in0=pst[0][:, 0:TT], scalar1=1.0 / 512, scalar2=None, op0=ALU.mult),
             reads=["pc_pst0"], writes=["pc_mean"])
        c.op("dve", lambda e: e.tensor_tensor(out=msq[:], in0=mean[:], in1=mean[:], op=ALU.mult),
             reads=["pc_mean"], writes=["pc_msq"])
        c.op("dve", lambda e: e.scalar_tensor_tensor(out=msq[:], in0=pst[1][:, 0:TT], scalar=1.0 / 512, in1=msq[:],
                                                     op0=ALU.mult, op1=ALU.subtract),
             reads=["pc_pst1", "pc_msq"], writes=["pc_msq"])
        c.op("dve", lambda e: e.tensor_scalar(out=msq[:], in0=msq[:], scalar1=EPS, scalar2=None, op0=ALU.add),
             reads=["pc_msq"], writes=["pc_msq"])
        c.op("act", lambda e: e.activation(out=rstd[:], in_=msq[:], func=AF.Sqrt), reads=["pc_msq"], writes=["pc_rstd"])
        c.op("dve", lambda e: e.reciprocal(out=rstd[:], in_=rstd[:]), reads=["pc_rstd"], writes=["pc_rstd"])
        for ch in range(4):
            z = zb[ch % 2]; zk = "pc_zb%d" % (ch % 2)
            c.dma(z[:], projT[O_ZB + ch * 128:O_ZB + (ch + 1) * 128, t0:t0 + TT], writes=[zk])
            t_ = tmp[ch % 2]; tk = "pc_tmp%d" % (ch % 2)
            c.op("dve", lambda e: e.tensor_tensor(out=t_[:], in0=cv[:, ch, :], in1=mean[:], op=ALU.subtract),
                 reads=["pc_cv%d" % ch, "pc_mean"], writes=[tk])
            c.op("pool", lambda e: e.tensor_tensor(out=t_[:], in0=t_[:], in1=rstd[:], op=ALU.mult),
                 reads=[tk, "pc_rstd"], writes=[tk])
            c.op("act", lambda e: e.activation(out=t_[:], in_=t_[:], func=AF.Silu, scale=cvec[:, 1, ch:ch + 1],
                                               bias=cvec[:, 2, ch:ch + 1]),
                 reads=[tk, "pc_cvec"], writes=[tk])
            o_ = yo[ch % 2]; ok = "pc_yo%d" % (ch % 2)
            c.op("dve", lambda e: e.tensor_tensor(out=o_[:], in0=t_[:], in1=z[:], op=ALU.mult),
                 reads=[tk, zk], writes=[ok])
            c.dma(yT[Y_B + ch * 128:Y_B + (ch + 1) * 128, t0:t0 + TT], o_[:], reads=[ok])
    c.end_phase()


def phase_fox(c, T, projT, smallT, fb_d, negmask_d, identb_d, fxq, fxk, yT):
    nc = c.nc
    c.begin_phase()
    QB = min(512, T)
    NT = T // 128
    fb = c.sb("pf_fb", [NHA, 1], F32)
    nfb = c.sb("pf_nfb", [NHA, 1], F32)
    c.dma(fb[:], fb_d, writes=["pf_fb"])
    c.op("dve", lambda e: e.tensor_scalar(out=nfb[:], in0=fb[:], scalar1=-1.0, scalar2=None, op0=ALU.mult),
         reads=["pf_fb"], writes=["pf_nfb"])
    PT = min(2048, T)
    onesr = c.sb("pf_onesr", [NHA, PT], F32)
    c.op("pool", lambda e: e.memset(onesr[:], 1.0), writes=["pf_onesr"])
    fin = c.sb("pf_fin", [NHA, PT], F32)
    l_ = c.sb("pf_l", [NHA, PT], F32)
    fn = [c.sb("pf_fn%d" % i, [NHA, PT], F32) for i in range(2)]
    hi = c.sb("pf_hi", [NHA, PT], BF16)
    r1 = c.sb("pf_r1", [NHA, PT], F32)
    mid = c.sb("pf_mid", [NHA, PT], BF16)
    lo = c.sb("pf_lo", [NHA, PT], BF16)
    nhi = c.sb("pf_nhi", [NHA, PT], BF16)
    for pi, t0 in enumerate(range(0, T, PT)):
        f_ = fn[pi % 2]; fk = "pf_fn%d" % (pi % 2)
        c.dma(fin[:], smallT[4:8, t0:t0 + PT], writes=["pf_fin"])
        c.op("act", lambda e: e.activation(out=l_[:], in_=fin[:], func=AF.Exp, scale=-1.0, bias=nfb[:]),
             reads=["pf_fin", "pf_nfb"], writes=["pf_l"])
        c.op("act", lambda e: e.activation(out=l_[:], in_=l_[:], func=AF.Ln, bias=1.0), reads=["pf_l"], writes=["pf_l"])
        if pi == 0:
            c.op("dve", lambda e: e.tensor_tensor_scan(out=f_[:], data0=onesr[:], data1=l_[:], initial=0.0,
                                                       op0=ALU.mult, op1=ALU.add),
                 reads=["pf_onesr", "pf_l"], writes=[fk])
        else:
            pf_ = fn[(pi - 1) % 2]
            c.op("dve", lambda e: e.tensor_tensor_scan(out=f_[:], data0=onesr[:], data1=l_[:], initial=pf_[:, PT - 1:PT],
                                                       op0=ALU.mult, op1=ALU.add),
                 reads=["pf_onesr", "pf_l", "pf_fn%d" % ((pi - 1) % 2)], writes=[fk])
        c.op("dve", lambda e: e.tensor_copy(out=hi[:], in_=f_[:]), reads=[fk], writes=["pf_hi"])
        c.op("dve", lambda e: e.tensor_tensor(out=r1[:], in0=f_[:], in1=hi[:], op=ALU.subtract),
             reads=[fk, "pf_hi"], writes=["pf_r1"])
        c.op("dve", lambda e: e.tensor_copy(out=mid[:], in_=r1[:]), reads=["pf_r1"], writes=["pf_mid"])
        c.op("dve", lambda e: e.tensor_tensor(out=r1[:], in0=r1[:], in1=mid[:], op=ALU.subtract),
             reads=["pf_r1", "pf_mid"], writes=["pf_r1"])
        c.op("dve", lambda e: e.tensor_copy(out=lo[:], in_=r1[:]), reads=["pf_r1"], writes=["pf_lo"])
        c.op("dve", lambda e: e.tensor_scalar(out=nhi[:], in0=hi[:], scalar1=-1.0, scalar2=None, op0=ALU.mult),
             reads=["pf_hi"], writes=["pf_nhi"])
        c.dma(fxq[:, t0:t0 + PT], nhi[:], reads=["pf_nhi"], writes=["fxq"])
        c.dma(fxk[:, 0, t0:t0 + PT], hi[:], reads=["pf_hi"], writes=["fxk"])
        c.dma(fxk[:, 1, t0:t0 + PT], mid[:], reads=["pf_mid"], writes=["fxk"])
        c.dma(fxk[:, 2, t0:t0 + PT], lo[:], reads=["pf_lo"], writes=["fxk"])
    negm = c.sb("pf_negm", [128, 4, 512], BF16)
    identb = c.sb("pf_identb", [128, 128], BF16)
    ones64 = c.sb("pf_ones64", [128, 64], BF16)
    c.dma(negm[:], negmask_d, writes=["pf_negm"])
    c.dma(identb[:], identb_d, writes=["pf_identb"])
    c.op("pool", lambda e: e.memset(ones64[:], 1.0), writes=["pf_ones64"])
    qp = [c.sb("pf_qp%d" % i, [68, T], BF16) for i in range(2)]
    kp = [c.sb("pf_kp%d" % i, [68, T], BF16) for i in range(2)]
    vT = [c.sb("pf_vT%d" % i, [64, T], BF16) for i in range(2)]
    vt = [c.sb("pf_vt%d" % i, [128, NT, 64], BF16) for i in range(2)]
    att = [c.sb("pf_att%d" % i, [128, 512], BF16) for i in range(3)]
    rec = c.sb("pf_rec", [64, 512], F32)
    o_ = c.sb("pf_o", [64, 512], F32)
    zt = [c.sb("pf_zt%d" % i, [64, 512], BF16) for i in range(2)]
    yo = [c.sb("pf_yo%d" % i, [64, 512], BF16) for i in range(2)]
    psz = [c.ps("pf_psz%d" % i, [128, 512]) for i in range(3)]
    psn = [c.ps("pf_psn%d" % i, [64, 512]) for i in range(2)]
    psd = [c.ps("pf_psd%d" % i, [64, 512]) for i in range(2)]
    pst = c.ps("pf_pst", [128, 8, 64], BF16)

    def setup(h):
        b = h % 2
        q_, k_, vT_, vt_ = qp[b], kp[b], vT[b], vt[b]
        qk, kk, vTk, vtk = "pf_qp%d" % b, "pf_kp%d" % b, "pf_vT%d" % b, "pf_vt%d" % b
        c.op("pool", lambda e: e.memset(q_[64:68, :], 1.0), writes=[qk])
        c.op("pool", lambda e: e.memset(k_[64:68, :], 1.0), writes=[kk])
        c.dma(q_[0:64, :], projT[O_QD + h * 64:O_QD + (h + 1) * 64, :], writes=[qk])
        c.dma(k_[0:64, :], projT[O_KD + h * 64:O_KD + (h + 1) * 64, :], writes=[kk])
        c.dma(q_[64:65, :], fxq[h:h + 1, :], reads=["fxq"], writes=[qk])
        c.dma(k_[65:68, :], fxk[h, :, :], reads=["fxk"], writes=[kk])
        c.dma(vT_[:], projT[O_VD + h * 64:O_VD + (h + 1) * 64, :], writes=[vTk])
        for g in range(0, NT, 8):
            n = min(8, NT - g)
            for j in range(n):
                c.op("pe", lambda e: e.transpose(pst[:, j, :], vT_[:, (g + j) * 128:(g + j + 1) * 128], identb[0:64, 0:64]),
                     reads=[vTk, "pf_identb"], writes=["pf_pst"])
            c.op("dve", lambda e: e.tensor_copy(out=vt_[:, g:g + n, :], in_=pst[:, 0:n, :]), reads=["pf_pst"], writes=[vtk])

    def stage1(d):
        h, qb, kt, nk, i = d
        b = h % 2
        t0 = qb * QB; s0 = kt * 128
        p = psz[i % 3]; pk = "pf_psz%d" % (i % 3)
        a = att[i % 3]; ak = "pf_att%d" % (i % 3)
        diag = s0 >= t0
        c.op("pe", lambda e: e.matmul(p[:, 0:QB], kp[b][:, s0:s0 + 128], qp[b][:, t0:t0 + QB], start=True, stop=not diag),
             reads=["pf_qp%d" % b, "pf_kp%d" % b], writes=[pk])
        if diag:
            j = (s0 - t0) // 128
            c.op("pe", lambda e: e.matmul(p[:, 0:QB], identb[:], negm[:, j, 0:QB], start=False, stop=True),
                 reads=["pf_identb", "pf_negm"], writes=[pk])
        c.op("act", lambda e: e.activation(out=a[:, 0:QB], in_=p[:, 0:QB], func=AF.Exp), reads=[pk], writes=[ak])

    def stage2(d):
        h, qb, kt, nk, i = d
        b = h % 2
        t0 = qb * QB
        a = att[i % 3]; ak = "pf_att%d" % (i % 3)
        pn, pnk = psn[qb % 2], "pf_psn%d" % (qb % 2)
        pd, pdk = psd[qb % 2], "pf_psd%d" % (qb % 2)
        c.op("pe", lambda e: e.matmul(pn[:, 0:QB], vt[b][:, kt, :], a[:, 0:QB], start=(kt == 0), stop=(kt == nk - 1)),
             reads=["pf_vt%d" % b, ak], writes=[pnk])
        c.op("pe", lambda e: e.matmul(pd[:, 0:QB], ones64[:], a[:, 0:QB], start=(kt == 0), stop=(kt == nk - 1)),
             reads=["pf_ones64", ak], writes=[pdk])
        if kt == nk - 1:
            z_ = zt[qb % 2]; zk = "pf_zt%d" % (qb % 2)
            y_ = yo[qb % 2]; yk = "pf_yo%d" % (qb % 2)
            c.dma(z_[:, 0:QB], projT[O_ZD + h * 64:O_ZD + (h + 1) * 64, t0:t0 + QB], writes=[zk])
            c.op("dve", lambda e: e.reciprocal(out=rec[:, 0:QB], in_=pd[:, 0:QB]), reads=[pdk], writes=["pf_rec"])
            c.op("dve", lambda e: e.tensor_tensor(out=o_[:, 0:QB], in0=pn[:, 0:QB], in1=rec[:, 0:QB], op=ALU.mult),
                 reads=[pnk, "pf_rec"], writes=["pf_o"])
            c.op("pool", lambda e: e.tensor_tensor(out=y_[:, 0:QB], in0=o_[:, 0:QB], in1=z_[:, 0:QB], op=ALU.mult),
                 reads=["pf_o", zk], writes=[yk])
            c.dma(yT[Y_D + h * 64:Y_D + (h + 1) * 64, t0:t0 + QB], y_[:, 0:QB], reads=[yk])

    blocks = []
    i = 0
    for h in range(NHA):
        for qb in range(T // QB):
            nk = (qb * QB + QB) // 128
            for kt in range(nk):
                blocks.append((h, qb, kt, nk, i)); i += 1
    setup(0)
    n = len(blocks)
    for s_ in range(n + 1):
        if s_ < n:
            stage1(blocks[s_])
        if s_ >= 1:
            d = blocks[s_ - 1]
            stage2(d)
            if d[1] == 0 and d[2] == 0 and d[0] + 1 < NHA:
                setup(d[0] + 1)
    c.end_phase()


def phase_sb(c, T, projT, negmask_d, mask01_d, identb_d, negu_d, yT):
    nc = c.nc
    c.begin_phase()
    QB = min(512, T)
    NT = T // 128
    negm = c.sb("sb_negm", [128, 4, 512], BF16)
    m01 = c.sb("sb_m01", [128, 4, 512], BF16)
    identb = c.sb("sb_identb", [128, 128], BF16)
    negu = c.sb("sb_negu", [128, 128], BF16)
    negones = c.sb("sb_negones", [128, 128], BF16)
    c.dma(negm[:], negmask_d, writes=["sb_negm"])
    c.dma(m01[:], mask01_d, writes=["sb_m01"])
    c.dma(identb[:], identb_d, writes=["sb_identb"])
    c.dma(negu[:], negu_d, writes=["sb_negu"])
    c.op("pool", lambda e: e.memset(negones[:], -1.0), writes=["sb_negones"])
    qp = [c.sb("sb_qp%d" % i, [64, T], BF16) for i in range(2)]
    kp = [c.sb("sb_kp%d" % i, [64, T], BF16) for i in range(2)]
    vT = [c.sb("sb_vT%d" % i, [64, T], BF16) for i in range(2)]
    vt = [c.sb("sb_vt%d" % i, [128, NT, 64], BF16) for i in range(2)]
    ee = [c.sb("sb_e%d" % i, [128, 512], F32) for i in range(2)]
    sp = [c.sb("sb_sp%d" % i, [128, 512], BF16) for i in range(3)]
    att = [c.sb("sb_att%d" % i, [128, 512], BF16) for i in range(3)]
    sl = [c.sb("sb_sl%d" % i, [128, 512], BF16) for i in range(2)]
    zt = [c.sb("sb_zt%d" % i, [64, 512], BF16) for i in range(2)]
    yo = [c.sb("sb_yo%d" % i, [64, 512], BF16) for i in range(2)]
    psa = [c.ps("sb_psa%d" % i, [128, 512]) for i in range(2)]
    psb = [c.ps("sb_psb%d" % i, [128, 512]) for i in range(2)]
    pso = [c.ps("sb_pso%d" % i, [64, 512]) for i in range(2)]
    pst = c.ps("sb_pst", [128, 8, 64], BF16)

    def setup(h):
        b = h % 2
        q_, k_, vT_, vt_ = qp[b], kp[b], vT[b], vt[b]
        qk, kk, vTk, vtk = "sb_qp%d" % b, "sb_kp%d" % b, "sb_vT%d" % b, "sb_vt%d" % b
        c.dma(q_[:], projT[O_QC + h * 64:O_QC + (h + 1) * 64, :], writes=[qk])
        c.dma(k_[:], projT[O_KC + h * 64:O_KC + (h + 1) * 64, :], writes=[kk])
        c.dma(vT_[:], projT[O_VC + h * 64:O_VC + (h + 1) * 64, :], writes=[vTk])
        for g in range(0, NT, 8):
            n = min(8, NT - g)
            for j in range(n):
                c.op("pe", lambda e: e.transpose(pst[:, j, :], vT_[:, (g + j) * 128:(g + j + 1) * 128], identb[0:64, 0:64]),
                     reads=[vTk, "sb_identb"], writes=["sb_pst"])
            c.op("dve", lambda e: e.tensor_copy(out=vt_[:, g:g + n, :], in_=pst[:, 0:n, :]), reads=["sb_pst"], writes=[vtk])

    def stA(d):
        h, qb, kt, nk, i, si = d
        b = h % 2
        t0 = qb * QB; s0 = kt * 128
        pa = psa[i % 2]; pak = "sb_psa%d" % (i % 2)
        e_ = ee[i % 2]; ek = "sb_e%d" % (i % 2)
        s_ = sp[i % 3]; sk = "sb_sp%d" % (i % 3)
        c.op("pe", lambda e: e.matmul(pa[:, 0:QB], kp[b][:, s0:s0 + 128], qp[b][:, t0:t0 + QB], start=True, stop=True),
             reads=["sb_qp%d" % b, "sb_kp%d" % b], writes=[pak])
        c.op("act", lambda e: e.activation(out=e_[:, 0:QB], in_=pa[:, 0:QB], func=AF.Exp), reads=[pak], writes=[ek])
        c.op("act", lambda e: e.activation(out=s_[:, 0:QB], in_=e_[:, 0:QB], func=AF.Ln, bias=1.0), reads=[ek], writes=[sk])
        if s0 >= t0:
            j = (s0 - t0) // 128
            c.op("pool", lambda e: e.tensor_tensor(out=s_[:, 0:QB], in0=s_[:, 0:QB], in1=m01[:, j, 0:QB], op=ALU.mult),
                 reads=[sk, "sb_m01"], writes=[sk])

    def stB(d):
        h, qb, kt, nk, i, si = d
        b = h % 2
        t0 = qb * QB; s0 = kt * 128
        first = kt == nk - 1
        diag = s0 >= t0
        pb = psb[i % 2]; pbk = "sb_psb%d" % (i % 2)
        s_ = sp[i % 3]; sk = "sb_sp%d" % (i % 3)
        a = att[i % 3]; ak = "sb_att%d" % (i % 3)
        c.op("pe", lambda e: e.matmul(pb[:, 0:QB], kp[b][:, s0:s0 + 128], qp[b][:, t0:t0 + QB], start=True, stop=False),
             reads=["sb_qp%d" % b, "sb_kp%d" % b], writes=[pbk])
        if diag:
            j = (s0 - t0) // 128
            c.op("pe", lambda e: e.matmul(pb[:, 0:QB], identb[:], negm[:, j, 0:QB], start=False, stop=False),
                 reads=["sb_identb", "sb_negm"], writes=[pbk])
        sl_ = sl[si % 2]; slk = "sb_sl%d" % (si % 2)
        if not first:
            c.op("pe", lambda e: e.matmul(pb[:, 0:QB], negones[:], sl_[:, 0:QB], start=False, stop=False),
                 reads=["sb_negones", slk], writes=[pbk])
        c.op("pe", lambda e: e.matmul(pb[:, 0:QB], negu[:], s_[:, 0:QB], start=False, stop=True),
             reads=["sb_negu", sk], writes=[pbk])
        c.op("act", lambda e: e.activation(out=a[:, 0:QB], in_=pb[:, 0:QB], func=AF.Exp), reads=[pbk], writes=[ak])
        if kt > 0:
            nsl = sl[(si + 1) % 2]; nslk = "sb_sl%d" % ((si + 1) % 2)
            if first:
                c.op("dve", lambda e: e.tensor_copy(out=nsl[:, 0:QB], in_=s_[:, 0:QB]), reads=[sk], writes=[nslk])
            else:
                c.op("dve", lambda e: e.tensor_tensor(out=nsl[:, 0:QB], in0=sl_[:, 0:QB], in1=s_[:, 0:QB], op=ALU.add),
                     reads=[slk, sk], writes=[nslk])

    def stC(d):
        h, qb, kt, nk, i, si = d
        b = h % 2
        t0 = qb * QB
        first = kt == nk - 1
        a = att[i % 3]; ak = "sb_att%d" % (i % 3)
        po, pok = pso[qb % 2], "sb_pso%d" % (qb % 2)
        c.op("pe", lambda e: e.matmul(po[:, 0:QB], vt[b][:, kt, :], a[:, 0:QB], start=first, stop=(kt == 0)),
             reads=["sb_vt%d" % b, ak], writes=[pok])
        if kt == 0:
            z_ = zt[qb % 2]; zk = "sb_zt%d" % (qb % 2)
            y_ = yo[qb % 2]; yk = "sb_yo%d" % (qb % 2)
            c.dma(z_[:, 0:QB], projT[O_ZC + h * 64:O_ZC + (h + 1) * 64, t0:t0 + QB], writes=[zk])
            c.op("dve", lambda e: e.tensor_tensor(out=y_[:, 0:QB], in0=po[:, 0:QB], in1=z_[:, 0:QB], op=ALU.mult),
                 reads=[pok, zk], writes=[yk])
            c.dma(yT[Y_C + h * 64:Y_C + (h + 1) * 64, t0:t0 + QB], y_[:, 0:QB], reads=[yk])

    blocks = []
    i = 0
    si = 0
    for h in range(NHA):
        for qb in range(T // QB):
            nk = (qb * QB + QB) // 128
            for kt in range(nk - 1, -1, -1):
                blocks.append((h, qb, kt, nk, i, si)); i += 1
                if kt > 0:
                    si += 1
    setup(0)
    n = len(blocks)
    for s_i in range(n + 2):
        if s_i < n:
            stA(blocks[s_i])
        if 1 <= s_i <= n:
            stB(blocks[s_i - 1])
        if s_i >= 2:
            d = blocks[s_i - 2]
            stC(d)
            if d[1] == 0 and d[2] == d[3] - 1 and d[0] + 1 < NHA:
                setup(d[0] + 1)
    c.end_phase()


def phase_gdn_pre(c, T, projT, cw4_d, ident_d, ones_f, gqkv):
    nc = c.nc
    c.begin_phase()
    TT = min(512, T)
    cw4 = c.sb("g0_cw4", [128, 3 * NHG, 4], F32)
    ident = c.sb("g0_ident", [128, 128], F32)
    ones = c.sb("g0_ones", [128, 128], F32)
    dg = c.sb("g0_dg", [128, 12 * NHG, 128], BF16)
    c.dma(cw4[:], cw4_d, writes=["g0_cw4"])
    c.dma(ident[:], ident_d, writes=["g0_ident"])
    c.dma(ones[:], ones_f, writes=["g0_ones"])
    for ch in range(3 * NHG):
        for k in range(4):
            c.op("dve", lambda e: e.tensor_scalar(out=dg[:, ch * 4 + k, :], in0=ident[:], scalar1=cw4[:, ch, k:k + 1],
                                                  scalar2=None, op0=ALU.mult),
                 reads=["g0_cw4", "g0_ident"], writes=["g0_dg"])
    hq = [c.sb("g0_hq%d" % i, [128, 4 + T], BF16) for i in range(2)]
    cs = [c.sb("g0_c%d" % i, [128, TT], F32) for i in range(2)]
    sq = [c.sb("g0_sq%d" % i, [128, TT], F32) for i in range(2)]
    rt = [c.sb("g0_rt%d" % i, [128, TT], F32) for i in range(2)]
    oo = [c.sb("g0_o%d" % i, [128, TT], F32) for i in range(2)]
    ps = [c.ps("g0_ps%d" % i, [128, 512]) for i in range(2)]
    pq = [c.ps("g0_pq%d" % i, [128, 512]) for i in range(2)]
    it = 0
    for ch in range(3 * NHG):
        h_ = hq[ch % 2]; hk = "g0_hq%d" % (ch % 2)
        c.op("pool", lambda e: e.memset(h_[:, 0:4], 0.0), writes=[hk])
        c.dma(h_[:, 4:4 + T], projT[O_QA + ch * 128:O_QA + (ch + 1) * 128, :], writes=[hk])
        for tt in range(T // TT):
            t0 = tt * TT
            i2 = it % 2; it += 1
            p = ps[i2]; pk = "g0_ps%d" % i2
            for k in range(4):
                c.op("pe", lambda e: e.matmul(p[:, 0:TT], dg[:, ch * 4 + k, :], h_[:, 1 + t0 + k:1 + t0 + k + TT],
                                              start=(k == 0), stop=(k == 3)),
                     reads=["g0_dg", hk], writes=[pk])
            c_ = cs[i2]; ck = "g0_c%d" % i2
            c.op("act", lambda e: e.activation(out=c_[:], in_=p[:, 0:TT], func=AF.Silu), reads=[pk], writes=[ck])
            if ch < 2 * NHG:
                s_ = sq[i2]; sk = "g0_sq%d" % i2
                c.op("dve", lambda e: e.tensor_tensor(out=s_[:], in0=c_[:], in1=c_[:], op=ALU.mult), reads=[ck], writes=[sk])
                q = pq[i2]; qk = "g0_pq%d" % i2
                c.op("pe", lambda e: e.matmul(q[:, 0:TT], ones[:], s_[:], start=True, stop=True),
                     reads=["g0_ones", sk], writes=[qk])
                r_ = rt[i2]; rk = "g0_rt%d" % i2
                c.op("dve", lambda e: e.tensor_scalar(out=r_[:], in0=q[:, 0:TT], scalar1=1e-6, scalar2=None, op0=ALU.add),
                     reads=[qk], writes=[rk])
                c.op("act", lambda e: e.activation(out=r_[:], in_=r_[:], func=AF.Sqrt), reads=[rk], writes=[rk])
                c.op("dve", lambda e: e.reciprocal(out=r_[:], in_=r_[:]), reads=[rk], writes=[rk])
                o_ = oo[i2]; ok = "g0_o%d" % i2
                sc = 128 ** -0.5 if ch < NHG else 1.0
                c.op("dve", lambda e: e.scalar_tensor_tensor(out=o_[:], in0=c_[:], scalar=sc, in1=r_[:], op0=ALU.mult,
                                                             op1=ALU.mult), reads=[ck, rk], writes=[ok])
                c.dma(gqkv[ch, :, t0:t0 + TT], o_[:], reads=[ok], writes=["gqkv"])
            else:
                c.dma(gqkv[ch, :, t0:t0 + TT], c_[:], reads=[ck], writes=["gqkv"])
    c.end_phase()


def phase_gdn(c, T, projT, smallT, gqkv, alog_d, dtb_d, gn_d, ident_d, ones_f, triu_d, sl_d, mks_d, mki_d, yT):
    nc = c.nc
    c.begin_phase()
    NC = T // 128
    GS = 4 if NC >= 4 else NC
    W = GS * 128
    ident = c.sb("g_ident", [128, 128], F32)
    ones = c.sb("g_ones", [128, 128], F32)
    triu = c.sb("g_triu", [128, 128], F32)
    slm = c.sb("g_sl", [128, 128], F32)
    mks = c.sb("g_mks", [128, GS, 128], F32)
    mki = c.sb("g_mki", [128, GS, 128], F32)
    identg = c.sb("g_identg", [128, GS, 128], F32)
    gnb = c.sb("g_gnb", [128, 128], F32)
    alog = c.sb("g_alog", [128, NHG], F32)
    dtb = c.sb("g_dtb", [128, NHG], F32)
    nea = c.sb("g_nea", [128, NHG], F32)
    c.dma(ident[:], ident_d, writes=["g_ident"])
    c.dma(ones[:], ones_f, writes=["g_ones"])
    c.dma(triu[:], triu_d, writes=["g_triu"])
    c.dma(slm[:], sl_d, writes=["g_sl"])
    for g in range(GS):
        c.dma(mks[:, g, :], mks_d, writes=["g_mks"])
        c.dma(mki[:, g, :], mki_d, writes=["g_mki"])
        c.dma(identg[:, g, :], ident_d, writes=["g_identg"])
    c.dma(gnb[:], gn_d, writes=["g_gnb"])
    c.dma(alog[:], alog_d, writes=["g_alog"])
    c.dma(dtb[:], dtb_d, writes=["g_dtb"])
    c.op("act", lambda e: e.activation(out=nea[:], in_=alog[:], func=AF.Exp), reads=["g_alog"], writes=["g_nea"])
    c.op("dve", lambda e: e.tensor_scalar(out=nea[:], in0=nea[:], scalar1=-1.0, scalar2=None, op0=ALU.mult),
         reads=["g_nea"], writes=["g_nea"])
    banks = [c.ps("g_pb%d" % i, [128, GS, 128]) for i in range(4)]
    pscan_b = [c.ps("g_pscan%d" % i, [128, 512]) for i in range(4)]

    class _PS:
        def __getitem__(self, idx):
            return pscan_b[idx[1]][:, 0:128]
    pscan = _PS()
    bi = [0]

    def bank():
        i = bi[0] % 4
        bi[0] += 1
        return banks[i], "g_pb%d" % i

    sm = c.sb("g_sm", [2 * NHG, T], F32)
    c.dma(sm[:], smallT[0:2 * NHG, :], writes=["g_sm"])
    abt = c.sb("g_abt", [128, NC, 2 * NHG], F32)
    for n0 in range(0, NC, 64):
        nn = min(64, NC - n0)
        pbs, pbsk = bank()
        psmall = pbs[:].rearrange("p g c -> p (g c)")
        for n in range(nn):
            c.op("pe", lambda e: e.transpose(psmall[:, n * 4:(n + 1) * 4], sm[:, (n0 + n) * 128:(n0 + n + 1) * 128],
                                             ident[0:4, 0:4]), reads=["g_sm", "g_ident"], writes=[pbsk])
        c.op("dve", lambda e: e.tensor_copy(out=abt[:, n0:n0 + nn, :],
                                            in_=psmall[:, 0:nn * 4].rearrange("p (n k) -> p n k", k=4)),
             reads=[pbsk], writes=["g_abt"])

    if DBG_STOP == 1:
        c.end_phase(); return

    def t2(name):
        return c.sb(name, [128, NC], F32)
    gg, beta, gc, gl, egc, egl, kdf, bgc, tmpn = [t2("g_" + n) for n in
                                                  ("gg", "beta", "gc", "gl", "egc", "egl", "kdf", "bgc", "tmpn")]

    def grp(name, n=2):
        return [c.sb("%s%d" % (name, i), [128, GS, 128], F32) for i in range(n)]
    kT, qT, vT = grp("g_kT"), grp("g_qT"), grp("g_vT")
    ktok, vtok = grp("g_ktok", 1)[0], grp("g_vtok", 1)[0]
    trig, E, decs, deci, L, Aq, AqT = [grp("g_" + n, 1)[0] for n in ("trig", "E", "decs", "deci", "L", "Aq", "AqT")]
    X, Y = grp("g_X"), grp("g_Y")
    R = grp("g_R", 1)[0]
    vb, kbg, kdec, u_, wT = [grp("g_" + n, 1)[0] for n in ("vb", "kbg", "kdec", "u", "wT")]
    o_, osq, on = [grp("g_" + n, 1)[0] for n in ("o", "osq", "on")]
    vnew = [c.sb("g_vnew%d" % i, [128, 128], F32) for i in range(2)]
    tq = [c.sb("g_tq%d" % i, [128, 128], F32) for i in range(2)]
    S = [c.sb("g_S%d" % i, [128, 128], F32) for i in range(2)]
    rs = c.sb("g_rs", [128, GS], F32)
    zt = [c.sb("g_zt%d" % i, [128, W], BF16) for i in range(2)]
    yo = [c.sb("g_yo%d" % i, [128, W], BF16) for i in range(2)]

    for h in range(NHG):
        c.op("act", lambda e: e.activation(out=tmpn[:], in_=abt[:, :, h], func=AF.Exp, bias=dtb[:, h:h + 1]),
             reads=["g_abt", "g_dtb"], writes=["g_tmpn"])
        c.op("act", lambda e: e.activation(out=tmpn[:], in_=tmpn[:], func=AF.Ln, bias=1.0), reads=["g_tmpn"], writes=["g_tmpn"])
        c.op("dve", lambda e: e.tensor_scalar(out=gg[:], in0=tmpn[:], scalar1=nea[:, h:h + 1], scalar2=None, op0=ALU.mult),
             reads=["g_tmpn", "g_nea"], writes=["g_gg"])
        c.op("act", lambda e: e.activation(out=beta[:], in_=abt[:, :, NHG + h], func=AF.Sigmoid), reads=["g_abt"], writes=["g_beta"])
        pbs, pbsk = bank()
        psmall = pbs[:].rearrange("p g c -> p (g c)")
        c.op("pe", lambda e: e.matmul(psmall[:, 0:NC], triu[:], gg[:], start=True, stop=True),
             reads=["g_triu", "g_gg"], writes=[pbsk])
        c.op("dve", lambda e: e.tensor_copy(out=gc[:], in_=psmall[:, 0:NC]), reads=[pbsk], writes=["g_gc"])
        pbs, pbsk = bank()
        psmall = pbs[:].rearrange("p g c -> p (g c)")
        c.op("pe", lambda e: e.matmul(psmall[:, 0:NC], ones[:], gg[:], start=True, stop=True),
             reads=["g_ones", "g_gg"], writes=[pbsk])
        c.op("dve", lambda e: e.tensor_copy(out=gl[:], in_=psmall[:, 0:NC]), reads=[pbsk], writes=["g_gl"])
        c.op("act", lambda e: e.activation(out=egc[:], in_=gc[:], func=AF.Exp), reads=["g_gc"], writes=["g_egc"])
        c.op("act", lambda e: e.activation(out=egl[:], in_=gl[:], func=AF.Exp), reads=["g_gl"], writes=["g_egl"])
        c.op("dve", lambda e: e.tensor_tensor(out=kdf[:], in0=gl[:], in1=gc[:], op=ALU.subtract),
             reads=["g_gl", "g_gc"], writes=["g_kdf"])
        c.op("act", lambda e: e.activation(out=kdf[:], in_=kdf[:], func=AF.Exp), reads=["g_kdf"], writes=["g_kdf"])
        c.op("dve", lambda e: e.tensor_tensor(out=bgc[:], in0=beta[:], in1=egc[:], op=ALU.mult),
             reads=["g_beta", "g_egc"], writes=["g_bgc"])
        c.op("pool", lambda e: e.memset(S[0][:], 0.0), writes=["g_S0"])
        sidx = 0
        if DBG_STOP == 2:
            c.end_phase(); return
        NG = NC // GS

        def gload(h_, gi_):
            t0_ = gi_ * W
            b2_ = gi_ % 2
            c.dma(qT[b2_][:], gqkv[h_, :, t0_:t0_ + W].rearrange("p (g c) -> p g c", c=128), reads=["gqkv"],
                  writes=["g_qT%d" % b2_])
            c.dma(kT[b2_][:], gqkv[NHG + h_, :, t0_:t0_ + W].rearrange("p (g c) -> p g c", c=128), reads=["gqkv"],
                  writes=["g_kT%d" % b2_])
            c.dma(vT[b2_][:], gqkv[2 * NHG + h_, :, t0_:t0_ + W].rearrange("p (g c) -> p g c", c=128), reads=["gqkv"],
                  writes=["g_vT%d" % b2_])
            c.dma(zt[b2_][:], projT[O_ZA + h_ * 128:O_ZA + (h_ + 1) * 128, t0_:t0_ + W], writes=["g_zt%d" % b2_])
        if h == 0 or NG % 2 == 1:
            gload(h, 0)
        for gi in range(NG):
            t0 = gi * W
            b2 = gi % 2
            kT_, qT_, vT_ = kT[b2], qT[b2], vT[b2]
            kTk, qTk, vTk = "g_kT%d" % b2, "g_qT%d" % b2, "g_vT%d" % b2
            z_ = zt[b2]; zk = "g_zt%d" % b2
            if NG % 2 == 0 or NG == 1:
                if gi + 1 < NG:
                    gload(h, gi + 1)
                elif h + 1 < NHG and NG % 2 == 0:
                    gload(h + 1, 0)
            pb, pbk = bank()
            for g in range(GS):
                c.op("pe", lambda e: e.transpose(pb[:, g, :], kT_[:, g, :], ident[:]), reads=[kTk, "g_ident"], writes=[pbk])
            c.op("act", lambda e: e.activation(out=ktok[:], in_=pb[:], func=AF.Identity), reads=[pbk], writes=["g_ktok"])
            pb, pbk = bank()
            for g in range(GS):
                c.op("pe", lambda e: e.transpose(pb[:, g, :], vT_[:, g, :], ident[:]), reads=[vTk, "g_ident"], writes=[pbk])
            c.op("dve", lambda e: e.tensor_copy(out=vtok[:], in_=pb[:]), reads=[pbk], writes=["g_vtok"])
            for g in range(GS):
                n = gi * GS + g
                c.op("dve", lambda e: e.tensor_scalar(out=trig[:, g, :], in0=triu[:], scalar1=gg[:, n:n + 1], scalar2=None,
                                                      op0=ALU.mult), reads=["g_triu", "g_gg"], writes=["g_trig"])
            pb, pbk = bank()
            for g in range(GS):
                c.op("pe", lambda e: e.matmul(pb[:, g, :], trig[:, g, :], slm[:], start=True, stop=True),
                     reads=["g_trig", "g_sl"], writes=[pbk])
            c.op("act", lambda e: e.activation(out=E[:], in_=pb[:], func=AF.Exp), reads=[pbk], writes=["g_E"])
            c.op("pool", lambda e: e.tensor_tensor(out=decs[:], in0=E[:], in1=mks[:], op=ALU.mult),
                 reads=["g_E", "g_mks"], writes=["g_decs"])
            c.op("pool", lambda e: e.tensor_tensor(out=deci[:], in0=E[:], in1=mki[:], op=ALU.mult),
                 reads=["g_E", "g_mki"], writes=["g_deci"])
            pb, pbk = bank()
            for g in range(GS):
                c.op("pe", lambda e: e.matmul(pb[:, g, :], kT_[:, g, :], kT_[:, g, :], start=True, stop=True),
                     reads=[kTk], writes=[pbk])
            for g in range(GS):
                n = gi * GS + g
                c.op("dve", lambda e: e.scalar_tensor_tensor(out=L[:, g, :], in0=pb[:, g, :], scalar=beta[:, n:n + 1],
                                                             in1=decs[:, g, :], op0=ALU.mult, op1=ALU.mult),
                     reads=[pbk, "g_beta", "g_decs"], writes=["g_L"])
            pb, pbk = bank()
            for g in range(GS):
                c.op("pe", lambda e: e.matmul(pb[:, g, :], qT_[:, g, :], kT_[:, g, :], start=True, stop=True),
                     reads=[qTk, kTk], writes=[pbk])
            c.op("dve", lambda e: e.tensor_tensor(out=Aq[:], in0=pb[:], in1=deci[:], op=ALU.mult),
                 reads=[pbk, "g_deci"], writes=["g_Aq"])
            if DBG_STOP == 3:
                c.end_phase(); return
            pb, pbk = bank()
            for g in range(GS):
                c.op("pe", lambda e: e.transpose(pb[:, g, :], Aq[:, g, :], ident[:]), reads=["g_Aq", "g_ident"], writes=[pbk])
            if DBG_STOP == 29:
                c.end_phase(); return
            c.op("act", lambda e: e.activation(out=AqT[:], in_=pb[:], func=AF.Identity), reads=[pbk], writes=["g_AqT"])
            if DBG_STOP == 30:
                c.end_phase(); return
            pb, pbk = bank()
            for g in range(GS):
                c.op("pe", lambda e: e.transpose(pb[:, g, :], L[:, g, :], ident[:]), reads=["g_L", "g_ident"], writes=[pbk])
            if DBG_STOP == 305:
                c.end_phase(); return
            c.op("act", lambda e: e.activation(out=X[0][:], in_=pb[:], func=AF.Identity), reads=[pbk], writes=["g_X0"])
            if DBG_STOP == 306:
                c.end_phase(); return
            c.op("dve", lambda e: e.tensor_tensor(out=R[:], in0=identg[:], in1=X[0][:], op=ALU.subtract),
                 reads=["g_X0", "g_identg"], writes=["g_R"])
            Yc, Yk = L, "g_L"
            Xc, Xk = X[0], "g_X0"
            if DBG_STOP == 31:
                c.end_phase(); return
            for lvl in range(6):
                if DBG_STOP == 32 + lvl and lvl > 0:
                    c.end_phase(); return
                last = lvl == 5
                nX, nXk = X[(lvl + 1) % 2], "g_X%d" % ((lvl + 1) % 2)
                nY, nYk = Y[lvl % 2], "g_Y%d" % (lvl % 2)
                if not last:
                    pbx, pbxk = bank()
                    for g in range(GS):
                        c.op("pe", lambda e: e.matmul(pbx[:, g, :], Yc[:, g, :], Xc[:, g, :], start=True, stop=True),
                             reads=[Yk, Xk], writes=[pbxk])
                pby, pbyk = bank()
                for g in range(GS):
                    c.op("pe", lambda e: e.matmul(pby[:, g, :], Xc[:, g, :], Yc[:, g, :], start=True, stop=True),
                         reads=[Yk, Xk], writes=[pbyk])
                c.op("dve", lambda e: e.tensor_copy(out=nY[:], in_=pby[:]), reads=[pbyk], writes=[nYk])
                if not last:
                    c.op("act", lambda e: e.activation(out=nX[:], in_=pbx[:], func=AF.Identity), reads=[pbxk], writes=[nXk])
                pbr, pbrk = bank()
                for g in range(GS):
                    c.op("pe", lambda e: e.matmul(pbr[:, g, :], nY[:, g, :], R[:, g, :], start=True, stop=True),
                         reads=[nYk, "g_R"], writes=[pbrk])
                c.op("dve", lambda e: e.tensor_tensor(out=R[:], in0=R[:], in1=pbr[:], op=ALU.add),
                     reads=["g_R", pbrk], writes=["g_R"])
                Yc, Yk = nY, nYk
                Xc, Xk = nX, nXk
            if DBG_STOP == 4:
                c.end_phase(); return
            for g in range(GS):
                n = gi * GS + g
                c.op("pool", lambda e: e.tensor_scalar(out=vb[:, g, :], in0=vtok[:, g, :], scalar1=beta[:, n:n + 1],
                                                       scalar2=None, op0=ALU.mult), reads=["g_vtok", "g_beta"], writes=["g_vb"])
                c.op("pool", lambda e: e.tensor_scalar(out=kbg[:, g, :], in0=ktok[:, g, :], scalar1=bgc[:, n:n + 1],
                                                       scalar2=None, op0=ALU.mult), reads=["g_ktok", "g_bgc"], writes=["g_kbg"])
                c.op("pool", lambda e: e.tensor_scalar(out=kdec[:, g, :], in0=ktok[:, g, :], scalar1=kdf[:, n:n + 1],
                                                       scalar2=None, op0=ALU.mult), reads=["g_ktok", "g_kdf"], writes=["g_kdec"])
            pb, pbk = bank()
            for g in range(GS):
                c.op("pe", lambda e: e.matmul(pb[:, g, :], R[:, g, :], vb[:, g, :], start=True, stop=True),
                     reads=["g_R", "g_vb"], writes=[pbk])
            c.op("act", lambda e: e.activation(out=u_[:], in_=pb[:], func=AF.Identity), reads=[pbk], writes=["g_u"])
            pb, pbk = bank()
            for g in range(GS):
                c.op("pe", lambda e: e.matmul(pb[:, g, :], kbg[:, g, :], R[:, g, :], start=True, stop=True),
                     reads=["g_R", "g_kbg"], writes=[pbk])
            c.op("dve", lambda e: e.tensor_copy(out=wT[:], in_=pb[:]), reads=[pbk], writes=["g_wT"])
            for g in range(GS):
                n = gi * GS + g
                Sc, Sk = S[sidx % 2], "g_S%d" % (sidx % 2)
                Sn, Snk = S[(sidx + 1) % 2], "g_S%d" % ((sidx + 1) % 2)
                sidx += 1
                vn, vnk = vnew[n % 2], "g_vnew%d" % (n % 2)
                tq_, tqk = tq[n % 2], "g_tq%d" % (n % 2)
                c.op("pe", lambda e: e.matmul(pscan[:, 0, :], wT[:, g, :], Sc[:], start=True, stop=True),
                     reads=["g_wT", Sk], writes=["g_ps0"])
                c.op("dve", lambda e: e.tensor_tensor(out=vn[:], in0=u_[:, g, :], in1=pscan[:, 0, :], op=ALU.subtract),
                     reads=["g_u", "g_ps0"], writes=[vnk])
                c.op("pe", lambda e: e.matmul(pscan[:, 1, :], qT_[:, g, :], Sc[:], start=True, stop=True),
                     reads=[qTk, Sk], writes=["g_ps1"])
                c.op("pe", lambda e: e.matmul(pscan[:, 2, :], AqT[:, g, :], vn[:], start=True, stop=True),
                     reads=["g_AqT", vnk], writes=["g_ps2"])
                c.op("pe", lambda e: e.matmul(pscan[:, 3, :], kdec[:, g, :], vn[:], start=True, stop=True),
                     reads=["g_kdec", vnk], writes=["g_ps3"])
                c.op("act", lambda e: e.activation(out=tq_[:], in_=pscan[:, 1, :], func=AF.Identity, scale=egc[:, n:n + 1]),
                     reads=["g_ps1", "g_egc"], writes=[tqk])
                c.op("dve", lambda e: e.tensor_tensor(out=o_[:, g, :], in0=tq_[:], in1=pscan[:, 2, :], op=ALU.add),
                     reads=[tqk, "g_ps2"], writes=["g_o"])
                c.op("dve", lambda e: e.scalar_tensor_tensor(out=Sn[:], in0=Sc[:], scalar=egl[:, n:n + 1], in1=pscan[:, 3, :],
                                                             op0=ALU.mult, op1=ALU.add),
                     reads=[Sk, "g_egl", "g_ps3"], writes=[Snk])
            if DBG_STOP == 5:
                c.end_phase(); return
            c.op("pool", lambda e: e.tensor_tensor(out=osq[:], in0=o_[:], in1=o_[:], op=ALU.mult), reads=["g_o"], writes=["g_osq"])
            c.op("dve", lambda e: e.tensor_reduce(out=rs[:], in_=osq[:], axis=mybir.AxisListType.X, op=ALU.add),
                 reads=["g_osq"], writes=["g_rs"])
            c.op("dve", lambda e: e.tensor_scalar(out=rs[:], in0=rs[:], scalar1=1.0 / 128, scalar2=EPS, op0=ALU.mult,
                                                  op1=ALU.add), reads=["g_rs"], writes=["g_rs"])
            c.op("act", lambda e: e.activation(out=rs[:], in_=rs[:], func=AF.Sqrt), reads=["g_rs"], writes=["g_rs"])
            c.op("dve", lambda e: e.reciprocal(out=rs[:], in_=rs[:]), reads=["g_rs"], writes=["g_rs"])
            for g in range(GS):
                c.op("dve", lambda e: e.scalar_tensor_tensor(out=on[:, g, :], in0=o_[:, g, :], scalar=rs[:, g:g + 1],
                                                             in1=gnb[:], op0=ALU.mult, op1=ALU.mult),
                     reads=["g_o", "g_rs", "g_gnb"], writes=["g_on"])
            pb, pbk = bank()
            for g in range(GS):
                c.op("pe", lambda e: e.transpose(pb[:, g, :], on[:, g, :], ident[:]), reads=["g_on", "g_ident"], writes=[pbk])
            y_ = yo[b2]; yk = "g_yo%d" % b2
            c.op("dve", lambda e: e.tensor_tensor(out=y_[:], in0=pb[:].rearrange("p g c -> p (g c)"), in1=z_[:], op=ALU.mult),
                 reads=[pbk, zk], writes=[yk])
            c.dma(yT[Y_A + h * 128:Y_A + (h + 1) * 128, t0:t0 + W], y_[:], reads=[yk])
    c.end_phase()


PAIRS = [[0, 1], [2, 3], [4, 5], [6, 7]]


def build(T, nlayers=2, only=None, pairs=PAIRS):
    nc = bass.Bass("TRN2", target_bir_lowering=False)

    def di(n, shape, dt=F32):
        return nc.dram_tensor(n, shape, dt, kind="ExternalInput").ap()

    def ds(n, shape, dt=F32):
        return nc.dram_tensor(n, shape, dt, kind="Internal").ap()
    L = nlayers
    xT = di("xT", [D, T])
    pT = di("pT", [L, 256, T])
    w_in = di("w_in", [L, D, PWL])
    w_small = di("w_small", [L, D, 8])
    w_branch = di("w_branch", [L, 4, 512, 1024])
    w_out = di("w_out", [L, 1024, 1024])
    w_ple = di("w_ple", [L, 256, 1024])
    w_pg = di("w_pg", [L, 1024, 1024])
    b_gate = di("b_gate", [L, 128, 32])
    b_pg = di("b_pg", [L, 128, 8])
    ln_g = di("ln_g", [L, 128, 8])
    ln_b = di("ln_b", [L, 128, 8])
    cw = di("cw", [L, 128, 4, 31])
    cvec = di("cvec", [L, 128, 3, 4])
    cw4 = di("cw4", [L, 128, 3 * NHG, 4])
    alog = di("alog", [L, 128, NHG])
    dtb = di("dtb", [L, 128, NHG])
    gn = di("gn", [L, 128, 128])
    fb = di("fb", [L, NHA, 1])
    ident = di("ident", [128, 128])
    ones_f = di("ones_f", [128, 128])
    identb = di("identb", [128, 128], BF16)
    negm_i = di("negm_i", [128, 4, 512], BF16)
    negm_s = di("negm_s", [128, 4, 512], BF16)
    m01_s = di("m01_s", [128, 4, 512], BF16)
    negu = di("negu", [128, 128], BF16)
    triu = di("triu", [128, 128])
    slm = di("slm", [128, 128])
    mki = di("mki", [128, 128])
    outT = nc.dram_tensor("outT", [D, T], F32, kind="ExternalOutput").ap()
    projT = ds("projT", [PWL, T], BF16)
    smallT = ds("smallT", [8, T])
    yT = ds("yT", [YL, T], BF16)
    yg = [ds("yg%d" % i, [256, T], BF16) for i in range(6)]
    gqkv = ds("gqkv", [3 * NHG, 128, T])
    fxq = ds("fxq", [NHA, T], BF16)
    fxk = ds("fxk", [NHA, 3, T], BF16)
    xmid = [ds("xmid%d" % i, [D, T]) for i in range(max(L - 1, 1))]
    with ExitStack() as es:
        c = Ctx(nc, es)
        xin = xT
        for l in range(L):
            xo = outT if l == L - 1 else xmid[l]
            on = lambda n: only is None or n in only
            if on("proj"):
                phase_proj(c, T, xin, w_in[l], w_small[l], b_gate[l], projT, smallT)
            if on("conv"):
                phase_conv(c, T, projT, cw[l], cvec[l], ident, ones_f, yT)
            if on("fox"):
                phase_fox(c, T, projT, smallT, fb[l], negm_i, identb, fxq, fxk, yT)
            if on("sb"):
                phase_sb(c, T, projT, negm_s, m01_s, identb, negu, yT)
            if on("gdn_pre"):
                phase_gdn_pre(c, T, projT, cw4[l], ident, ones_f, gqkv)
            if on("gdn"):
                phase_gdn(c, T, projT, smallT, gqkv, alog[l], dtb[l], gn[l], ident, ones_f, triu, slm, slm, mki, yT)
            if on("out"):
                c.barrier()
                for j, row in enumerate((Y_A, Y_A + 128, Y_C, Y_C + 128, Y_D, Y_D + 128)):
                    c.collective("AllGather", [yT[row:row + 128, :]], [yg[j]], pairs)
                c.barrier()
                phase_out(c, T, xin, pT[l], yT, yg, projT, w_branch[l], w_out[l], w_ple[l],
                          w_pg[l], b_pg[l], ln_g[l], ln_b[l], ones_f, xo)
            xin = xo
        c.finish()
        ninst = c.ninst
    return nc, ninst


def host_inputs(x_b, p_b, w, L, r):
    import ml_dtypes
    bf = lambda a: np.ascontiguousarray(a).astype(ml_dtypes.bfloat16)
    f32 = lambda a: np.ascontiguousarray(a, dtype=np.float32)
    v8 = lambda v: f32(np.stack([v[l].reshape(8, 128).T for l in range(L)]))
    rep = lambda v: f32(np.stack([np.broadcast_to(v[l][None, :], (128, v[l].shape[0])) for l in range(L)]))
    v4 = lambda v: v.reshape(4, 128).T
    ar = np.arange
    gh = np.concatenate([(NHG * r + h) * 128 + ar(128) for h in range(NHG)])
    ah = np.concatenate([(NHA * r + h) * 64 + ar(64) for h in range(NHA)])
    cols = np.concatenate([R_QA + gh, R_KA + gh, R_VA + gh, R_ZA + gh,
                           R_GL + ar(512), R_GG + ar(512), R_ZB + ar(512),
                           R_QC + ah, R_KC + ah, R_VC + ah, R_ZC + ah,
                           R_QD + ah, R_KD + ah, R_VD + ah, R_ZD + ah,
                           R_G + ar(4096)])
    assert cols.shape[0] == PWL
    scols = np.concatenate([R_AA + NHG * r + ar(NHG), R_BA + NHG * r + ar(NHG), R_FD + NHA * r + ar(NHA)])
    gch = np.concatenate([gh, 512 + gh, 1024 + gh])
    s_ = ar(128)[:, None, None]; j_ = ar(4)[None, :, None]; q_ = ar(512)[None, None, :]
    incl = (s_ + 128 * j_ <= q_); strict = (s_ + 128 * j_ < q_)
    jj = ar(128)[:, None]; ss = ar(128)[None, :]
    d = {
        "xT": f32(x_b.T), "pT": f32(np.stack([p_b[l].T for l in range(L)])),
        "w_in": f32(w["w_in"][:L][:, :, cols]), "w_small": f32(w["w_in"][:L][:, :, scols]),
        "w_branch": f32(w["w_branch"][:L]), "w_out": f32(w["w_out"][:L]),
        "w_ple": f32(w["w_ple"][:L]), "w_pg": f32(w["w_ple_gate"][:L]),
        "b_gate": f32(np.stack([w["b_gate"][l].reshape(32, 128).T for l in range(L)])),
        "b_pg": v8(w["b_ple_gate"]), "ln_g": v8(w["ln_g"]), "ln_b": v8(w["ln_b"]),
        "cw": f32(np.stack([w["conv_dw"][l].T.reshape(4, 128, 31).transpose(1, 0, 2) for l in range(L)])),
        "cvec": f32(np.stack([np.stack([v4(w["conv_dw_bias"][l]), v4(w["conv_ln_g"][l]), v4(w["conv_ln_b"][l])], axis=1)
                              for l in range(L)])),
        "cw4": f32(np.stack([w["conv_qkv"][l][:, gch].T.reshape(3 * NHG, 128, 4).transpose(1, 0, 2) for l in range(L)])),
        "alog": rep(w["a_log"][:, NHG * r:NHG * (r + 1)]), "dtb": rep(w["dt_bias"][:, NHG * r:NHG * (r + 1)]),
        "gn": rep(w["gdn_norm"]),
        "fb": f32(np.stack([w["forget_bias"][l][NHA * r:NHA * (r + 1)].reshape(NHA, 1) for l in range(L)])),
        "ident": np.eye(128, dtype=np.float32), "ones_f": np.ones((128, 128), np.float32),
        "identb": bf(np.eye(128, dtype=np.float32)),
        "negm_i": bf(np.where(incl, 0.0, -30000.0).astype(np.float32)),
        "negm_s": bf(np.where(strict, 0.0, -30000.0).astype(np.float32)),
        "m01_s": bf(strict.astype(np.float32)),
        "negu": bf(-(jj >= ss).astype(np.float32)),
        "triu": (jj <= ss).astype(np.float32), "slm": (jj > ss).astype(np.float32), "mki": (jj >= ss).astype(np.float32),
    }
    return d


_CACHE = {}


def kernel(**inputs):
    x = np.asarray(inputs["x"])
    p = np.asarray(inputs["p"])
    B, T, _ = x.shape
    w = {k: np.asarray(v) for k, v in inputs.items() if k not in ("x", "p")}
    L = w["w_in"].shape[0]
    ncores = 2 * B
    pairs = [[2 * b, 2 * b + 1] for b in range(B)]
    key = (T, L, B)
    if key not in _CACHE:
        _CACHE[key] = build(T, L, pairs=pairs)[0]
    nc = _CACHE[key]
    in_maps = [host_inputs(x[i // 2], p[:, i // 2], w, L, i % 2) for i in range(ncores)]
    res = run_bass_kernel_spmd(nc, in_maps, core_ids=list(range(ncores)))
    out = np.stack([np.asarray(res.results[2 * b]["outT"]).T for b in range(B)])
    return np.ascontiguousarray(out, dtype=np.float32)
```

```python
import numpy as np
from contextlib import ExitStack
import concourse.bass as bass
import concourse.mybir as mybir
from concourse.bass_utils import run_bass_kernel_spmd

F32 = mybir.dt.float32
BF16 = mybir.dt.bfloat16
AF = mybir.ActivationFunctionType
ALU = mybir.AluOpType

D = 1024
PW = 11792
PWL = 8704
NHA = 4
NHG = 2
YL = 1280
Y_A, Y_B, Y_C, Y_D = 0, 256, 768, 1024
ALPHA = 4 ** 0.25
EPS = 1e-5
SEM_EPOCH = 20000
DBG_STOP = 0

R_QA, R_KA, R_VA, R_ZA, R_AA, R_BA = 0, 512, 1024, 1536, 2048, 2052
R_GL, R_GG, R_ZB = 2056, 2568, 3080
R_QC, R_KC, R_VC, R_ZC = 3592, 4104, 4616, 5128
R_QD, R_KD, R_VD, R_ZD, R_FD = 5640, 6152, 6664, 7176, 7688
R_G = 7696
O_QA, O_KA, O_VA, O_ZA = 0, 256, 512, 768
O_GL, O_GG, O_ZB = 1024, 1536, 2048
O_QC, O_KC, O_VC, O_ZC = 2560, 2816, 3072, 3328
O_QD, O_KD, O_VD, O_ZD = 3584, 3840, 4096, 4352
O_G = 4608


class Ctx:
    def __init__(self, nc, es):
        self.nc, self.es = nc, es
        self.eng = {"pe": nc.tensor, "act": nc.scalar, "dve": nc.vector, "pool": nc.gpsimd, "sp": nc.sync}
        self.sem = {}
        self.cnt = {}
        self.nsem = 0
        for e in self.eng:
            self._new_sem(e)
        self.waited = {e: {} for e in self.eng}
        self.lastw = {}
        self.readers = {}
        self.dsem = [es.enter_context(nc.semaphore("dq%d" % i)) for i in range(24)]
        self.dcnt = [0] * 24
        self.dnext = 0
        self.ninst = 0

    def _new_sem(self, e):
        self.sem[e] = self.es.enter_context(self.nc.semaphore("s_%s_%d" % (e, self.nsem)))
        self.nsem += 1
        self.cnt[e] = 0

    def _wait(self, e, tok):
        sem, val, src = tok
        if src == "pe" and e == "pe":
            return
        w = self.waited[e]
        k = id(sem)
        if w.get(k, 0) >= val:
            return
        w[k] = val
        self.eng[e].wait_ge(sem, val)

    def _deps(self, e, reads, writes):
        for k in reads:
            t = self.lastw.get(k)
            if t is not None:
                self._wait(e, t)
        for k in writes:
            t = self.lastw.get(k)
            if t is not None:
                self._wait(e, t)
            rd = self.readers.get(k)
            if rd:
                for key, t in rd.items():
                    self._wait(e, t)

    def _commit(self, tok, reads, writes):
        for k in writes:
            self.lastw[k] = tok
            self.readers[k] = {}
        for k in reads:
            rd = self.readers.setdefault(k, {})
            if tok[2] == "dma":
                rd[("dma", id(tok[0]))] = tok
            else:
                rd[tok[2]] = tok

    def op(self, e, fn, reads=(), writes=()):
        self._deps(e, reads, writes)
        ins = fn(self.eng[e])
        if self.cnt[e] >= SEM_EPOCH:
            self._new_sem(e)
        self.cnt[e] += 1
        ins.then_inc(self.sem[e], 1)
        tok = (self.sem[e], self.cnt[e], e)
        self._commit(tok, reads, writes)
        self.ninst += 1
        return tok

    def dma(self, out, in_, reads=(), writes=(), q="sp"):
        j = self.dnext
        self.dnext = (self.dnext + 1) % len(self.dsem)
        self._deps(q, reads, writes)
        if self.dcnt[j] > 0:
            self._wait(q, (self.dsem[j], self.dcnt[j], "dma"))
        self.eng[q].dma_start(out=out, in_=in_).then_inc(self.dsem[j], 16)
        self.dcnt[j] += 16
        tok = (self.dsem[j], self.dcnt[j], "dma")
        self._commit(tok, reads, writes)
        self.ninst += 1
        return tok

    def collective(self, kind, ins, outs, groups, reads=(), writes=()):
        if not hasattr(self, "ccsem"):
            self.ccsem = self.es.enter_context(self.nc.semaphore("ccsem"))
            self.cccnt = 0
        self._deps("pool", reads, writes)
        self.nc.gpsimd.collective_compute(kind, ALU.bypass, replica_groups=groups, ins=ins, outs=outs).then_inc(self.ccsem)
        self.cccnt += 1
        tok = (self.ccsem, self.cccnt, "dma")
        self._commit(tok, reads, writes)
        for e in self.eng:
            self._wait(e, tok)
        return tok

    def finish(self):
        for j in range(len(self.dsem)):
            if self.dcnt[j] > 0:
                self._wait("sp", (self.dsem[j], self.dcnt[j], "dma"))
        for e in ("pe", "act", "dve", "pool"):
            if self.cnt[e] > 0:
                self._wait("sp", (self.sem[e], self.cnt[e], e))

    def sb(self, name, shape, dt):
        return self.pes.enter_context(self.nc.sbuf_tensor("%s_%d" % (name, self.phase_no), shape, dt))

    def ps(self, name, shape, dt=F32):
        return self.pes.enter_context(self.nc.psum_tensor("%s_%d" % (name, self.phase_no), shape, dt))

    def begin_phase(self):
        self.pes = ExitStack()
        self.phase_no = getattr(self, "phase_no", 0) + 1

    def end_phase(self):
        self.barrier()
        self.pes.close()

    def barrier(self):
        for e in self.eng:
            for j in range(len(self.dsem)):
                if self.dcnt[j] > 0:
                    self._wait(e, (self.dsem[j], self.dcnt[j], "dma"))
            for f in ("pe", "act", "dve", "pool"):
                if f != e and self.cnt[f] > 0:
                    self._wait(e, (self.sem[f], self.cnt[f], f))
        self.lastw.clear()
        self.readers.clear()


def proj_chunks():
    ch = []

    def add(o, n, kind):
        for i in range(n // 128):
            ch.append((o + 128 * i, 128, kind))
    add(O_QA, 768, 0)
    add(O_ZA, 256, 1)
    add(O_GL, 512, 0)
    add(O_GG, 512, 2)
    add(O_ZB, 512, 1)
    add(O_QC, 256, 3)
    add(O_KC, 512, 0)
    add(O_ZC, 256, 1)
    add(O_QD, 256, 3)
    add(O_KD, 512, 0)
    add(O_ZD, 256, 1)
    add(O_G, 4096, 4)
    return ch


def phase_proj(c, T, xT, w_in, w_small, b_gate, projT, smallT):
    nc = c.nc
    c.begin_phase()
    ST = min(4096, T)
    nst = T // ST
    nsub = ST // 512
    xs = [c.sb("p1_xs%d" % i, [128, 8, 512], F32) for i in range(2)]
    xb = c.sb("p1_xb", [128, 8, ST], BF16)
    ws = [c.sb("p1_ws%d" % i, [128, 8, 512], F32) for i in range(2)]
    wb = [c.sb("p1_wb%d" % i, [128, 8, 512], BF16) for i in range(2)]
    wss = c.sb("p1_wss", [128, 8, 8], F32)
    wsb = c.sb("p1_wsb", [128, 8, 8], BF16)
    bg = c.sb("p1_bg", [128, 32], F32)
    ob = [c.sb("p1_ob%d" % i, [128, 4, 512], BF16) for i in range(3)]
    osm = c.sb("p1_osm", [8, 512], F32)
    pss = [c.ps("p1_ps%d" % i, [128, 512]) for i in range(4)]
    c.dma(bg[:], b_gate, writes=["p1_bg"])
    c.dma(wss[:], w_small.rearrange("(k p) n -> p k n", p=128), writes=["p1_wss"])
    c.op("pool", lambda e: e.tensor_copy(out=wsb[:], in_=wss[:]), reads=["p1_wss"], writes=["p1_wsb"])
    chunks = proj_chunks()
    groups = [chunks[i:i + 4] for i in range(0, len(chunks), 4)]
    xTv = xT.rearrange("(k p) t -> p k t", p=128)
    w_v = w_in.rearrange("(k p) n -> p k n", p=128)
    it = 0
    for st in range(nst):
        for sub in range(nsub):
            t0 = st * ST + sub * 512
            s = xs[(st * nsub + sub) % 2]
            key = "p1_xs%d" % ((st * nsub + sub) % 2)
            c.dma(s[:], xTv[:, :, t0:t0 + 512], writes=[key])
            c.op("pool", lambda e: e.tensor_copy(out=xb[:, :, sub * 512:(sub + 1) * 512], in_=s[:]),
                 reads=[key], writes=["p1_xb%d" % sub])
        for sub in range(nsub):
            t0 = st * ST + sub * 512
            p = pss[it % 4]; pk = "p1_ps%d" % (it % 4)
            for k in range(8):
                c.op("pe", lambda e: e.matmul(p[0:8, :], wsb[:, k, :], xb[:, k, sub * 512:(sub + 1) * 512],
                                              start=(k == 0), stop=(k == 7)),
                     reads=["p1_wsb", "p1_xb%d" % sub], writes=[pk])
            c.op("dve", lambda e: e.tensor_copy(out=osm[:], in_=p[0:8, :]), reads=[pk], writes=["p1_osm"])
            c.dma(smallT[:, t0:t0 + 512], osm[:], reads=["p1_osm"])
            it += 1
        def wload(gi):
            g0 = groups[gi][0][0]
            gw = 128 * len(groups[gi])
            wsl = ws[gi % 2]; wbl = wb[gi % 2]
            c.dma(wsl[:, :, 0:gw], w_v[:, :, g0:g0 + gw], writes=["p1_ws%d" % (gi % 2)])
            c.op("pool", lambda e: e.tensor_copy(out=wbl[:, :, 0:gw], in_=wsl[:, :, 0:gw]), reads=["p1_ws%d" % (gi % 2)],
                 writes=["p1_wb%d" % (gi % 2)])
        wload(0)
        for gi, grp_ in enumerate(groups):
            g0 = grp_[0][0]
            assert all(grp_[j][0] == g0 + 128 * j for j in range(len(grp_)))
            wbl = wb[gi % 2]
            if gi + 1 < len(groups):
                wload(gi + 1)
            for sub in range(nsub):
                t0 = st * ST + sub * 512
                obi = (gi * nsub + sub) % 3
                for cj, (off, wd, kind) in enumerate(grp_):
                    p = pss[it % 4]; pk = "p1_ps%d" % (it % 4)
                    o = ob[obi][:, cj, :]; ok = "p1_ob%d" % obi
                    for k in range(8):
                        c.op("pe", lambda e: e.matmul(p[:], wbl[:, k, cj * 128:(cj + 1) * 128],
                                                      xb[:, k, sub * 512:(sub + 1) * 512],
                                                      start=(k == 0), stop=(k == 7)),
                             reads=["p1_wb%d" % (gi % 2), "p1_xb%d" % sub], writes=[pk])
                    if kind == 0:
                        c.op("dve", lambda e: e.tensor_copy(out=o, in_=p[:]), reads=[pk], writes=[ok])
                    elif kind == 3:
                        c.op("dve", lambda e: e.tensor_scalar(out=o, in0=p[:], scalar1=0.125, scalar2=None,
                                                              op0=ALU.mult), reads=[pk], writes=[ok])
                    elif kind == 1:
                        c.op("act", lambda e: e.activation(out=o, in_=p[:], func=AF.Silu), reads=[pk], writes=[ok])
                    elif kind == 2:
                        c.op("act", lambda e: e.activation(out=o, in_=p[:], func=AF.Sigmoid), reads=[pk], writes=[ok])
                    else:
                        gidx = (off - O_G) // 128
                        c.op("act", lambda e: e.activation(out=o, in_=p[:], func=AF.Sigmoid, bias=bg[:, gidx:gidx + 1]),
                             reads=[pk, "p1_bg"], writes=[ok])
                    it += 1
                ng = len(grp_)
                c.dma(projT[g0:g0 + 128 * ng, t0:t0 + 512].rearrange("(c p) t -> p c t", p=128), ob[obi][:, 0:ng, :],
                      reads=["p1_ob%d" % obi])
    c.end_phase()


def load_w_bf16(c, dst, dst_key, src_rows, stg, n):
    i = c.stg_i = getattr(c, "stg_i", 0) + 1
    s = stg[i % len(stg)]
    sk = "stg%d" % (i % len(stg))
    c.dma(s[:, 0:n], src_rows, writes=[sk])
    c.op("pool", lambda e: e.tensor_copy(out=dst, in_=s[:, 0:n]), reads=[sk], writes=[dst_key])


def phase_out(c, T, xT, pT, yT, yg, projT, w_branch, w_out, w_ple, w_pg, b_pg, ln_g, ln_b, ones_f, outT):
    nc = c.nc
    c.begin_phase()
    TT = 256
    stg = [c.sb("po_stg%d" % i, [128, 1024], F32) for i in range(2)]
    wbr = c.sb("po_wbr", [128, 16, 1024], BF16)
    wo = c.sb("po_wo", [128, 8, 1024], BF16)
    wpg = c.sb("po_wpg", [128, 8, 1024], BF16)
    wpl = c.sb("po_wpl", [128, 2, 1024], BF16)
    vb = c.sb("po_vec", [128, 3, 8], F32)
    ones = c.sb("po_ones", [128, 128], F32)
    c.dma(vb[:, 0, :], b_pg, writes=["po_vec"])
    c.dma(vb[:, 1, :], ln_g, writes=["po_vec"])
    c.dma(vb[:, 2, :], ln_b, writes=["po_vec"])
    c.dma(ones[:], ones_f, writes=["po_ones"])
    wbv = w_branch.rearrange("b (k p) n -> p (b k) n", p=128)
    for i in range(16):
        load_w_bf16(c, wbr[:, i, :], "po_wbr", wbv[:, i, :], stg, 1024)
    for nm, dst, src, nk in (("po_wo", wo, w_out, 8), ("po_wpg", wpg, w_pg, 8), ("po_wpl", wpl, w_ple, 2)):
        sv = src.rearrange("(k p) n -> p k n", p=128)
        for i in range(nk):
            load_w_bf16(c, dst[:, i, :], nm, sv[:, i, :], stg, 1024)
    ys = [c.sb("po_y%d" % i, [128, 16, TT], BF16) for i in range(2)]
    gt = [c.sb("po_gt%d" % i, [128, 8, TT], BF16) for i in range(2)]
    xs_ = [c.sb("po_x%d" % i, [128, 8, TT], F32) for i in range(2)]
    pfs = [c.sb("po_pf%d" % i, [128, 2, TT], F32) for i in range(2)]
    pbs = [c.sb("po_pb%d" % i, [128, 2, TT], BF16) for i in range(2)]
    m = c.sb("po_m", [128, 8, TT], F32)
    mb = c.sb("po_mb", [128, 8, TT], BF16)
    rs_ = [c.sb("po_r%d_" % i, [128, 8, TT], F32) for i in range(2)]
    rbs_ = [c.sb("po_rb%d_" % i, [128, 8, TT], BF16) for i in range(2)]
    tmp = [c.sb("po_tmp%d" % i, [128, TT], F32) for i in range(2)]
    gp = [c.sb("po_gp%d" % i, [128, TT], F32) for i in range(4)]
    sq = [c.sb("po_sq%d" % i, [128, TT], F32) for i in range(8)]
    mean = c.sb("po_mean", [128, TT], F32)
    msq = c.sb("po_msq", [128, TT], F32)
    rstd = c.sb("po_rstd", [128, TT], F32)
    ot = [c.sb("po_ot%d" % i, [128, TT], F32) for i in range(4)]
    ps = [c.ps("po_ps%d" % i, [128, 512]) for i in range(4)]
    pstat = [c.ps("po_pst%d" % i, [128, 512]) for i in range(2)]
    xv = xT.rearrange("(k p) t -> p k t", p=128)
    pv = pT.rearrange("(k p) t -> p k t", p=128)
    ov = outT.rearrange("(k p) t -> p k t", p=128)
    itc = [0]

    def stL(tt):
        t0 = tt * TT
        y = ys[tt % 2]; yk_ = "po_y%d" % (tt % 2)
        x = xs_[tt % 2]; xk_ = "po_x%d" % (tt % 2)
        pf = pfs[tt % 2]; pfk = "po_pf%d" % (tt % 2)
        pb = pbs[tt % 2]; pbk_ = "po_pb%d" % (tt % 2)
        c.dma(y[:, 4:8, :], yT[Y_B:Y_B + 512, t0:t0 + TT].rearrange("(k p) t -> p k t", p=128), writes=[yk_])
        for gi_, br_ in enumerate((0, 2, 3)):
            for par in range(2):
                i0_ = br_ * 4 + par
                c.dma(y[:, i0_:i0_ + 3:2, :], yg[gi_ * 2 + par][:, t0:t0 + TT].rearrange("(r p) t -> p r t", p=128),
                      writes=[yk_])
        c.dma(x[:], xv[:, :, t0:t0 + TT], writes=[xk_])
        c.dma(pf[:], pv[:, :, t0:t0 + TT], writes=[pfk])
        c.op("pool", lambda e: e.tensor_copy(out=pb[:], in_=pf[:]), reads=[pfk], writes=[pbk_])

    def stA(tt):
        t0 = tt * TT
        y = ys[tt % 2]; yk_ = "po_y%d" % (tt % 2)
        x = xs_[tt % 2]; xk_ = "po_x%d" % (tt % 2)
        pf = pfs[tt % 2]; pfk = "po_pf%d" % (tt % 2)
        pb = pbs[tt % 2]; pbk_ = "po_pb%d" % (tt % 2)
        pst_s = pstat[0]; pst_q = pstat[1]
        r = rs_[tt % 2]; rb = rbs_[tt % 2]; rp_ = "po_r%d_" % (tt % 2); rbp_ = "po_rb%d_" % (tt % 2)
        for br in range(4):
            g = gt[br % 2]; gk = "po_gt%d" % (br % 2)
            c.dma(g[:], projT[O_G + br * 1024:O_G + (br + 1) * 1024, t0:t0 + TT].rearrange("(k p) t -> p k t", p=128),
                  writes=[gk])
            for fo in range(8):
                p = ps[itc[0] % 4]; pk = "po_ps%d" % (itc[0] % 4); itc[0] += 1
                for kc in range(4):
                    c.op("pe", lambda e: e.matmul(p[:, 0:TT], wbr[:, br * 4 + kc, fo * 128:(fo + 1) * 128],
                                                  y[:, br * 4 + kc, :], start=(kc == 0), stop=(kc == 3)),
                         reads=["po_wbr", yk_], writes=[pk])
                mk = "po_m%d" % fo
                if br == 0:
                    c.op("dve", lambda e: e.tensor_tensor(out=m[:, fo, :], in0=p[:, 0:TT], in1=g[:, fo, :], op=ALU.mult),
                         reads=[pk, gk], writes=[mk])
                else:
                    t_ = tmp[itc[0] % 2]; tk = "po_tmp%d" % (itc[0] % 2)
                    c.op("dve", lambda e: e.tensor_tensor(out=t_[:], in0=p[:, 0:TT], in1=g[:, fo, :], op=ALU.mult),
                         reads=[pk, gk], writes=[tk])
                    if br < 3:
                        c.op("pool", lambda e: e.tensor_tensor(out=m[:, fo, :], in0=m[:, fo, :], in1=t_[:], op=ALU.add),
                             reads=[mk, tk], writes=[mk])
                    else:
                        c.op("pool", lambda e: e.tensor_tensor(out=mb[:, fo, :], in0=m[:, fo, :], in1=t_[:], op=ALU.add),
                             reads=[mk, tk], writes=["po_mb%d" % fo])

    def stB(tt):
        t0 = tt * TT
        y = ys[tt % 2]; yk_ = "po_y%d" % (tt % 2)
        x = xs_[tt % 2]; xk_ = "po_x%d" % (tt % 2)
        pf = pfs[tt % 2]; pfk = "po_pf%d" % (tt % 2)
        pb = pbs[tt % 2]; pbk_ = "po_pb%d" % (tt % 2)
        pst_s = pstat[0]; pst_q = pstat[1]
        r = rs_[tt % 2]; rb = rbs_[tt % 2]; rp_ = "po_r%d_" % (tt % 2); rbp_ = "po_rb%d_" % (tt % 2)
        for fo in range(8):
            p = ps[itc[0] % 4]; pk = "po_ps%d" % (itc[0] % 4); itc[0] += 1
            for k in range(8):
                c.op("pe", lambda e: e.matmul(p[:, 0:TT], wo[:, k, fo * 128:(fo + 1) * 128], mb[:, k, :],
                                              start=(k == 0), stop=(k == 7)),
                     reads=["po_wo"] + ["po_mb%d" % k], writes=[pk])
            c.op("dve", lambda e: e.scalar_tensor_tensor(out=r[:, fo, :], in0=x[:, fo, :], scalar=ALPHA, in1=p[:, 0:TT],
                                                         op0=ALU.mult, op1=ALU.add),
                 reads=[pk, xk_], writes=[rp_ + str(fo)])
            c.op("act", lambda e: e.activation(out=rb[:, fo, :], in_=r[:, fo, :], func=AF.Identity),
                 reads=[rp_ + str(fo)], writes=[rbp_ + str(fo)])

    def stC(tt):
        t0 = tt * TT
        y = ys[tt % 2]; yk_ = "po_y%d" % (tt % 2)
        x = xs_[tt % 2]; xk_ = "po_x%d" % (tt % 2)
        pf = pfs[tt % 2]; pfk = "po_pf%d" % (tt % 2)
        pb = pbs[tt % 2]; pbk_ = "po_pb%d" % (tt % 2)
        pst_s = pstat[0]; pst_q = pstat[1]
        r = rs_[tt % 2]; rb = rbs_[tt % 2]; rp_ = "po_r%d_" % (tt % 2); rbp_ = "po_rb%d_" % (tt % 2)
        for fo in range(8):
            p = ps[itc[0] % 4]; pk = "po_ps%d" % (itc[0] % 4); itc[0] += 1
            for k in range(8):
                c.op("pe", lambda e: e.matmul(p[:, 0:TT], wpg[:, k, fo * 128:(fo + 1) * 128], rb[:, k, :],
                                              start=(k == 0), stop=(k == 7)),
                     reads=["po_wpg", rbp_ + str(k)], writes=[pk])
            g_ = gp[fo % 4]; gk = "po_gp%d" % (fo % 4)
            c.op("act", lambda e: e.activation(out=g_[:], in_=p[:, 0:TT], func=AF.Sigmoid, bias=vb[:, 0, fo:fo + 1]),
                 reads=[pk, "po_vec"], writes=[gk])
            p2 = ps[itc[0] % 4]; pk2 = "po_ps%d" % (itc[0] % 4); itc[0] += 1
            for k in range(2):
                c.op("pe", lambda e: e.matmul(p2[:, 0:TT], wpl[:, k, fo * 128:(fo + 1) * 128], pb[:, k, :],
                                              start=(k == 0), stop=(k == 1)),
                     reads=["po_wpl", pbk_], writes=[pk2])
            t_ = tmp[fo % 2]; tk = "po_tmp%d" % (fo % 2)
            c.op("dve", lambda e: e.tensor_tensor(out=t_[:], in0=p2[:, 0:TT], in1=g_[:], op=ALU.mult),
                 reads=[pk2, gk], writes=[tk])
            c.op("pool", lambda e: e.tensor_tensor(out=r[:, fo, :], in0=r[:, fo, :], in1=t_[:], op=ALU.add),
                 reads=[rp_ + str(fo), tk], writes=[rp_ + str(fo)])
        for fo in range(8):
            s_ = sq[fo]; sk = "po_sq%d" % fo
            c.op("act", lambda e: e.activation(out=s_[:], in_=r[:, fo, :], func=AF.Square),
                 reads=[rp_ + str(fo)], writes=[sk])
        for fo in range(8):
            s_ = sq[fo]; sk = "po_sq%d" % fo
            c.op("pe", lambda e: e.matmul(pst_s[:, 0:TT], ones[:], r[:, fo, :], start=(fo == 0), stop=(fo == 7)),
                 reads=["po_ones", rp_ + str(fo)], writes=["po_pst0"])
            c.op("pe", lambda e: e.matmul(pst_q[:, 0:TT], ones[:], s_[:], start=(fo == 0), stop=(fo == 7)),
                 reads=["po_ones", sk], writes=["po_pst1"])

    def stD(tt):
        t0 = tt * TT
        y = ys[tt % 2]; yk_ = "po_y%d" % (tt % 2)
        x = xs_[tt % 2]; xk_ = "po_x%d" % (tt % 2)
        pf = pfs[tt % 2]; pfk = "po_pf%d" % (tt % 2)
        pb = pbs[tt % 2]; pbk_ = "po_pb%d" % (tt % 2)
        pst_s = pstat[0]; pst_q = pstat[1]
        r = rs_[tt % 2]; rb = rbs_[tt % 2]; rp_ = "po_r%d_" % (tt % 2); rbp_ = "po_rb%d_" % (tt % 2)
        c.op("dve", lambda e: e.tensor_scalar(out=mean[:], in0=pst_s[:, 0:TT], scalar1=1.0 / D, scalar2=None, op0=ALU.mult),
             reads=["po_pst0"], writes=["po_mean"])
        c.op("dve", lambda e: e.tensor_tensor(out=msq[:], in0=mean[:], in1=mean[:], op=ALU.mult),
             reads=["po_mean"], writes=["po_msq"])
        c.op("dve", lambda e: e.scalar_tensor_tensor(out=msq[:], in0=pst_q[:, 0:TT], scalar=1.0 / D, in1=msq[:],
                                                     op0=ALU.mult, op1=ALU.subtract),
             reads=["po_pst1", "po_msq"], writes=["po_msq"])
        c.op("dve", lambda e: e.tensor_scalar(out=msq[:], in0=msq[:], scalar1=EPS, scalar2=None, op0=ALU.add),
             reads=["po_msq"], writes=["po_msq"])
        c.op("act", lambda e: e.activation(out=rstd[:], in_=msq[:], func=AF.Sqrt),
             reads=["po_msq"], writes=["po_rstd"])
        c.op("dve", lambda e: e.reciprocal(out=rstd[:], in_=rstd[:]), reads=["po_rstd"], writes=["po_rstd"])
        for fo in range(8):
            t_ = tmp[fo % 2]; tk = "po_tmp%d" % (fo % 2)
            o_ = ot[fo % 4]; ok = "po_ot%d" % (fo % 4)
            c.op("dve", lambda e: e.tensor_tensor(out=t_[:], in0=r[:, fo, :], in1=mean[:], op=ALU.subtract),
                 reads=[rp_ + str(fo), "po_mean"], writes=[tk])
            c.op("pool", lambda e: e.tensor_tensor(out=t_[:], in0=t_[:], in1=rstd[:], op=ALU.mult),
                 reads=[tk, "po_rstd"], writes=[tk])
            c.op("act", lambda e: e.activation(out=o_[:], in_=t_[:], func=AF.Identity, scale=vb[:, 1, fo:fo + 1],
                                               bias=vb[:, 2, fo:fo + 1]),
                 reads=[tk, "po_vec"], writes=[ok])
            c.dma(ov[:, fo, t0:t0 + TT], o_[:], reads=[ok])

    NTT = T // TT
    stL(0)
    stA(0)
    stB(0)
    if NTT > 1:
        stL(1)
        stA(1)
    for tt in range(NTT):
        stC(tt)
        if tt + 1 < NTT:
            stB(tt + 1)
        if tt + 2 < NTT:
            stL(tt + 2)
        stD(tt)
        if tt + 2 < NTT:
            stA(tt + 2)
    c.end_phase()


def phase_conv(c, T, projT, cw_d, cvec_d, ident_d, ones_f, yT):
    nc = c.nc
    c.begin_phase()
    TT = 512 if T >= 512 else T
    cw = c.sb("pc_cw", [128, 4, 31], F32)
    cvec = c.sb("pc_cvec", [128, 3, 4], F32)
    ident = c.sb("pc_ident", [128, 128], F32)
    ones = c.sb("pc_ones", [128, 128], F32)
    dg = c.sb("pc_dg", [128, 124, 128], BF16)
    hb = c.sb("pc_hb", [128, 4, 32 + T], BF16)
    c.dma(cw[:], cw_d, writes=["pc_cw"])
    c.dma(cvec[:], cvec_d, writes=["pc_cvec"])
    c.dma(ident[:], ident_d, writes=["pc_ident"])
    c.dma(ones[:], ones_f, writes=["pc_ones"])
    for ch in range(4):
        for k in range(31):
            c.op("dve", lambda e: e.tensor_scalar(out=dg[:, ch * 31 + k, :], in0=ident[:], scalar1=cw[:, ch, k:k + 1],
                                                  scalar2=None, op0=ALU.mult),
                 reads=["pc_cw", "pc_ident"], writes=["pc_dg"])
    ld = [c.sb("pc_ld%d" % i, [128, 2048], BF16) for i in range(4)]
    PT = min(2048, T)
    i = 0
    for ch in range(4):
        c.op("pool", lambda e: e.memset(hb[:, ch, 0:32], 0.0), writes=["pc_hb%d" % ch])
        for t0 in range(0, T, PT):
            a = ld[i % 4]; ak = "pc_ld%d" % (i % 4); i += 1
            b = ld[i % 4]; bk = "pc_ld%d" % (i % 4); i += 1
            c.dma(a[:, 0:PT], projT[O_GL + ch * 128:O_GL + (ch + 1) * 128, t0:t0 + PT], writes=[ak])
            c.dma(b[:, 0:PT], projT[O_GG + ch * 128:O_GG + (ch + 1) * 128, t0:t0 + PT], writes=[bk])
            c.op("pool", lambda e: e.tensor_tensor(out=hb[:, ch, 32 + t0:32 + t0 + PT], in0=a[:, 0:PT], in1=b[:, 0:PT],
                                                   op=ALU.mult),
                 reads=[ak, bk], writes=["pc_hb%d" % ch])
    cv = c.sb("pc_cv", [128, 4, TT], F32)
    sq = [c.sb("pc_sq%d" % i, [128, TT], F32) for i in range(2)]
    zb = [c.sb("pc_zb%d" % i, [128, TT], BF16) for i in range(2)]
    mean = c.sb("pc_mean", [128, TT], F32)
    msq = c.sb("pc_msq", [128, TT], F32)
    rstd = c.sb("pc_rstd", [128, TT], F32)
    tmp = [c.sb("pc_tmp%d" % i, [128, TT], F32) for i in range(2)]
    yo = [c.sb("pc_yo%d" % i, [128, TT], BF16) for i in range(2)]
    ps = [c.ps("pc_ps%d" % i, [128, 512]) for i in range(3)]
    pst = [c.ps("pc_pst%d" % i, [128, 512]) for i in range(2)]
    it = 0
    for tt in range(T // TT):
        t0 = tt * TT
        for ch in range(4):
            p = ps[it % 3]; pk = "pc_ps%d" % (it % 3); it += 1
            for k in range(31):
                c.op("pe", lambda e: e.matmul(p[:, 0:TT], dg[:, ch * 31 + k, :], hb[:, ch, 2 + t0 + k:2 + t0 + k + TT],
                                              start=(k == 0), stop=(k == 30)),
                     reads=["pc_dg", "pc_hb%d" % ch], writes=[pk])
            c.op("act", lambda e: e.activation(out=cv[:, ch, :], in_=p[:, 0:TT], func=AF.Identity, bias=cvec[:, 0, ch:ch + 1]),
                 reads=[pk, "pc_cvec"], writes=["pc_cv%d" % ch])
            s_ = sq[ch % 2]; sk = "pc_sq%d" % (ch % 2)
            c.op("act", lambda e: e.activation(out=s_[:], in_=cv[:, ch, :], func=AF.Square),
                 reads=["pc_cv%d" % ch], writes=[sk])
            c.op("pe", lambda e: e.matmul(pst[0][:, 0:TT], ones[:], cv[:, ch, :], start=(ch == 0), stop=(ch == 3)),
                 reads=["pc_ones", "pc_cv%d" % ch], writes=["pc_pst0"])
            c.op("pe", lambda e: e.matmul(pst[1][:, 0:TT], ones[:], s_[:], start=(ch == 0), stop=(ch == 3)),
                 reads=["pc_ones", sk], writes=["pc_pst1"])
        c.op("dve", lambda e: e.tensor_scalar(out=mean[:], in0=pst[0][:, 0:TT], scalar1=1.0 / 512, scalar2=None, op0=ALU.mult),
             reads=["pc_pst0"], writes=["pc_mean"])
        c.op("dve", lambda e: e.tensor_tensor(out=msq[:], in0=mean[:], in1=mean[:], op=ALU.mult),
             reads=["pc_mean"], writes=["pc_msq"])
        c.op("dve", lambda e: e.scalar_tensor_tensor(out=msq[:], in0=pst[1][:, 0:TT], scalar=1.0 / 512, in1=msq[:],
                                                     op0=ALU.mult, op1=ALU.subtract),
             reads=["pc_pst1", "pc_msq"], writes=["pc_msq"])
        c.op("dve", lambda e: e.tensor_scalar(out=msq[:], in0=msq[:], scalar1=EPS, scalar2=None, op0=ALU.add),
             reads=["pc_msq"], writes=["pc_msq"])
        c.op("act", lambda e: e.activation(out=rstd[:], in_=msq[:], func=AF.Sqrt), reads=["pc_msq"], writes=["pc_rstd"])
        c.op("dve", lambda e: e.reciprocal(out=rstd[:], in_=rstd[:]), reads=["pc_rstd"], writes=["pc_rstd"])
        for ch in range(4):
            z = zb[ch % 2]; zk = "pc_zb%d" % (ch % 2)
            c.dma(z[:], projT[O_ZB + ch * 128:O_ZB + (ch + 1) * 128, t0:t0 + TT], writes=[zk])
            t_ = tmp[ch % 2]; tk = "pc_tmp%d" % (ch % 2)
            c.op("dve", lambda e: e.tensor_tensor(out=t_[:], in0=cv[:, ch, :], in1=mean[:], op=ALU.subtract),
                 reads=["pc_cv%d" % ch, "pc_mean"], writes=[tk])
            c.op("pool", lambda e: e.tensor_tensor(out=t_[:], in0=t_[:], in1=rstd[:], op=ALU.mult),
                 reads=[tk, "pc_rstd"], writes=[tk])
            c.op("act", lambda e: e.activation(out=t_[:], in_=t_[:], func=AF.Silu, scale=cvec[:, 1, ch:ch + 1],
                                               bias=cvec[:, 2, ch:ch + 1]),
                 reads=[tk, "pc_cvec"], writes=[tk])
            o_ = yo[ch % 2]; ok = "pc_yo%d" % (ch % 2)
            c.op("dve", lambda e: e.tensor_tensor(out=o_[:], in0=t_[:], in1=z[:], op=ALU.mult),
                 reads=[tk, zk], writes=[ok])
            c.dma(yT[Y_B + ch * 128:Y_B + (ch + 1) * 128, t0:t0 + TT], o_[:], reads=[ok])
    c.end_phase()


def phase_fox(c, T, projT, smallT, fb_d, negmask_d, identb_d, sel_d, fxq, fxk, yT):
    nc = c.nc
    c.begin_phase()
    QB = min(512, T)
    NT = T // 128
    fb = c.sb("pf_fb", [NHA, 1], F32)
    nfb = c.sb("pf_nfb", [NHA, 1], F32)
    c.dma(fb[:], fb_d, writes=["pf_fb"])
    c.op("dve", lambda e: e.tensor_scalar(out=nfb[:], in0=fb[:], scalar1=-1.0, scalar2=None, op0=ALU.mult),
         reads=["pf_fb"], writes=["pf_nfb"])
    PT = min(1024, T)
    onesr = c.sb("pf_onesr", [NHA, PT], F32)
    c.op("pool", lambda e: e.memset(onesr[:], 1.0), writes=["pf_onesr"])
    fin = c.sb("pf_fin", [NHA, PT], F32)
    l_ = c.sb("pf_l", [NHA, PT], F32)
    fn = [c.sb("pf_fn%d" % i, [NHA, PT], F32) for i in range(2)]
    hi = c.sb("pf_hi", [NHA, PT], BF16)
    r1 = c.sb("pf_r1", [NHA, PT], F32)
    mid = c.sb("pf_mid", [NHA, PT], BF16)
    lo = c.sb("pf_lo", [NHA, PT], BF16)
    nhi = c.sb("pf_nhi", [NHA, PT], BF16)
    for pi, t0 in enumerate(range(0, T, PT)):
        f_ = fn[pi % 2]; fk = "pf_fn%d" % (pi % 2)
        c.dma(fin[:], smallT[4:8, t0:t0 + PT], writes=["pf_fin"])
        c.op("act", lambda e: e.activation(out=l_[:], in_=fin[:], func=AF.Exp, scale=-1.0, bias=nfb[:]),
             reads=["pf_fin", "pf_nfb"], writes=["pf_l"])
        c.op("act", lambda e: e.activation(out=l_[:], in_=l_[:], func=AF.Ln, bias=1.0), reads=["pf_l"], writes=["pf_l"])
        if pi == 0:
            c.op("dve", lambda e: e.tensor_tensor_scan(out=f_[:], data0=onesr[:], data1=l_[:], initial=0.0,
                                                       op0=ALU.mult, op1=ALU.add),
                 reads=["pf_onesr", "pf_l"], writes=[fk])
        else:
            pf_ = fn[(pi - 1) % 2]
            c.op("dve", lambda e: e.tensor_tensor_scan(out=f_[:], data0=onesr[:], data1=l_[:], initial=pf_[:, PT - 1:PT],
                                                       op0=ALU.mult, op1=ALU.add),
                 reads=["pf_onesr", "pf_l", "pf_fn%d" % ((pi - 1) % 2)], writes=[fk])
        c.op("dve", lambda e: e.tensor_copy(out=hi[:], in_=f_[:]), reads=[fk], writes=["pf_hi"])
        c.op("dve", lambda e: e.tensor_tensor(out=r1[:], in0=f_[:], in1=hi[:], op=ALU.subtract),
             reads=[fk, "pf_hi"], writes=["pf_r1"])
        c.op("dve", lambda e: e.tensor_copy(out=mid[:], in_=r1[:]), reads=["pf_r1"], writes=["pf_mid"])
        c.op("dve", lambda e: e.tensor_tensor(out=r1[:], in0=r1[:], in1=mid[:], op=ALU.subtract),
             reads=["pf_r1", "pf_mid"], writes=["pf_r1"])
        c.op("dve", lambda e: e.tensor_copy(out=lo[:], in_=r1[:]), reads=["pf_r1"], writes=["pf_lo"])
        c.op("dve", lambda e: e.tensor_scalar(out=nhi[:], in0=hi[:], scalar1=-1.0, scalar2=None, op0=ALU.mult),
             reads=["pf_hi"], writes=["pf_nhi"])
        c.dma(fxq[:, t0:t0 + PT], nhi[:], reads=["pf_nhi"], writes=["fxq"])
        c.dma(fxk[:, 0, t0:t0 + PT], hi[:], reads=["pf_hi"], writes=["fxk"])
        c.dma(fxk[:, 1, t0:t0 + PT], mid[:], reads=["pf_mid"], writes=["fxk"])
        c.dma(fxk[:, 2, t0:t0 + PT], lo[:], reads=["pf_lo"], writes=["fxk"])
    negm = c.sb("pf_negm", [128, 4, 512], BF16)
    identb = c.sb("pf_identb", [128, 128], BF16)
    ones64 = c.sb("pf_ones64", [128, 64], BF16)
    c.dma(negm[:], negmask_d, writes=["pf_negm"])
    c.dma(identb[:], identb_d, writes=["pf_identb"])
    c.op("pool", lambda e: e.memset(ones64[:], 1.0), writes=["pf_ones64"])
    qp = [c.sb("pf_qp%d" % i, [128, T], BF16) for i in range(2)]
    kp = [c.sb("pf_kp%d" % i, [128, T], BF16) for i in range(2)]
    vT = [c.sb("pf_vT%d" % i, [64, T], BF16) for i in range(2)]
    vt = [c.sb("pf_vt%d" % i, [128, NT, 128], BF16) for i in range(2)]
    att = [c.sb("pf_att%d" % i, [128, 512], BF16) for i in range(3)]
    rec = c.sb("pf_rec", [64, 512], F32)
    rech = c.sb("pf_rech", [128, 512], F32)
    sel = c.sb("pf_sel", [128, 128], F32)
    c.dma(sel[:], sel_d, writes=["pf_sel"])
    c.op("pool", lambda e: e.memset(rech[0:64, :], 0.0), writes=["pf_rech"])
    for i in range(2):
        c.op("pool", lambda e: e.memset(qp[i][64:128, :], 0.0), writes=["pf_qp%d" % i])
        c.op("pool", lambda e: e.memset(kp[i][64:128, :], 0.0), writes=["pf_kp%d" % i])
        c.op("pool", lambda e: e.memset(vt[i][:, :, 64:128], 1.0), writes=["pf_vt%d" % i])
    o_ = c.sb("pf_o", [64, 512], F32)
    zt = [c.sb("pf_zt%d" % i, [64, 512], BF16) for i in range(2)]
    yo = [c.sb("pf_yo%d" % i, [64, 512], BF16) for i in range(2)]
    psz = [c.ps("pf_psz%d" % i, [128, 512]) for i in range(3)]
    psn = [c.ps("pf_psn%d" % i, [128, 512]) for i in range(2)]
    psd = [c.ps("pf_psd%d" % i, [128, 512]) for i in range(2)]
    pst = c.ps("pf_pst", [128, 8, 64], BF16)

    def setup(h):
        b = h % 2
        q_, k_, vT_, vt_ = qp[b], kp[b], vT[b], vt[b]
        qk, kk, vTk, vtk = "pf_qp%d" % b, "pf_kp%d" % b, "pf_vT%d" % b, "pf_vt%d" % b
        c.op("pool", lambda e: e.memset(q_[64:68, :], 1.0), writes=[qk])
        c.op("pool", lambda e: e.memset(k_[64:68, :], 1.0), writes=[kk])
        c.dma(q_[0:64, :], projT[O_QD + h * 64:O_QD + (h + 1) * 64, :], writes=[qk])
        c.dma(k_[0:64, :], projT[O_KD + h * 64:O_KD + (h + 1) * 64, :], writes=[kk])
        c.dma(q_[64:65, :], fxq[h:h + 1, :], reads=["fxq"], writes=[qk])
        c.dma(k_[65:68, :], fxk[h, :, :], reads=["fxk"], writes=[kk])
        c.dma(vT_[:], projT[O_VD + h * 64:O_VD + (h + 1) * 64, :], writes=[vTk])
        for g in range(0, NT, 8):
            n = min(8, NT - g)
            for j in range(n):
                c.op("pe", lambda e: e.transpose(pst[:, j, :], vT_[:, (g + j) * 128:(g + j + 1) * 128], identb[0:64, 0:64]),
                     reads=[vTk, "pf_identb"], writes=["pf_pst"])
            c.op("dve", lambda e: e.tensor_copy(out=vt_[:, g:g + n, 0:64], in_=pst[:, 0:n, :]), reads=["pf_pst"], writes=[vtk])

    def stage1(d):
        h, qb, kt, nk, i = d
        b = h % 2
        t0 = qb * QB; s0 = kt * 128
        p = psz[i % 3]; pk = "pf_psz%d" % (i % 3)
        a = att[i % 3]; ak = "pf_att%d" % (i % 3)
        diag = s0 >= t0
        c.op("pe", lambda e: e.matmul(p[:, 0:QB], kp[b][:, s0:s0 + 128], qp[b][:, t0:t0 + QB], start=True, stop=not diag),
             reads=["pf_qp%d" % b, "pf_kp%d" % b], writes=[pk])
        if diag:
            j = (s0 - t0) // 128
            c.op("pe", lambda e: e.matmul(p[:, 0:QB], identb[:], negm[:, j, 0:QB], start=False, stop=True),
                 reads=["pf_identb", "pf_negm"], writes=[pk])
        c.op("act", lambda e: e.activation(out=a[:, 0:QB], in_=p[:, 0:QB], func=AF.Exp), reads=[pk], writes=[ak])

    def stage2(d):
        h, qb, kt, nk, i = d
        b = h % 2
        t0 = qb * QB
        a = att[i % 3]; ak = "pf_att%d" % (i % 3)
        pn, pnk = psn[qb % 2], "pf_psn%d" % (qb % 2)
        pd, pdk = psd[qb % 2], "pf_psd%d" % (qb % 2)
        c.op("pe", lambda e: e.matmul(pn[:, 0:QB], vt[b][:, kt, :], a[:, 0:QB], start=(kt == 0), stop=(kt == nk - 1)),
             reads=["pf_vt%d" % b, ak], writes=[pnk])
        if kt == nk - 1:
            z_ = zt[qb % 2]; zk = "pf_zt%d" % (qb % 2)
            y_ = yo[qb % 2]; yk = "pf_yo%d" % (qb % 2)
            c.dma(z_[:, 0:QB], projT[O_ZD + h * 64:O_ZD + (h + 1) * 64, t0:t0 + QB], writes=[zk])
            c.op("dve", lambda e: e.reciprocal(out=rech[64:128, 0:QB], in_=pn[64:128, 0:QB]), reads=[pnk], writes=["pf_rech"])
            c.op("pe", lambda e: e.matmul(pd[:, 0:QB], sel[:], rech[:, 0:QB], start=True, stop=True),
                 reads=["pf_sel", "pf_rech"], writes=[pdk])
            c.op("dve", lambda e: e.tensor_copy(out=rec[:, 0:QB], in_=pd[0:64, 0:QB]), reads=[pdk], writes=["pf_rec"])
            c.op("dve", lambda e: e.tensor_tensor(out=o_[:, 0:QB], in0=pn[0:64, 0:QB], in1=rec[:, 0:QB], op=ALU.mult),
                 reads=[pnk, "pf_rec"], writes=["pf_o"])
            c.op("pool", lambda e: e.tensor_tensor(out=y_[:, 0:QB], in0=o_[:, 0:QB], in1=z_[:, 0:QB], op=ALU.mult),
                 reads=["pf_o", zk], writes=[yk])
            c.dma(yT[Y_D + h * 64:Y_D + (h + 1) * 64, t0:t0 + QB], y_[:, 0:QB], reads=[yk])

    blocks = []
    i = 0
    for h in range(NHA):
        for qb in range(T // QB):
            nk = (qb * QB + QB) // 128
            for kt in range(nk):
                blocks.append((h, qb, kt, nk, i)); i += 1
    setup(0)
    n = len(blocks)
    LAG = 2
    for s_ in range(n + LAG):
        if s_ < n:
            stage1(blocks[s_])
        if s_ >= LAG:
            d = blocks[s_ - LAG]
            stage2(d)
            if d[1] == 0 and d[2] == 0 and d[0] + 1 < NHA:
                setup(d[0] + 1)
    c.end_phase()


def phase_sb(c, T, projT, negmask_d, mask01_d, identb_d, negu_d, yT):
    nc = c.nc
    c.begin_phase()
    QB = min(512, T)
    NT = T // 128
    negm = c.sb("sb_negm", [128, 4, 512], BF16)
    m01 = c.sb("sb_m01", [128, 4, 512], BF16)
    identb = c.sb("sb_identb", [128, 128], BF16)
    negu = c.sb("sb_negu", [128, 128], BF16)
    negones = c.sb("sb_negones", [128, 128], BF16)
    c.dma(negm[:], negmask_d, writes=["sb_negm"])
    c.dma(m01[:], mask01_d, writes=["sb_m01"])
    c.dma(identb[:], identb_d, writes=["sb_identb"])
    c.dma(negu[:], negu_d, writes=["sb_negu"])
    c.op("pool", lambda e: e.memset(negones[:], -1.0), writes=["sb_negones"])
    qp = [c.sb("sb_qp%d" % i, [128, T], BF16) for i in range(2)]
    kp = [c.sb("sb_kp%d" % i, [128, T], BF16) for i in range(2)]
    vT = [c.sb("sb_vT%d" % i, [64, T], BF16) for i in range(2)]
    vt = [c.sb("sb_vt%d" % i, [128, NT, 128], BF16) for i in range(2)]
    for i in range(2):
        c.op("pool", lambda e: e.memset(qp[i][64:128, :], 0.0), writes=["sb_qp%d" % i])
        c.op("pool", lambda e: e.memset(kp[i][64:128, :], 0.0), writes=["sb_kp%d" % i])
        c.op("pool", lambda e: e.memset(vt[i][:, :, 64:128], 0.0), writes=["sb_vt%d" % i])
    ee = [c.sb("sb_e%d" % i, [128, 512], F32) for i in range(2)]
    sp = [c.sb("sb_sp%d" % i, [128, 512], BF16) for i in range(3)]
    att = [c.sb("sb_att%d" % i, [128, 512], BF16) for i in range(3)]
    sl = [c.sb("sb_sl%d" % i, [128, 512], BF16) for i in range(2)]
    zt = [c.sb("sb_zt%d" % i, [64, 512], BF16) for i in range(2)]
    yo = [c.sb("sb_yo%d" % i, [64, 512], BF16) for i in range(2)]
    psa = [c.ps("sb_psa%d" % i, [128, 512]) for i in range(2)]
    psb = [c.ps("sb_psb%d" % i, [128, 512]) for i in range(2)]
    pso = [c.ps("sb_pso%d" % i, [128, 512]) for i in range(2)]
    pst = c.ps("sb_pst", [128, 8, 64], BF16)

    def setup(h):
        b = h % 2
        q_, k_, vT_, vt_ = qp[b], kp[b], vT[b], vt[b]
        qk, kk, vTk, vtk = "sb_qp%d" % b, "sb_kp%d" % b, "sb_vT%d" % b, "sb_vt%d" % b
        c.dma(q_[0:64, :], projT[O_QC + h * 64:O_QC + (h + 1) * 64, :], writes=[qk])
        c.dma(k_[0:64, :], projT[O_KC + h * 64:O_KC + (h + 1) * 64, :], writes=[kk])
        c.dma(vT_[:], projT[O_VC + h * 64:O_VC + (h + 1) * 64, :], writes=[vTk])
        for g in range(0, NT, 8):
            n = min(8, NT - g)
            for j in range(n):
                c.op("pe", lambda e: e.transpose(pst[:, j, :], vT_[:, (g + j) * 128:(g + j + 1) * 128], identb[0:64, 0:64]),
                     reads=[vTk, "sb_identb"], writes=["sb_pst"])
            c.op("dve", lambda e: e.tensor_copy(out=vt_[:, g:g + n, 0:64], in_=pst[:, 0:n, :]), reads=["sb_pst"], writes=[vtk])

    def stA(d):
        h, qb, kt, nk, i, si = d
        b = h % 2
        t0 = qb * QB; s0 = kt * 128
        pa = psa[i % 2]; pak = "sb_psa%d" % (i % 2)
        e_ = ee[i % 2]; ek = "sb_e%d" % (i % 2)
        s_ = sp[i % 3]; sk = "sb_sp%d" % (i % 3)
        c.op("pe", lambda e: e.matmul(pa[:, 0:QB], kp[b][:, s0:s0 + 128], qp[b][:, t0:t0 + QB], start=True, stop=True),
             reads=["sb_qp%d" % b, "sb_kp%d" % b], writes=[pak])
        c.op("act", lambda e: e.activation(out=e_[:, 0:QB], in_=pa[:, 0:QB], func=AF.Exp), reads=[pak], writes=[ek])
        c.op("act", lambda e: e.activation(out=s_[:, 0:QB], in_=e_[:, 0:QB], func=AF.Ln, bias=1.0), reads=[ek], writes=[sk])
        if s0 >= t0:
            j = (s0 - t0) // 128
            c.op("pool", lambda e: e.tensor_tensor(out=s_[:, 0:QB], in0=s_[:, 0:QB], in1=m01[:, j, 0:QB], op=ALU.mult),
                 reads=[sk, "sb_m01"], writes=[sk])

    def stB(d):
        h, qb, kt, nk, i, si = d
        b = h % 2
        t0 = qb * QB; s0 = kt * 128
        first = kt == nk - 1
        diag = s0 >= t0
        pb = psb[i % 2]; pbk = "sb_psb%d" % (i % 2)
        s_ = sp[i % 3]; sk = "sb_sp%d" % (i % 3)
        a = att[i % 3]; ak = "sb_att%d" % (i % 3)
        c.op("pe", lambda e: e.matmul(pb[:, 0:QB], kp[b][:, s0:s0 + 128], qp[b][:, t0:t0 + QB], start=True, stop=False),
             reads=["sb_qp%d" % b, "sb_kp%d" % b], writes=[pbk])
        if diag:
            j = (s0 - t0) // 128
            c.op("pe", lambda e: e.matmul(pb[:, 0:QB], identb[:], negm[:, j, 0:QB], start=False, stop=False),
                 reads=["sb_identb", "sb_negm"], writes=[pbk])
        sl_ = sl[si % 2]; slk = "sb_sl%d" % (si % 2)
        if not first:
            c.op("pe", lambda e: e.matmul(pb[:, 0:QB], negones[:], sl_[:, 0:QB], start=False, stop=False),
                 reads=["sb_negones", slk], writes=[pbk])
        c.op("pe", lambda e: e.matmul(pb[:, 0:QB], negu[:], s_[:, 0:QB], start=False, stop=True),
             reads=["sb_negu", sk], writes=[pbk])
        c.op("act", lambda e: e.activation(out=a[:, 0:QB], in_=pb[:, 0:QB], func=AF.Exp), reads=[pbk], writes=[ak])
        if kt > 0:
            nsl = sl[(si + 1) % 2]; nslk = "sb_sl%d" % ((si + 1) % 2)
            if first:
                c.op("dve", lambda e: e.tensor_copy(out=nsl[:, 0:QB], in_=s_[:, 0:QB]), reads=[sk], writes=[nslk])
            else:
                c.op("dve", lambda e: e.tensor_tensor(out=nsl[:, 0:QB], in0=sl_[:, 0:QB], in1=s_[:, 0:QB], op=ALU.add),
                     reads=[slk, sk], writes=[nslk])

    def stC(d):
        h, qb, kt, nk, i, si = d
        b = h % 2
        t0 = qb * QB
        first = kt == nk - 1
        a = att[i % 3]; ak = "sb_att%d" % (i % 3)
        po, pok = pso[qb % 2], "sb_pso%d" % (qb % 2)
        c.op("pe", lambda e: e.matmul(po[:, 0:QB], vt[b][:, kt, :], a[:, 0:QB], start=first, stop=(kt == 0)),
             reads=["sb_vt%d" % b, ak], writes=[pok])
        if kt == 0:
            z_ = zt[qb % 2]; zk = "sb_zt%d" % (qb % 2)
            y_ = yo[qb % 2]; yk = "sb_yo%d" % (qb % 2)
            c.dma(z_[:, 0:QB], projT[O_ZC + h * 64:O_ZC + (h + 1) * 64, t0:t0 + QB], writes=[zk])
            c.op("dve", lambda e: e.tensor_tensor(out=y_[:, 0:QB], in0=po[0:64, 0:QB], in1=z_[:, 0:QB], op=ALU.mult),
                 reads=[pok, zk], writes=[yk])
            c.dma(yT[Y_C + h * 64:Y_C + (h + 1) * 64, t0:t0 + QB], y_[:, 0:QB], reads=[yk])

    blocks = []
    i = 0
    si = 0
    for h in range(NHA):
        for qb in range(T // QB):
            nk = (qb * QB + QB) // 128
            for kt in range(nk - 1, -1, -1):
                blocks.append((h, qb, kt, nk, i, si)); i += 1
                if kt > 0:
                    si += 1
    setup(0)
    n = len(blocks)
    LB, LC = 2, 3
    for s_i in range(n + LC):
        if s_i < n:
            stA(blocks[s_i])
        if LB <= s_i < n + LB:
            stB(blocks[s_i - LB])
        if s_i >= LC:
            d = blocks[s_i - LC]
            stC(d)
            if d[1] == 0 and d[2] == d[3] - 1 and d[0] + 1 < NHA:
                setup(d[0] + 1)
    c.end_phase()


def phase_gdn_pre(c, T, projT, cw4_d, ident_d, ones_f, gqkv):
    nc = c.nc
    c.begin_phase()
    TT = min(512, T)
    cw4 = c.sb("g0_cw4", [128, 3 * NHG, 4], F32)
    ident = c.sb("g0_ident", [128, 128], F32)
    ones = c.sb("g0_ones", [128, 128], F32)
    dg = c.sb("g0_dg", [128, 12 * NHG, 128], BF16)
    c.dma(cw4[:], cw4_d, writes=["g0_cw4"])
    c.dma(ident[:], ident_d, writes=["g0_ident"])
    c.dma(ones[:], ones_f, writes=["g0_ones"])
    for ch in range(3 * NHG):
        for k in range(4):
            c.op("dve", lambda e: e.tensor_scalar(out=dg[:, ch * 4 + k, :], in0=ident[:], scalar1=cw4[:, ch, k:k + 1],
                                                  scalar2=None, op0=ALU.mult),
                 reads=["g0_cw4", "g0_ident"], writes=["g0_dg"])
    hq = [c.sb("g0_hq%d" % i, [128, 4 + T], BF16) for i in range(2)]
    cs = [c.sb("g0_c%d" % i, [128, TT], F32) for i in range(2)]
    sq = [c.sb("g0_sq%d" % i, [128, TT], F32) for i in range(2)]
    rt = [c.sb("g0_rt%d" % i, [128, TT], F32) for i in range(2)]
    oo = [c.sb("g0_o%d" % i, [128, TT], F32) for i in range(2)]
    ps = [c.ps("g0_ps%d" % i, [128, 512]) for i in range(2)]
    pq = [c.ps("g0_pq%d" % i, [128, 512]) for i in range(2)]
    it = 0
    for ch in range(3 * NHG):
        h_ = hq[ch % 2]; hk = "g0_hq%d" % (ch % 2)
        c.op("pool", lambda e: e.memset(h_[:, 0:4], 0.0), writes=[hk])
        c.dma(h_[:, 4:4 + T], projT[O_QA + ch * 128:O_QA + (ch + 1) * 128, :], writes=[hk])
        for tt in range(T // TT):
            t0 = tt * TT
            i2 = it % 2; it += 1
            p = ps[i2]; pk = "g0_ps%d" % i2
            for k in range(4):
                c.op("pe", lambda e: e.matmul(p[:, 0:TT], dg[:, ch * 4 + k, :], h_[:, 1 + t0 + k:1 + t0 + k + TT],
                                              start=(k == 0), stop=(k == 3)),
                     reads=["g0_dg", hk], writes=[pk])
            c_ = cs[i2]; ck = "g0_c%d" % i2
            c.op("act", lambda e: e.activation(out=c_[:], in_=p[:, 0:TT], func=AF.Silu), reads=[pk], writes=[ck])
            if ch < 2 * NHG:
                s_ = sq[i2]; sk = "g0_sq%d" % i2
                c.op("dve", lambda e: e.tensor_tensor(out=s_[:], in0=c_[:], in1=c_[:], op=ALU.mult), reads=[ck], writes=[sk])
                q = pq[i2]; qk = "g0_pq%d" % i2
                c.op("pe", lambda e: e.matmul(q[:, 0:TT], ones[:], s_[:], start=True, stop=True),
                     reads=["g0_ones", sk], writes=[qk])
                r_ = rt[i2]; rk = "g0_rt%d" % i2
                c.op("dve", lambda e: e.tensor_scalar(out=r_[:], in0=q[:, 0:TT], scalar1=1e-6, scalar2=None, op0=ALU.add),
                     reads=[qk], writes=[rk])
                c.op("act", lambda e: e.activation(out=r_[:], in_=r_[:], func=AF.Sqrt), reads=[rk], writes=[rk])
                c.op("dve", lambda e: e.reciprocal(out=r_[:], in_=r_[:]), reads=[rk], writes=[rk])
                o_ = oo[i2]; ok = "g0_o%d" % i2
                sc = 128 ** -0.5 if ch < NHG else 1.0
                c.op("dve", lambda e: e.scalar_tensor_tensor(out=o_[:], in0=c_[:], scalar=sc, in1=r_[:], op0=ALU.mult,
                                                             op1=ALU.mult), reads=[ck, rk], writes=[ok])
                c.dma(gqkv[ch, :, t0:t0 + TT], o_[:], reads=[ok], writes=["gqkv"])
            else:
                c.dma(gqkv[ch, :, t0:t0 + TT], c_[:], reads=[ck], writes=["gqkv"])
    c.end_phase()


def phase_gdn(c, T, projT, smallT, gqkv, alog_d, dtb_d, gn_d, ident_d, ones_f, triu_d, sl_d, mks_d, mki_d, yT):
    nc = c.nc
    c.begin_phase()
    NC = T // 128
    GS = 4 if NC >= 4 else NC
    W = GS * 128
    ident = c.sb("g_ident", [128, 128], F32)
    ones = c.sb("g_ones", [128, 128], F32)
    triu = c.sb("g_triu", [128, 128], F32)
    slm = c.sb("g_sl", [128, 128], F32)
    mks = c.sb("g_mks", [128, GS, 128], F32)
    mki = c.sb("g_mki", [128, GS, 128], F32)
    identg = c.sb("g_identg", [128, GS, 128], F32)
    gnb = c.sb("g_gnb", [128, 128], F32)
    alog = c.sb("g_alog", [128, NHG], F32)
    dtb = c.sb("g_dtb", [128, NHG], F32)
    nea = c.sb("g_nea", [128, NHG], F32)
    c.dma(ident[:], ident_d, writes=["g_ident"])
    c.dma(ones[:], ones_f, writes=["g_ones"])
    c.dma(triu[:], triu_d, writes=["g_triu"])
    c.dma(slm[:], sl_d, writes=["g_sl"])
    for g in range(GS):
        c.dma(mks[:, g, :], mks_d, writes=["g_mks"])
        c.dma(mki[:, g, :], mki_d, writes=["g_mki"])
        c.dma(identg[:, g, :], ident_d, writes=["g_identg"])
    c.dma(gnb[:], gn_d, writes=["g_gnb"])
    c.dma(alog[:], alog_d, writes=["g_alog"])
    c.dma(dtb[:], dtb_d, writes=["g_dtb"])
    c.op("act", lambda e: e.activation(out=nea[:], in_=alog[:], func=AF.Exp), reads=["g_alog"], writes=["g_nea"])
    c.op("dve", lambda e: e.tensor_scalar(out=nea[:], in0=nea[:], scalar1=-1.0, scalar2=None, op0=ALU.mult),
         reads=["g_nea"], writes=["g_nea"])
    banks = [c.ps("g_pb%d" % i, [128, GS, 128]) for i in range(4)]
    pscan_b = [c.ps("g_pscan%d" % i, [128, 512]) for i in range(4)]

    class _PS:
        def __getitem__(self, idx):
            return pscan_b[idx[1]][:, 0:128]
    pscan = _PS()
    bi = [0]

    def bank():
        i = bi[0] % 4
        bi[0] += 1
        return banks[i], "g_pb%d" % i

    sm = c.sb("g_sm", [2 * NHG, T], F32)
    c.dma(sm[:], smallT[0:2 * NHG, :], writes=["g_sm"])
    abt = c.sb("g_abt", [128, NC, 2 * NHG], F32)
    for n0 in range(0, NC, 64):
        nn = min(64, NC - n0)
        pbs, pbsk = bank()
        psmall = pbs[:].rearrange("p g c -> p (g c)")
        for n in range(nn):
            c.op("pe", lambda e: e.transpose(psmall[:, n * 4:(n + 1) * 4], sm[:, (n0 + n) * 128:(n0 + n + 1) * 128],
                                             ident[0:4, 0:4]), reads=["g_sm", "g_ident"], writes=[pbsk])
        c.op("dve", lambda e: e.tensor_copy(out=abt[:, n0:n0 + nn, :],
                                            in_=psmall[:, 0:nn * 4].rearrange("p (n k) -> p n k", k=4)),
             reads=[pbsk], writes=["g_abt"])

    if DBG_STOP == 1:
        c.end_phase(); return

    def t2(name):
        return c.sb(name, [128, NC], F32)
    gg, beta, gc, gl, egc, egl, kdf, bgc, tmpn = [t2("g_" + n) for n in
                                                  ("gg", "beta", "gc", "gl", "egc", "egl", "kdf", "bgc", "tmpn")]

    def grp(name, n=2):
        return [c.sb("%s%d" % (name, i), [128, GS, 128], F32) for i in range(n)]
    kT, qT, vT = grp("g_kT"), grp("g_qT"), grp("g_vT")
    ktok, vtok = grp("g_ktok", 1)[0], grp("g_vtok", 1)[0]
    trig, E, decs, deci, L, Aq, AqT = [grp("g_" + n, 1)[0] for n in ("trig", "E", "decs", "deci", "L", "Aq", "AqT")]
    X, Y = grp("g_X"), grp("g_Y")
    R = grp("g_R", 1)[0]
    vb, kbg, kdec, u_, wT = [grp("g_" + n, 1)[0] for n in ("vb", "kbg", "kdec", "u", "wT")]
    o_, osq, on = [grp("g_" + n, 1)[0] for n in ("o", "osq", "on")]
    vnew = [c.sb("g_vnew%d" % i, [128, 128], F32) for i in range(2)]
    tq = [c.sb("g_tq%d" % i, [128, 128], F32) for i in range(2)]
    S = [c.sb("g_S%d" % i, [128, 128], F32) for i in range(2)]
    rs = c.sb("g_rs", [128, GS], F32)
    zt = [c.sb("g_zt%d" % i, [128, W], BF16) for i in range(2)]
    yo = [c.sb("g_yo%d" % i, [128, W], BF16) for i in range(2)]

    for h in range(NHG):
        c.op("act", lambda e: e.activation(out=tmpn[:], in_=abt[:, :, h], func=AF.Exp, bias=dtb[:, h:h + 1]),
             reads=["g_abt", "g_dtb"], writes=["g_tmpn"])
        c.op("act", lambda e: e.activation(out=tmpn[:], in_=tmpn[:], func=AF.Ln, bias=1.0), reads=["g_tmpn"], writes=["g_tmpn"])
        c.op("dve", lambda e: e.tensor_scalar(out=gg[:], in0=tmpn[:], scalar1=nea[:, h:h + 1], scalar2=None, op0=ALU.mult),
             reads=["g_tmpn", "g_nea"], writes=["g_gg"])
        c.op("act", lambda e: e.activation(out=beta[:], in_=abt[:, :, NHG + h], func=AF.Sigmoid), reads=["g_abt"], writes=["g_beta"])
        pbs, pbsk = bank()
        psmall = pbs[:].rearrange("p g c -> p (g c)")
        c.op("pe", lambda e: e.matmul(psmall[:, 0:NC], triu[:], gg[:], start=True, stop=True),
             reads=["g_triu", "g_gg"], writes=[pbsk])
        c.op("dve", lambda e: e.tensor_copy(out=gc[:], in_=psmall[:, 0:NC]), reads=[pbsk], writes=["g_gc"])
        pbs, pbsk = bank()
        psmall = pbs[:].rearrange("p g c -> p (g c)")
        c.op("pe", lambda e: e.matmul(psmall[:, 0:NC], ones[:], gg[:], start=True, stop=True),
             reads=["g_ones", "g_gg"], writes=[pbsk])
        c.op("dve", lambda e: e.tensor_copy(out=gl[:], in_=psmall[:, 0:NC]), reads=[pbsk], writes=["g_gl"])
        c.op("act", lambda e: e.activation(out=egc[:], in_=gc[:], func=AF.Exp), reads=["g_gc"], writes=["g_egc"])
        c.op("act", lambda e: e.activation(out=egl[:], in_=gl[:], func=AF.Exp), reads=["g_gl"], writes=["g_egl"])
        c.op("dve", lambda e: e.tensor_tensor(out=kdf[:], in0=gl[:], in1=gc[:], op=ALU.subtract),
             reads=["g_gl", "g_gc"], writes=["g_kdf"])
        c.op("act", lambda e: e.activation(out=kdf[:], in_=kdf[:], func=AF.Exp), reads=["g_kdf"], writes=["g_kdf"])
        c.op("dve", lambda e: e.tensor_tensor(out=bgc[:], in0=beta[:], in1=egc[:], op=ALU.mult),
             reads=["g_beta", "g_egc"], writes=["g_bgc"])
        c.op("pool", lambda e: e.memset(S[0][:], 0.0), writes=["g_S0"])
        sidx = 0
        if DBG_STOP == 2:
            c.end_phase(); return
        NG = NC // GS

        def gload(h_, gi_):
            t0_ = gi_ * W
            b2_ = gi_ % 2
            c.dma(qT[b2_][:], gqkv[h_, :, t0_:t0_ + W].rearrange("p (g c) -> p g c", c=128), reads=["gqkv"],
                  writes=["g_qT%d" % b2_])
            c.dma(kT[b2_][:], gqkv[NHG + h_, :, t0_:t0_ + W].rearrange("p (g c) -> p g c", c=128), reads=["gqkv"],
                  writes=["g_kT%d" % b2_])
            c.dma(vT[b2_][:], gqkv[2 * NHG + h_, :, t0_:t0_ + W].rearrange("p (g c) -> p g c", c=128), reads=["gqkv"],
                  writes=["g_vT%d" % b2_])
            c.dma(zt[b2_][:], projT[O_ZA + h_ * 128:O_ZA + (h_ + 1) * 128, t0_:t0_ + W], writes=["g_zt%d" % b2_])
        if h == 0 or NG % 2 == 1:
            gload(h, 0)
        for gi in range(NG):
            t0 = gi * W
            b2 = gi % 2
            kT_, qT_, vT_ = kT[b2], qT[b2], vT[b2]
            kTk, qTk, vTk = "g_kT%d" % b2, "g_qT%d" % b2, "g_vT%d" % b2
            z_ = zt[b2]; zk = "g_zt%d" % b2
            if NG % 2 == 0 or NG == 1:
                if gi + 1 < NG:
                    gload(h, gi + 1)
                elif h + 1 < NHG and NG % 2 == 0:
                    gload(h + 1, 0)
            pb, pbk = bank()
            for g in range(GS):
                c.op("pe", lambda e: e.transpose(pb[:, g, :], kT_[:, g, :], ident[:]), reads=[kTk, "g_ident"], writes=[pbk])
            c.op("act", lambda e: e.activation(out=ktok[:], in_=pb[:], func=AF.Identity), reads=[pbk], writes=["g_ktok"])
            pb, pbk = bank()
            for g in range(GS):
                c.op("pe", lambda e: e.transpose(pb[:, g, :], vT_[:, g, :], ident[:]), reads=[vTk, "g_ident"], writes=[pbk])
            c.op("dve", lambda e: e.tensor_copy(out=vtok[:], in_=pb[:]), reads=[pbk], writes=["g_vtok"])
            for g in range(GS):
                n = gi * GS + g
                c.op("dve", lambda e: e.tensor_scalar(out=trig[:, g, :], in0=triu[:], scalar1=gg[:, n:n + 1], scalar2=None,
                                                      op0=ALU.mult), reads=["g_triu", "g_gg"], writes=["g_trig"])
            pb, pbk = bank()
            for g in range(GS):
                c.op("pe", lambda e: e.matmul(pb[:, g, :], trig[:, g, :], slm[:], start=True, stop=True),
                     reads=["g_trig", "g_sl"], writes=[pbk])
            c.op("act", lambda e: e.activation(out=E[:], in_=pb[:], func=AF.Exp), reads=[pbk], writes=["g_E"])
            c.op("pool", lambda e: e.tensor_tensor(out=decs[:], in0=E[:], in1=mks[:], op=ALU.mult),
                 reads=["g_E", "g_mks"], writes=["g_decs"])
            c.op("pool", lambda e: e.tensor_tensor(out=deci[:], in0=E[:], in1=mki[:], op=ALU.mult),
                 reads=["g_E", "g_mki"], writes=["g_deci"])
            pb, pbk = bank()
            for g in range(GS):
                c.op("pe", lambda e: e.matmul(pb[:, g, :], kT_[:, g, :], kT_[:, g, :], start=True, stop=True),
                     reads=[kTk], writes=[pbk])
            for g in range(GS):
                n = gi * GS + g
                c.op("dve", lambda e: e.scalar_tensor_tensor(out=L[:, g, :], in0=pb[:, g, :], scalar=beta[:, n:n + 1],
                                                             in1=decs[:, g, :], op0=ALU.mult, op1=ALU.mult),
                     reads=[pbk, "g_beta", "g_decs"], writes=["g_L"])
            pb, pbk = bank()
            for g in range(GS):
                c.op("pe", lambda e: e.matmul(pb[:, g, :], qT_[:, g, :], kT_[:, g, :], start=True, stop=True),
                     reads=[qTk, kTk], writes=[pbk])
            c.op("dve", lambda e: e.tensor_tensor(out=Aq[:], in0=pb[:], in1=deci[:], op=ALU.mult),
                 reads=[pbk, "g_deci"], writes=["g_Aq"])
            if DBG_STOP == 3:
                c.end_phase(); return
            pb, pbk = bank()
            for g in range(GS):
                c.op("pe", lambda e: e.transpose(pb[:, g, :], Aq[:, g, :], ident[:]), reads=["g_Aq", "g_ident"], writes=[pbk])
            if DBG_STOP == 29:
                c.end_phase(); return
            c.op("act", lambda e: e.activation(out=AqT[:], in_=pb[:], func=AF.Identity), reads=[pbk], writes=["g_AqT"])
            if DBG_STOP == 30:
                c.end_phase(); return
            pb, pbk = bank()
            for g in range(GS):
                c.op("pe", lambda e: e.transpose(pb[:, g, :], L[:, g, :], ident[:]), reads=["g_L", "g_ident"], writes=[pbk])
            if DBG_STOP == 305:
                c.end_phase(); return
            c.op("act", lambda e: e.activation(out=X[0][:], in_=pb[:], func=AF.Identity), reads=[pbk], writes=["g_X0"])
            if DBG_STOP == 306:
                c.end_phase(); return
            c.op("dve", lambda e: e.tensor_tensor(out=R[:], in0=identg[:], in1=X[0][:], op=ALU.subtract),
                 reads=["g_X0", "g_identg"], writes=["g_R"])
            Yc, Yk = L, "g_L"
            Xc, Xk = X[0], "g_X0"
            if DBG_STOP == 31:
                c.end_phase(); return
            for lvl in range(6):
                if DBG_STOP == 32 + lvl and lvl > 0:
                    c.end_phase(); return
                last = lvl == 5
                nX, nXk = X[(lvl + 1) % 2], "g_X%d" % ((lvl + 1) % 2)
                nY, nYk = Y[lvl % 2], "g_Y%d" % (lvl % 2)
                if not last:
                    pbx, pbxk = bank()
                    for g in range(GS):
                        c.op("pe", lambda e: e.matmul(pbx[:, g, :], Yc[:, g, :], Xc[:, g, :], start=True, stop=True),
                             reads=[Yk, Xk], writes=[pbxk])
                pby, pbyk = bank()
                for g in range(GS):
                    c.op("pe", lambda e: e.matmul(pby[:, g, :], Xc[:, g, :], Yc[:, g, :], start=True, stop=True),
                         reads=[Yk, Xk], writes=[pbyk])
                c.op("dve", lambda e: e.tensor_copy(out=nY[:], in_=pby[:]), reads=[pbyk], writes=[nYk])
                if not last:
                    c.op("act", lambda e: e.activation(out=nX[:], in_=pbx[:], func=AF.Identity), reads=[pbxk], writes=[nXk])
                pbr, pbrk = bank()
                for g in range(GS):
                    c.op("pe", lambda e: e.matmul(pbr[:, g, :], nY[:, g, :], R[:, g, :], start=True, stop=True),
                         reads=[nYk, "g_R"], writes=[pbrk])
                c.op("dve", lambda e: e.tensor_tensor(out=R[:], in0=R[:], in1=pbr[:], op=ALU.add),
                     reads=["g_R", pbrk], writes=["g_R"])
                Yc, Yk = nY, nYk
                Xc, Xk = nX, nXk
            if DBG_STOP == 4:
                c.end_phase(); return
            for g in range(GS):
                n = gi * GS + g
                c.op("pool", lambda e: e.tensor_scalar(out=vb[:, g, :], in0=vtok[:, g, :], scalar1=beta[:, n:n + 1],
                                                       scalar2=None, op0=ALU.mult), reads=["g_vtok", "g_beta"], writes=["g_vb"])
                c.op("pool", lambda e: e.tensor_scalar(out=kbg[:, g, :], in0=ktok[:, g, :], scalar1=bgc[:, n:n + 1],
                                                       scalar2=None, op0=ALU.mult), reads=["g_ktok", "g_bgc"], writes=["g_kbg"])
                c.op("pool", lambda e: e.tensor_scalar(out=kdec[:, g, :], in0=ktok[:, g, :], scalar1=kdf[:, n:n + 1],
                                                       scalar2=None, op0=ALU.mult), reads=["g_ktok", "g_kdf"], writes=["g_kdec"])
            pb, pbk = bank()
            for g in range(GS):
                c.op("pe", lambda e: e.matmul(pb[:, g, :], R[:, g, :], vb[:, g, :], start=True, stop=True),
                     reads=["g_R", "g_vb"], writes=[pbk])
            c.op("act", lambda e: e.activation(out=u_[:], in_=pb[:], func=AF.Identity), reads=[pbk], writes=["g_u"])
            pb, pbk = bank()
            for g in range(GS):
                c.op("pe", lambda e: e.matmul(pb[:, g, :], kbg[:, g, :], R[:, g, :], start=True, stop=True),
                     reads=["g_R", "g_kbg"], writes=[pbk])
            c.op("dve", lambda e: e.tensor_copy(out=wT[:], in_=pb[:]), reads=[pbk], writes=["g_wT"])
            for g in range(GS):
                n = gi * GS + g
                Sc, Sk = S[sidx % 2], "g_S%d" % (sidx % 2)
                Sn, Snk = S[(sidx + 1) % 2], "g_S%d" % ((sidx + 1) % 2)
                sidx += 1
                vn, vnk = vnew[n % 2], "g_vnew%d" % (n % 2)
                tq_, tqk = tq[n % 2], "g_tq%d" % (n % 2)
                c.op("pe", lambda e: e.matmul(pscan[:, 0, :], wT[:, g, :], Sc[:], start=True, stop=True),
                     reads=["g_wT", Sk], writes=["g_ps0"])
                c.op("dve", lambda e: e.tensor_tensor(out=vn[:], in0=u_[:, g, :], in1=pscan[:, 0, :], op=ALU.subtract),
                     reads=["g_u", "g_ps0"], writes=[vnk])
                c.op("pe", lambda e: e.matmul(pscan[:, 1, :], qT_[:, g, :], Sc[:], start=True, stop=True),
                     reads=[qTk, Sk], writes=["g_ps1"])
                c.op("pe", lambda e: e.matmul(pscan[:, 2, :], AqT[:, g, :], vn[:], start=True, stop=True),
                     reads=["g_AqT", vnk], writes=["g_ps2"])
                c.op("pe", lambda e: e.matmul(pscan[:, 3, :], kdec[:, g, :], vn[:], start=True, stop=True),
                     reads=["g_kdec", vnk], writes=["g_ps3"])
                c.op("act", lambda e: e.activation(out=tq_[:], in_=pscan[:, 1, :], func=AF.Identity, scale=egc[:, n:n + 1]),
                     reads=["g_ps1", "g_egc"], writes=[tqk])
                c.op("dve", lambda e: e.tensor_tensor(out=o_[:, g, :], in0=tq_[:], in1=pscan[:, 2, :], op=ALU.add),
                     reads=[tqk, "g_ps2"], writes=["g_o"])
                c.op("dve", lambda e: e.scalar_tensor_tensor(out=Sn[:], in0=Sc[:], scalar=egl[:, n:n + 1], in1=pscan[:, 3, :],
                                                             op0=ALU.mult, op1=ALU.add),
                     reads=[Sk, "g_egl", "g_ps3"], writes=[Snk])
            if DBG_STOP == 5:
                c.end_phase(); return
            c.op("pool", lambda e: e.tensor_tensor(out=osq[:], in0=o_[:], in1=o_[:], op=ALU.mult), reads=["g_o"], writes=["g_osq"])
            c.op("dve", lambda e: e.tensor_reduce(out=rs[:], in_=osq[:], axis=mybir.AxisListType.X, op=ALU.add),
                 reads=["g_osq"], writes=["g_rs"])
            c.op("dve", lambda e: e.tensor_scalar(out=rs[:], in0=rs[:], scalar1=1.0 / 128, scalar2=EPS, op0=ALU.mult,
                                                  op1=ALU.add), reads=["g_rs"], writes=["g_rs"])
            c.op("act", lambda e: e.activation(out=rs[:], in_=rs[:], func=AF.Sqrt), reads=["g_rs"], writes=["g_rs"])
            c.op("dve", lambda e: e.reciprocal(out=rs[:], in_=rs[:]), reads=["g_rs"], writes=["g_rs"])
            for g in range(GS):
                c.op("dve", lambda e: e.scalar_tensor_tensor(out=on[:, g, :], in0=o_[:, g, :], scalar=rs[:, g:g + 1],
                                                             in1=gnb[:], op0=ALU.mult, op1=ALU.mult),
                     reads=["g_o", "g_rs", "g_gnb"], writes=["g_on"])
            pb, pbk = bank()
            for g in range(GS):
                c.op("pe", lambda e: e.transpose(pb[:, g, :], on[:, g, :], ident[:]), reads=["g_on", "g_ident"], writes=[pbk])
            y_ = yo[b2]; yk = "g_yo%d" % b2
            c.op("dve", lambda e: e.tensor_tensor(out=y_[:], in0=pb[:].rearrange("p g c -> p (g c)"), in1=z_[:], op=ALU.mult),
                 reads=[pbk, zk], writes=[yk])
            c.dma(yT[Y_A + h * 128:Y_A + (h + 1) * 128, t0:t0 + W], y_[:], reads=[yk])
    c.end_phase()


PAIRS = [[0, 1], [2, 3], [4, 5], [6, 7]]


def build(T, nlayers=2, only=None, pairs=PAIRS):
    nc = bass.Bass("TRN2", target_bir_lowering=False)

    def di(n, shape, dt=F32):
        return nc.dram_tensor(n, shape, dt, kind="ExternalInput").ap()

    def ds(n, shape, dt=F32):
        return nc.dram_tensor(n, shape, dt, kind="Internal").ap()
    L = nlayers
    xT = di("xT", [D, T])
    pT = di("pT", [L, 256, T])
    w_in = di("w_in", [L, D, PWL])
    w_small = di("w_small", [L, D, 8])
    w_branch = di("w_branch", [L, 4, 512, 1024])
    w_out = di("w_out", [L, 1024, 1024])
    w_ple = di("w_ple", [L, 256, 1024])
    w_pg = di("w_pg", [L, 1024, 1024])
    b_gate = di("b_gate", [L, 128, 32])
    b_pg = di("b_pg", [L, 128, 8])
    ln_g = di("ln_g", [L, 128, 8])
    ln_b = di("ln_b", [L, 128, 8])
    cw = di("cw", [L, 128, 4, 31])
    cvec = di("cvec", [L, 128, 3, 4])
    cw4 = di("cw4", [L, 128, 3 * NHG, 4])
    alog = di("alog", [L, 128, NHG])
    dtb = di("dtb", [L, 128, NHG])
    gn = di("gn", [L, 128, 128])
    fb = di("fb", [L, NHA, 1])
    ident = di("ident", [128, 128])
    ones_f = di("ones_f", [128, 128])
    identb = di("identb", [128, 128], BF16)
    negm_i = di("negm_i", [128, 4, 512], BF16)
    negm_s = di("negm_s", [128, 4, 512], BF16)
    m01_s = di("m01_s", [128, 4, 512], BF16)
    negu = di("negu", [128, 128], BF16)
    triu = di("triu", [128, 128])
    slm = di("slm", [128, 128])
    mki = di("mki", [128, 128])
    sel64 = di("sel64", [128, 128])
    outT = nc.dram_tensor("outT", [D, T], F32, kind="ExternalOutput").ap()
    projT = ds("projT", [PWL, T], BF16)
    smallT = ds("smallT", [8, T])
    yT = ds("yT", [YL, T], BF16)
    yg = [ds("yg%d" % i, [256, T], BF16) for i in range(6)]
    gqkv = ds("gqkv", [3 * NHG, 128, T])
    fxq = ds("fxq", [NHA, T], BF16)
    fxk = ds("fxk", [NHA, 3, T], BF16)
    xmid = [ds("xmid%d" % i, [D, T]) for i in range(max(L - 1, 1))]
    with ExitStack() as es:
        c = Ctx(nc, es)
        xin = xT
        for l in range(L):
            xo = outT if l == L - 1 else xmid[l]
            on = lambda n: only is None or n in only
            if on("proj"):
                phase_proj(c, T, xin, w_in[l], w_small[l], b_gate[l], projT, smallT)
            if on("conv"):
                phase_conv(c, T, projT, cw[l], cvec[l], ident, ones_f, yT)
            if on("fox"):
                phase_fox(c, T, projT, smallT, fb[l], negm_i, identb, sel64, fxq, fxk, yT)
            if on("sb"):
                phase_sb(c, T, projT, negm_s, m01_s, identb, negu, yT)
            if on("gdn_pre"):
                phase_gdn_pre(c, T, projT, cw4[l], ident, ones_f, gqkv)
            if on("gdn"):
                phase_gdn(c, T, projT, smallT, gqkv, alog[l], dtb[l], gn[l], ident, ones_f, triu, slm, slm, mki, yT)
            if on("out"):
                c.barrier()
                for j, row in enumerate((Y_A, Y_A + 128, Y_C, Y_C + 128, Y_D, Y_D + 128)):
                    c.collective("AllGather", [yT[row:row + 128, :]], [yg[j]], pairs)
                c.barrier()
                phase_out(c, T, xin, pT[l], yT, yg, projT, w_branch[l], w_out[l], w_ple[l],
                          w_pg[l], b_pg[l], ln_g[l], ln_b[l], ones_f, xo)
            xin = xo
        c.finish()
        ninst = c.ninst
    return nc, ninst


def host_inputs(x_b, p_b, w, L, r):
    import ml_dtypes
    bf = lambda a: np.ascontiguousarray(a).astype(ml_dtypes.bfloat16)
    f32 = lambda a: np.ascontiguousarray(a, dtype=np.float32)
    v8 = lambda v: f32(np.stack([v[l].reshape(8, 128).T for l in range(L)]))
    rep = lambda v: f32(np.stack([np.broadcast_to(v[l][None, :], (128, v[l].shape[0])) for l in range(L)]))
    v4 = lambda v: v.reshape(4, 128).T
    ar = np.arange
    gh = np.concatenate([(NHG * r + h) * 128 + ar(128) for h in range(NHG)])
    ah = np.concatenate([(NHA * r + h) * 64 + ar(64) for h in range(NHA)])
    cols = np.concatenate([R_QA + gh, R_KA + gh, R_VA + gh, R_ZA + gh,
                           R_GL + ar(512), R_GG + ar(512), R_ZB + ar(512),
                           R_QC + ah, R_KC + ah, R_VC + ah, R_ZC + ah,
                           R_QD + ah, R_KD + ah, R_VD + ah, R_ZD + ah,
                           R_G + ar(4096)])
    assert cols.shape[0] == PWL
    scols = np.concatenate([R_AA + NHG * r + ar(NHG), R_BA + NHG * r + ar(NHG), R_FD + NHA * r + ar(NHA)])
    gch = np.concatenate([gh, 512 + gh, 1024 + gh])
    s_ = ar(128)[:, None, None]; j_ = ar(4)[None, :, None]; q_ = ar(512)[None, None, :]
    incl = (s_ + 128 * j_ <= q_); strict = (s_ + 128 * j_ < q_)
    jj = ar(128)[:, None]; ss = ar(128)[None, :]
    d = {
        "xT": f32(x_b.T), "pT": f32(np.stack([p_b[l].T for l in range(L)])),
        "w_in": f32(w["w_in"][:L][:, :, cols]), "w_small": f32(w["w_in"][:L][:, :, scols]),
        "w_branch": f32(w["w_branch"][:L]), "w_out": f32(w["w_out"][:L]),
        "w_ple": f32(w["w_ple"][:L]), "w_pg": f32(w["w_ple_gate"][:L]),
        "b_gate": f32(np.stack([w["b_gate"][l].reshape(32, 128).T for l in range(L)])),
        "b_pg": v8(w["b_ple_gate"]), "ln_g": v8(w["ln_g"]), "ln_b": v8(w["ln_b"]),
        "cw": f32(np.stack([w["conv_dw"][l].T.reshape(4, 128, 31).transpose(1, 0, 2) for l in range(L)])),
        "cvec": f32(np.stack([np.stack([v4(w["conv_dw_bias"][l]), v4(w["conv_ln_g"][l]), v4(w["conv_ln_b"][l])], axis=1)
                              for l in range(L)])),
        "cw4": f32(np.stack([w["conv_qkv"][l][:, gch].T.reshape(3 * NHG, 128, 4).transpose(1, 0, 2) for l in range(L)])),
        "alog": rep(w["a_log"][:, NHG * r:NHG * (r + 1)]), "dtb": rep(w["dt_bias"][:, NHG * r:NHG * (r + 1)]),
        "gn": rep(w["gdn_norm"]),
        "fb": f32(np.stack([w["forget_bias"][l][NHA * r:NHA * (r + 1)].reshape(NHA, 1) for l in range(L)])),
        "ident": np.eye(128, dtype=np.float32), "ones_f": np.ones((128, 128), np.float32),
        "identb": bf(np.eye(128, dtype=np.float32)),
        "negm_i": bf(np.where(incl, 0.0, -30000.0).astype(np.float32)),
        "negm_s": bf(np.where(strict, 0.0, -30000.0).astype(np.float32)),
        "m01_s": bf(strict.astype(np.float32)),
        "negu": bf(-(jj >= ss).astype(np.float32)),
        "triu": (jj <= ss).astype(np.float32), "slm": (jj > ss).astype(np.float32), "mki": (jj >= ss).astype(np.float32),
        "sel64": (jj == ss + 64).astype(np.float32),
    }
    return d


_CACHE = {}


def kernel(**inputs):
    x = np.asarray(inputs["x"])
    p = np.asarray(inputs["p"])
    B, T, _ = x.shape
    w = {k: np.asarray(v) for k, v in inputs.items() if k not in ("x", "p")}
    L = w["w_in"].shape[0]
    ncores = 2 * B
    pairs = [[2 * b, 2 * b + 1] for b in range(B)]
    key = (T, L, B)
    if key not in _CACHE:
        _CACHE[key] = build(T, L, pairs=pairs)[0]
    nc = _CACHE[key]
    in_maps = [host_inputs(x[i // 2], p[:, i // 2], w, L, i % 2) for i in range(ncores)]
    res = run_bass_kernel_spmd(nc, in_maps, core_ids=list(range(ncores)))
    out = np.stack([np.asarray(res.results[2 * b]["outT"]).T for b in range(B)])
    return np.ascontiguousarray(out, dtype=np.float32)
```

```python
import numpy as np
from contextlib import ExitStack
import concourse.bass as bass
import concourse.mybir as mybir
from concourse.bass_utils import run_bass_kernel_spmd

F32 = mybir.dt.float32
BF16 = mybir.dt.bfloat16
AF = mybir.ActivationFunctionType
ALU = mybir.AluOpType

D = 1024
PW = 11792
PWL = 8704
NHA = 4
NHG = 2
YL = 1280
Y_A, Y_B, Y_C, Y_D = 0, 256, 768, 1024
ALPHA = 4 ** 0.25
EPS = 1e-5
SEM_EPOCH = 20000
DBG_STOP = 0

R_QA, R_KA, R_VA, R_ZA, R_AA, R_BA = 0, 512, 1024, 1536, 2048, 2052
R_GL, R_GG, R_ZB = 2056, 2568, 3080
R_QC, R_KC, R_VC, R_ZC = 3592, 4104, 4616, 5128
R_QD, R_KD, R_VD, R_ZD, R_FD = 5640, 6152, 6664, 7176, 7688
R_G = 7696
O_QA, O_KA, O_VA, O_ZA = 0, 256, 512, 768
O_GL, O_GG, O_ZB = 1024, 1536, 2048
O_QC, O_KC, O_VC, O_ZC = 2560, 2816, 3072, 3328
O_QD, O_KD, O_VD, O_ZD = 3584, 3840, 4096, 4352
O_G = 4608


class Ctx:
    def __init__(self, nc, es):
        self.nc, self.es = nc, es
        self.eng = {"pe": nc.tensor, "act": nc.scalar, "dve": nc.vector, "pool": nc.gpsimd, "sp": nc.sync}
        self.sem = {}
        self.cnt = {}
        self.nsem = 0
        for e in self.eng:
            self._new_sem(e)
        self.waited = {e: {} for e in self.eng}
        self.lastw = {}
        self.readers = {}
        self.dsem = [es.enter_context(nc.semaphore("dq%d" % i)) for i in range(24)]
        self.dcnt = [0] * 24
        self.dnext = 0
        self.ninst = 0

    def _new_sem(self, e):
        self.sem[e] = self.es.enter_context(self.nc.semaphore("s_%s_%d" % (e, self.nsem)))
        self.nsem += 1
        self.cnt[e] = 0

    def _wait(self, e, tok):
        sem, val, src = tok
        if src == "pe" and e == "pe":
            return
        w = self.waited[e]
        k = id(sem)
        if w.get(k, 0) >= val:
            return
        w[k] = val
        self.eng[e].wait_ge(sem, val)

    def _deps(self, e, reads, writes):
        for k in reads:
            t = self.lastw.get(k)
            if t is not None:
                self._wait(e, t)
        for k in writes:
            t = self.lastw.get(k)
            if t is not None:
                self._wait(e, t)
            rd = self.readers.get(k)
            if rd:
                for key, t in rd.items():
                    self._wait(e, t)

    def _commit(self, tok, reads, writes):
        for k in writes:
            self.lastw[k] = tok
            self.readers[k] = {}
        for k in reads:
            rd = self.readers.setdefault(k, {})
            if tok[2] == "dma":
                rd[("dma", id(tok[0]))] = tok
            else:
                rd[tok[2]] = tok

    def op(self, e, fn, reads=(), writes=()):
        self._deps(e, reads, writes)
        ins = fn(self.eng[e])
        if self.cnt[e] >= SEM_EPOCH:
            self._new_sem(e)
        self.cnt[e] += 1
        ins.then_inc(self.sem[e], 1)
        tok = (self.sem[e], self.cnt[e], e)
        self._commit(tok, reads, writes)
        self.ninst += 1
        return tok

    def dma(self, out, in_, reads=(), writes=(), q="sp"):
        j = self.dnext
        self.dnext = (self.dnext + 1) % len(self.dsem)
        self._deps(q, reads, writes)
        if self.dcnt[j] > 0:
            self._wait(q, (self.dsem[j], self.dcnt[j], "dma"))
        self.eng[q].dma_start(out=out, in_=in_).then_inc(self.dsem[j], 16)
        self.dcnt[j] += 16
        tok = (self.dsem[j], self.dcnt[j], "dma")
        self._commit(tok, reads, writes)
        self.ninst += 1
        return tok

    def collective(self, kind, ins, outs, groups, reads=(), writes=()):
        if not hasattr(self, "ccsem"):
            self.ccsem = self.es.enter_context(self.nc.semaphore("ccsem"))
            self.cccnt = 0
        self._deps("pool", reads, writes)
        self.nc.gpsimd.collective_compute(kind, ALU.bypass, replica_groups=groups, ins=ins, outs=outs).then_inc(self.ccsem)
        self.cccnt += 1
        tok = (self.ccsem, self.cccnt, "dma")
        self._commit(tok, reads, writes)
        for e in self.eng:
            self._wait(e, tok)
        return tok

    def finish(self):
        for j in range(len(self.dsem)):
            if self.dcnt[j] > 0:
                self._wait("sp", (self.dsem[j], self.dcnt[j], "dma"))
        for e in ("pe", "act", "dve", "pool"):
            if self.cnt[e] > 0:
                self._wait("sp", (self.sem[e], self.cnt[e], e))

    def sb(self, name, shape, dt):
        return self.pes.enter_context(self.nc.sbuf_tensor("%s_%d" % (name, self.phase_no), shape, dt))

    def ps(self, name, shape, dt=F32):
        return self.pes.enter_context(self.nc.psum_tensor("%s_%d" % (name, self.phase_no), shape, dt))

    def begin_phase(self):
        self.pes = ExitStack()
        self.phase_no = getattr(self, "phase_no", 0) + 1

    def end_phase(self):
        self.barrier()
        self.pes.close()

    def barrier(self):
        for e in self.eng:
            for j in range(len(self.dsem)):
                if self.dcnt[j] > 0:
                    self._wait(e, (self.dsem[j], self.dcnt[j], "dma"))
            for f in ("pe", "act", "dve", "pool"):
                if f != e and self.cnt[f] > 0:
                    self._wait(e, (self.sem[f], self.cnt[f], f))
        self.lastw.clear()
        self.readers.clear()


def proj_chunks():
    ch = []

    def add(o, n, kind):
        for i in range(n // 128):
            ch.append((o + 128 * i, 128, kind))
    add(O_QA, 768, 0)
    add(O_ZA, 256, 1)
    add(O_GL, 512, 0)
    add(O_GG, 512, 2)
    add(O_ZB, 512, 1)
    add(O_QC, 256, 3)
    add(O_KC, 512, 0)
    add(O_ZC, 256, 1)
    add(O_QD, 256, 3)
    add(O_KD, 512, 0)
    add(O_ZD, 256, 1)
    add(O_G, 4096, 4)
    return ch


def phase_proj(c, T, xT, w_in, w_small, b_gate, projT, smallT):
    nc = c.nc
    c.begin_phase()
    ST = min(4096, T)
    nst = T // ST
    nsub = ST // 512
    xs = [c.sb("p1_xs%d" % i, [128, 8, 512], F32) for i in range(2)]
    xb = c.sb("p1_xb", [128, 8, ST], BF16)
    ws = [c.sb("p1_ws%d" % i, [128, 8, 512], F32) for i in range(2)]
    wb = [c.sb("p1_wb%d" % i, [128, 8, 512], BF16) for i in range(2)]
    wss = c.sb("p1_wss", [128, 8, 8], F32)
    wsb = c.sb("p1_wsb", [128, 8, 8], BF16)
    bg = c.sb("p1_bg", [128, 32], F32)
    ob = [c.sb("p1_ob%d" % i, [128, 4, 512], BF16) for i in range(3)]
    osm = c.sb("p1_osm", [8, 512], F32)
    pss = [c.ps("p1_ps%d" % i, [128, 512]) for i in range(4)]
    c.dma(bg[:], b_gate, writes=["p1_bg"])
    c.dma(wss[:], w_small.rearrange("(k p) n -> p k n", p=128), writes=["p1_wss"])
    c.op("pool", lambda e: e.tensor_copy(out=wsb[:], in_=wss[:]), reads=["p1_wss"], writes=["p1_wsb"])
    chunks = proj_chunks()
    groups = [chunks[i:i + 4] for i in range(0, len(chunks), 4)]
    xTv = xT.rearrange("(k p) t -> p k t", p=128)
    w_v = w_in.rearrange("(k p) n -> p k n", p=128)
    it = 0
    for st in range(nst):
        for sub in range(nsub):
            t0 = st * ST + sub * 512
            s = xs[(st * nsub + sub) % 2]
            key = "p1_xs%d" % ((st * nsub + sub) % 2)
            c.dma(s[:], xTv[:, :, t0:t0 + 512], writes=[key])
            c.op("pool", lambda e: e.tensor_copy(out=xb[:, :, sub * 512:(sub + 1) * 512], in_=s[:]),
                 reads=[key], writes=["p1_xb%d" % sub])
        for sub in range(nsub):
            t0 = st * ST + sub * 512
            p = pss[it % 4]; pk = "p1_ps%d" % (it % 4)
            for k in range(8):
                c.op("pe", lambda e: e.matmul(p[0:8, :], wsb[:, k, :], xb[:, k, sub * 512:(sub + 1) * 512],
                                              start=(k == 0), stop=(k == 7)),
                     reads=["p1_wsb", "p1_xb%d" % sub], writes=[pk])
            c.op("dve", lambda e: e.tensor_copy(out=osm[:], in_=p[0:8, :]), reads=[pk], writes=["p1_osm"])
            c.dma(smallT[:, t0:t0 + 512], osm[:], reads=["p1_osm"])
            it += 1
        def wload(gi):
            g0 = groups[gi][0][0]
            gw = 128 * len(groups[gi])
            wsl = ws[gi % 2]; wbl = wb[gi % 2]
            c.dma(wsl[:, :, 0:gw], w_v[:, :, g0:g0 + gw], writes=["p1_ws%d" % (gi % 2)])
            c.op("pool", lambda e: e.tensor_copy(out=wbl[:, :, 0:gw], in_=wsl[:, :, 0:gw]), reads=["p1_ws%d" % (gi % 2)],
                 writes=["p1_wb%d" % (gi % 2)])
        wload(0)
        for gi, grp_ in enumerate(groups):
            g0 = grp_[0][0]
            assert all(grp_[j][0] == g0 + 128 * j for j in range(len(grp_)))
            wbl = wb[gi % 2]
            if gi + 1 < len(groups):
                wload(gi + 1)
            for sub in range(nsub):
                t0 = st * ST + sub * 512
                obi = (gi * nsub + sub) % 3
                for cj, (off, wd, kind) in enumerate(grp_):
                    p = pss[it % 4]; pk = "p1_ps%d" % (it % 4)
                    o = ob[obi][:, cj, :]; ok = "p1_ob%d" % obi
                    for k in range(8):
                        c.op("pe", lambda e: e.matmul(p[:], wbl[:, k, cj * 128:(cj + 1) * 128],
                                                      xb[:, k, sub * 512:(sub + 1) * 512],
                                                      start=(k == 0), stop=(k == 7)),
                             reads=["p1_wb%d" % (gi % 2), "p1_xb%d" % sub], writes=[pk])
                    if kind == 0:
                        c.op("dve", lambda e: e.tensor_copy(out=o, in_=p[:]), reads=[pk], writes=[ok])
                    elif kind == 3:
                        c.op("dve", lambda e: e.tensor_scalar(out=o, in0=p[:], scalar1=0.125, scalar2=None,
                                                              op0=ALU.mult), reads=[pk], writes=[ok])
                    elif kind == 1:
                        c.op("act", lambda e: e.activation(out=o, in_=p[:], func=AF.Silu), reads=[pk], writes=[ok])
                    elif kind == 2:
                        c.op("act", lambda e: e.activation(out=o, in_=p[:], func=AF.Sigmoid), reads=[pk], writes=[ok])
                    else:
                        gidx = (off - O_G) // 128
                        c.op("act", lambda e: e.activation(out=o, in_=p[:], func=AF.Sigmoid, bias=bg[:, gidx:gidx + 1]),
                             reads=[pk, "p1_bg"], writes=[ok])
                    it += 1
                ng = len(grp_)
                c.dma(projT[g0:g0 + 128 * ng, t0:t0 + 512].rearrange("(c p) t -> p c t", p=128), ob[obi][:, 0:ng, :],
                      reads=["p1_ob%d" % obi])
    c.end_phase()


def load_w_bf16(c, dst, dst_key, src_rows, stg, n):
    i = c.stg_i = getattr(c, "stg_i", 0) + 1
    s = stg[i % len(stg)]
    sk = "stg%d" % (i % len(stg))
    c.dma(s[:, 0:n], src_rows, writes=[sk])
    c.op("pool", lambda e: e.tensor_copy(out=dst, in_=s[:, 0:n]), reads=[sk], writes=[dst_key])


def phase_out(c, T, xT, pT, yT, yg, projT, w_branch, w_out, w_ple, w_pg, b_pg, ln_g, ln_b, ones_f, outT):
    nc = c.nc
    c.begin_phase()
    TT = 256
    stg = [c.sb("po_stg%d" % i, [128, 1024], F32) for i in range(2)]
    wbr = c.sb("po_wbr", [128, 16, 1024], BF16)
    wo = c.sb("po_wo", [128, 8, 1024], BF16)
    wpg = c.sb("po_wpg", [128, 8, 1024], BF16)
    wpl = c.sb("po_wpl", [128, 2, 1024], BF16)
    vb = c.sb("po_vec", [128, 3, 8], F32)
    ones = c.sb("po_ones", [128, 128], F32)
    c.dma(vb[:, 0, :], b_pg, writes=["po_vec"])
    c.dma(vb[:, 1, :], ln_g, writes=["po_vec"])
    c.dma(vb[:, 2, :], ln_b, writes=["po_vec"])
    c.dma(ones[:], ones_f, writes=["po_ones"])
    wbv = w_branch.rearrange("b (k p) n -> p (b k) n", p=128)
    for i in range(16):
        load_w_bf16(c, wbr[:, i, :], "po_wbr", wbv[:, i, :], stg, 1024)
    for nm, dst, src, nk in (("po_wo", wo, w_out, 8), ("po_wpg", wpg, w_pg, 8), ("po_wpl", wpl, w_ple, 2)):
        sv = src.rearrange("(k p) n -> p k n", p=128)
        for i in range(nk):
            load_w_bf16(c, dst[:, i, :], nm, sv[:, i, :], stg, 1024)
    ys = [c.sb("po_y%d" % i, [128, 16, TT], BF16) for i in range(2)]
    gt = [c.sb("po_gt%d" % i, [128, 8, TT], BF16) for i in range(2)]
    xs_ = [c.sb("po_x%d" % i, [128, 8, TT], F32) for i in range(2)]
    pfs = [c.sb("po_pf%d" % i, [128, 2, TT], F32) for i in range(2)]
    pbs = [c.sb("po_pb%d" % i, [128, 2, TT], BF16) for i in range(2)]
    m = c.sb("po_m", [128, 8, TT], F32)
    mb = c.sb("po_mb", [128, 8, TT], BF16)
    rs_ = [c.sb("po_r%d_" % i, [128, 8, TT], F32) for i in range(2)]
    rbs_ = [c.sb("po_rb%d_" % i, [128, 8, TT], BF16) for i in range(2)]
    tmp = [c.sb("po_tmp%d" % i, [128, TT], F32) for i in range(2)]
    gp = [c.sb("po_gp%d" % i, [128, TT], F32) for i in range(4)]
    sq = [c.sb("po_sq%d" % i, [128, TT], F32) for i in range(8)]
    mean = c.sb("po_mean", [128, TT], F32)
    msq = c.sb("po_msq", [128, TT], F32)
    rstd = c.sb("po_rstd", [128, TT], F32)
    ot = [c.sb("po_ot%d" % i, [128, TT], F32) for i in range(4)]
    ps = [c.ps("po_ps%d" % i, [128, 512]) for i in range(4)]
    pstat = [c.ps("po_pst%d" % i, [128, 512]) for i in range(2)]
    xv = xT.rearrange("(k p) t -> p k t", p=128)
    pv = pT.rearrange("(k p) t -> p k t", p=128)
    ov = outT.rearrange("(k p) t -> p k t", p=128)
    itc = [0]

    def stL(tt):
        t0 = tt * TT
        y = ys[tt % 2]; yk_ = "po_y%d" % (tt % 2)
        x = xs_[tt % 2]; xk_ = "po_x%d" % (tt % 2)
        pf = pfs[tt % 2]; pfk = "po_pf%d" % (tt % 2)
        pb = pbs[tt % 2]; pbk_ = "po_pb%d" % (tt % 2)
        c.dma(y[:, 4:8, :], yT[Y_B:Y_B + 512, t0:t0 + TT].rearrange("(k p) t -> p k t", p=128), writes=[yk_])
        for gi_, br_ in enumerate((0, 2, 3)):
            for par in range(2):
                i0_ = br_ * 4 + par
                c.dma(y[:, i0_:i0_ + 3:2, :], yg[gi_ * 2 + par][:, t0:t0 + TT].rearrange("(r p) t -> p r t", p=128),
                      writes=[yk_])
        c.dma(x[:], xv[:, :, t0:t0 + TT], writes=[xk_])
        c.dma(pf[:], pv[:, :, t0:t0 + TT], writes=[pfk])
        c.op("pool", lambda e: e.tensor_copy(out=pb[:], in_=pf[:]), reads=[pfk], writes=[pbk_])

    def stA(tt):
        t0 = tt * TT
        y = ys[tt % 2]; yk_ = "po_y%d" % (tt % 2)
        x = xs_[tt % 2]; xk_ = "po_x%d" % (tt % 2)
        pf = pfs[tt % 2]; pfk = "po_pf%d" % (tt % 2)
        pb = pbs[tt % 2]; pbk_ = "po_pb%d" % (tt % 2)
        pst_s = pstat[0]; pst_q = pstat[1]
        r = rs_[tt % 2]; rb = rbs_[tt % 2]; rp_ = "po_r%d_" % (tt % 2); rbp_ = "po_rb%d_" % (tt % 2)
        for br in range(4):
            g = gt[br % 2]; gk = "po_gt%d" % (br % 2)
            c.dma(g[:], projT[O_G + br * 1024:O_G + (br + 1) * 1024, t0:t0 + TT].rearrange("(k p) t -> p k t", p=128),
                  writes=[gk])
            for fo in range(8):
                p = ps[itc[0] % 4]; pk = "po_ps%d" % (itc[0] % 4); itc[0] += 1
                for kc in range(4):
                    c.op("pe", lambda e: e.matmul(p[:, 0:TT], wbr[:, br * 4 + kc, fo * 128:(fo + 1) * 128],
                                                  y[:, br * 4 + kc, :], start=(kc == 0), stop=(kc == 3)),
                         reads=["po_wbr", yk_], writes=[pk])
                mk = "po_m%d" % fo
                if br == 0:
                    c.op("dve", lambda e: e.tensor_tensor(out=m[:, fo, :], in0=p[:, 0:TT], in1=g[:, fo, :], op=ALU.mult),
                         reads=[pk, gk], writes=[mk])
                else:
                    t_ = tmp[itc[0] % 2]; tk = "po_tmp%d" % (itc[0] % 2)
                    c.op("dve", lambda e: e.tensor_tensor(out=t_[:], in0=p[:, 0:TT], in1=g[:, fo, :], op=ALU.mult),
                         reads=[pk, gk], writes=[tk])
                    if br < 3:
                        c.op("pool", lambda e: e.tensor_tensor(out=m[:, fo, :], in0=m[:, fo, :], in1=t_[:], op=ALU.add),
                             reads=[mk, tk], writes=[mk])
                    else:
                        c.op("pool", lambda e: e.tensor_tensor(out=mb[:, fo, :], in0=m[:, fo, :], in1=t_[:], op=ALU.add),
                             reads=[mk, tk], writes=["po_mb%d" % fo])

    def stB(tt):
        t0 = tt * TT
        y = ys[tt % 2]; yk_ = "po_y%d" % (tt % 2)
        x = xs_[tt % 2]; xk_ = "po_x%d" % (tt % 2)
        pf = pfs[tt % 2]; pfk = "po_pf%d" % (tt % 2)
        pb = pbs[tt % 2]; pbk_ = "po_pb%d" % (tt % 2)
        pst_s = pstat[0]; pst_q = pstat[1]
        r = rs_[tt % 2]; rb = rbs_[tt % 2]; rp_ = "po_r%d_" % (tt % 2); rbp_ = "po_rb%d_" % (tt % 2)
        for fo in range(8):
            p = ps[itc[0] % 4]; pk = "po_ps%d" % (itc[0] % 4); itc[0] += 1
            for k in range(8):
                c.op("pe", lambda e: e.matmul(p[:, 0:TT], wo[:, k, fo * 128:(fo + 1) * 128], mb[:, k, :],
                                              start=(k == 0), stop=(k == 7)),
                     reads=["po_wo"] + ["po_mb%d" % k], writes=[pk])
            c.op("dve", lambda e: e.scalar_tensor_tensor(out=r[:, fo, :], in0=x[:, fo, :], scalar=ALPHA, in1=p[:, 0:TT],
                                                         op0=ALU.mult, op1=ALU.add),
                 reads=[pk, xk_], writes=[rp_ + str(fo)])
            c.op("act", lambda e: e.activation(out=rb[:, fo, :], in_=r[:, fo, :], func=AF.Identity),
                 reads=[rp_ + str(fo)], writes=[rbp_ + str(fo)])

    def stC(tt):
        t0 = tt * TT
        y = ys[tt % 2]; yk_ = "po_y%d" % (tt % 2)
        x = xs_[tt % 2]; xk_ = "po_x%d" % (tt % 2)
        pf = pfs[tt % 2]; pfk = "po_pf%d" % (tt % 2)
        pb = pbs[tt % 2]; pbk_ = "po_pb%d" % (tt % 2)
        pst_s = pstat[0]; pst_q = pstat[1]
        r = rs_[tt % 2]; rb = rbs_[tt % 2]; rp_ = "po_r%d_" % (tt % 2); rbp_ = "po_rb%d_" % (tt % 2)
        for fo in range(8):
            p = ps[itc[0] % 4]; pk = "po_ps%d" % (itc[0] % 4); itc[0] += 1
            for k in range(8):
                c.op("pe", lambda e: e.matmul(p[:, 0:TT], wpg[:, k, fo * 128:(fo + 1) * 128], rb[:, k, :],
                                              start=(k == 0), stop=(k == 7)),
                     reads=["po_wpg", rbp_ + str(k)], writes=[pk])
            g_ = gp[fo % 4]; gk = "po_gp%d" % (fo % 4)
            c.op("act", lambda e: e.activation(out=g_[:], in_=p[:, 0:TT], func=AF.Sigmoid, bias=vb[:, 0, fo:fo + 1]),
                 reads=[pk, "po_vec"], writes=[gk])
            p2 = ps[itc[0] % 4]; pk2 = "po_ps%d" % (itc[0] % 4); itc[0] += 1
            for k in range(2):
                c.op("pe", lambda e: e.matmul(p2[:, 0:TT], wpl[:, k, fo * 128:(fo + 1) * 128], pb[:, k, :],
                                              start=(k == 0), stop=(k == 1)),
                     reads=["po_wpl", pbk_], writes=[pk2])
            t_ = tmp[fo % 2]; tk = "po_tmp%d" % (fo % 2)
            c.op("dve", lambda e: e.tensor_tensor(out=t_[:], in0=p2[:, 0:TT], in1=g_[:], op=ALU.mult),
                 reads=[pk2, gk], writes=[tk])
            c.op("pool", lambda e: e.tensor_tensor(out=r[:, fo, :], in0=r[:, fo, :], in1=t_[:], op=ALU.add),
                 reads=[rp_ + str(fo), tk], writes=[rp_ + str(fo)])
        for fo in range(8):
            s_ = sq[fo]; sk = "po_sq%d" % fo
            c.op("act", lambda e: e.activation(out=s_[:], in_=r[:, fo, :], func=AF.Square),
                 reads=[rp_ + str(fo)], writes=[sk])
        for fo in range(8):
            s_ = sq[fo]; sk = "po_sq%d" % fo
            c.op("pe", lambda e: e.matmul(pst_s[:, 0:TT], ones[:], r[:, fo, :], start=(fo == 0), stop=(fo == 7)),
                 reads=["po_ones", rp_ + str(fo)], writes=["po_pst0"])
            c.op("pe", lambda e: e.matmul(pst_q[:, 0:TT], ones[:], s_[:], start=(fo == 0), stop=(fo == 7)),
                 reads=["po_ones", sk], writes=["po_pst1"])

    def stD(tt):
        t0 = tt * TT
        y = ys[tt % 2]; yk_ = "po_y%d" % (tt % 2)
        x = xs_[tt % 2]; xk_ = "po_x%d" % (tt % 2)
        pf = pfs[tt % 2]; pfk = "po_pf%d" % (tt % 2)
        pb = pbs[tt % 2]; pbk_ = "po_pb%d" % (tt % 2)
        pst_s = pstat[0]; pst_q = pstat[1]
        r = rs_[tt % 2]; rb = rbs_[tt % 2]; rp_ = "po_r%d_" % (tt % 2); rbp_ = "po_rb%d_" % (tt % 2)
        c.op("dve", lambda e: e.tensor_scalar(out=mean[:], in0=pst_s[:, 0:TT], scalar1=1.0 / D, scalar2=None, op0=ALU.mult),
             reads=["po_pst0"], writes=["po_mean"])
        c.op("dve", lambda e: e.tensor_tensor(out=msq[:], in0=mean[:], in1=mean[:], op=ALU.mult),
             reads=["po_mean"], writes=["po_msq"])
        c.op("dve", lambda e: e.scalar_tensor_tensor(out=msq[:], in0=pst_q[:, 0:TT], scalar=1.0 / D, in1=msq[:],
                                                     op0=ALU.mult, op1=ALU.subtract),
             reads=["po_pst1", "po_msq"], writes=["po_msq"])
        c.op("dve", lambda e: e.tensor_scalar(out=msq[:], in0=msq[:], scalar1=EPS, scalar2=None, op0=ALU.add),
             reads=["po_msq"], writes=["po_msq"])
        c.op("act", lambda e: e.activation(out=rstd[:], in_=msq[:], func=AF.Sqrt),
             reads=["po_msq"], writes=["po_rstd"])
        c.op("dve", lambda e: e.reciprocal(out=rstd[:], in_=rstd[:]), reads=["po_rstd"], writes=["po_rstd"])
        for fo in range(8):
            t_ = tmp[fo % 2]; tk = "po_tmp%d" % (fo % 2)
            o_ = ot[fo % 4]; ok = "po_ot%d" % (fo % 4)
            c.op("dve", lambda e: e.tensor_tensor(out=t_[:], in0=r[:, fo, :], in1=mean[:], op=ALU.subtract),
                 reads=[rp_ + str(fo), "po_mean"], writes=[tk])
            c.op("pool", lambda e: e.tensor_tensor(out=t_[:], in0=t_[:], in1=rstd[:], op=ALU.mult),
                 reads=[tk, "po_rstd"], writes=[tk])
            c.op("act", lambda e: e.activation(out=o_[:], in_=t_[:], func=AF.Identity, scale=vb[:, 1, fo:fo + 1],
                                               bias=vb[:, 2, fo:fo + 1]),
                 reads=[tk, "po_vec"], writes=[ok])
            c.dma(ov[:, fo, t0:t0 + TT], o_[:], reads=[ok])

    NTT = T // TT
    stL(0)
    stA(0)
    stB(0)
    if NTT > 1:
        stL(1)
        stA(1)
    for tt in range(NTT):
        stC(tt)
        if tt + 1 < NTT:
            stB(tt + 1)
        if tt + 2 < NTT:
            stL(tt + 2)
        stD(tt)
        if tt + 2 < NTT:
            stA(tt + 2)
    c.end_phase()


def phase_conv(c, T, projT, cw_d, cvec_d, ident_d, ones_f, yT):
    nc = c.nc
    c.begin_phase()
    TT = 512 if T >= 512 else T
    cw = c.sb("pc_cw", [128, 4, 31], F32)
    cvec = c.sb("pc_cvec", [128, 3, 4], F32)
    ident = c.sb("pc_ident", [128, 128], F32)
    ones = c.sb("pc_ones", [128, 128], F32)
    dg = c.sb("pc_dg", [128, 124, 128], BF16)
    hb = c.sb("pc_hb", [128, 4, 32 + T], BF16)
    c.dma(cw[:], cw_d, writes=["pc_cw"])
    c.dma(cvec[:], cvec_d, writes=["pc_cvec"])
    c.dma(ident[:], ident_d, writes=["pc_ident"])
    c.dma(ones[:], ones_f, writes=["pc_ones"])
    for ch in range(4):
        for k in range(31):
            c.op("dve", lambda e: e.tensor_scalar(out=dg[:, ch * 31 + k, :], in0=ident[:], scalar1=cw[:, ch, k:k + 1],
                                                  scalar2=None, op0=ALU.mult),
                 reads=["pc_cw", "pc_ident"], writes=["pc_dg"])
    ld = [c.sb("pc_ld%d" % i, [128, 2048], BF16) for i in range(4)]
    PT = min(2048, T)
    i = 0
    for ch in range(4):
        c.op("pool", lambda e: e.memset(hb[:, ch, 0:32], 0.0), writes=["pc_hb%d" % ch])
        for t0 in range(0, T, PT):
            a = ld[i % 4]; ak = "pc_ld%d" % (i % 4); i += 1
            b = ld[i % 4]; bk = "pc_ld%d" % (i % 4); i += 1
            c.dma(a[:, 0:PT], projT[O_GL + ch * 128:O_GL + (ch + 1) * 128, t0:t0 + PT], writes=[ak])
            c.dma(b[:, 0:PT], projT[O_GG + ch * 128:O_GG + (ch + 1) * 128, t0:t0 + PT], writes=[bk])
            c.op("pool", lambda e: e.tensor_tensor(out=hb[:, ch, 32 + t0:32 + t0 + PT], in0=a[:, 0:PT], in1=b[:, 0:PT],
                                                   op=ALU.mult),
                 reads=[ak, bk], writes=["pc_hb%d" % ch])
    cv = c.sb("pc_cv", [128, 4, TT], F32)
    sq = [c.sb("pc_sq%d" % i, [128, TT], F32) for i in range(4)]
    zb = [c.sb("pc_zb%d" % i, [128, TT], BF16) for i in range(4)]
    mean = c.sb("pc_mean", [128, TT], F32)
    msq = c.sb("pc_msq", [128, TT], F32)
    rstd = c.sb("pc_rstd", [128, TT], F32)
    tmp = [c.sb("pc_tmp%d" % i, [128, TT], F32) for i in range(2)]
    yo = [c.sb("pc_yo%d" % i, [128, TT], BF16) for i in range(2)]
    ps = [c.ps("pc_ps%d" % i, [128, 512]) for i in range(3)]
    pst = [c.ps("pc_pst%d" % i, [128, 512]) for i in range(2)]
    it = 0
    for tt in range(T // TT):
        t0 = tt * TT
        for ch in range(4):
            p = ps[it % 3]; pk = "pc_ps%d" % (it % 3); it += 1
            for k in range(31):
                c.op("pe", lambda e: e.matmul(p[:, 0:TT], dg[:, ch * 31 + k, :], hb[:, ch, 2 + t0 + k:2 + t0 + k + TT],
                                              start=(k == 0), stop=(k == 30)),
                     reads=["pc_dg", "pc_hb%d" % ch], writes=[pk])
            c.op("act", lambda e: e.activation(out=cv[:, ch, :], in_=p[:, 0:TT], func=AF.Identity, bias=cvec[:, 0, ch:ch + 1]),
                 reads=[pk, "pc_cvec"], writes=["pc_cv%d" % ch])
            s_ = sq[ch]; sk = "pc_sq%d" % ch
            c.op("act", lambda e: e.activation(out=s_[:], in_=cv[:, ch, :], func=AF.Square),
                 reads=["pc_cv%d" % ch], writes=[sk])
            z = zb[ch]; zk = "pc_zb%d" % ch
            c.dma(z[:], projT[O_ZB + ch * 128:O_ZB + (ch + 1) * 128, t0:t0 + TT], writes=[zk])
        for ch in range(4):
            s_ = sq[ch]; sk = "pc_sq%d" % ch
            c.op("pe", lambda e: e.matmul(pst[0][:, 0:TT], ones[:], cv[:, ch, :], start=(ch == 0), stop=(ch == 3)),
                 reads=["pc_ones", "pc_cv%d" % ch], writes=["pc_pst0"])
            c.op("pe", lambda e: e.matmul(pst[1][:, 0:TT], ones[:], s_[:], start=(ch == 0), stop=(ch == 3)),
                 reads=["pc_ones", sk], writes=["pc_pst1"])
        c.op("dve", lambda e: e.tensor_scalar(out=mean[:], in0=pst[0][:, 0:TT], scalar1=1.0 / 512, scalar2=None, op0=ALU.mult),
             reads=["pc_pst0"], writes=["pc_mean"])
        c.op("dve", lambda e: e.tensor_tensor(out=msq[:], in0=mean[:], in1=mean[:], op=ALU.mult),
             reads=["pc_mean"], writes=["pc_msq"])
        c.op("dve", lambda e: e.scalar_tensor_tensor(out=msq[:], in0=pst[1][:, 0:TT], scalar=1.0 / 512, in1=msq[:],
                                                     op0=ALU.mult, op1=ALU.subtract),
             reads=["pc_pst1", "pc_msq"], writes=["pc_msq"])
        c.op("dve", lambda e: e.tensor_scalar(out=msq[:], in0=msq[:], scalar1=EPS, scalar2=None, op0=ALU.add),
             reads=["pc_msq"], writes=["pc_msq"])
        c.op("act", lambda e: e.activation(out=rstd[:], in_=msq[:], func=AF.Sqrt), reads=["pc_msq"], writes=["pc_rstd"])
        c.op("dve", lambda e: e.reciprocal(out=rstd[:], in_=rstd[:]), reads=["pc_rstd"], writes=["pc_rstd"])
        for ch in range(4):
            z = zb[ch]; zk = "pc_zb%d" % ch
            t_ = tmp[ch % 2]; tk = "pc_tmp%d" % (ch % 2)
            c.op("dve", lambda e: e.tensor_tensor(out=t_[:], in0=cv[:, ch, :], in1=mean[:], op=ALU.subtract),
                 reads=["pc_cv%d" % ch, "pc_mean"], writes=[tk])
            c.op("pool", lambda e: e.tensor_tensor(out=t_[:], in0=t_[:], in1=rstd[:], op=ALU.mult),
                 reads=[tk, "pc_rstd"], writes=[tk])
            c.op("act", lambda e: e.activation(out=t_[:], in_=t_[:], func=AF.Silu, scale=cvec[:, 1, ch:ch + 1],
                                               bias=cvec[:, 2, ch:ch + 1]),
                 reads=[tk, "pc_cvec"], writes=[tk])
            o_ = yo[ch % 2]; ok = "pc_yo%d" % (ch % 2)
            c.op("dve", lambda e: e.tensor_tensor(out=o_[:], in0=t_[:], in1=z[:], op=ALU.mult),
                 reads=[tk, zk], writes=[ok])
            c.dma(yT[Y_B + ch * 128:Y_B + (ch + 1) * 128, t0:t0 + TT], o_[:], reads=[ok])
    c.end_phase()


def phase_fox(c, T, projT, smallT, fb_d, negmask_d, identb_d, sel_d, fxq, fxk, yT):
    nc = c.nc
    c.begin_phase()
    QB = min(512, T)
    NT = T // 128
    fb = c.sb("pf_fb", [NHA, 1], F32)
    nfb = c.sb("pf_nfb", [NHA, 1], F32)
    c.dma(fb[:], fb_d, writes=["pf_fb"])
    c.op("dve", lambda e: e.tensor_scalar(out=nfb[:], in0=fb[:], scalar1=-1.0, scalar2=None, op0=ALU.mult),
         reads=["pf_fb"], writes=["pf_nfb"])
    PT = min(1024, T)
    onesr = c.sb("pf_onesr", [NHA, PT], F32)
    c.op("pool", lambda e: e.memset(onesr[:], 1.0), writes=["pf_onesr"])
    fin = c.sb("pf_fin", [NHA, PT], F32)
    l_ = c.sb("pf_l", [NHA, PT], F32)
    fn = [c.sb("pf_fn%d" % i, [NHA, PT], F32) for i in range(2)]
    hi = c.sb("pf_hi", [NHA, PT], BF16)
    r1 = c.sb("pf_r1", [NHA, PT], F32)
    mid = c.sb("pf_mid", [NHA, PT], BF16)
    lo = c.sb("pf_lo", [NHA, PT], BF16)
    nhi = c.sb("pf_nhi", [NHA, PT], BF16)
    for pi, t0 in enumerate(range(0, T, PT)):
        f_ = fn[pi % 2]; fk = "pf_fn%d" % (pi % 2)
        c.dma(fin[:], smallT[4:8, t0:t0 + PT], writes=["pf_fin"])
        c.op("act", lambda e: e.activation(out=l_[:], in_=fin[:], func=AF.Exp, scale=-1.0, bias=nfb[:]),
             reads=["pf_fin", "pf_nfb"], writes=["pf_l"])
        c.op("act", lambda e: e.activation(out=l_[:], in_=l_[:], func=AF.Ln, bias=1.0), reads=["pf_l"], writes=["pf_l"])
        if pi == 0:
            c.op("dve", lambda e: e.tensor_tensor_scan(out=f_[:], data0=onesr[:], data1=l_[:], initial=0.0,
                                                       op0=ALU.mult, op1=ALU.add),
                 reads=["pf_onesr", "pf_l"], writes=[fk])
        else:
            pf_ = fn[(pi - 1) % 2]
            c.op("dve", lambda e: e.tensor_tensor_scan(out=f_[:], data0=onesr[:], data1=l_[:], initial=pf_[:, PT - 1:PT],
                                                       op0=ALU.mult, op1=ALU.add),
                 reads=["pf_onesr", "pf_l", "pf_fn%d" % ((pi - 1) % 2)], writes=[fk])
        c.op("dve", lambda e: e.tensor_copy(out=hi[:], in_=f_[:]), reads=[fk], writes=["pf_hi"])
        c.op("dve", lambda e: e.tensor_tensor(out=r1[:], in0=f_[:], in1=hi[:], op=ALU.subtract),
             reads=[fk, "pf_hi"], writes=["pf_r1"])
        c.op("dve", lambda e: e.tensor_copy(out=mid[:], in_=r1[:]), reads=["pf_r1"], writes=["pf_mid"])
        c.op("dve", lambda e: e.tensor_tensor(out=r1[:], in0=r1[:], in1=mid[:], op=ALU.subtract),
             reads=["pf_r1", "pf_mid"], writes=["pf_r1"])
        c.op("dve", lambda e: e.tensor_copy(out=lo[:], in_=r1[:]), reads=["pf_r1"], writes=["pf_lo"])
        c.op("dve", lambda e: e.tensor_scalar(out=nhi[:], in0=hi[:], scalar1=-1.0, scalar2=None, op0=ALU.mult),
             reads=["pf_hi"], writes=["pf_nhi"])
        c.dma(fxq[:, t0:t0 + PT], nhi[:], reads=["pf_nhi"], writes=["fxq"])
        c.dma(fxk[:, 0, t0:t0 + PT], hi[:], reads=["pf_hi"], writes=["fxk"])
        c.dma(fxk[:, 1, t0:t0 + PT], mid[:], reads=["pf_mid"], writes=["fxk"])
        c.dma(fxk[:, 2, t0:t0 + PT], lo[:], reads=["pf_lo"], writes=["fxk"])
    negm = c.sb("pf_negm", [128, 4, 512], BF16)
    identb = c.sb("pf_identb", [128, 128], BF16)
    ones64 = c.sb("pf_ones64", [128, 64], BF16)
    c.dma(negm[:], negmask_d, writes=["pf_negm"])
    c.dma(identb[:], identb_d, writes=["pf_identb"])
    c.op("pool", lambda e: e.memset(ones64[:], 1.0), writes=["pf_ones64"])
    qp = [c.sb("pf_qp%d" % i, [128, T], BF16) for i in range(2)]
    kp = [c.sb("pf_kp%d" % i, [128, T], BF16) for i in range(2)]
    vT = [c.sb("pf_vT%d" % i, [64, T], BF16) for i in range(2)]
    vt = [c.sb("pf_vt%d" % i, [128, NT, 128], BF16) for i in range(2)]
    att = [c.sb("pf_att%d" % i, [128, 512], BF16) for i in range(3)]
    rec = c.sb("pf_rec", [64, 512], F32)
    rech = c.sb("pf_rech", [128, 512], F32)
    sel = c.sb("pf_sel", [128, 128], F32)
    c.dma(sel[:], sel_d, writes=["pf_sel"])
    c.op("pool", lambda e: e.memset(rech[0:64, :], 0.0), writes=["pf_rech"])
    for i in range(2):
        c.op("pool", lambda e: e.memset(qp[i][64:128, :], 0.0), writes=["pf_qp%d" % i])
        c.op("pool", lambda e: e.memset(kp[i][64:128, :], 0.0), writes=["pf_kp%d" % i])
        c.op("pool", lambda e: e.memset(vt[i][:, :, 64:128], 1.0), writes=["pf_vt%d" % i])
    o_ = c.sb("pf_o", [64, 512], F32)
    zt = [c.sb("pf_zt%d" % i, [64, 512], BF16) for i in range(2)]
    yo = [c.sb("pf_yo%d" % i, [64, 512], BF16) for i in range(2)]
    psz = [c.ps("pf_psz%d" % i, [128, 512]) for i in range(3)]
    psn = [c.ps("pf_psn%d" % i, [128, 512]) for i in range(2)]
    psd = [c.ps("pf_psd%d" % i, [128, 512]) for i in range(2)]
    pst = c.ps("pf_pst", [128, 8, 64], BF16)

    def setup(h):
        b = h % 2
        q_, k_, vT_, vt_ = qp[b], kp[b], vT[b], vt[b]
        qk, kk, vTk, vtk = "pf_qp%d" % b, "pf_kp%d" % b, "pf_vT%d" % b, "pf_vt%d" % b
        c.op("pool", lambda e: e.memset(q_[64:68, :], 1.0), writes=[qk])
        c.op("pool", lambda e: e.memset(k_[64:68, :], 1.0), writes=[kk])
        c.dma(q_[0:64, :], projT[O_QD + h * 64:O_QD + (h + 1) * 64, :], writes=[qk])
        c.dma(k_[0:64, :], projT[O_KD + h * 64:O_KD + (h + 1) * 64, :], writes=[kk])
        c.dma(q_[64:65, :], fxq[h:h + 1, :], reads=["fxq"], writes=[qk])
        c.dma(k_[65:68, :], fxk[h, :, :], reads=["fxk"], writes=[kk])
        c.dma(vT_[:], projT[O_VD + h * 64:O_VD + (h + 1) * 64, :], writes=[vTk])
        for g in range(0, NT, 8):
            n = min(8, NT - g)
            for j in range(n):
                c.op("pe", lambda e: e.transpose(pst[:, j, :], vT_[:, (g + j) * 128:(g + j + 1) * 128], identb[0:64, 0:64]),
                     reads=[vTk, "pf_identb"], writes=["pf_pst"])
            c.op("dve", lambda e: e.tensor_copy(out=vt_[:, g:g + n, 0:64], in_=pst[:, 0:n, :]), reads=["pf_pst"], writes=[vtk])

    def stage1(d):
        h, qb, kt, nk, i = d
        b = h % 2
        t0 = qb * QB; s0 = kt * 128
        p = psz[i % 3]; pk = "pf_psz%d" % (i % 3)
        a = att[i % 3]; ak = "pf_att%d" % (i % 3)
        diag = s0 >= t0
        c.op("pe", lambda e: e.matmul(p[:, 0:QB], kp[b][:, s0:s0 + 128], qp[b][:, t0:t0 + QB], start=True, stop=not diag),
             reads=["pf_qp%d" % b, "pf_kp%d" % b], writes=[pk])
        if diag:
            j = (s0 - t0) // 128
            c.op("pe", lambda e: e.matmul(p[:, 0:QB], identb[:], negm[:, j, 0:QB], start=False, stop=True),
                 reads=["pf_identb", "pf_negm"], writes=[pk])
        c.op("act", lambda e: e.activation(out=a[:, 0:QB], in_=p[:, 0:QB], func=AF.Exp), reads=[pk], writes=[ak])

    def stage2(d):
        h, qb, kt, nk, i = d
        b = h % 2
        t0 = qb * QB
        a = att[i % 3]; ak = "pf_att%d" % (i % 3)
        pn, pnk = psn[qb % 2], "pf_psn%d" % (qb % 2)
        pd, pdk = psd[qb % 2], "pf_psd%d" % (qb % 2)
        c.op("pe", lambda e: e.matmul(pn[:, 0:QB], vt[b][:, kt, :], a[:, 0:QB], start=(kt == 0), stop=(kt == nk - 1)),
             reads=["pf_vt%d" % b, ak], writes=[pnk])
        if kt == nk - 1:
            z_ = zt[qb % 2]; zk = "pf_zt%d" % (qb % 2)
            y_ = yo[qb % 2]; yk = "pf_yo%d" % (qb % 2)
            c.dma(z_[:, 0:QB], projT[O_ZD + h * 64:O_ZD + (h + 1) * 64, t0:t0 + QB], writes=[zk])
            c.op("dve", lambda e: e.reciprocal(out=rech[64:128, 0:QB], in_=pn[64:128, 0:QB]), reads=[pnk], writes=["pf_rech"])
            c.op("pe", lambda e: e.matmul(pd[:, 0:QB], sel[:], rech[:, 0:QB], start=True, stop=True),
                 reads=["pf_sel", "pf_rech"], writes=[pdk])
            c.op("dve", lambda e: e.tensor_copy(out=rec[:, 0:QB], in_=pd[0:64, 0:QB]), reads=[pdk], writes=["pf_rec"])
            c.op("dve", lambda e: e.tensor_tensor(out=o_[:, 0:QB], in0=pn[0:64, 0:QB], in1=rec[:, 0:QB], op=ALU.mult),
                 reads=[pnk, "pf_rec"], writes=["pf_o"])
            c.op("pool", lambda e: e.tensor_tensor(out=y_[:, 0:QB], in0=o_[:, 0:QB], in1=z_[:, 0:QB], op=ALU.mult),
                 reads=["pf_o", zk], writes=[yk])
            c.dma(yT[Y_D + h * 64:Y_D + (h + 1) * 64, t0:t0 + QB], y_[:, 0:QB], reads=[yk])

    blocks = []
    i = 0
    for h in range(NHA):
        for qb in range(T // QB):
            nk = (qb * QB + QB) // 128
            for kt in range(nk):
                blocks.append((h, qb, kt, nk, i)); i += 1
    setup(0)
    n = len(blocks)
    LAG = 2
    for s_ in range(n + LAG):
        if s_ < n:
            stage1(blocks[s_])
        if s_ >= LAG:
            d = blocks[s_ - LAG]
            stage2(d)
            if d[1] == 0 and d[2] == 0 and d[0] + 1 < NHA:
                setup(d[0] + 1)
    c.end_phase()


def phase_sb(c, T, projT, negmask_d, mask01_d, identb_d, negu_d, yT):
    nc = c.nc
    c.begin_phase()
    QB = min(512, T)
    NT = T // 128
    negm = c.sb("sb_negm", [128, 4, 512], BF16)
    m01 = c.sb("sb_m01", [128, 4, 512], BF16)
    identb = c.sb("sb_identb", [128, 128], BF16)
    negu = c.sb("sb_negu", [128, 128], BF16)
    negones = c.sb("sb_negones", [128, 128], BF16)
    c.dma(negm[:], negmask_d, writes=["sb_negm"])
    c.dma(m01[:], mask01_d, writes=["sb_m01"])
    c.dma(identb[:], identb_d, writes=["sb_identb"])
    c.dma(negu[:], negu_d, writes=["sb_negu"])
    c.op("pool", lambda e: e.memset(negones[:], -1.0), writes=["sb_negones"])
    qp = [c.sb("sb_qp%d" % i, [128, T], BF16) for i in range(2)]
    kp = [c.sb("sb_kp%d" % i, [128, T], BF16) for i in range(2)]
    vT = [c.sb("sb_vT%d" % i, [64, T], BF16) for i in range(2)]
    vt = [c.sb("sb_vt%d" % i, [128, NT, 128], BF16) for i in range(2)]
    for i in range(2):
        c.op("pool", lambda e: e.memset(qp[i][64:128, :], 0.0), writes=["sb_qp%d" % i])
        c.op("pool", lambda e: e.memset(kp[i][64:128, :], 0.0), writes=["sb_kp%d" % i])
        c.op("pool", lambda e: e.memset(vt[i][:, :, 64:128], 0.0), writes=["sb_vt%d" % i])
    ee = [c.sb("sb_e%d" % i, [128, 512], F32) for i in range(2)]
    sp = [c.sb("sb_sp%d" % i, [128, 512], BF16) for i in range(3)]
    att = [c.sb("sb_att%d" % i, [128, 512], BF16) for i in range(3)]
    sl = [c.sb("sb_sl%d" % i, [128, 512], BF16) for i in range(2)]
    zt = [c.sb("sb_zt%d" % i, [64, 512], BF16) for i in range(2)]
    yo = [c.sb("sb_yo%d" % i, [64, 512], BF16) for i in range(2)]
    psa = [c.ps("sb_psa%d" % i, [128, 512]) for i in range(2)]
    psb = [c.ps("sb_psb%d" % i, [128, 512]) for i in range(2)]
    pso = [c.ps("sb_pso%d" % i, [128, 512]) for i in range(2)]
    pst = c.ps("sb_pst", [128, 8, 64], BF16)

    def setup(h):
        b = h % 2
        q_, k_, vT_, vt_ = qp[b], kp[b], vT[b], vt[b]
        qk, kk, vTk, vtk = "sb_qp%d" % b, "sb_kp%d" % b, "sb_vT%d" % b, "sb_vt%d" % b
        c.dma(q_[0:64, :], projT[O_QC + h * 64:O_QC + (h + 1) * 64, :], writes=[qk])
        c.dma(k_[0:64, :], projT[O_KC + h * 64:O_KC + (h + 1) * 64, :], writes=[kk])
        c.dma(vT_[:], projT[O_VC + h * 64:O_VC + (h + 1) * 64, :], writes=[vTk])
        for g in range(0, NT, 8):
            n = min(8, NT - g)
            for j in range(n):
                c.op("pe", lambda e: e.transpose(pst[:, j, :], vT_[:, (g + j) * 128:(g + j + 1) * 128], identb[0:64, 0:64]),
                     reads=[vTk, "sb_identb"], writes=["sb_pst"])
            c.op("dve", lambda e: e.tensor_copy(out=vt_[:, g:g + n, 0:64], in_=pst[:, 0:n, :]), reads=["sb_pst"], writes=[vtk])

    def stA(d):
        h, qb, kt, nk, i, si = d
        b = h % 2
        t0 = qb * QB; s0 = kt * 128
        pa = psa[i % 2]; pak = "sb_psa%d" % (i % 2)
        e_ = ee[i % 2]; ek = "sb_e%d" % (i % 2)
        s_ = sp[i % 3]; sk = "sb_sp%d" % (i % 3)
        c.op("pe", lambda e: e.matmul(pa[:, 0:QB], kp[b][:, s0:s0 + 128], qp[b][:, t0:t0 + QB], start=True, stop=True),
             reads=["sb_qp%d" % b, "sb_kp%d" % b], writes=[pak])
        c.op("act", lambda e: e.activation(out=e_[:, 0:QB], in_=pa[:, 0:QB], func=AF.Exp), reads=[pak], writes=[ek])
        c.op("act", lambda e: e.activation(out=s_[:, 0:QB], in_=e_[:, 0:QB], func=AF.Ln, bias=1.0), reads=[ek], writes=[sk])
        if s0 >= t0:
            j = (s0 - t0) // 128
            c.op("pool", lambda e: e.tensor_tensor(out=s_[:, 0:QB], in0=s_[:, 0:QB], in1=m01[:, j, 0:QB], op=ALU.mult),
                 reads=[sk, "sb_m01"], writes=[sk])

    def stB(d):
        h, qb, kt, nk, i, si = d
        b = h % 2
        t0 = qb * QB; s0 = kt * 128
        first = kt == nk - 1
        diag = s0 >= t0
        pb = psb[i % 2]; pbk = "sb_psb%d" % (i % 2)
        s_ = sp[i % 3]; sk = "sb_sp%d" % (i % 3)
        a = att[i % 3]; ak = "sb_att%d" % (i % 3)
        c.op("pe", lambda e: e.matmul(pb[:, 0:QB], kp[b][:, s0:s0 + 128], qp[b][:, t0:t0 + QB], start=True, stop=False),
             reads=["sb_qp%d" % b, "sb_kp%d" % b], writes=[pbk])
        if diag:
            j = (s0 - t0) // 128
            c.op("pe", lambda e: e.matmul(pb[:, 0:QB], identb[:], negm[:, j, 0:QB], start=False, stop=False),
                 reads=["sb_identb", "sb_negm"], writes=[pbk])
        sl_ = sl[si % 2]; slk = "sb_sl%d" % (si % 2)
        if not first:
            c.op("pe", lambda e: e.matmul(pb[:, 0:QB], negones[:], sl_[:, 0:QB], start=False, stop=False),
                 reads=["sb_negones", slk], writes=[pbk])
        c.op("pe", lambda e: e.matmul(pb[:, 0:QB], negu[:], s_[:, 0:QB], start=False, stop=True),
             reads=["sb_negu", sk], writes=[pbk])
        c.op("act", lambda e: e.activation(out=a[:, 0:QB], in_=pb[:, 0:QB], func=AF.Exp), reads=[pbk], writes=[ak])
        if kt > 0:
            nsl = sl[(si + 1) % 2]; nslk = "sb_sl%d" % ((si + 1) % 2)
            if first:
                c.op("dve", lambda e: e.tensor_copy(out=nsl[:, 0:QB], in_=s_[:, 0:QB]), reads=[sk], writes=[nslk])
            else:
                c.op("dve", lambda e: e.tensor_tensor(out=nsl[:, 0:QB], in0=sl_[:, 0:QB], in1=s_[:, 0:QB], op=ALU.add),
                     reads=[slk, sk], writes=[nslk])

    def stC(d):
        h, qb, kt, nk, i, si = d
        b = h % 2
        t0 = qb * QB
        first = kt == nk - 1
        a = att[i % 3]; ak = "sb_att%d" % (i % 3)
        po, pok = pso[qb % 2], "sb_pso%d" % (qb % 2)
        c.op("pe", lambda e: e.matmul(po[:, 0:QB], vt[b][:, kt, :], a[:, 0:QB], start=first, stop=(kt == 0)),
             reads=["sb_vt%d" % b, ak], writes=[pok])
        if kt == 0:
            z_ = zt[qb % 2]; zk = "sb_zt%d" % (qb % 2)
            y_ = yo[qb % 2]; yk = "sb_yo%d" % (qb % 2)
            c.dma(z_[:, 0:QB], projT[O_ZC + h * 64:O_ZC + (h + 1) * 64, t0:t0 + QB], writes=[zk])
            c.op("dve", lambda e: e.tensor_tensor(out=y_[:, 0:QB], in0=po[0:64, 0:QB], in1=z_[:, 0:QB], op=ALU.mult),
                 reads=[pok, zk], writes=[yk])
            c.dma(yT[Y_C + h * 64:Y_C + (h + 1) * 64, t0:t0 + QB], y_[:, 0:QB], reads=[yk])

    blocks = []
    i = 0
    si = 0
    for h in range(NHA):
        for qb in range(T // QB):
            nk = (qb * QB + QB) // 128
            for kt in range(nk - 1, -1, -1):
                blocks.append((h, qb, kt, nk, i, si)); i += 1
                if kt > 0:
                    si += 1
    setup(0)
    n = len(blocks)
    LB, LC = 2, 3
    for s_i in range(n + LC):
        if s_i < n:
            stA(blocks[s_i])
        if LB <= s_i < n + LB:
            stB(blocks[s_i - LB])
        if s_i >= LC:
            d = blocks[s_i - LC]
            stC(d)
            if d[1] == 0 and d[2] == d[3] - 1 and d[0] + 1 < NHA:
                setup(d[0] + 1)
    c.end_phase()


def phase_gdn_pre(c, T, projT, cw4_d, ident_d, ones_f, gqkv):
    nc = c.nc
    c.begin_phase()
    TT = min(512, T)
    cw4 = c.sb("g0_cw4", [128, 3 * NHG, 4], F32)
    ident = c.sb("g0_ident", [128, 128], F32)
    ones = c.sb("g0_ones", [128, 128], F32)
    dg = c.sb("g0_dg", [128, 12 * NHG, 128], BF16)
    c.dma(cw4[:], cw4_d, writes=["g0_cw4"])
    c.dma(ident[:], ident_d, writes=["g0_ident"])
    c.dma(ones[:], ones_f, writes=["g0_ones"])
    for ch in range(3 * NHG):
        for k in range(4):
            c.op("dve", lambda e: e.tensor_scalar(out=dg[:, ch * 4 + k, :], in0=ident[:], scalar1=cw4[:, ch, k:k + 1],
                                                  scalar2=None, op0=ALU.mult),
                 reads=["g0_cw4", "g0_ident"], writes=["g0_dg"])
    hq = [c.sb("g0_hq%d" % i, [128, 4 + T], BF16) for i in range(2)]
    cs = [c.sb("g0_c%d" % i, [128, TT], F32) for i in range(2)]
    sq = [c.sb("g0_sq%d" % i, [128, TT], F32) for i in range(2)]
    rt = [c.sb("g0_rt%d" % i, [128, TT], F32) for i in range(2)]
    oo = [c.sb("g0_o%d" % i, [128, TT], F32) for i in range(2)]
    ps = [c.ps("g0_ps%d" % i, [128, 512]) for i in range(2)]
    pq = [c.ps("g0_pq%d" % i, [128, 512]) for i in range(2)]
    tiles = [(ch, tt) for ch in range(3 * NHG) for tt in range(T // TT)]

    def p1(i):
        ch, tt = tiles[i]
        t0 = tt * TT
        h_ = hq[ch % 2]; hk = "g0_hq%d" % (ch % 2)
        if tt == 0:
            c.op("pool", lambda e: e.memset(h_[:, 0:4], 0.0), writes=[hk])
            c.dma(h_[:, 4:4 + T], projT[O_QA + ch * 128:O_QA + (ch + 1) * 128, :], writes=[hk])
        i2 = i % 2
        p = ps[i2]; pk = "g0_ps%d" % i2
        for k in range(4):
            c.op("pe", lambda e: e.matmul(p[:, 0:TT], dg[:, ch * 4 + k, :], h_[:, 1 + t0 + k:1 + t0 + k + TT],
                                          start=(k == 0), stop=(k == 3)),
                 reads=["g0_dg", hk], writes=[pk])
        c_ = cs[i2]; ck = "g0_c%d" % i2
        c.op("act", lambda e: e.activation(out=c_[:], in_=p[:, 0:TT], func=AF.Silu), reads=[pk], writes=[ck])

    def p2(i):
        ch, tt = tiles[i]
        t0 = tt * TT
        i2 = i % 2
        c_ = cs[i2]; ck = "g0_c%d" % i2
        if ch < 2 * NHG:
            s_ = sq[i2]; sk = "g0_sq%d" % i2
            c.op("dve", lambda e: e.tensor_tensor(out=s_[:], in0=c_[:], in1=c_[:], op=ALU.mult), reads=[ck], writes=[sk])
            q = pq[i2]; qk = "g0_pq%d" % i2
            c.op("pe", lambda e: e.matmul(q[:, 0:TT], ones[:], s_[:], start=True, stop=True),
                 reads=["g0_ones", sk], writes=[qk])
            r_ = rt[i2]; rk = "g0_rt%d" % i2
            c.op("dve", lambda e: e.tensor_scalar(out=r_[:], in0=q[:, 0:TT], scalar1=1e-6, scalar2=None, op0=ALU.add),
                 reads=[qk], writes=[rk])
            c.op("act", lambda e: e.activation(out=r_[:], in_=r_[:], func=AF.Sqrt), reads=[rk], writes=[rk])
            c.op("dve", lambda e: e.reciprocal(out=r_[:], in_=r_[:]), reads=[rk], writes=[rk])
            o_ = oo[i2]; ok = "g0_o%d" % i2
            sc = 128 ** -0.5 if ch < NHG else 1.0
            c.op("dve", lambda e: e.scalar_tensor_tensor(out=o_[:], in0=c_[:], scalar=sc, in1=r_[:], op0=ALU.mult,
                                                         op1=ALU.mult), reads=[ck, rk], writes=[ok])
            c.dma(gqkv[ch, :, t0:t0 + TT], o_[:], reads=[ok], writes=["gqkv"])
        else:
            c.dma(gqkv[ch, :, t0:t0 + TT], c_[:], reads=[ck], writes=["gqkv"])
    p1(0)
    for i in range(len(tiles)):
        if i + 1 < len(tiles):
            p1(i + 1)
        p2(i)
    c.end_phase()


def phase_gdn(c, T, projT, smallT, gqkv, alog_d, dtb_d, gn_d, ident_d, ones_f, triu_d, sl_d, mks_d, mki_d, yT):
    nc = c.nc
    c.begin_phase()
    NC = T // 128
    GS = 4 if NC >= 4 else NC
    W = GS * 128
    ident = c.sb("g_ident", [128, 128], F32)
    ones = c.sb("g_ones", [128, 128], F32)
    triu = c.sb("g_triu", [128, 128], F32)
    slm = c.sb("g_sl", [128, 128], F32)
    mks = c.sb("g_mks", [128, GS, 128], F32)
    mki = c.sb("g_mki", [128, GS, 128], F32)
    identg = c.sb("g_identg", [128, GS, 128], F32)
    gnb = c.sb("g_gnb", [128, 128], F32)
    alog = c.sb("g_alog", [128, NHG], F32)
    dtb = c.sb("g_dtb", [128, NHG], F32)
    nea = c.sb("g_nea", [128, NHG], F32)
    c.dma(ident[:], ident_d, writes=["g_ident"])
    c.dma(ones[:], ones_f, writes=["g_ones"])
    c.dma(triu[:], triu_d, writes=["g_triu"])
    c.dma(slm[:], sl_d, writes=["g_sl"])
    for g in range(GS):
        c.dma(mks[:, g, :], mks_d, writes=["g_mks"])
        c.dma(mki[:, g, :], mki_d, writes=["g_mki"])
        c.dma(identg[:, g, :], ident_d, writes=["g_identg"])
    c.dma(gnb[:], gn_d, writes=["g_gnb"])
    c.dma(alog[:], alog_d, writes=["g_alog"])
    c.dma(dtb[:], dtb_d, writes=["g_dtb"])
    c.op("act", lambda e: e.activation(out=nea[:], in_=alog[:], func=AF.Exp), reads=["g_alog"], writes=["g_nea"])
    c.op("dve", lambda e: e.tensor_scalar(out=nea[:], in0=nea[:], scalar1=-1.0, scalar2=None, op0=ALU.mult),
         reads=["g_nea"], writes=["g_nea"])
    banks = [c.ps("g_pb%d" % i, [128, GS, 128]) for i in range(4)]
    pscan_b = [c.ps("g_pscan%d" % i, [128, 512]) for i in range(4)]

    class _PS:
        def __getitem__(self, idx):
            return pscan_b[idx[1]][:, 0:128]
    pscan = _PS()
    bi = [0]

    def bank():
        i = bi[0] % 4
        bi[0] += 1
        return banks[i], "g_pb%d" % i

    sm = c.sb("g_sm", [2 * NHG, T], F32)
    c.dma(sm[:], smallT[0:2 * NHG, :], writes=["g_sm"])
    abt = c.sb("g_abt", [128, NC, 2 * NHG], F32)
    for n0 in range(0, NC, 64):
        nn = min(64, NC - n0)
        pbs, pbsk = bank()
        psmall = pbs[:].rearrange("p g c -> p (g c)")
        for n in range(nn):
            c.op("pe", lambda e: e.transpose(psmall[:, n * 4:(n + 1) * 4], sm[:, (n0 + n) * 128:(n0 + n + 1) * 128],
                                             ident[0:4, 0:4]), reads=["g_sm", "g_ident"], writes=[pbsk])
        c.op("dve", lambda e: e.tensor_copy(out=abt[:, n0:n0 + nn, :],
                                            in_=psmall[:, 0:nn * 4].rearrange("p (n k) -> p n k", k=4)),
             reads=[pbsk], writes=["g_abt"])

    if DBG_STOP == 1:
        c.end_phase(); return

    def t2(name):
        return c.sb(name, [128, NC], F32)
    gg, beta, gc, gl, egc, egl, kdf, bgc, tmpn = [t2("g_" + n) for n in
                                                  ("gg", "beta", "gc", "gl", "egc", "egl", "kdf", "bgc", "tmpn")]

    def grp(name, n=2):
        return [c.sb("%s%d" % (name, i), [128, GS, 128], F32) for i in range(n)]
    kT, qT, vT = grp("g_kT"), grp("g_qT"), grp("g_vT")
    ktok, vtok = grp("g_ktok", 1)[0], grp("g_vtok", 1)[0]
    trig, E, decs, deci, L, Aq, AqT = [grp("g_" + n, 1)[0] for n in ("trig", "E", "decs", "deci", "L", "Aq", "AqT")]
    X, Y = grp("g_X"), grp("g_Y")
    R = grp("g_R", 1)[0]
    vb, kbg, kdec, u_, wT = [grp("g_" + n, 1)[0] for n in ("vb", "kbg", "kdec", "u", "wT")]
    o_, osq, on = [grp("g_" + n, 1)[0] for n in ("o", "osq", "on")]
    vnew = [c.sb("g_vnew%d" % i, [128, 128], F32) for i in range(2)]
    tq = [c.sb("g_tq%d" % i, [128, 128], F32) for i in range(2)]
    S = [c.sb("g_S%d" % i, [128, 128], F32) for i in range(2)]
    rs = c.sb("g_rs", [128, GS], F32)
    zt = [c.sb("g_zt%d" % i, [128, W], BF16) for i in range(2)]
    yo = [c.sb("g_yo%d" % i, [128, W], BF16) for i in range(2)]

    for h in range(NHG):
        c.op("act", lambda e: e.activation(out=tmpn[:], in_=abt[:, :, h], func=AF.Exp, bias=dtb[:, h:h + 1]),
             reads=["g_abt", "g_dtb"], writes=["g_tmpn"])
        c.op("act", lambda e: e.activation(out=tmpn[:], in_=tmpn[:], func=AF.Ln, bias=1.0), reads=["g_tmpn"], writes=["g_tmpn"])
        c.op("dve", lambda e: e.tensor_scalar(out=gg[:], in0=tmpn[:], scalar1=nea[:, h:h + 1], scalar2=None, op0=ALU.mult),
             reads=["g_tmpn", "g_nea"], writes=["g_gg"])
        c.op("act", lambda e: e.activation(out=beta[:], in_=abt[:, :, NHG + h], func=AF.Sigmoid), reads=["g_abt"], writes=["g_beta"])
        pbs, pbsk = bank()
        psmall = pbs[:].rearrange("p g c -> p (g c)")
        c.op("pe", lambda e: e.matmul(psmall[:, 0:NC], triu[:], gg[:], start=True, stop=True),
             reads=["g_triu", "g_gg"], writes=[pbsk])
        c.op("dve", lambda e: e.tensor_copy(out=gc[:], in_=psmall[:, 0:NC]), reads=[pbsk], writes=["g_gc"])
        pbs, pbsk = bank()
        psmall = pbs[:].rearrange("p g c -> p (g c)")
        c.op("pe", lambda e: e.matmul(psmall[:, 0:NC], ones[:], gg[:], start=True, stop=True),
             reads=["g_ones", "g_gg"], writes=[pbsk])
        c.op("dve", lambda e: e.tensor_copy(out=gl[:], in_=psmall[:, 0:NC]), reads=[pbsk], writes=["g_gl"])
        c.op("act", lambda e: e.activation(out=egc[:], in_=gc[:], func=AF.Exp), reads=["g_gc"], writes=["g_egc"])
        c.op("act", lambda e: e.activation(out=egl[:], in_=gl[:], func=AF.Exp), reads=["g_gl"], writes=["g_egl"])
        c.op("dve", lambda e: e.tensor_tensor(out=kdf[:], in0=gl[:], in1=gc[:], op=ALU.subtract),
             reads=["g_gl", "g_gc"], writes=["g_kdf"])
        c.op("act", lambda e: e.activation(out=kdf[:], in_=kdf[:], func=AF.Exp), reads=["g_kdf"], writes=["g_kdf"])
        c.op("dve", lambda e: e.tensor_tensor(out=bgc[:], in0=beta[:], in1=egc[:], op=ALU.mult),
             reads=["g_beta", "g_egc"], writes=["g_bgc"])
        c.op("pool", lambda e: e.memset(S[0][:], 0.0), writes=["g_S0"])
        sidx = 0
        if DBG_STOP == 2:
            c.end_phase(); return
        NG = NC // GS

        def gload(h_, gi_):
            t0_ = gi_ * W
            b2_ = gi_ % 2
            c.dma(qT[b2_][:], gqkv[h_, :, t0_:t0_ + W].rearrange("p (g c) -> p g c", c=128), reads=["gqkv"],
                  writes=["g_qT%d" % b2_])
            c.dma(kT[b2_][:], gqkv[NHG + h_, :, t0_:t0_ + W].rearrange("p (g c) -> p g c", c=128), reads=["gqkv"],
                  writes=["g_kT%d" % b2_])
            c.dma(vT[b2_][:], gqkv[2 * NHG + h_, :, t0_:t0_ + W].rearrange("p (g c) -> p g c", c=128), reads=["gqkv"],
                  writes=["g_vT%d" % b2_])
            c.dma(zt[b2_][:], projT[O_ZA + h_ * 128:O_ZA + (h_ + 1) * 128, t0_:t0_ + W], writes=["g_zt%d" % b2_])
        if h == 0 or NG % 2 == 1:
            gload(h, 0)
        for gi in range(NG):
            t0 = gi * W
            b2 = gi % 2
            kT_, qT_, vT_ = kT[b2], qT[b2], vT[b2]
            kTk, qTk, vTk = "g_kT%d" % b2, "g_qT%d" % b2, "g_vT%d" % b2
            z_ = zt[b2]; zk = "g_zt%d" % b2
            if NG % 2 == 0 or NG == 1:
                if gi + 1 < NG:
                    gload(h, gi + 1)
                elif h + 1 < NHG and NG % 2 == 0:
                    gload(h + 1, 0)
            pb, pbk = bank()
            for g in range(GS):
                c.op("pe", lambda e: e.transpose(pb[:, g, :], kT_[:, g, :], ident[:]), reads=[kTk, "g_ident"], writes=[pbk])
            c.op("act", lambda e: e.activation(out=ktok[:], in_=pb[:], func=AF.Identity), reads=[pbk], writes=["g_ktok"])
            pb, pbk = bank()
            for g in range(GS):
                c.op("pe", lambda e: e.transpose(pb[:, g, :], vT_[:, g, :], ident[:]), reads=[vTk, "g_ident"], writes=[pbk])
            c.op("dve", lambda e: e.tensor_copy(out=vtok[:], in_=pb[:]), reads=[pbk], writes=["g_vtok"])
            for g in range(GS):
                n = gi * GS + g
                c.op("dve", lambda e: e.tensor_scalar(out=trig[:, g, :], in0=triu[:], scalar1=gg[:, n:n + 1], scalar2=None,
                                                      op0=ALU.mult), reads=["g_triu", "g_gg"], writes=["g_trig"])
            pb, pbk = bank()
            for g in range(GS):
                c.op("pe", lambda e: e.matmul(pb[:, g, :], trig[:, g, :], slm[:], start=True, stop=True),
                     reads=["g_trig", "g_sl"], writes=[pbk])
            c.op("act", lambda e: e.activation(out=E[:], in_=pb[:], func=AF.Exp), reads=[pbk], writes=["g_E"])
            c.op("pool", lambda e: e.tensor_tensor(out=decs[:], in0=E[:], in1=mks[:], op=ALU.mult),
                 reads=["g_E", "g_mks"], writes=["g_decs"])
            c.op("pool", lambda e: e.tensor_tensor(out=deci[:], in0=E[:], in1=mki[:], op=ALU.mult),
                 reads=["g_E", "g_mki"], writes=["g_deci"])
            pb, pbk = bank()
            for g in range(GS):
                c.op("pe", lambda e: e.matmul(pb[:, g, :], kT_[:, g, :], kT_[:, g, :], start=True, stop=True),
                     reads=[kTk], writes=[pbk])
            for g in range(GS):
                n = gi * GS + g
                c.op("dve", lambda e: e.scalar_tensor_tensor(out=L[:, g, :], in0=pb[:, g, :], scalar=beta[:, n:n + 1],
                                                             in1=decs[:, g, :], op0=ALU.mult, op1=ALU.mult),
                     reads=[pbk, "g_beta", "g_decs"], writes=["g_L"])
            pb, pbk = bank()
            for g in range(GS):
                c.op("pe", lambda e: e.matmul(pb[:, g, :], qT_[:, g, :], kT_[:, g, :], start=True, stop=True),
                     reads=[qTk, kTk], writes=[pbk])
            c.op("dve", lambda e: e.tensor_tensor(out=Aq[:], in0=pb[:], in1=deci[:], op=ALU.mult),
                 reads=[pbk, "g_deci"], writes=["g_Aq"])
            if DBG_STOP == 3:
                c.end_phase(); return
            pb, pbk = bank()
            for g in range(GS):
                c.op("pe", lambda e: e.transpose(pb[:, g, :], Aq[:, g, :], ident[:]), reads=["g_Aq", "g_ident"], writes=[pbk])
            if DBG_STOP == 29:
                c.end_phase(); return
            c.op("act", lambda e: e.activation(out=AqT[:], in_=pb[:], func=AF.Identity), reads=[pbk], writes=["g_AqT"])
            if DBG_STOP == 30:
                c.end_phase(); return
            pb, pbk = bank()
            for g in range(GS):
                c.op("pe", lambda e: e.transpose(pb[:, g, :], L[:, g, :], ident[:]), reads=["g_L", "g_ident"], writes=[pbk])
            if DBG_STOP == 305:
                c.end_phase(); return
            c.op("act", lambda e: e.activation(out=X[0][:], in_=pb[:], func=AF.Identity), reads=[pbk], writes=["g_X0"])
            if DBG_STOP == 306:
                c.end_phase(); return
            c.op("dve", lambda e: e.tensor_tensor(out=R[:], in0=identg[:], in1=X[0][:], op=ALU.subtract),
                 reads=["g_X0", "g_identg"], writes=["g_R"])
            Yc, Yk = L, "g_L"
            Xc, Xk = X[0], "g_X0"
            if DBG_STOP == 31:
                c.end_phase(); return
            for lvl in range(6):
                if DBG_STOP == 32 + lvl and lvl > 0:
                    c.end_phase(); return
                last = lvl == 5
                nX, nXk = X[(lvl + 1) % 2], "g_X%d" % ((lvl + 1) % 2)
                nY, nYk = Y[lvl % 2], "g_Y%d" % (lvl % 2)
                if not last:
                    pbx, pbxk = bank()
                    for g in range(GS):
                        c.op("pe", lambda e: e.matmul(pbx[:, g, :], Yc[:, g, :], Xc[:, g, :], start=True, stop=True),
                             reads=[Yk, Xk], writes=[pbxk])
                pby, pbyk = bank()
                for g in range(GS):
                    c.op("pe", lambda e: e.matmul(pby[:, g, :], Xc[:, g, :], Yc[:, g, :], start=True, stop=True),
                         reads=[Yk, Xk], writes=[pbyk])
                c.op("dve", lambda e: e.tensor_copy(out=nY[:], in_=pby[:]), reads=[pbyk], writes=[nYk])
                if not last:
                    c.op("act", lambda e: e.activation(out=nX[:], in_=pbx[:], func=AF.Identity), reads=[pbxk], writes=[nXk])
                pbr, pbrk = bank()
                for g in range(GS):
                    c.op("pe", lambda e: e.matmul(pbr[:, g, :], nY[:, g, :], R[:, g, :], start=True, stop=True),
                         reads=[nYk, "g_R"], writes=[pbrk])
                c.op("dve", lambda e: e.tensor_tensor(out=R[:], in0=R[:], in1=pbr[:], op=ALU.add),
                     reads=["g_R", pbrk], writes=["g_R"])
                Yc, Yk = nY, nYk
                Xc, Xk = nX, nXk
            if DBG_STOP == 4:
                c.end_phase(); return
            for g in range(GS):
                n = gi * GS + g
                c.op("pool", lambda e: e.tensor_scalar(out=vb[:, g, :], in0=vtok[:, g, :], scalar1=beta[:, n:n + 1],
                                                       scalar2=None, op0=ALU.mult), reads=["g_vtok", "g_beta"], writes=["g_vb"])
                c.op("pool", lambda e: e.tensor_scalar(out=kbg[:, g, :], in0=ktok[:, g, :], scalar1=bgc[:, n:n + 1],
                                                       scalar2=None, op0=ALU.mult), reads=["g_ktok", "g_bgc"], writes=["g_kbg"])
                c.op("pool", lambda e: e.tensor_scalar(out=kdec[:, g, :], in0=ktok[:, g, :], scalar1=kdf[:, n:n + 1],
                                                       scalar2=None, op0=ALU.mult), reads=["g_ktok", "g_kdf"], writes=["g_kdec"])
            pb, pbk = bank()
            for g in range(GS):
                c.op("pe", lambda e: e.matmul(pb[:, g, :], R[:, g, :], vb[:, g, :], start=True, stop=True),
                     reads=["g_R", "g_vb"], writes=[pbk])
            c.op("act", lambda e: e.activation(out=u_[:], in_=pb[:], func=AF.Identity), reads=[pbk], writes=["g_u"])
            pb, pbk = bank()
            for g in range(GS):
                c.op("pe", lambda e: e.matmul(pb[:, g, :], kbg[:, g, :], R[:, g, :], start=True, stop=True),
                     reads=["g_R", "g_kbg"], writes=[pbk])
            c.op("dve", lambda e: e.tensor_copy(out=wT[:], in_=pb[:]), reads=[pbk], writes=["g_wT"])
            for g in range(GS):
                n = gi * GS + g
                Sc, Sk = S[sidx % 2], "g_S%d" % (sidx % 2)
                Sn, Snk = S[(sidx + 1) % 2], "g_S%d" % ((sidx + 1) % 2)
                sidx += 1
                vn, vnk = vnew[n % 2], "g_vnew%d" % (n % 2)
                tq_, tqk = tq[n % 2], "g_tq%d" % (n % 2)
                c.op("pe", lambda e: e.matmul(pscan[:, 0, :], wT[:, g, :], Sc[:], start=True, stop=True),
                     reads=["g_wT", Sk], writes=["g_ps0"])
                c.op("dve", lambda e: e.tensor_tensor(out=vn[:], in0=u_[:, g, :], in1=pscan[:, 0, :], op=ALU.subtract),
                     reads=["g_u", "g_ps0"], writes=[vnk])
                c.op("pe", lambda e: e.matmul(pscan[:, 1, :], qT_[:, g, :], Sc[:], start=True, stop=True),
                     reads=[qTk, Sk], writes=["g_ps1"])
                c.op("pe", lambda e: e.matmul(pscan[:, 2, :], AqT[:, g, :], vn[:], start=True, stop=True),
                     reads=["g_AqT", vnk], writes=["g_ps2"])
                c.op("pe", lambda e: e.matmul(pscan[:, 3, :], kdec[:, g, :], vn[:], start=True, stop=True),
                     reads=["g_kdec", vnk], writes=["g_ps3"])
                c.op("act", lambda e: e.activation(out=tq_[:], in_=pscan[:, 1, :], func=AF.Identity, scale=egc[:, n:n + 1]),
                     reads=["g_ps1", "g_egc"], writes=[tqk])
                c.op("dve", lambda e: e.tensor_tensor(out=o_[:, g, :], in0=tq_[:], in1=pscan[:, 2, :], op=ALU.add),
                     reads=[tqk, "g_ps2"], writes=["g_o"])
                c.op("dve", lambda e: e.scalar_tensor_tensor(out=Sn[:], in0=Sc[:], scalar=egl[:, n:n + 1], in1=pscan[:, 3, :],
                                                             op0=ALU.mult, op1=ALU.add),
                     reads=[Sk, "g_egl", "g_ps3"], writes=[Snk])
            if DBG_STOP == 5:
                c.end_phase(); return
            c.op("pool", lambda e: e.tensor_tensor(out=osq[:], in0=o_[:], in1=o_[:], op=ALU.mult), reads=["g_o"], writes=["g_osq"])
            c.op("dve", lambda e: e.tensor_reduce(out=rs[:], in_=osq[:], axis=mybir.AxisListType.X, op=ALU.add),
                 reads=["g_osq"], writes=["g_rs"])
            c.op("dve", lambda e: e.tensor_scalar(out=rs[:], in0=rs[:], scalar1=1.0 / 128, scalar2=EPS, op0=ALU.mult,
                                                  op1=ALU.add), reads=["g_rs"], writes=["g_rs"])
            c.op("act", lambda e: e.activation(out=rs[:], in_=rs[:], func=AF.Sqrt), reads=["g_rs"], writes=["g_rs"])
            c.op("dve", lambda e: e.reciprocal(out=rs[:], in_=rs[:]), reads=["g_rs"], writes=["g_rs"])
            for g in range(GS):
                c.op("dve", lambda e: e.scalar_tensor_tensor(out=on[:, g, :], in0=o_[:, g, :], scalar=rs[:, g:g + 1],
                                                             in1=gnb[:], op0=ALU.mult, op1=ALU.mult),
                     reads=["g_o", "g_rs", "g_gnb"], writes=["g_on"])
            pb, pbk = bank()
            for g in range(GS):
                c.op("pe", lambda e: e.transpose(pb[:, g, :], on[:, g, :], ident[:]), reads=["g_on", "g_ident"], writes=[pbk])
            y_ = yo[b2]; yk = "g_yo%d" % b2
            c.op("dve", lambda e: e.tensor_tensor(out=y_[:], in0=pb[:].rearrange("p g c -> p (g c)"), in1=z_[:], op=ALU.mult),
                 reads=[pbk, zk], writes=[yk])
            c.dma(yT[Y_A + h * 128:Y_A + (h + 1) * 128, t0:t0 + W], y_[:], reads=[yk])
    c.end_phase()


PAIRS = [[0, 1], [2, 3], [4, 5], [6, 7]]


def build(T, nlayers=2, only=None, pairs=PAIRS):
    nc = bass.Bass("TRN2", target_bir_lowering=False)

    def di(n, shape, dt=F32):
        return nc.dram_tensor(n, shape, dt, kind="ExternalInput").ap()

    def ds(n, shape, dt=F32):
        return nc.dram_tensor(n, shape, dt, kind="Internal").ap()
    L = nlayers
    xT = di("xT", [D, T])
    pT = di("pT", [L, 256, T])
    w_in = di("w_in", [L, D, PWL])
    w_small = di("w_small", [L, D, 8])
    w_branch = di("w_branch", [L, 4, 512, 1024])
    w_out = di("w_out", [L, 1024, 1024])
    w_ple = di("w_ple", [L, 256, 1024])
    w_pg = di("w_pg", [L, 1024, 1024])
    b_gate = di("b_gate", [L, 128, 32])
    b_pg = di("b_pg", [L, 128, 8])
    ln_g = di("ln_g", [L, 128, 8])
    ln_b = di("ln_b", [L, 128, 8])
    cw = di("cw", [L, 128, 4, 31])
    cvec = di("cvec", [L, 128, 3, 4])
    cw4 = di("cw4", [L, 128, 3 * NHG, 4])
    alog = di("alog", [L, 128, NHG])
    dtb = di("dtb", [L, 128, NHG])
    gn = di("gn", [L, 128, 128])
    fb = di("fb", [L, NHA, 1])
    ident = di("ident", [128, 128])
    ones_f = di("ones_f", [128, 128])
    identb = di("identb", [128, 128], BF16)
    negm_i = di("negm_i", [128, 4, 512], BF16)
    negm_s = di("negm_s", [128, 4, 512], BF16)
    m01_s = di("m01_s", [128, 4, 512], BF16)
    negu = di("negu", [128, 128], BF16)
    triu = di("triu", [128, 128])
    slm = di("slm", [128, 128])
    mki = di("mki", [128, 128])
    sel64 = di("sel64", [128, 128])
    outT = nc.dram_tensor("outT", [D, T], F32, kind="ExternalOutput").ap()
    projT = ds("projT", [PWL, T], BF16)
    smallT = ds("smallT", [8, T])
    yT = ds("yT", [YL, T], BF16)
    yg = [ds("yg%d" % i, [256, T], BF16) for i in range(6)]
    gqkv = ds("gqkv", [3 * NHG, 128, T])
    fxq = ds("fxq", [NHA, T], BF16)
    fxk = ds("fxk", [NHA, 3, T], BF16)
    xmid = [ds("xmid%d" % i, [D, T]) for i in range(max(L - 1, 1))]
    with ExitStack() as es:
        c = Ctx(nc, es)
        xin = xT
        for l in range(L):
            xo = outT if l == L - 1 else xmid[l]
            on = lambda n: only is None or n in only
            if on("proj"):
                phase_proj(c, T, xin, w_in[l], w_small[l], b_gate[l], projT, smallT)
            if on("conv"):
                phase_conv(c, T, projT, cw[l], cvec[l], ident, ones_f, yT)
            if on("fox"):
                phase_fox(c, T, projT, smallT, fb[l], negm_i, identb, sel64, fxq, fxk, yT)
            if on("sb"):
                phase_sb(c, T, projT, negm_s, m01_s, identb, negu, yT)
            if on("gdn_pre"):
                phase_gdn_pre(c, T, projT, cw4[l], ident, ones_f, gqkv)
            if on("gdn"):
                phase_gdn(c, T, projT, smallT, gqkv, alog[l], dtb[l], gn[l], ident, ones_f, triu, slm, slm, mki, yT)
            if on("out"):
                c.barrier()
                for j, row in enumerate((Y_A, Y_A + 128, Y_C, Y_C + 128, Y_D, Y_D + 128)):
                    c.collective("AllGather", [yT[row:row + 128, :]], [yg[j]], pairs)
                c.barrier()
                phase_out(c, T, xin, pT[l], yT, yg, projT, w_branch[l], w_out[l], w_ple[l],
                          w_pg[l], b_pg[l], ln_g[l], ln_b[l], ones_f, xo)
            xin = xo
        c.finish()
        ninst = c.ninst
    return nc, ninst


def host_inputs(x_b, p_b, w, L, r):
    import ml_dtypes
    bf = lambda a: np.ascontiguousarray(a).astype(ml_dtypes.bfloat16)
    f32 = lambda a: np.ascontiguousarray(a, dtype=np.float32)
    v8 = lambda v: f32(np.stack([v[l].reshape(8, 128).T for l in range(L)]))
    rep = lambda v: f32(np.stack([np.broadcast_to(v[l][None, :], (128, v[l].shape[0])) for l in range(L)]))
    v4 = lambda v: v.reshape(4, 128).T
    ar = np.arange
    gh = np.concatenate([(NHG * r + h) * 128 + ar(128) for h in range(NHG)])
    ah = np.concatenate([(NHA * r + h) * 64 + ar(64) for h in range(NHA)])
    cols = np.concatenate([R_QA + gh, R_KA + gh, R_VA + gh, R_ZA + gh,
                           R_GL + ar(512), R_GG + ar(512), R_ZB + ar(512),
                           R_QC + ah, R_KC + ah, R_VC + ah, R_ZC + ah,
                           R_QD + ah, R_KD + ah, R_VD + ah, R_ZD + ah,
                           R_G + ar(4096)])
    assert cols.shape[0] == PWL
    scols = np.concatenate([R_AA + NHG * r + ar(NHG), R_BA + NHG * r + ar(NHG), R_FD + NHA * r + ar(NHA)])
    gch = np.concatenate([gh, 512 + gh, 1024 + gh])
    s_ = ar(128)[:, None, None]; j_ = ar(4)[None, :, None]; q_ = ar(512)[None, None, :]
    incl = (s_ + 128 * j_ <= q_); strict = (s_ + 128 * j_ < q_)
    jj = ar(128)[:, None]; ss = ar(128)[None, :]
    d = {
        "xT": f32(x_b.T), "pT": f32(np.stack([p_b[l].T for l in range(L)])),
        "w_in": f32(w["w_in"][:L][:, :, cols]), "w_small": f32(w["w_in"][:L][:, :, scols]),
        "w_branch": f32(w["w_branch"][:L]), "w_out": f32(w["w_out"][:L]),
        "w_ple": f32(w["w_ple"][:L]), "w_pg": f32(w["w_ple_gate"][:L]),
        "b_gate": f32(np.stack([w["b_gate"][l].reshape(32, 128).T for l in range(L)])),
        "b_pg": v8(w["b_ple_gate"]), "ln_g": v8(w["ln_g"]), "ln_b": v8(w["ln_b"]),
        "cw": f32(np.stack([w["conv_dw"][l].T.reshape(4, 128, 31).transpose(1, 0, 2) for l in range(L)])),
        "cvec": f32(np.stack([np.stack([v4(w["conv_dw_bias"][l]), v4(w["conv_ln_g"][l]), v4(w["conv_ln_b"][l])], axis=1)
                              for l in range(L)])),
        "cw4": f32(np.stack([w["conv_qkv"][l][:, gch].T.reshape(3 * NHG, 128, 4).transpose(1, 0, 2) for l in range(L)])),
        "alog": rep(w["a_log"][:, NHG * r:NHG * (r + 1)]), "dtb": rep(w["dt_bias"][:, NHG * r:NHG * (r + 1)]),
        "gn": rep(w["gdn_norm"]),
        "fb": f32(np.stack([w["forget_bias"][l][NHA * r:NHA * (r + 1)].reshape(NHA, 1) for l in range(L)])),
        "ident": np.eye(128, dtype=np.float32), "ones_f": np.ones((128, 128), np.float32),
        "identb": bf(np.eye(128, dtype=np.float32)),
        "negm_i": bf(np.where(incl, 0.0, -30000.0).astype(np.float32)),
        "negm_s": bf(np.where(strict, 0.0, -30000.0).astype(np.float32)),
        "m01_s": bf(strict.astype(np.float32)),
        "negu": bf(-(jj >= ss).astype(np.float32)),
        "triu": (jj <= ss).astype(np.float32), "slm": (jj > ss).astype(np.float32), "mki": (jj >= ss).astype(np.float32),
        "sel64": (jj == ss + 64).astype(np.float32),
    }
    return d


_CACHE = {}


def kernel(**inputs):
    x = np.asarray(inputs["x"])
    p = np.asarray(inputs["p"])
    B, T, _ = x.shape
    w = {k: np.asarray(v) for k, v in inputs.items() if k not in ("x", "p")}
    L = w["w_in"].shape[0]
    ncores = 2 * B
    pairs = [[2 * b, 2 * b + 1] for b in range(B)]
    key = (T, L, B)
    if key not in _CACHE:
        _CACHE[key] = build(T, L, pairs=pairs)[0]
    nc = _CACHE[key]
    in_maps = [host_inputs(x[i // 2], p[:, i // 2], w, L, i % 2) for i in range(ncores)]
    res = run_bass_kernel_spmd(nc, in_maps, core_ids=list(range(ncores)))
    out = np.stack([np.asarray(res.results[2 * b]["outT"]).T for b in range(B)])
    return np.ascontiguousarray(out, dtype=np.float32)
```
